# Optimizing a Trainium2 kernel written in Bass

```python
import math
import jax, jax.numpy as jnp
from jax import lax
import numpy as np

D_MODEL = 1024
BATCH = 4
SEQ = 8192
DEPTH = 2
DEC_BATCH = 16
DEC_SEQ = 64
PAST_LEN = 4096

CHUNK = 64
N_MIXERS = 2
N_SSM_LAYERS = (DEPTH + 1) // 2
N_FOX_LAYERS = DEPTH // 2
SSM_GROUP = 16
SSM_GROUPS = D_MODEL // SSM_GROUP
SSM_STATE = 64
SSM_DT_MIN = 1e-3
SSM_DT_MAX = 1e-1
FOX_HEADS = 16
FOX_HEAD_DIM = D_MODEL // FOX_HEADS
FOX_Q_BLOCK = 128
FOX_BIAS_INIT = 3.0
D_FF = 2816
CONV_WIDTH = 3
NORM_EPS = 1e-6
NEG_INF = -1e30

kernel_name = "hybrid_s5_fox_convffn_stream_step"


def rms_norm(x, g):
    x32 = x.astype(jnp.float32)
    y = x32 * lax.rsqrt(jnp.mean(x32 * x32, axis=-1, keepdims=True) + NORM_EPS)
    return (y * g.astype(jnp.float32)).astype(x.dtype)


def _cmul(ar, ai, br, bi):
    return ar * br - ai * bi, ar * bi + ai * br


def _ssm_combine(earlier, later):
    a1r, a1i, b1r, b1i = earlier
    a2r, a2i, b2r, b2i = later
    ar, ai = _cmul(a2r, a2i, a1r, a1i)
    br, bi = _cmul(a2r, a2i, b1r, b1i)
    return ar, ai, br + b2r, bi + b2i


def s5_mixer(u, h0_re, h0_im, a_re, a_im, log_step, b_re, b_im, c_re, c_im, d_skip, w_glu):
    n, l, _ = u.shape
    f32 = jnp.float32
    lam_re = a_re.astype(f32)
    lam_im = a_im.astype(f32)
    dt = jnp.exp(log_step.astype(f32))[:, None]
    mag = jnp.exp(lam_re * dt)
    abar_re = mag * jnp.cos(lam_im * dt)
    abar_im = mag * jnp.sin(lam_im * dt)
    den = lam_re * lam_re + lam_im * lam_im
    nr = abar_re - 1.0
    z_re = (nr * lam_re + abar_im * lam_im) / den
    z_im = (abar_im * lam_re - nr * lam_im) / den
    bb_re, bb_im = _cmul(z_re[..., None], z_im[..., None], b_re.astype(f32), b_im.astype(f32))
    ug = u.astype(f32).reshape(n, l, SSM_GROUPS, SSM_GROUP)
    bu_re = jnp.einsum('nlgh,gph->nlgp', ug, bb_re)
    bu_im = jnp.einsum('nlgh,gph->nlgp', ug, bb_im)
    h0r, h0i = _cmul(abar_re, abar_im, h0_re.astype(f32), h0_im.astype(f32))
    bu_re = bu_re.at[:, 0].add(h0r)
    bu_im = bu_im.at[:, 0].add(h0i)
    a_r = jnp.broadcast_to(abar_re, (1, l) + abar_re.shape)
    a_i = jnp.broadcast_to(abar_im, (1, l) + abar_im.shape)
    _, _, s_re, s_im = lax.associative_scan(_ssm_combine, (a_r, a_i, bu_re, bu_im), axis=1)
    y = (jnp.einsum('nlgp,ghp->nlgh', s_re, c_re.astype(f32))
         - jnp.einsum('nlgp,ghp->nlgh', s_im, c_im.astype(f32)))
    y = y.reshape(n, l, D_MODEL) + d_skip.astype(f32) * u.astype(f32)
    g = jax.nn.gelu(y).astype(u.dtype)
    val, gate = jnp.split(g @ w_glu, 2, axis=-1)
    return val * jax.nn.sigmoid(gate), s_re[:, -1], s_im[:, -1]


def fox_project(u, w_qkvf, b_f):
    n, l, _ = u.shape
    proj = u @ w_qkvf
    q = proj[..., :D_MODEL].reshape(n, l, FOX_HEADS, FOX_HEAD_DIM)
    k = proj[..., D_MODEL:2 * D_MODEL].reshape(n, l, FOX_HEADS, FOX_HEAD_DIM)
    v = proj[..., 2 * D_MODEL:3 * D_MODEL].reshape(n, l, FOX_HEADS, FOX_HEAD_DIM)
    logf = jax.nn.log_sigmoid(proj[..., 3 * D_MODEL:].astype(jnp.float32) + b_f.astype(jnp.float32))
    return q, k, v, logf


def fox_attend(q, k, v, c_q, c_k, q_pos, k_pos):
    s = jnp.einsum('nqhe,nkhe->nhqk', q, k, preferred_element_type=jnp.float32) * (FOX_HEAD_DIM ** -0.5)
    s = s + (jnp.transpose(c_q, (0, 2, 1))[..., :, None] - jnp.transpose(c_k, (0, 2, 1))[..., None, :])
    mask = k_pos[None, :] <= q_pos[:, None]
    s = jnp.where(mask, s, NEG_INF)
    p = jax.nn.softmax(s, axis=-1)
    return jnp.einsum('nhqk,nkhe->nqhe', p.astype(v.dtype), v)


def fox_prompt(u, w_qkvf, b_f, w_o):
    n, l, _ = u.shape
    q, k, v, logf = fox_project(u, w_qkvf, b_f)
    c = jnp.cumsum(logf, axis=1)
    nb = l // FOX_Q_BLOCK
    qb = q.reshape(n, nb, FOX_Q_BLOCK, FOX_HEADS, FOX_HEAD_DIM).transpose(1, 0, 2, 3, 4)
    cb = c.reshape(n, nb, FOX_Q_BLOCK, FOX_HEADS).transpose(1, 0, 2, 3)
    pos = jnp.arange(l)
    pb = pos.reshape(nb, FOX_Q_BLOCK)
    ob = lax.map(lambda blk: fox_attend(blk[0], k, v, blk[1], c, blk[2], pos), (qb, cb, pb))
    o = ob.transpose(1, 0, 2, 3, 4).reshape(n, l, D_MODEL)
    return o @ w_o, k, v, logf


def fox_sample(u, cache_k, cache_v, cache_logf, w_qkvf, b_f, w_o):
    n, t, _ = u.shape
    past = cache_k.shape[1]
    q, k, v, logf = fox_project(u, w_qkvf, b_f)
    k_all = jnp.concatenate([cache_k.astype(k.dtype), k], axis=1)
    v_all = jnp.concatenate([cache_v.astype(v.dtype), v], axis=1)
    c = jnp.cumsum(jnp.concatenate([cache_logf.astype(jnp.float32), logf], axis=1), axis=1)
    k_pos = jnp.arange(past + t)
    q_pos = past + jnp.arange(t)
    o = fox_attend(q, k_all, v_all, c[:, past:], c, q_pos, k_pos).reshape(n, t, D_MODEL)
    return o @ w_o, k, v, logf


def conv_ffn(x, hist, w_up, w_gate, conv_w, conv_b, w_down):
    l = x.shape[1]
    a = x @ w_up
    a_pad = jnp.concatenate([hist.astype(a.dtype), a], axis=1)
    conv = a_pad[:, 0:l] * conv_w[0]
    for j in range(1, CONV_WIDTH):
        conv = conv + a_pad[:, j:j + l] * conv_w[j]
    h = jax.nn.gelu(conv + conv_b) * (x @ w_gate)
    return h @ w_down, a_pad[:, -(CONV_WIDTH - 1):]


def setup_inputs(seed: int = 0) -> dict:
    key = jax.random.key(seed)
    ks = jax.random.split(key, 32)
    f32 = jnp.float32

    def nrm(k, shape, scale):
        return scale * jax.random.normal(k, shape, f32)

    n_idx = jnp.arange(SSM_STATE, dtype=f32)
    ssm_shape = (N_SSM_LAYERS, SSM_GROUPS, SSM_STATE)
    return {
        "x_prompt": nrm(ks[0], (BATCH, SEQ, D_MODEL), 1.0),
        "x_sample": nrm(ks[1], (DEC_BATCH, DEC_SEQ, D_MODEL), 1.0),
        "state_ssm_re": nrm(ks[2], (N_SSM_LAYERS, DEC_BATCH, SSM_GROUPS, SSM_STATE), 0.1),
        "state_ssm_im": nrm(ks[3], (N_SSM_LAYERS, DEC_BATCH, SSM_GROUPS, SSM_STATE), 0.1),
        "cache_fox_k": nrm(ks[4], (N_FOX_LAYERS, DEC_BATCH, PAST_LEN, FOX_HEADS, FOX_HEAD_DIM), 1.0),
        "cache_fox_v": nrm(ks[5], (N_FOX_LAYERS, DEC_BATCH, PAST_LEN, FOX_HEADS, FOX_HEAD_DIM), 1.0),
        "cache_fox_logf": jax.nn.log_sigmoid(FOX_BIAS_INIT + nrm(ks[6], (N_FOX_LAYERS, DEC_BATCH, PAST_LEN, FOX_HEADS), 1.0)),
        "state_ffn_conv": nrm(ks[7], (DEPTH, DEC_BATCH, CONV_WIDTH - 1, D_FF), 1.0),
        "norm_mix": 1.0 + nrm(ks[8], (DEPTH, D_MODEL), 0.05),
        "norm_ffn": 1.0 + nrm(ks[9], (DEPTH, D_MODEL), 0.05),
        "norm_final": 1.0 + nrm(ks[10], (D_MODEL,), 0.05),
        "ssm_a_re": -0.5 + nrm(ks[11], ssm_shape, 0.01),
        "ssm_a_im": jnp.broadcast_to(np.pi * n_idx, ssm_shape) + nrm(ks[12], ssm_shape, 0.01),
        "ssm_log_step": jax.random.uniform(ks[13], (N_SSM_LAYERS, SSM_GROUPS), f32,
                                           minval=math.log(SSM_DT_MIN), maxval=math.log(SSM_DT_MAX)),
        "ssm_b_re": nrm(ks[14], (N_SSM_LAYERS, SSM_GROUPS, SSM_STATE, SSM_GROUP), (2 * SSM_GROUP) ** -0.5),
        "ssm_b_im": nrm(ks[15], (N_SSM_LAYERS, SSM_GROUPS, SSM_STATE, SSM_GROUP), (2 * SSM_GROUP) ** -0.5),
        "ssm_c_re": nrm(ks[16], (N_SSM_LAYERS, SSM_GROUPS, SSM_GROUP, SSM_STATE), SSM_STATE ** -0.5),
        "ssm_c_im": nrm(ks[17], (N_SSM_LAYERS, SSM_GROUPS, SSM_GROUP, SSM_STATE), SSM_STATE ** -0.5),
        "ssm_d": nrm(ks[18], (N_SSM_LAYERS, D_MODEL), 1.0),
        "ssm_w_glu": nrm(ks[19], (N_SSM_LAYERS, D_MODEL, 2 * D_MODEL), D_MODEL ** -0.5),
        "fox_w_qkvf": nrm(ks[20], (N_FOX_LAYERS, D_MODEL, 3 * D_MODEL + FOX_HEADS), D_MODEL ** -0.5),
        "fox_b_f": FOX_BIAS_INIT + nrm(ks[21], (N_FOX_LAYERS, FOX_HEADS), 0.5),
        "fox_w_o": nrm(ks[22], (N_FOX_LAYERS, D_MODEL, D_MODEL), D_MODEL ** -0.5),
        "ffn_w_up": nrm(ks[23], (DEPTH, D_MODEL, D_FF), D_MODEL ** -0.5),
        "ffn_w_gate": nrm(ks[24], (DEPTH, D_MODEL, D_FF), D_MODEL ** -0.5),
        "ffn_conv_w": nrm(ks[25], (DEPTH, CONV_WIDTH, D_FF), CONV_WIDTH ** -0.5),
        "ffn_conv_b": nrm(ks[26], (DEPTH, D_FF), 0.01),
        "ffn_w_down": nrm(ks[27], (DEPTH, D_FF, D_MODEL), D_FF ** -0.5),
    }


def reference(x_prompt, x_sample, state_ssm_re, state_ssm_im, cache_fox_k, cache_fox_v,
              cache_fox_logf, state_ffn_conv, norm_mix, norm_ffn, norm_final,
              ssm_a_re, ssm_a_im, ssm_log_step, ssm_b_re, ssm_b_im, ssm_c_re, ssm_c_im,
              ssm_d, ssm_w_glu, fox_w_qkvf, fox_b_f, fox_w_o,
              ffn_w_up, ffn_w_gate, ffn_conv_w, ffn_conv_b, ffn_w_down):
    xp, xs = x_prompt, x_sample
    n_p = xp.shape[0]
    ssm_re_p, ssm_im_p, ssm_re_s, ssm_im_s = [], [], [], []
    k_p, v_p, lf_p, k_s, v_s, lf_s = [], [], [], [], [], []
    conv_p, conv_s = [], []
    for i in range(DEPTH):
        j = i // N_MIXERS
        up = rms_norm(xp, norm_mix[i])
        us = rms_norm(xs, norm_mix[i])
        if i % N_MIXERS == 0:
            ssm_w = (ssm_a_re[j], ssm_a_im[j], ssm_log_step[j], ssm_b_re[j], ssm_b_im[j],
                     ssm_c_re[j], ssm_c_im[j], ssm_d[j], ssm_w_glu[j])
            zero_state = jnp.zeros((n_p, SSM_GROUPS, SSM_STATE), jnp.float32)
            mp, hr_p, hi_p = s5_mixer(up, zero_state, zero_state, *ssm_w)
            ms, hr_s, hi_s = s5_mixer(us, state_ssm_re[j], state_ssm_im[j], *ssm_w)
            ssm_re_p.append(hr_p)
            ssm_im_p.append(hi_p)
            ssm_re_s.append(hr_s)
            ssm_im_s.append(hi_s)
        else:
            mp, kp, vp, lp = fox_prompt(up, fox_w_qkvf[j], fox_b_f[j], fox_w_o[j])
            ms, kn, vn, ln = fox_sample(us, cache_fox_k[j], cache_fox_v[j], cache_fox_logf[j],
                                        fox_w_qkvf[j], fox_b_f[j], fox_w_o[j])
            k_p.append(kp)
            v_p.append(vp)
            lf_p.append(lp)
            k_s.append(kn)
            v_s.append(vn)
            lf_s.append(ln)
        xp = xp + mp
        xs = xs + ms
        hp = rms_norm(xp, norm_ffn[i])
        hs = rms_norm(xs, norm_ffn[i])
        ffn_w = (ffn_w_up[i], ffn_w_gate[i], ffn_conv_w[i], ffn_conv_b[i], ffn_w_down[i])
        fp, cp = conv_ffn(hp, jnp.zeros((n_p, CONV_WIDTH - 1, D_FF), hp.dtype), *ffn_w)
        fs, cs = conv_ffn(hs, state_ffn_conv[i], *ffn_w)
        conv_p.append(cp)
        conv_s.append(cs)
        xp = xp + fp
        xs = xs + fs
    y_prompt = rms_norm(xp, norm_final)
    y_sample = rms_norm(xs, norm_final)
    new_ssm_re_p = jnp.stack(ssm_re_p)
    new_ssm_im_p = jnp.stack(ssm_im_p)
    new_fox_k_p = jnp.stack(k_p)
    new_fox_v_p = jnp.stack(v_p)
    new_fox_logf_p = jnp.stack(lf_p)
    new_ffn_conv_p = jnp.stack(conv_p)
    new_ssm_re_s = jnp.stack(ssm_re_s)
    new_ssm_im_s = jnp.stack(ssm_im_s)
    new_fox_k_s = jnp.stack(k_s)
    new_fox_v_s = jnp.stack(v_s)
    new_fox_logf_s = jnp.stack(lf_s)
    new_ffn_conv_s = jnp.stack(conv_s)
    return (y_prompt, y_sample,
            new_ssm_re_p, new_ssm_im_p, new_fox_k_p, new_fox_v_p, new_fox_logf_p, new_ffn_conv_p,
            new_ssm_re_s, new_ssm_im_s, new_fox_k_s, new_fox_v_s, new_fox_logf_s, new_ffn_conv_s)
```

```python
import contextlib
import math
import numpy as np
import concourse.bass as bass
import concourse.mybir as mybir
from concourse.bass_utils import run_bass_kernel_spmd

F32 = mybir.dt.float32
BF16 = mybir.dt.bfloat16
AF = mybir.ActivationFunctionType
ALU = mybir.AluOpType
AX = mybir.AxisListType

D = 1024
DFF = 2816
NH = 16
HD = 64
HALF = 4096
NPAIR = 32
EPS = 1e-6
TWO_PI = 2.0 * math.pi

ENGS = ("pe", "act", "dve", "pool", "sp")
NO_SELF_WAIT = ("pe", "sp")
DMA_RING = 8


class Op:
    __slots__ = ("eng", "fn", "deps", "dma", "idx", "needed", "cnt", "slot", "tgt")

    def __init__(self, eng, fn, deps, dma, idx):
        self.eng, self.fn, self.deps, self.dma, self.idx = eng, fn, deps, dma, idx
        self.needed = False
        self.cnt = 0
        self.slot = -1
        self.tgt = 0


class Sched:
    def __init__(self, nc):
        self.nc = nc
        self.ops = {e: [] for e in ENGS}
        self.last_w = {}
        self.readers = {}
        self.ndma = {e: 0 for e in ENGS}
        self.last_slot = {}

    def add(self, eng, fn, reads=(), writes=(), dma=False):
        lst = self.ops[eng]
        idx = len(lst)
        deps = set()
        for r in reads:
            lw = self.last_w.get(r)
            if lw is not None:
                deps.add(lw)
        for w in writes:
            lw = self.last_w.get(w)
            if lw is not None:
                deps.add(lw)
            rd = self.readers.get(w)
            if rd:
                for e, i in rd.items():
                    deps.add((e, i))
        op = Op(eng, fn, deps, dma, idx)
        if dma:
            k = self.ndma[eng]
            self.ndma[eng] = k + 1
            op.slot = k % DMA_RING
            op.tgt = 16 * (k // DMA_RING + 1)
            op.needed = True
            prev = self.last_slot.get((eng, op.slot))
            if prev is not None:
                deps.add((eng, prev))
            self.last_slot[(eng, op.slot)] = idx
        lst.append(op)
        for r in reads:
            self.readers.setdefault(r, {})[eng] = idx
        for w in writes:
            self.last_w[w] = (eng, idx)
            self.readers[w] = {}
        return op

    def pe(self, fn, reads=(), writes=()):
        return self.add("pe", fn, reads, writes)

    def act(self, fn, reads=(), writes=()):
        return self.add("act", fn, reads, writes)

    def dve(self, fn, reads=(), writes=()):
        return self.add("dve", fn, reads, writes)

    def pool(self, fn, reads=(), writes=()):
        return self.add("pool", fn, reads, writes)

    def on(self, eng, fn, reads=(), writes=()):
        return self.add(eng, fn, reads, writes)

    def dma(self, out, in_, reads=(), writes=(), q="sp", **kw):
        return self.add(q, lambda e: e.dma_start(out=out, in_=in_, **kw), reads, writes, dma=True)

    def barrier(self):
        deps = set()
        for e in ENGS:
            lst = self.ops[e]
            seen = set()
            got_last = False
            for j in range(len(lst) - 1, -1, -1):
                o = lst[j]
                if o.dma:
                    if o.slot not in seen:
                        seen.add(o.slot)
                        deps.add((e, j))
                elif not got_last and o.fn is not None:
                    got_last = True
                    deps.add((e, j))
                if got_last and len(seen) >= DMA_RING:
                    break
        for e in ENGS:
            self.ops[e].append(Op(e, None, set(deps), False, len(self.ops[e])))
        self.last_w.clear()
        self.readers.clear()

    def emit(self, final_wait_eng="sp"):
        nc = self.nc
        for e in ENGS:
            for op in self.ops[e]:
                for (de, di) in op.deps:
                    dop = self.ops[de][di]
                    if dop.dma:
                        continue
                    if de == e and e in NO_SELF_WAIT and op.fn is not None:
                        continue
                    if de == e and op.fn is None:
                        continue
                    dop.needed = True
        finals = []
        for e in ENGS:
            if self.ops[e]:
                last = None
                for o in reversed(self.ops[e]):
                    if not o.dma and o.fn is not None:
                        last = o
                        break
                if last is not None:
                    last.needed = True
                    finals.append((e, last.idx))
                seen = {}
                for op in self.ops[e]:
                    if op.dma:
                        seen[op.slot] = op.idx
                for sl, ix in seen.items():
                    finals.append((e, ix))
        for e in ENGS:
            c = 0
            for op in self.ops[e]:
                if op.needed and not op.dma and op.fn is not None:
                    c += 1
                    op.cnt = c
        with contextlib.ExitStack() as st:
            esem = {e: st.enter_context(nc.semaphore("s_" + e)) for e in ENGS}
            dsem = {e: [st.enter_context(nc.semaphore("d_%s%d" % (e, i))) for i in range(DMA_RING)]
                    for e in ENGS if self.ndma[e] > 0}
            block = st.enter_context(nc.Block())
            ops = self.ops

            def run(ename, eng):
                waited = {}
                for op in ops[ename]:
                    for (de, di) in sorted(op.deps):
                        dop = ops[de][di]
                        if dop.dma:
                            key = ("d", de, dop.slot)
                            if waited.get(key, 0) >= dop.tgt:
                                continue
                            waited[key] = dop.tgt
                            eng.wait_ge(dsem[de][dop.slot], dop.tgt)
                        else:
                            if de == ename and (ename in NO_SELF_WAIT or op.fn is None):
                                continue
                            key = ("e", de)
                            if waited.get(key, 0) >= dop.cnt:
                                continue
                            waited[key] = dop.cnt
                            eng.wait_ge(esem[de], dop.cnt)
                    if op.fn is None:
                        continue
                    ins = op.fn(eng)
                    if op.dma:
                        ins.then_inc(dsem[ename][op.slot], 16)
                    elif op.needed:
                        ins.then_inc(esem[ename], 1)
                if ename == final_wait_eng:
                    for (de, di) in finals:
                        dop = ops[de][di]
                        if dop.dma:
                            eng.wait_ge(dsem[de][dop.slot], dop.tgt)
                        elif de != ename:
                            eng.wait_ge(esem[de], dop.cnt)

            @block.tensor
            def _(eng):
                run("pe", eng)

            @block.scalar
            def _(eng):
                run("act", eng)

            @block.vector
            def _(eng):
                run("dve", eng)

            @block.gpsimd
            def _(eng):
                run("pool", eng)

            @block.sync
            def _(eng):
                run("sp", eng)


STAGE = 9


def build(stage=STAGE):
    nc = bass.Bass("TRN2", target_bir_lowering=False)
    S = Sched(nc)
    st = contextlib.ExitStack()

    def din(name, shape, dt=F32):
        return nc.dram_tensor(name, list(shape), dt, kind="ExternalInput").ap()

    def dout(name, shape, dt=F32):
        return nc.dram_tensor(name, list(shape), dt, kind="ExternalOutput").ap()

    def dscr(name, shape, dt=F32):
        return nc.dram_tensor(name, list(shape), dt).ap()

    ARENA_F32 = 53200
    arena = st.enter_context(nc.sbuf_tensor("arena", [128, ARENA_F32], F32))
    aoff = [0]

    def sb(name, shape, dt=F32):
        n = 1
        for d_ in shape[1:]:
            n *= d_
        nf = n if dt == F32 else (n + 1) // 2
        nf = (nf + 7) // 8 * 8
        o = aoff[0]
        assert o + nf <= ARENA_F32, ("arena overflow", name, o, nf)
        aoff[0] = o + nf
        v = arena[:, o:o + nf]
        if dt != F32:
            v = v.bitcast(dt)
        v = v[:, 0:n]
        if len(shape) == 2:
            return v
        letters = "abcdefg"[:len(shape) - 1]
        pat = "p (" + " ".join(letters) + ") -> p " + " ".join(letters)
        kw = {letters[i]: shape[1 + i] for i in range(len(shape) - 2)}
        return v.rearrange(pat, **kw)

    def ps(name, shape, dt=F32):
        return st.enter_context(nc.psum_tensor("p_" + name, list(shape), dt))

    xp = din("xp", [2 * HALF, D])
    xs = din("xs", [128, D])
    h0r = din("h0r", [128, 2, NPAIR])
    h0i = din("h0i", [128, 2, NPAIR])
    lam_r_d = din("lam_r", [128, NPAIR])
    lam_i_d = din("lam_i", [128, NPAIR])
    lstep_d = din("lstep", [128, NPAIR])
    b_r_d = din("b_r", [128, NPAIR, 16])
    b_i_d = din("b_i", [128, NPAIR, 16])
    c_r_d = din("c_r", [128, NPAIR, 16])
    c_i_d = din("c_i", [128, NPAIR, 16])
    gmix_d = din("gmix", [2, 128, D])
    gffn_d = din("gffn", [2, 128, D])
    gfin_d = din("gfin", [128, D])
    dskip_d = din("dskip", [128, D])
    ident_d = din("ident", [128, 128])
    tmask_d = din("tmask", [128, 128])

    o_ssm_p = dout("o_ssm_p", [2, 128, NPAIR])
    o_ssm_s = dout("o_ssm_s", [2, 2, 128, NPAIR])
    NT = 2 * HALF + 128
    gact = dscr("gscr", [NT, D], BF16)

    pbank = [ps("pb%d" % i, [128, 512], F32) for i in range(8)]

    ident = sb("ident", [128, 128]); identb = sb("identb", [128, 128], BF16)
    tmask = sb("tmask", [128, 128])
    S.dma(ident[:], ident_d, writes=["ident"])
    S.dma(tmask[:], tmask_d, writes=["tmask"])
    S.dve(lambda e: e.tensor_copy(out=identb[:], in_=ident[:]), ["ident"], ["identb"])
    negpi = sb("negpi", [128, 1])
    S.pool(lambda e: e.memset(negpi[:], -math.pi), [], ["negpi"])
    tri = sb("tri", [128, 128], BF16); onesb = sb("onesb", [128, 128], BF16); cmask = sb("cmask", [128, 128])
    bft = sb("bft", [128, NH]); pmk = sb("pmk", [128, 1]); one1 = sb("one1", [128, 1])
    cstg = sb("cstg", [128, 128])
    S.dma(cstg[:], din("tri", [128, 128]), writes=["cstg"])
    S.dve(lambda e: e.tensor_copy(out=tri[:], in_=cstg[:]), ["cstg"], ["tri"])
    S.pool(lambda e: e.memset(onesb[:], 1.0), [], ["onesb"])
    S.pool(lambda e: e.memset(one1[:], 1.0), [], ["one1"])
    S.dma(cmask[:], din("cmask", [128, 128]), writes=["cmask"])
    S.dma(bft[:], din("bft", [128, NH]), writes=["bft"])
    S.dma(pmk[:], din("pmk", [128, 1]), writes=["pmk"])
    ssn = sb("ssn", [128, 4]); rvn = sb("rvn", [128, 4])
    mP = aoff[0]
    mP2 = mP

    lam_r = sb("lam_r", [128, NPAIR]); lam_i = sb("lam_i", [128, NPAIR]); dtt = sb("dtt", [128, NPAIR])
    lrdt = sb("lrdt", [128, NPAIR]); lidt = sb("lidt", [128, NPAIR])
    Er = sb("Er", [128, 9, NPAIR]); Ei = sb("Ei", [128, 9, NPAIR])
    Emr = sb("Emr", [128, 9, NPAIR]); Emi = sb("Emi", [128, 9, NPAIR])
    x_t = sb("x_t", [128, 8, D])
    _al = [x_t[:, i, 0:NPAIR * 16].rearrange("p (a h) -> p a h", h=16) for i in range(8)]
    Br, Bi, Cr, Ci, Bbr, Bbi, t16a, t16b = _al
    tA = sb("tA", [128, NPAIR]); tB = sb("tB", [128, NPAIR]); tC = sb("tC", [128, NPAIR]); tD = sb("tD", [128, NPAIR])
    zr = sb("zr", [128, NPAIR]); zi = sb("zi", [128, NPAIR])
    S.dma(lam_r[:], lam_r_d, writes=["lam_r"]); S.dma(lam_i[:], lam_i_d, writes=["lam_i"])
    S.dma(dtt[:], lstep_d, writes=["dtt"])
    S.dma(Br[:], b_r_d, writes=["Br"]); S.dma(Bi[:], b_i_d, writes=["Bi"])
    S.dma(Cr[:], c_r_d, writes=["Cr"]); S.dma(Ci[:], c_i_d, writes=["Ci"])
    S.act(lambda e: e.activation(out=dtt[:], in_=dtt[:], func=AF.Exp), ["dtt"], ["dtt"])
    S.dve(lambda e: e.tensor_tensor(out=lrdt[:], in0=lam_r[:], in1=dtt[:], op=ALU.mult), ["lam_r", "dtt"], ["lrdt"])
    S.dve(lambda e: e.tensor_tensor(out=lidt[:], in0=lam_i[:], in1=dtt[:], op=ALU.mult), ["lam_i", "dtt"], ["lidt"])
    S.dve(lambda e: e.memset(Er[:, 0, :], 1.0), [], ["E"])
    S.dve(lambda e: e.memset(Ei[:, 0, :], 0.0), [], ["E"])
    halfpi = sb("halfpi", [128, 1])
    S.pool(lambda e: e.memset(halfpi[:], 0.5 * math.pi), [], ["halfpi"])
    S.act(lambda e: e.activation(out=tA[:], in_=lidt[:], func=AF.Sin, scale=0.125), ["lidt"], ["tA"])
    S.act(lambda e: e.activation(out=tB[:], in_=lidt[:], func=AF.Sin, scale=-0.125, bias=halfpi[:, 0:1]), ["lidt", "halfpi"], ["tB"])
    for _ in range(3):
        S.dve(lambda e: e.tensor_tensor(out=tC[:], in0=tA[:], in1=tB[:], op=ALU.mult), ["tA", "tB", "tC"], ["tC"])
        S.dve(lambda e: e.tensor_tensor(out=tD[:], in0=tA[:], in1=tA[:], op=ALU.mult), ["tA", "tD"], ["tD"])
        S.dve(lambda e: e.tensor_scalar(out=tA[:], in0=tC[:], scalar1=2.0, scalar2=None, op0=ALU.mult), ["tC", "tA"], ["tA"])
        S.dve(lambda e: e.tensor_scalar(out=tB[:], in0=tD[:], scalar1=-2.0, scalar2=1.0, op0=ALU.mult, op1=ALU.add), ["tD", "tB"], ["tB"])
    S.act(lambda e: e.activation(out=tC[:], in_=lrdt[:], func=AF.Exp), ["lrdt", "tC"], ["tC"])
    S.dve(lambda e: e.tensor_tensor(out=Er[:, 1, :], in0=tB[:], in1=tC[:], op=ALU.mult), ["tB", "tC"], ["E"])
    S.dve(lambda e: e.tensor_tensor(out=Ei[:, 1, :], in0=tA[:], in1=tC[:], op=ALU.mult), ["tA", "tC"], ["E"])
    for k in range(2, 9):
        S.dve(lambda e, k=k: e.tensor_tensor(out=tA[:], in0=Er[:, k - 1, :], in1=Er[:, 1, :], op=ALU.mult), ["E", "tA"], ["tA"])
        S.dve(lambda e, k=k: e.tensor_tensor(out=tB[:], in0=Ei[:, k - 1, :], in1=Ei[:, 1, :], op=ALU.mult), ["E", "tB"], ["tB"])
        S.dve(lambda e, k=k: e.tensor_tensor(out=tC[:], in0=Er[:, k - 1, :], in1=Ei[:, 1, :], op=ALU.mult), ["E", "tC"], ["tC"])
        S.dve(lambda e, k=k: e.tensor_tensor(out=tD[:], in0=Ei[:, k - 1, :], in1=Er[:, 1, :], op=ALU.mult), ["E", "tD"], ["tD"])
        S.dve(lambda e, k=k: e.tensor_tensor(out=Er[:, k, :], in0=tA[:], in1=tB[:], op=ALU.subtract), ["tA", "tB", "E"], ["E"])
        S.dve(lambda e, k=k: e.tensor_tensor(out=Ei[:, k, :], in0=tC[:], in1=tD[:], op=ALU.add), ["tC", "tD", "E"], ["E"])
    for k in range(1, 9):
        S.act(lambda e, k=k: e.activation(out=tC[:], in_=lrdt[:], func=AF.Exp, scale=-2.0 * k), ["lrdt", "tC"], ["tC"])
        S.dve(lambda e, k=k: e.tensor_tensor(out=Emr[:, k, :], in0=Er[:, k, :], in1=tC[:], op=ALU.mult), ["E", "tC"], ["E"])
        S.dve(lambda e, k=k: e.scalar_tensor_tensor(out=Emi[:, k, :], in0=Ei[:, k, :], scalar=-1.0, in1=tC[:], op0=ALU.mult, op1=ALU.mult), ["E", "tC"], ["E"])
    S.dve(lambda e: e.tensor_tensor(out=tA[:], in0=lam_r[:], in1=lam_r[:], op=ALU.mult), ["lam_r", "tA"], ["tA"])
    S.dve(lambda e: e.tensor_tensor(out=tB[:], in0=lam_i[:], in1=lam_i[:], op=ALU.mult), ["lam_i", "tB"], ["tB"])
    S.dve(lambda e: e.tensor_tensor(out=tA[:], in0=tA[:], in1=tB[:], op=ALU.add), ["tA", "tB"], ["tA"])
    S.dve(lambda e: e.reciprocal(out=tA[:], in_=tA[:]), ["tA"], ["tA"])
    S.dve(lambda e: e.tensor_scalar(out=tB[:], in0=Er[:, 1, :], scalar1=-1.0, scalar2=None, op0=ALU.add), ["E", "tB"], ["tB"])
    S.dve(lambda e: e.tensor_tensor(out=tC[:], in0=tB[:], in1=lam_r[:], op=ALU.mult), ["tB", "lam_r", "tC"], ["tC"])
    S.dve(lambda e: e.tensor_tensor(out=tD[:], in0=Ei[:, 1, :], in1=lam_i[:], op=ALU.mult), ["E", "lam_i", "tD"], ["tD"])
    S.dve(lambda e: e.tensor_tensor(out=tC[:], in0=tC[:], in1=tD[:], op=ALU.add), ["tC", "tD"], ["tC"])
    S.dve(lambda e: e.tensor_tensor(out=zr[:], in0=tC[:], in1=tA[:], op=ALU.mult), ["tC", "tA"], ["zr"])
    S.dve(lambda e: e.tensor_tensor(out=tC[:], in0=Ei[:, 1, :], in1=lam_r[:], op=ALU.mult), ["E", "lam_r", "tC"], ["tC"])
    S.dve(lambda e: e.tensor_tensor(out=tD[:], in0=tB[:], in1=lam_i[:], op=ALU.mult), ["tB", "lam_i", "tD"], ["tD"])
    S.dve(lambda e: e.tensor_tensor(out=tC[:], in0=tC[:], in1=tD[:], op=ALU.subtract), ["tC", "tD"], ["tC"])
    S.dve(lambda e: e.tensor_tensor(out=zi[:], in0=tC[:], in1=tA[:], op=ALU.mult), ["tC", "tA"], ["zi"])

    import os
    KPREP = int(os.environ.get("KPREP", "9"))

    def bch(ap2, n=16):
        return ap2.unsqueeze(2).to_broadcast([128, ap2.shape[1], n])

    S.dve(lambda e: e.tensor_tensor(out=t16a[:], in0=Br[:], in1=bch(zr[:]), op=ALU.mult), ["Br", "zr", "t16a"], ["t16a"])
    S.dve(lambda e: e.tensor_tensor(out=t16b[:], in0=Bi[:], in1=bch(zi[:]), op=ALU.mult), ["Bi", "zi", "t16b"], ["t16b"])
    S.dve(lambda e: e.tensor_tensor(out=Bbr[:], in0=t16a[:], in1=t16b[:], op=ALU.subtract), ["t16a", "t16b"], ["Bbr"])
    S.dve(lambda e: e.tensor_tensor(out=t16a[:], in0=Bi[:], in1=bch(zr[:]), op=ALU.mult), ["Bi", "zr", "t16a"], ["t16a"])
    S.dve(lambda e: e.tensor_tensor(out=t16b[:], in0=Br[:], in1=bch(zi[:]), op=ALU.mult), ["Br", "zi", "t16b"], ["t16b"])
    S.dve(lambda e: e.tensor_tensor(out=Bbi[:], in0=t16a[:], in1=t16b[:], op=ALU.add), ["t16a", "t16b"], ["Bbi"])

    WinReT = sb("WinReT", [128, NPAIR, 128], BF16)
    WinImT = sb("WinImT", [128, NPAIR, 128], BF16)
    Toep = sb("Toep", [128, 2 * NPAIR, 128], BF16)
    WoRe = sb("WoRe", [128, 2, NPAIR, 128], BF16)
    WoIm = sb("WoIm", [128, 2, NPAIR, 128], BF16)
    S.pool(lambda e: e.memset(WoRe[:], 0.0), [], ["WoRe"])
    S.pool(lambda e: e.memset(WoIm[:], 0.0), [], ["WoIm"])
    NPC = 2
    XRe = sb("XRe", [128, NPC, 8, 16]); XIm = sb("XIm", [128, NPC, 8, 16])
    WiRe = sb("WiRe", [128, NPC, 8, 16]); WiIm = sb("WiIm", [128, NPC, 8, 16])
    WoReF = sb("WoReF", [128, NPC, 8, 16]); WoImF = sb("WoImF", [128, NPC, 8, 16])
    XReB = sb("XReB", [128, NPC, 128], BF16); XImB = sb("XImB", [128, NPC, 128], BF16)
    pa = sb("pa", [128, NPC, 16]); pb_ = sb("pb_", [128, NPC, 16])

    def cmul_into(outr, outi, ar, ai, br, bi, tagr, tagi, negate_im=False):
        S.dve(lambda e: e.tensor_tensor(out=pa[:], in0=br, in1=bch(ar), op=ALU.mult), ["E", "Bbr", "Bbi", "Cr", "Ci", "pa"], ["pa"])
        S.dve(lambda e: e.tensor_tensor(out=pb_[:], in0=bi, in1=bch(ai), op=ALU.mult), ["E", "Bbr", "Bbi", "Cr", "Ci", "pb_"], ["pb_"])
        S.dve(lambda e: e.tensor_tensor(out=outr, in0=pa[:], in1=pb_[:], op=ALU.subtract), ["pa", "pb_"], [tagr])
        S.dve(lambda e: e.tensor_tensor(out=pa[:], in0=bi, in1=bch(ar), op=ALU.mult), ["E", "Bbr", "Bbi", "Cr", "Ci", "pa"], ["pa"])
        S.dve(lambda e: e.tensor_tensor(out=pb_[:], in0=br, in1=bch(ai), op=ALU.mult), ["E", "Bbr", "Bbi", "Cr", "Ci", "pb_"], ["pb_"])
        if negate_im:
            S.dve(lambda e: e.scalar_tensor_tensor(out=outi, in0=pa[:], scalar=-1.0, in1=pb_[:], op0=ALU.mult, op1=ALU.subtract), ["pa", "pb_"], [tagi])
        else:
            S.dve(lambda e: e.tensor_tensor(out=outi, in0=pa[:], in1=pb_[:], op=ALU.add), ["pa", "pb_"], [tagi])

    for q in range(NPAIR // NPC):
        psl = slice(q * NPC, (q + 1) * NPC)
        for s in range(8):
            cmul_into(XRe[:, :, s, :], XIm[:, :, s, :], Emr[:, s + 1, psl], Emi[:, s + 1, psl], Bbr[:, psl, :], Bbi[:, psl, :], "XRe", "XIm")
            cmul_into(WiRe[:, :, s, :], WiIm[:, :, s, :], Er[:, 7 - s, psl], Ei[:, 7 - s, psl], Bbr[:, psl, :], Bbi[:, psl, :], "WiRe", "WiIm")
            cmul_into(WoReF[:, :, s, :], WoImF[:, :, s, :], Er[:, s + 1, psl], Ei[:, s + 1, psl], Cr[:, psl, :], Ci[:, psl, :], "WoReF", "WoImF", negate_im=True)
        for g2 in range(2):
            rs = slice(64 * g2, 64 * g2 + 64)
            S.act(lambda e, psl=psl, rs=rs, g2=g2: e.copy(out=WoRe[rs, g2, psl, :], in_=WoReF[rs].rearrange("p a s h -> p a (s h)")), ["WoReF", "WoRe"], ["WoRe"])
            S.act(lambda e, psl=psl, rs=rs, g2=g2: e.copy(out=WoIm[rs, g2, psl, :], in_=WoImF[rs].rearrange("p a s h -> p a (s h)")), ["WoImF", "WoIm"], ["WoIm"])
        S.act(lambda e: e.copy(out=XReB[:], in_=XRe[:].rearrange("p a s h -> p a (s h)")), ["XRe"], ["XReB"])
        S.act(lambda e: e.copy(out=XImB[:], in_=XIm[:].rearrange("p a s h -> p a (s h)")), ["XIm"], ["XImB"])
        if KPREP <= 3:
            continue
        for a in range(NPC):
            pair = q * NPC + a
            bk = pbank[a % 2]; bkn0 = 'pb%d' % (a % 2)
            S.pe(lambda e, a=a, bk=bk: e.transpose(out=bk[:, 0:128], in_=WiRe[:, a].rearrange("p s h -> p (s h)"), identity=ident[:]), ["WiRe", "ident"], ["pb%d" % (a % 2)])
            S.pe(lambda e, a=a, bk=bk: e.transpose(out=bk[:, 128:256], in_=WiIm[:, a].rearrange("p s h -> p (s h)"), identity=ident[:]), ["WiIm", "ident"], ["pb%d" % (a % 2)])
            for g2 in range(2 if KPREP >= 5 else 0):
                rs = slice(64 * g2, 64 * g2 + 64)
                cs = slice(128 * g2, 128 * g2 + 128)
                bk = pbank[2 + a % 2]
                S.pe(lambda e, a=a, bk=bk, rs=rs, cs=cs, q=q, g2=g2: e.matmul(bk[:, cs], lhsT=XReB[:, a, :], rhs=WoRe[:, g2, q * NPC + a, :], start=True, stop=False),
                     ["XReB", "WoRe"], ["pb%d" % (2 + a % 2)])
                S.pe(lambda e, a=a, bk=bk, rs=rs, cs=cs, q=q, g2=g2: e.matmul(bk[:, cs], lhsT=XImB[:, a, :], rhs=WoIm[:, g2, q * NPC + a, :], start=False, stop=True),
                     ["XImB", "WoIm"], ["pb%d" % (2 + a % 2)])
            S.act(lambda e, pair=pair, bk=pbank[a % 2]: e.copy(out=WinReT[:, pair, :], in_=bk[:, 0:128]), ["pb%d" % (a % 2)], ["WinReT"])
            S.act(lambda e, pair=pair, bk=pbank[a % 2]: e.copy(out=WinImT[:, pair, :], in_=bk[:, 128:256]), ["pb%d" % (a % 2)], ["WinImT"])
            for g2 in range(2 if KPREP >= 5 else 0):
                S.dve(lambda e, pair=pair, g2=g2, bk2=pbank[2 + a % 2]: e.tensor_tensor(out=Toep[:, 2 * pair + g2, :], in0=bk2[:, 128 * g2:128 + 128 * g2], in1=tmask[:], op=ALU.mult),
                      ["pb%d" % (2 + a % 2), "tmask"], ["Toep"])

    M1 = sb("M1", [128, 2, NPAIR]); M2n = sb("M2n", [128, NPAIR]); M2p = sb("M2p", [128, NPAIR])
    S.dve(lambda e: e.tensor_copy(out=M1[:, 0, :], in_=Er[:, 8, :]), ["E"], ["M1"])
    S.dve(lambda e: e.tensor_copy(out=M1[:, 1, :], in_=Er[:, 8, :]), ["E"], ["M1"])
    S.dve(lambda e: e.tensor_copy(out=M2p[:], in_=Ei[:, 8, :]), ["E"], ["M2"])
    S.dve(lambda e: e.tensor_scalar(out=M2n[:], in0=Ei[:, 8, :], scalar1=-1.0, scalar2=None, op0=ALU.mult), ["E"], ["M2"])

    gmix0 = sb("gmix0", [128, D]); dg = sb("dg", [128, D])
    S.dma(gmix0[:], gmix_d[0], writes=["gmix0"])
    S.dma(dg[:], dskip_d, writes=["dg"])
    S.dve(lambda e: e.tensor_tensor(out=dg[:], in0=dg[:], in1=gmix0[:], op=ALU.mult), ["dg", "gmix0"], ["dg"])

    S.dve(lambda e: e.memset(x_t[:, 0, 0:1], 0.0), ["Br", "Bi", "Cr", "Ci", "Bbr", "Bbi", "t16a", "t16b"], ["x_t"])
    ss = sb("ss", [128, 8]); rinv = sb("rinv", [128, 8])
    ub = sb("ub", [128, 64, 8, 16], BF16)
    UT = sb("UT", [128, 64, 128], BF16)
    SH = sb("SH", [128, 2, NPAIR, 128])
    Hpb = sb("Hpb", [128, 2, NPAIR, 128], BF16)
    carry = sb("carry", [128, 2, NPAIR])
    T1 = {"dve": sb("T1d", [128, 2, NPAIR // 2]), "pool": sb("T1p", [128, 2, NPAIR // 2])}
    T2 = {"dve": sb("T2d", [128, 2, NPAIR // 2]), "pool": sb("T2p", [128, 2, NPAIR // 2])}
    ytmps = [sb("ytmp0", [128, 8, 16]), sb("ytmp1", [128, 8, 16])]
    junk = UT[:].rearrange("p g c -> p (g c)")[:, 0:D]
    Gb = ub[:].rearrange("p g t h -> p (g t h)").rearrange("p (t d) -> p t d", t=8)

    def s5_macro(src_ap, nvalid, mt_tag, seqs, g_out_ap, final_out):
        P = nvalid
        S.dma(x_t[0:P], src_ap.rearrange("(c t) d -> c t d", t=8), writes=["x_t"])
        for t in range(8):
            S.act(lambda e, t=t: e.activation(out=junk[0:P], in_=x_t[0:P, t, :], func=AF.Square, accum_out=ss[0:P, t:t + 1]), ["x_t"], ["UT", "ss"])
        S.dve(lambda e: e.tensor_scalar(out=rinv[0:P], in0=ss[0:P], scalar1=1.0 / D, scalar2=EPS, op0=ALU.mult, op1=ALU.add), ["ss"], ["rinv"])
        S.act(lambda e: e.activation(out=rinv[0:P], in_=rinv[0:P], func=AF.Sqrt), ["rinv"], ["rinv"])
        S.dve(lambda e: e.reciprocal(out=rinv[0:P], in_=rinv[0:P]), ["rinv"], ["rinv"])
        for t in range(8):
            S.dve(lambda e, t=t: e.scalar_tensor_tensor(out=ub[0:P, :, t, :], in0=x_t[0:P, t, :].rearrange("p (g h) -> p g h", h=16), scalar=rinv[0:P, t:t + 1],
                                                        in1=gmix0[0:P].rearrange("p (g h) -> p g h", h=16), op0=ALU.mult, op1=ALU.mult),
                  ["x_t", "rinv", "gmix0", "ub"], ["ub"])
        for g8 in range(8):
            bk = pbank[g8 % 2]; bkn = "pb%d" % (g8 % 2)
            bkb = bk[:].bitcast(BF16)
            for j in range(8):
                g = g8 * 8 + j
                S.pe(lambda e, g=g, j=j, bkb=bkb: e.transpose(out=bkb[:, j * 128:j * 128 + P], in_=ub[0:P, g].rearrange("p t h -> p (t h)"), identity=identb[0:P, 0:P]),
                     ["ub", "identb"], [bkn])
            S.act(lambda e, g8=g8, bkb=bkb: e.copy(out=UT[:, g8 * 8:(g8 + 1) * 8, 0:P], in_=bkb[:, 0:1024].rearrange("p (j c) -> p j c", j=8)[:, :, 0:P]), [bkn], ["UT"])
        for q in range(8):
            bre = pbank[4 + 2 * (q % 2)]; bim = pbank[5 + 2 * (q % 2)]
            nre = "pb%d" % (4 + 2 * (q % 2)); nim = "pb%d" % (5 + 2 * (q % 2))
            for a in range(4):
                pair = q * 4 + a
                for g2 in range(2):
                    g = 2 * pair + g2
                    S.pe(lambda e, a=a, g2=g2, g=g, pair=pair, bre=bre: e.matmul(bre[64 * g2:64 * g2 + 64, a * 128:a * 128 + P], lhsT=WinReT[:, pair, 64 * g2:64 * g2 + 64], rhs=UT[:, g, 0:P], start=True, stop=True),
                         ["WinReT", "UT"], [nre])
                    S.pe(lambda e, a=a, g2=g2, g=g, pair=pair, bim=bim: e.matmul(bim[64 * g2:64 * g2 + 64, a * 128:a * 128 + P], lhsT=WinImT[:, pair, 64 * g2:64 * g2 + 64], rhs=UT[:, g, 0:P], start=True, stop=True),
                         ["WinImT", "UT"], [nim])
            S.act(lambda e, q=q, bre=bre: e.copy(out=SH[:, 0, q * 4:q * 4 + 4, 0:P], in_=bre[:].rearrange("p (a c) -> p a c", a=4)[:, :, 0:P]), [nre], ["SHr"])
            S.dve(lambda e, q=q, bim=bim: e.tensor_copy(out=SH[:, 1, q * 4:q * 4 + 4, 0:P], in_=bim[:].rearrange("p (a c) -> p a c", a=4)[:, :, 0:P]), [nim], ["SHi"])
        for (c0, c1, init) in seqs:
            for ei, en in enumerate(("dve", "pool")):
                hp = slice(ei * 16, ei * 16 + 16)
                t1 = T1[en]; t2 = T2[en]
                if init[0] == "h0":
                    S.dma(carry[:, 0, hp], h0r[:, init[1], hp], writes=["carry" + en], q="sp")
                    S.dma(carry[:, 1, hp], h0i[:, init[1], hp], writes=["carry" + en], q="sp")
                elif init[0] == "zero":
                    S.on(en, lambda e, hp=hp: e.memset(carry[:, :, hp], 0.0), [], ["carry" + en])
                for c in range(c0, c1):
                    if c == c0:
                        prev = carry[:, :, hp]; prev_r = carry[:, 0, hp]; prev_i = carry[:, 1, hp]
                    else:
                        prev = SH[:, :, hp, c - 1]; prev_r = SH[:, 0, hp, c - 1]; prev_i = SH[:, 1, hp, c - 1]
                    rd = ["SH" + en, "carry" + en, "M1", "M2", "SHr", "SHi"]
                    S.on(en, lambda e, prev=prev, hp=hp, c=c: e.tensor_copy(out=Hpb[:, :, hp, c], in_=prev), rd, ["Hpb" + en])
                    S.on(en, lambda e, prev=prev, hp=hp, t1=t1: e.tensor_tensor(out=t1[:], in0=prev, in1=M1[:, :, hp], op=ALU.mult), rd + ["t1" + en], ["t1" + en])
                    S.on(en, lambda e, prev_i=prev_i, hp=hp, t2=t2: e.tensor_tensor(out=t2[:, 0, :], in0=prev_i, in1=M2n[:, hp], op=ALU.mult), rd + ["t2" + en], ["t2" + en])
                    S.on(en, lambda e, prev_r=prev_r, hp=hp, t2=t2: e.tensor_tensor(out=t2[:, 1, :], in0=prev_r, in1=M2p[:, hp], op=ALU.mult), rd + ["t2" + en], ["t2" + en])
                    S.on(en, lambda e, t1=t1, t2=t2: e.tensor_tensor(out=t1[:], in0=t1[:], in1=t2[:], op=ALU.add), ["t1" + en, "t2" + en], ["t1" + en])
                    S.on(en, lambda e, t1=t1, hp=hp, c=c: e.tensor_tensor(out=SH[:, :, hp, c], in0=SH[:, :, hp, c], in1=t1[:], op=ALU.add), ["t1" + en, "SHr", "SHi", "SH" + en], ["SH" + en])
                S.on(en, lambda e, hp=hp, c1=c1: e.tensor_copy(out=carry[:, :, hp], in_=SH[:, :, hp, c1 - 1]), ["SH" + en, "SHr", "SHi"], ["carry" + en])
        for (cl, ore, oim) in final_out:
            S.dma(ore, SH[:, 0, :, cl], reads=["SHdve", "SHpool", "SHr", "SHi"], allow_slow_non_contiguous=True)
            S.dma(oim, SH[:, 1, :, cl], reads=["SHdve", "SHpool", "SHr", "SHi"], allow_slow_non_contiguous=True)
        for g in range(64):
            pair, g2 = g // 2, g % 2
            bk = pbank[2 + g % 2]; bkn = "pb%d" % (2 + g % 2)
            rs = slice(64 * g2, 64 * g2 + 64)
            S.pe(lambda e, g=g, bk=bk: e.matmul(bk[0:P, 0:128], lhsT=UT[:, g, 0:P], rhs=Toep[:, g, :], start=True, stop=False), ["UT", "Toep"], [bkn])
            S.pe(lambda e, pair=pair, rs=rs, bk=bk, g2=g2: e.matmul(bk[0:P, 0:128], lhsT=Hpb[:, 0, pair, 0:P], rhs=WoRe[:, g2, pair, :], start=False, stop=False), ["Hpbdve", "Hpbpool", "WoRe"], [bkn])
            S.pe(lambda e, pair=pair, rs=rs, bk=bk, g2=g2: e.matmul(bk[0:P, 0:128], lhsT=Hpb[:, 1, pair, 0:P], rhs=WoIm[:, g2, pair, :], start=False, stop=True), ["Hpbdve", "Hpbpool", "WoIm"], [bkn])
            dsl = slice(16 * g, 16 * g + 16)
            ytmp = ytmps[g % 2]; yn = "ytmp%d" % (g % 2)
            S.dve(lambda e, dsl=dsl, ytmp=ytmp: e.tensor_tensor(out=ytmp[0:P], in0=x_t[0:P, :, dsl], in1=rinv[0:P].unsqueeze(2).to_broadcast([P, 8, 16]), op=ALU.mult), ["x_t", "rinv"], [yn])
            S.dve(lambda e, dsl=dsl, ytmp=ytmp: e.tensor_tensor(out=ytmp[0:P], in0=ytmp[0:P], in1=dg[0:P, dsl].unsqueeze(1).to_broadcast([P, 8, 16]), op=ALU.mult), [yn, "dg"], [yn])
            S.dve(lambda e, bk=bk, ytmp=ytmp: e.tensor_tensor(out=ytmp[0:P], in0=ytmp[0:P], in1=bk[0:P, 0:128].rearrange("p (t h) -> p t h", t=8), op=ALU.add), [yn, bkn], [yn])
            S.act(lambda e, dsl=dsl, ytmp=ytmp: e.activation(out=Gb[0:P, :, dsl], in_=ytmp[0:P], func=AF.Gelu_apprx_tanh), [yn], ["ub"])
        S.dma(g_out_ap.rearrange("(c t) d -> c t d", t=8), Gb[0:P], reads=["ub"])

    import os
    KST = int(os.environ.get("KSTAGE", "9"))
    for mt in range(8 if KST >= 3 else (1 if KST == 2 else 0)):
        seqs = [(0, 128, ("zero",) if mt == 0 else ("carry",))]
        fin = [(127, o_ssm_p[0], o_ssm_p[1])] if mt == 7 or KST == 2 else []
        s5_macro(xp[mt * 1024:(mt + 1) * 1024, :], 128, mt, seqs, gact[mt * 1024:(mt + 1) * 1024, :], fin)
    if KST >= 1:
      s5_macro(xs, 16, 8, [(0, 8, ("h0", 0)), (8, 16, ("h0", 1))], gact[2 * HALF:2 * HALF + 128, :],
             [(7, o_ssm_s[0, 0], o_ssm_s[0, 1]), (15, o_ssm_s[1, 0], o_ssm_s[1, 1])])


    x1s = dscr("x1s", [NT, D]); x2s = dscr("x2s", [NT, D]); x3s = dscr("x3s", [NT, D])

    def xsrc(t0, n):
        return xp[t0:t0 + n, :] if t0 < 2 * HALF else xs[t0 - 2 * HALF:t0 - 2 * HALF + n, :]

    def load_weight(dst3, src2, nchunks, P, ncol, tag):
        stg = [sb(tag + "_s0", [128, ncol]), sb(tag + "_s1", [128, ncol])]
        engs = ["act", "pool", "dve"]
        for c in range(nchunks):
            b = stg[c % 2]; bn = tag + "_s%d" % (c % 2)
            S.dma(b[0:P], src2[c * P:(c + 1) * P, :], writes=[bn])
            en = engs[c % 3]
            if en == "act":
                S.act(lambda e, b=b, c=c: e.copy(out=dst3[0:P, c, :], in_=b[0:P]), [bn], [tag + str(c)])
            else:
                S.on(en, lambda e, b=b, c=c: e.tensor_copy(out=dst3[0:P, c, :], in_=b[0:P]), [bn], [tag + str(c)])


    def rms_to_bf16(x_ap, P, gain, out_ap, xname, oname, col):
        S.act(lambda e: e.activation(out=out_ap, in_=x_ap, func=AF.Square, accum_out=ssn[0:P, col:col + 1]), [xname], [oname, "ssn%d" % col])
        S.dve(lambda e: e.tensor_scalar(out=rvn[0:P, col:col + 1], in0=ssn[0:P, col:col + 1], scalar1=1.0 / D, scalar2=EPS, op0=ALU.mult, op1=ALU.add), ["ssn%d" % col], ["rvn%d" % col])
        S.act(lambda e: e.activation(out=rvn[0:P, col:col + 1], in_=rvn[0:P, col:col + 1], func=AF.Sqrt), ["rvn%d" % col], ["rvn%d" % col])
        S.dve(lambda e: e.reciprocal(out=rvn[0:P, col:col + 1], in_=rvn[0:P, col:col + 1]), ["rvn%d" % col], ["rvn%d" % col])
        S.dve(lambda e: e.scalar_tensor_tensor(out=out_ap, in0=x_ap, scalar=rvn[0:P, col:col + 1], in1=gain[0:P], op0=ALU.mult, op1=ALU.mult), [xname, "rvn%d" % col, oname], [oname])

    tcount = [0]

    def transpose8(src_ap, P, dstT_ap, sname, dname):
        bi = tcount[0] % 2
        tcount[0] += 1
        bkb = pbank[bi][:].bitcast(BF16)
        bn = "pb%d" % bi
        for dc in range(8):
            S.pe(lambda e, dc=dc: e.transpose(out=bkb[:, dc * 128:dc * 128 + P], in_=src_ap[:, dc * 128:(dc + 1) * 128], identity=identb[0:P, 0:P]), [sname], [bn])
        S.act(lambda e: e.copy(out=dstT_ap, in_=bkb[:, 0:1024].rearrange("p (j c) -> p j c", j=8)[:, :, 0:P]), [bn], [dname])

    S.barrier(); aoff[0] = mP2
    wglu_d = din("w_glu", [D, 2 * D])
    Wglu = sb("Wglu", [128, 8, 2 * D], BF16)
    m1 = aoff[0]
    load_weight(Wglu, wglu_d, 8, 128, 2 * D, "wg")
    S.barrier(); aoff[0] = m1
    Gt = [sb("Gt%d" % i, [128, D], BF16) for i in range(2)]
    Xt = [sb("Xt%d" % i, [128, D]) for i in range(2)]
    GTt = [sb("GTt%d" % i, [128, 8, 128], BF16) for i in range(2)]
    sgb = [sb("sgb%d" % i, [128, 512]) for i in range(2)]
    mmb = [sb("mmb%d" % i, [128, 512]) for i in range(2)]
    tilesB = [(t, 128) for t in range(0, NT, 128)]
    if KST < 4:
        tilesB = []

    def loadB(i):
        t0, P = tilesB[i]; b = i % 2
        S.dma(Gt[b][0:P], gact[t0:t0 + P, :], reads=["gscr"], writes=["Gt%d" % b])
        S.dma(Xt[b][0:P], xsrc(t0, P), writes=["Xt%d" % b])

    if tilesB:
        loadB(0)
    for i, (t0, P) in enumerate(tilesB):
        b = i % 2
        if i + 1 < len(tilesB):
            loadB(i + 1)
        transpose8(Gt[b][0:P], P, GTt[b][:, :, 0:P], "Gt%d" % b, "GTt%d" % b)
        for nh in range(2):
            j = 2 * i + nh
            vb = pbank[2 + 2 * (j % 3)]; gb = pbank[3 + 2 * (j % 3)]
            vn = "pb%d" % (2 + 2 * (j % 3)); gn = "pb%d" % (3 + 2 * (j % 3))
            for dc in range(8):
                S.pe(lambda e, dc=dc, vb=vb, nh=nh, b=b: e.matmul(vb[0:P, :], lhsT=GTt[b][:, dc, 0:P], rhs=Wglu[:, dc, nh * 512:(nh + 1) * 512], start=(dc == 0), stop=(dc == 7)), ["GTt%d" % b], [vn])
            for dc in range(8):
                S.pe(lambda e, dc=dc, gb=gb, nh=nh, b=b: e.matmul(gb[0:P, :], lhsT=GTt[b][:, dc, 0:P], rhs=Wglu[:, dc, D + nh * 512:D + (nh + 1) * 512], start=(dc == 0), stop=(dc == 7)), ["GTt%d" % b], [gn])
            sg = sgb[j % 2]; mm = mmb[j % 2]
            S.act(lambda e, sg=sg, gb=gb: e.activation(out=sg[0:P], in_=gb[0:P, :], func=AF.Sigmoid), [gn], ["sg%d" % (j % 2)])
            S.dve(lambda e, sg=sg, mm=mm, vb=vb: e.tensor_tensor(out=mm[0:P], in0=vb[0:P, :], in1=sg[0:P], op=ALU.mult), [vn, "sg%d" % (j % 2)], ["mm%d" % (j % 2)])
            S.pool(lambda e, mm=mm, b=b, nh=nh: e.tensor_tensor(out=Xt[b][0:P, nh * 512:(nh + 1) * 512], in0=Xt[b][0:P, nh * 512:(nh + 1) * 512], in1=mm[0:P], op=ALU.add), ["mm%d" % (j % 2), "Xt%d" % b], ["Xt%d" % b])
        S.dma(x1s[t0:t0 + P, :], Xt[b][0:P], reads=["Xt%d" % b], writes=["x1s"])

    wup_d = din("w_up", [2, D, DFF]); wgt_d = din("w_gate", [2, D, DFF]); wdn_d = din("w_down", [2, DFF, D])
    cw_d = din("conv_w", [2, 3, DFF]); cb_d = din("conv_b", [2, DFF]); cvst_d = din("cvst", [2, 2, 2, DFF])
    o_cvp = dout("o_cvp", [2, 2, DFF]); o_cvs = dout("o_cvs", [2, 2, 2, DFF])
    o_yp = dout("o_yp", [HALF, D]); o_ys = dout("o_ys", [128, D])
    NFC = DFF // 128

    def ffn_phase(layer, xin, xout, groups, gain_d, final):
        S.barrier(); aoff[0] = mP2
        Wup = sb("Wup", [128, 8, DFF], BF16); Wgt = sb("Wgt", [128, 8, DFF], BF16); Wdn = sb("Wdn", [128, NFC, D], BF16)
        cw = sb("cw", [128, NFC, 3]); cbb = sb("cbb", [128, NFC]); hist = sb("hist", [128, NFC, 2])
        gain = sb("gain", [128, D]); gfin = sb("gfinal", [128, D])
        m1 = aoff[0]
        load_weight(Wup, wup_d[layer], 8, 128, DFF, "wu")
        aoff[0] = m1
        S.barrier()
        load_weight(Wgt, wgt_d[layer], 8, 128, DFF, "wt")
        aoff[0] = m1
        S.barrier()
        load_weight(Wdn, wdn_d[layer], NFC, 128, D, "wd")
        for j in range(3):
            S.dma(cw[:, :, j], cw_d[layer, j].rearrange("(fc f) -> f fc", f=128), writes=["cw"], allow_slow_non_contiguous=True)
        S.dma(cbb[:], cb_d[layer].rearrange("(fc f) -> f fc", f=128), writes=["cbb"], allow_slow_non_contiguous=True)
        S.dma(gain[:], gain_d, writes=["gain"])
        S.dma(gfin[:], gfin_d, writes=["gfin"])
        S.barrier(); aoff[0] = m1
        Xg = [sb("Xg%d" % i, [128, 2, D]) for i in range(2)]
        hsb = sb("hsb", [128, D], BF16)
        hsT = [sb("hsT%d" % i, [128, 8, 256], BF16) for i in range(2)]
        hT = [sb("hT%d" % i, [128, NFC, 256], BF16) for i in range(2)]
        abuf = [sb("abuf%d" % i, [128, 258]) for i in range(2)]
        cbuf = [sb("cbuf%d" % i, [128, 256]) for i in range(2)]
        ybuf = sb("ybuf", [128, D])
        tbuf = [sb("tbuf%d" % i, [128, 256]) for i in range(2)]

        def tiles_of(g):
            n = g["n"]
            return [(0, 128), (1, 128)] if n == 256 else [(0, n)]

        def load(gi):
            g = groups[gi]; b = gi % 2
            if g["n"] == 256:
                S.dma(Xg[b][:], xin[g["t0"]:g["t0"] + 256, :].rearrange("(i p) d -> p i d", i=2), writes=["Xg%d" % b])
            else:
                S.dma(Xg[b][0:g["n"], 0, :], xin[g["t0"]:g["t0"] + g["n"], :], writes=["Xg%d" % b])

        def upgate(gi):
            g = groups[gi]; b = gi % 2; N = g["n"]
            if g["hist"] == "zero":
                S.pool(lambda e: e.memset(hist[:], 0.0), [], ["hist"])
            elif g["hist"] != "carry":
                for j in range(2):
                    S.dma(hist[:, :, j], cvst_d[layer, g["hist"], j].rearrange("(fc f) -> f fc", f=128), writes=["hist"], allow_slow_non_contiguous=True)
            for (ti, P) in tiles_of(g):
                rms_to_bf16(Xg[b][0:P, ti, :], P, gain, hsb[0:P], "Xg%d" % b, "hsb", ti)
                transpose8(hsb[0:P], P, hsT[b][:, :, ti * 128:ti * 128 + P], "hsb", "hsT%d" % b)
            for fc in range(NFC):
                ug = pbank[2 + fc % 2]; un = "pb%d" % (2 + fc % 2)
                for kc in range(8):
                    S.pe(lambda e, kc=kc, fc=fc, ug=ug: e.matmul(ug[:, 0:N], lhsT=Wup[:, kc, fc * 128:(fc + 1) * 128], rhs=hsT[b][:, kc, 0:N], start=(kc == 0), stop=(kc == 7)), ["hsT%d" % b], [un])
                for kc in range(8):
                    S.pe(lambda e, kc=kc, fc=fc, ug=ug: e.matmul(ug[:, 256:256 + N], lhsT=Wgt[:, kc, fc * 128:(fc + 1) * 128], rhs=hsT[b][:, kc, 0:N], start=(kc == 0), stop=(kc == 7)), ["hsT%d" % b], [un])
                ac = abuf[fc % 2]; an = "ab%d" % (fc % 2); ct = cbuf[fc % 2]; cn = "cb%d" % (fc % 2)
                S.pool(lambda e, ac=ac, fc=fc: e.tensor_copy(out=ac[:, 0:2], in_=hist[:, fc, :]), ["hist", an], [an])
                S.act(lambda e, ac=ac, ug=ug: e.copy(out=ac[:, 2:2 + N], in_=ug[:, 0:N]), [un, an], [an])
                S.act(lambda e, ct=ct, ug=ug, fc=fc: e.activation(out=ct[:, 0:N], in_=ug[:, 0:N], func=AF.Identity, scale=cw[:, fc, 2:3], bias=cbb[:, fc:fc + 1]), [un, "cw", "cbb", cn], [cn])
                S.dve(lambda e, ct=ct, ac=ac, fc=fc: e.scalar_tensor_tensor(out=ct[:, 0:N], in0=ac[:, 1:1 + N], scalar=cw[:, fc, 1:2], in1=ct[:, 0:N], op0=ALU.mult, op1=ALU.add), [an, cn, "cw"], [cn])
                tb = tbuf[fc % 2]; tn = "tb%d" % (fc % 2)
                S.pool(lambda e, tb=tb, ac=ac, fc=fc: e.tensor_scalar(out=tb[:, 0:N], in0=ac[:, 0:N], scalar1=cw[:, fc, 0:1], scalar2=None, op0=ALU.mult), [an, "cw", tn], [tn])
                S.pool(lambda e, ct=ct, tb=tb: e.tensor_tensor(out=ct[:, 0:N], in0=ct[:, 0:N], in1=tb[:, 0:N], op=ALU.add), [tn, cn], [cn])
                S.pool(lambda e, ac=ac, fc=fc: e.tensor_copy(out=hist[:, fc, :], in_=ac[:, N:N + 2]), [an, "hist"], ["hist"])
                S.act(lambda e, ct=ct: e.activation(out=ct[:, 0:N], in_=ct[:, 0:N], func=AF.Gelu_apprx_tanh), [cn], [cn])
                S.dve(lambda e, ct=ct, ug=ug, fc=fc: e.tensor_tensor(out=hT[b][:, fc, 0:N], in0=ct[:, 0:N], in1=ug[:, 256:256 + N], op=ALU.mult), [cn, un, "hT%d" % b], ["hT%d" % b])
            if g.get("conv_out") is not None:
                for j in range(2):
                    S.dma(g["conv_out"][j].rearrange("(fc f) -> f fc", f=128), hist[:, :, j], reads=["hist"], allow_slow_non_contiguous=True)

        dcount = [0]

        def down(gi):
            g = groups[gi]; b = gi % 2
            if not g["store"]:
                return
            for (ti, P) in tiles_of(g):
                for half in range(2):
                    k = dcount[0] % 4; dcount[0] += 1
                    ob = pbank[4 + k]; on_ = "pb%d" % (4 + k)
                    for fc in range(NFC):
                        S.pe(lambda e, fc=fc, ob=ob, ti=ti, half=half: e.matmul(ob[0:P, :], lhsT=hT[b][:, fc, ti * 128:ti * 128 + P], rhs=Wdn[:, fc, half * 512:(half + 1) * 512], start=(fc == 0), stop=(fc == NFC - 1)), ["hT%d" % b], [on_])
                    S.dve(lambda e, ob=ob, ti=ti, half=half: e.tensor_tensor(out=Xg[b][0:P, ti, half * 512:(half + 1) * 512], in0=Xg[b][0:P, ti, half * 512:(half + 1) * 512], in1=ob[0:P, :], op=ALU.add), [on_, "Xg%d" % b], ["Xg%d" % b])
                t0 = g["t0"] + ti * 128
                if final:
                    dst = g["yout"][ti * 128:ti * 128 + P, :]
                    S.act(lambda e, ti=ti: e.activation(out=ybuf[0:P], in_=Xg[b][0:P, ti, :], func=AF.Square, accum_out=ssn[0:P, 2:3]), ["Xg%d" % b, "ybuf"], ["ybuf", "ssn2"])
                    S.dve(lambda e: e.tensor_scalar(out=rvn[0:P, 2:3], in0=ssn[0:P, 2:3], scalar1=1.0 / D, scalar2=EPS, op0=ALU.mult, op1=ALU.add), ["ssn2"], ["rvn2"])
                    S.act(lambda e: e.activation(out=rvn[0:P, 2:3], in_=rvn[0:P, 2:3], func=AF.Sqrt), ["rvn2"], ["rvn2"])
                    S.dve(lambda e: e.reciprocal(out=rvn[0:P, 2:3], in_=rvn[0:P, 2:3]), ["rvn2"], ["rvn2"])
                    S.dve(lambda e, ti=ti: e.scalar_tensor_tensor(out=ybuf[0:P], in0=Xg[b][0:P, ti, :], scalar=rvn[0:P, 2:3], in1=gfin[0:P], op0=ALU.mult, op1=ALU.mult), ["Xg%d" % b, "rvn2", "ybuf"], ["ybuf"])
                    S.dma(dst, ybuf[0:P], reads=["ybuf"])
                else:
                    S.dma(xout[t0:t0 + P, :], Xg[b][0:P, ti, :], reads=["Xg%d" % b], writes=["xout"])

        if groups:
            load(0)
        for gi in range(len(groups)):
            upgate(gi)
            if gi > 0:
                down(gi - 1)
            if gi + 1 < len(groups):
                load(gi + 1)
        if groups:
            down(len(groups) - 1)

    groups0 = []
    for gi in range(2 * HALF // 256):
        groups0.append(dict(t0=gi * 256, n=256, hist="zero" if gi == 0 else "carry", store=True,
                            conv_out=o_cvp[0] if gi == 2 * HALF // 256 - 1 else None))
    for sq in range(2):
        groups0.append(dict(t0=2 * HALF + 64 * sq, n=64, hist=sq, store=True, conv_out=o_cvs[0, sq]))
    if KST >= 5:
        ffn_phase(0, x1s, x2s, groups0, gffn_d[0], False)


    wq_d = din("w_qkvf", [D, 3 * D + NH])
    kc_d = din("kcache", [2, HALF, D]); vc_d = din("vcache", [2, HALF, D]); lfc_d = din("lfcache", [2, HALF, NH])
    o_kp = dout("o_kp", [HALF, D]); o_vp = dout("o_vp", [HALF, D]); o_lfp = dout("o_lfp", [HALF, NH])
    o_ks = dout("o_ks", [128, D]); o_vs = dout("o_vs", [128, D]); o_lfs = dout("o_lfs", [128, NH])
    kts = dscr("kts", [NH, HD, NT], BF16); qts = dscr("qts", [NH, HD, NT], BF16); vsb = dscr("vsb", [NT, D], BF16)
    crow = dscr("crow", [NH, NT]); ccol = dscr("ccol", [NT, NH])
    kcT = dscr("kcT", [2, NH, HD, HALF], BF16); vcb = dscr("vcb", [2, HALF, D], BF16); ccc = dscr("ccc", [2, HALF, NH])

    def phase_d():
        S.barrier(); aoff[0] = mP2
        NQ = 3 * D + NH
        Wq = sb("Wq", [128, 8, NQ], BF16)
        gain = sb("gainq", [128, D])
        Lacc = sb("Lacc", [128, NH])
        m1 = aoff[0]
        load_weight(Wq, wq_d, 8, 128, NQ, "wq")
        S.dma(gain[:], gmix_d[1], writes=["gainq"])
        S.barrier(); aoff[0] = m1
        Xq = [sb("Xq%d" % i, [128, D]) for i in range(2)]
        hsq = sb("hsq", [128, D], BF16)
        hsTq = [sb("hsTq%d" % i, [128, 8, 128], BF16) for i in range(2)]
        Qb = [sb("Qb%d" % i, [128, D], BF16) for i in range(2)]
        Kf = [sb("Kf%d" % i, [128, D]) for i in range(2)]
        Kb = [sb("Kb%d" % i, [128, D], BF16) for i in range(2)]
        Vf = [sb("Vf%d" % i, [128, D]) for i in range(2)]
        Vb = [sb("Vb%d" % i, [128, D], BF16) for i in range(2)]
        QTt = [sb("QTt%d" % i, [128, 8, 128], BF16) for i in range(2)]
        KTt = [sb("KTt%d" % i, [128, 8, 128], BF16) for i in range(2)]
        zf = [sb("zf%d" % i, [128, NH]) for i in range(2)]
        lgf = [sb("lgf%d" % i, [128, NH]) for i in range(2)]
        r1 = sb("r1", [128, NH]); L3 = sb("L3", [128, 3, NH], BF16); La3 = sb("La3", [128, 3, NH], BF16); ra = sb("ra", [128, NH])
        csb = [sb("csb%d" % i, [128, NH]) for i in range(2)]
        cTs = [sb("cTs%d" % i, [NH, 128]) for i in range(2)]

        def split3(src, dst3, tmp, P, sname, dname, tname):
            S.dve(lambda e: e.tensor_copy(out=dst3[0:P, 0, :], in_=src[0:P]), [sname, dname], [dname])
            S.dve(lambda e: e.tensor_tensor(out=tmp[0:P], in0=src[0:P], in1=dst3[0:P, 0, :], op=ALU.subtract), [sname, dname, tname], [tname])
            S.dve(lambda e: e.tensor_copy(out=dst3[0:P, 1, :], in_=tmp[0:P]), [tname, dname], [dname])
            S.dve(lambda e: e.tensor_tensor(out=tmp[0:P], in0=tmp[0:P], in1=dst3[0:P, 1, :], op=ALU.subtract), [tname, dname], [tname])
            S.dve(lambda e: e.tensor_copy(out=dst3[0:P, 2, :], in_=tmp[0:P]), [tname, dname], [dname])

        def cumsum_path(lg, lgn, P, b, ccol_dst, crow_dst):
            split3(lg, L3, r1, P, lgn, "L3", "r1")
            split3(Lacc, La3, ra, 128, "Lacc", "La3", "ra")
            cb = pbank[3]
            for j in range(3):
                S.pe(lambda e, j=j: e.matmul(cb[0:P, 0:NH], lhsT=tri[0:P, 0:P], rhs=L3[0:P, j, :], start=(j == 0), stop=False), ["L3", "tri"], ["pb3"])
            for j in range(3):
                S.pe(lambda e, j=j: e.matmul(cb[0:P, 0:NH], lhsT=onesb[:, 0:P], rhs=La3[:, j, :], start=False, stop=(j == 2)), ["La3", "onesb"], ["pb3"])
            S.dve(lambda e: e.tensor_tensor(out=Lacc[0:P], in0=Lacc[0:P], in1=lg[0:P], op=ALU.add), ["Lacc", lgn, "La3"], ["Lacc"])
            S.act(lambda e: e.copy(out=csb[b][0:P], in_=cb[0:P, 0:NH]), ["pb3"], ["csb%d" % b])
            S.dma(ccol_dst, csb[b][0:P], reads=["csb%d" % b], writes=["ccol"])
            if crow_dst is not None:
                bi = tcount[0] % 2; tcount[0] += 1
                tb = pbank[bi]
                S.pe(lambda e: e.transpose(out=tb[0:NH, 0:P], in_=csb[b][0:P, :], identity=ident[0:P, 0:P]), ["csb%d" % b, "ident"], ["pb%d" % bi])
                S.act(lambda e: e.copy(out=cTs[b][0:NH, 0:P], in_=tb[0:NH, 0:P]), ["pb%d" % bi], ["cTs%d" % b])
                S.dma(crow_dst, cTs[b][0:NH, 0:P], reads=["cTs%d" % b], writes=["crow"])

        def kt_dst(scr, t0, P):
            return scr.rearrange("(hp h2) d t -> (h2 d) hp t", h2=2)[:, :, t0:t0 + P]

        tiles = []
        for i in range(2 * HALF // 128):
            tiles.append(dict(kind="full", t0=i * 128, P=128, reset=(i == 0)))
        for sq in range(2):
            for j in range(HALF // 128):
                tiles.append(dict(kind="cache", sq=sq, j=j, P=128, reset=(j == 0)))
            tiles.append(dict(kind="full", t0=2 * HALF + 64 * sq, P=64, reset=False))

        def load(i):
            tl = tiles[i]; b = i % 2; P = tl["P"]
            if tl["kind"] == "full":
                S.dma(Xq[b][0:P], x2s[tl["t0"]:tl["t0"] + P, :], writes=["Xq%d" % b])
            else:
                sq, j = tl["sq"], tl["j"]
                S.dma(Kf[b][:], kc_d[sq, j * 128:(j + 1) * 128, :], writes=["Kf%d" % b])
                S.dma(Vf[b][:], vc_d[sq, j * 128:(j + 1) * 128, :], writes=["Vf%d" % b])
                S.dma(lgf[b][:], lfc_d[sq, j * 128:(j + 1) * 128, :], writes=["lgf%d" % b])

        def do_tile(i, tl):
            b = i % 2; P = tl["P"]
            if i + 1 < len(tiles):
                load(i + 1)
            if tl["reset"]:
                S.dve(lambda e: e.memset(Lacc[:], 0.0), ["Lacc"], ["Lacc"])
            if tl["kind"] == "cache":
                sq, j = tl["sq"], tl["j"]
                S.pool(lambda e, b=b: e.tensor_copy(out=Kb[b][:], in_=Kf[b][:]), ["Kf%d" % b], ["Kb%d" % b])
                S.pool(lambda e, b=b: e.tensor_copy(out=Vb[b][:], in_=Vf[b][:]), ["Vf%d" % b], ["Vb%d" % b])
                transpose8(Kb[b][:], 128, KTt[b][:], "Kb%d" % b, "KTt%d" % b)
                S.dma(kt_dst(kcT[sq], j * 128, 128), KTt[b][:], reads=["KTt%d" % b], writes=["kcT"])
                S.dma(vcb[sq, j * 128:(j + 1) * 128, :], Vb[b][:], reads=["Vb%d" % b], writes=["vcb"])
                cumsum_path(lgf[b], "lgf%d" % b, 128, b, ccc[sq, j * 128:(j + 1) * 128, :], None)
                return
            t0 = tl["t0"]
            rms_to_bf16(Xq[b][0:P], P, gain, hsq[0:P], "Xq%d" % b, "hsq", 3)
            transpose8(hsq[0:P], P, hsTq[b][:, :, 0:P], "hsq", "hsTq%d" % b)
            hT_ = hsTq[b]; hn = "hsTq%d" % b

            def mm(bank, cols, c0, n):
                for kc in range(8):
                    S.pe(lambda e, kc=kc: e.matmul(pbank[bank][0:P, cols], lhsT=hT_[:, kc, 0:P], rhs=Wq[:, kc, c0:c0 + n], start=(kc == 0), stop=(kc == 7)), [hn], ["pb%d" % bank])
            mm(2, slice(0, NH), 3 * D, NH)
            mm(3, slice(0, 512), 0, 512); mm(4, slice(0, 512), 512, 512)
            mm(5, slice(0, 512), D, 512); mm(6, slice(0, 512), D + 512, 512)
            mm(7, slice(0, 512), 2 * D, 512)
            S.dve(lambda e, b=b: e.tensor_tensor(out=zf[b][0:P], in0=pbank[2][0:P, 0:NH], in1=bft[0:P], op=ALU.add), ["pb2", "bft"], ["zf%d" % b])
            mm(2, slice(0, 512), 2 * D + 512, 512)
            S.act(lambda e, b=b: e.activation(out=zf[b][0:P], in_=zf[b][0:P], func=AF.Exp, scale=-1.0), ["zf%d" % b], ["zf%d" % b])
            S.act(lambda e, b=b: e.activation(out=zf[b][0:P], in_=zf[b][0:P], func=AF.Ln, bias=one1[0:P, 0:1]), ["zf%d" % b, "one1"], ["zf%d" % b])
            S.dve(lambda e, b=b: e.tensor_scalar(out=lgf[b][0:P], in0=zf[b][0:P], scalar1=-1.0, scalar2=None, op0=ALU.mult), ["zf%d" % b], ["lgf%d" % b])
            for hf in range(2):
                S.act(lambda e, hf=hf, b=b: e.activation(out=Qb[b][0:P, hf * 512:(hf + 1) * 512], in_=pbank[3 + hf][0:P, :], func=AF.Copy, scale=HD ** -0.5), ["pb%d" % (3 + hf)], ["Qb%d" % b])
                S.dve(lambda e, hf=hf, b=b: e.tensor_copy(out=Kf[b][0:P, hf * 512:(hf + 1) * 512], in_=pbank[5 + hf][0:P, :]), ["pb%d" % (5 + hf)], ["Kf%d" % b])
            S.act(lambda e, b=b: e.copy(out=Vf[b][0:P, 0:512], in_=pbank[7][0:P, :]), ["pb7"], ["Vf%d" % b])
            S.act(lambda e, b=b: e.copy(out=Vf[b][0:P, 512:1024], in_=pbank[2][0:P, :]), ["pb2"], ["Vf%d" % b])
            S.pool(lambda e, b=b: e.tensor_copy(out=Kb[b][0:P], in_=Kf[b][0:P]), ["Kf%d" % b], ["Kb%d" % b])
            S.pool(lambda e, b=b: e.tensor_copy(out=Vb[b][0:P], in_=Vf[b][0:P]), ["Vf%d" % b], ["Vb%d" % b])
            transpose8(Qb[b][0:P], P, QTt[b][:, :, 0:P], "Qb%d" % b, "QTt%d" % b)
            transpose8(Kb[b][0:P], P, KTt[b][:, :, 0:P], "Kb%d" % b, "KTt%d" % b)
            S.dma(kt_dst(qts, t0, P), QTt[b][:, :, 0:P], reads=["QTt%d" % b], writes=["qts"])
            S.dma(kt_dst(kts, t0, P), KTt[b][:, :, 0:P], reads=["KTt%d" % b], writes=["kts"])
            S.dma(vsb[t0:t0 + P, :], Vb[b][0:P], reads=["Vb%d" % b], writes=["vsb"])
            if t0 >= 2 * HALF:
                o0 = t0 - 2 * HALF
                S.dma(o_ks[o0:o0 + P, :], Kf[b][0:P], reads=["Kf%d" % b]); S.dma(o_vs[o0:o0 + P, :], Vf[b][0:P], reads=["Vf%d" % b])
                S.dma(o_lfs[o0:o0 + P, :], lgf[b][0:P], reads=["lgf%d" % b])
            elif t0 >= HALF:
                o0 = t0 - HALF
                S.dma(o_kp[o0:o0 + P, :], Kf[b][0:P], reads=["Kf%d" % b]); S.dma(o_vp[o0:o0 + P, :], Vf[b][0:P], reads=["Vf%d" % b])
                S.dma(o_lfp[o0:o0 + P, :], lgf[b][0:P], reads=["lgf%d" % b])
            cumsum_path(lgf[b], "lgf%d" % b, P, b, ccol[t0:t0 + P, :], crow[:, t0:t0 + P])

        load(0)
        for i, tl in enumerate(tiles):
            do_tile(i, tl)

    if KST >= 6:
        phase_d()


    ats = dscr("ats", [NH, HD, NT], BF16)
    QS0 = HALF - 512

    def phase_e():
        S.barrier(); aoff[0] = mP2
        NKB = 2 * HALF // 128
        KA = [sb("KA%d" % i, [128, 2 * HALF], BF16) for i in range(2)]
        VA = [sb("VA%d" % i, [128, NKB, 65], BF16) for i in range(2)]
        CK = [sb("CK%d" % i, [128, NKB]) for i in range(2)]
        QA = [sb("QA%d" % i, [128, 512], BF16) for i in range(2)]
        crq = [sb("crq%d" % i, [128, 512]) for i in range(2)]
        cref = [sb("cref%d" % i, [128, 1]) for i in range(2)]
        cbias = [sb("cbias%d" % i, [128, NKB]) for i in range(2)]
        dq = sb("dq", [128, 512]); hq = sb("hq", [128, 512], BF16)
        PT = [sb("PT%d" % i, [128, 512], BF16) for i in range(4)]
        Osb = [sb("Osb%d" % i, [128, 512]) for i in range(2)]
        rt = sb("rt", [128, 512]); rt2 = sb("rt2", [128, 512]); rl3 = sb("rl3", [128, 3, 512], BF16)
        ATb = [sb("ATb%d" % i, [128, 512], BF16) for i in range(2)]
        for i in range(2):
            S.pool(lambda e, i=i: e.memset(KA[i][32:33, :], 1.0), [], ["KA%d" % i])
            S.pool(lambda e, i=i: e.memset(KA[i][64:65, :], 1.0), [], ["KA%d" % i])
            S.pool(lambda e, i=i: e.memset(VA[i][:, :, 64:65], 1.0), [], ["VA%d" % i])
            S.pool(lambda e, i=i: e.memset(QA[i][:], 0.0), [], ["QA%d" % i])
        sel = sb("sel", [128, 64], BF16)
        S.pool(lambda e: e.memset(sel[:], 0.0), [], ["sel"])
        S.pool(lambda e: e.memset(sel[64:65, :], 1.0), ["sel"], ["sel"])
        S.pool(lambda e: e.memset(rl3[:], 0.0), [], ["rl3"])
        cnt = {"s": 0, "q": 0, "p": 0}

        def rows67(dst, src, c0, c1, name, wname):
            S.dma(dst[0:32, c0:c1], src[0:32, :], writes=[wname])
            S.dma(dst[33:64, c0:c1], src[32:63, :], writes=[wname])
            S.dma(dst[65:66, c0:c1], src[63:64, :], writes=[wname])

        def attend(h, hb, q_t0, nq, blocks, pm_nb, out_t0):
            qb = cnt["q"] % 2; cnt["q"] += 1
            qn = "QA%d" % qb
            rows67(QA[qb], qts[h, :, q_t0:q_t0 + nq], 0, nq, qn, qn)
            S.dma(crq[qb][32:33, 0:nq], crow[h:h + 1, q_t0:q_t0 + nq], writes=["crq%d" % qb])
            S.dma(crq[qb][64:65, 0:nq], crow[h:h + 1, q_t0:q_t0 + nq], writes=["crq%d" % qb])
            S.dma(cref[qb][:], crow[h:h + 1, q_t0:q_t0 + 1].partition_broadcast(128), writes=["cref%d" % qb])
            S.dve(lambda e: e.tensor_scalar(out=dq[64:65, 0:nq], in0=crq[qb][64:65, 0:nq], scalar1=crq[qb][64:65, 0:1], scalar2=None, op0=ALU.subtract), ["crq%d" % qb, "dq64"], ["dq64"])
            S.dve(lambda e: e.tensor_copy(out=QA[qb][64:65, 0:nq], in_=dq[64:65, 0:nq]), ["dq64", qn], [qn])
            S.dve(lambda e: e.tensor_scalar(out=dq[32:33, 0:nq], in0=crq[qb][32:33, 0:nq], scalar1=crq[qb][32:33, 0:1], scalar2=None, op0=ALU.subtract), ["crq%d" % qb, "dq32"], ["dq32"])
            S.dve(lambda e: e.tensor_copy(out=hq[32:33, 0:nq], in_=dq[32:33, 0:nq]), ["dq32", "hq"], ["hq"])
            S.dve(lambda e: e.tensor_tensor(out=dq[32:33, 0:nq], in0=dq[32:33, 0:nq], in1=hq[32:33, 0:nq], op=ALU.subtract), ["dq32", "hq"], ["dq32"])
            S.dve(lambda e: e.tensor_copy(out=QA[qb][32:33, 0:nq], in_=dq[32:33, 0:nq]), ["dq32", qn], [qn])
            nbk = max(bk for (bk, nk, c0) in blocks) + 1
            S.dve(lambda e: e.tensor_scalar(out=cbias[qb][:, 0:nbk], in0=CK[hb][:, 0:nbk], scalar1=-1.0, scalar2=cref[qb][:, 0:1], op0=ALU.mult, op1=ALU.add), ["CK%d" % hb, "cref%d" % qb, "cbias%d" % qb], ["cbias%d" % qb])
            if pm_nb > 0:
                S.dve(lambda e: e.tensor_scalar(out=cbias[qb][:, 0:pm_nb], in0=cbias[qb][:, 0:pm_nb], scalar1=pmk[:, 0:1], scalar2=None, op0=ALU.add), ["cbias%d" % qb, "pmk"], ["cbias%d" % qb])
            ob = pbank[5 + qb]; obn = "pb%d" % (5 + qb)
            KEL = int(os.environ.get("KE_LVL", "9"))
            if KEL < 2:
                return
            for bi_, (bk, nk, c0) in enumerate(blocks):
                si = cnt["s"] % 3; cnt["s"] += 1
                pi = cnt["p"] % 4; cnt["p"] += 1
                sbk = pbank[2 + si]; sn = "pb%d" % (2 + si)
                cc = 0 if c0 is None else c0
                S.pe(lambda e, bk=bk, nk=nk, cc=cc, sbk=sbk: e.matmul(sbk[0:nk, cc:nq], lhsT=KA[hb][0:66, bk * 128:bk * 128 + nk], rhs=QA[qb][0:66, cc:nq], start=True, stop=True), ["KA%d" % hb, qn], [sn])
                if c0 is not None:
                    w = min(128, nq - cc)
                    S.dve(lambda e, nk=nk, cc=cc, sbk=sbk, w=w: e.tensor_tensor(out=sbk[0:nk, cc:cc + w], in0=sbk[0:nk, cc:cc + w], in1=cmask[0:nk, 0:w], op=ALU.add), [sn, "cmask"], [sn])
                S.act(lambda e, bk=bk, nk=nk, cc=cc, sbk=sbk, pi=pi: e.activation(out=PT[pi][0:nk, cc:nq], in_=sbk[0:nk, cc:nq], func=AF.Exp, bias=cbias[qb][0:nk, bk:bk + 1]), [sn, "cbias%d" % qb], ["PT%d" % pi])
                if KEL < 3:
                    continue
                S.pe(lambda e, bk=bk, nk=nk, cc=cc, pi=pi, bi_=bi_: e.matmul(ob[0:65, cc:nq], lhsT=VA[hb][0:nk, bk, 0:65], rhs=PT[pi][0:nk, cc:nq], start=(bi_ == 0), stop=(bi_ == len(blocks) - 1)), ["VA%d" % hb, "PT%d" % pi], [obn])
            if KEL < 4:
                return
            S.act(lambda e: e.copy(out=Osb[qb][0:65, 0:nq], in_=ob[0:65, 0:nq]), [obn], ["Osb%d" % qb])
            S.dve(lambda e: e.reciprocal(out=rt[64:65, 0:nq], in_=Osb[qb][64:65, 0:nq]), ["Osb%d" % qb, "rt"], ["rt"])
            S.dve(lambda e: e.tensor_copy(out=rl3[64:65, 0, 0:nq], in_=rt[64:65, 0:nq]), ["rt", "rl3"], ["rl3"])
            S.dve(lambda e: e.tensor_tensor(out=rt2[64:65, 0:nq], in0=rt[64:65, 0:nq], in1=rl3[64:65, 0, 0:nq], op=ALU.subtract), ["rt", "rl3", "rt2"], ["rt2"])
            S.dve(lambda e: e.tensor_copy(out=rl3[64:65, 1, 0:nq], in_=rt2[64:65, 0:nq]), ["rt2", "rl3"], ["rl3"])
            S.dve(lambda e: e.tensor_tensor(out=rt2[64:65, 0:nq], in0=rt2[64:65, 0:nq], in1=rl3[64:65, 1, 0:nq], op=ALU.subtract), ["rt2", "rl3"], ["rt2"])
            S.dve(lambda e: e.tensor_copy(out=rl3[64:65, 2, 0:nq], in_=rt2[64:65, 0:nq]), ["rt2", "rl3"], ["rl3"])
            bc = pbank[7]
            for j in range(3):
                S.pe(lambda e, j=j: e.matmul(bc[0:64, 0:nq], lhsT=sel[0:65, 0:64], rhs=rl3[0:65, j, 0:nq], start=(j == 0), stop=(j == 2)), ["rl3", "sel"], ["pb7"])
            S.dve(lambda e: e.tensor_tensor(out=ATb[qb][0:64, 0:nq], in0=Osb[qb][0:64, 0:nq], in1=bc[0:64, 0:nq], op=ALU.mult), ["Osb%d" % qb, "pb7", "ATb%d" % qb], ["ATb%d" % qb])
            S.dma(ats[h, :, out_t0:out_t0 + nq], ATb[qb][0:64, 0:nq], reads=["ATb%d" % qb], writes=["ats"])

        hcount = [0]
        for h in range(int(os.environ.get('KE_H', NH)) if KST >= 7 else 0):
            hb = hcount[0] % 2; hcount[0] += 1
            rows67(KA[hb], kts[h, :, 0:2 * HALF], 0, 2 * HALF, "k", "KA%d" % hb)
            S.dma(VA[hb][:, :, 0:64], vsb[0:2 * HALF, h * HD:(h + 1) * HD].rearrange("(kb p) d -> p kb d", p=128), writes=["VA%d" % hb])
            S.dma(CK[hb][:, :], ccol[0:2 * HALF, h:h + 1].rearrange("(kb p) o -> p (kb o)", p=128), writes=["CK%d" % hb], allow_slow_non_contiguous=True)
            for j in range(9):
                qs = QS0 + 512 * j
                nfull = qs // 128
                blocks = [(kb, 128, None) for kb in range(nfull)] + [(nfull + d_, 128, 128 * d_) for d_ in range(4)]
                attend(h, hb, qs, 512, blocks, (QS0 // 128) if j == 0 else (HALF // 128), qs)
        for sq in range(int(os.environ.get('KE_S', 2)) if KST >= 7 else 0):
            tn = 2 * HALF + 64 * sq
            for h in range(NH):
                hb = hcount[0] % 2; hcount[0] += 1
                rows67(KA[hb], kcT[sq, h], 0, HALF, "k", "KA%d" % hb)
                rows67(KA[hb], kts[h, :, tn:tn + 64], HALF, HALF + 64, "k", "KA%d" % hb)
                S.dma(VA[hb][:, 0:32, 0:64], vcb[sq, :, h * HD:(h + 1) * HD].rearrange("(kb p) d -> p kb d", p=128), writes=["VA%d" % hb])
                S.dma(VA[hb][0:64, 32, 0:64], vsb[tn:tn + 64, h * HD:(h + 1) * HD], writes=["VA%d" % hb])
                S.dma(CK[hb][:, 0:32], ccc[sq, :, h:h + 1].rearrange("(kb p) o -> p (kb o)", p=128), writes=["CK%d" % hb], allow_slow_non_contiguous=True)
                S.dma(CK[hb][0:64, 32:33], ccol[tn:tn + 64, h:h + 1], writes=["CK%d" % hb], allow_slow_non_contiguous=True)
                blocks = [(kb, 128, None) for kb in range(32)] + [(32, 128, 0)]
                attend(h, hb, tn, 64, blocks, 0, tn)

    if KST >= 7:
        phase_e()

    wo_d = din("w_o", [D, D])

    def phase_f():
        S.barrier(); aoff[0] = mP2
        Wo = sb("Wo", [128, 8, D], BF16)
        m1 = aoff[0]
        load_weight(Wo, wo_d, 8, 128, D, "wo")
        S.barrier(); aoff[0] = m1
        ATt = [sb("ATt%d" % i, [128, 8, 128], BF16) for i in range(2)]
        Xf = [sb("Xf%d" % i, [128, D]) for i in range(2)]
        tiles = [(t, 128) for t in range(QS0, NT, 128)]

        def load(i):
            t0, P = tiles[i]; b = i % 2
            for h2 in range(2):
                S.dma(ATt[b][64 * h2:64 * h2 + 64, :, 0:P], ats[:, :, t0:t0 + P].rearrange("(hp h2) d t -> h2 d hp t", h2=2)[h2], writes=["ATt%d" % b])
            S.dma(Xf[b][0:P], x2s[t0:t0 + P, :], writes=["Xf%d" % b])

        def do(i):
            t0, P = tiles[i]; b = i % 2
            if i + 1 < len(tiles):
                load(i + 1)
            for half in range(2 if not os.environ.get("KF_NOMM") else 0):
                k = (2 * i + half) % 4
                ob = pbank[2 + k]; on_ = "pb%d" % (2 + k)
                for h in range(8):
                    S.pe(lambda e, h=h, half=half, ob=ob: e.matmul(ob[0:P, :], lhsT=ATt[b][:, h, 0:P], rhs=Wo[:, h, half * 512:(half + 1) * 512], start=(h == 0), stop=(h == 7)), ["ATt%d" % b], [on_])
                S.dve(lambda e, half=half, ob=ob: e.tensor_tensor(out=Xf[b][0:P, half * 512:(half + 1) * 512], in0=Xf[b][0:P, half * 512:(half + 1) * 512], in1=ob[0:P, :], op=ALU.add), [on_, "Xf%d" % b], ["Xf%d" % b])
            S.dma(x3s[t0:t0 + P, :], Xf[b][0:P], reads=["Xf%d" % b], writes=["x3s"])

        load(0)
        for i in range(len(tiles)):
            do(i)

    if KST >= 8:
        phase_f()
        groups1 = [dict(t0=HALF - 256, n=256, hist="zero", store=False, conv_out=None)]
        for gi in range(HALF // 256):
            groups1.append(dict(t0=HALF + gi * 256, n=256, hist="carry", store=True, yout=o_yp[gi * 256:(gi + 1) * 256, :],
                                conv_out=o_cvp[1] if gi == HALF // 256 - 1 else None))
        for sq in range(2):
            groups1.append(dict(t0=2 * HALF + 64 * sq, n=64, hist=sq, store=True, yout=o_ys[64 * sq:64 * sq + 64, :], conv_out=o_cvs[1, sq]))
        if not os.environ.get("KSKIPG"):
            ffn_phase(1, x3s, None, groups1, gffn_d[1], True)

    if os.environ.get("KDBG"):
        dbgG = dout("dbgG", [128, D], BF16); dbgX1 = dout("dbgX1", [128, D]); dbgX2 = dout("dbgX2", [128, D])
        S.barrier()
        S.dma(dbgG, gact[2 * HALF:2 * HALF + 128, :])
        S.dma(dbgX1, x1s[2 * HALF:2 * HALF + 128, :])
        S.dma(dbgX2, x2s[2 * HALF:2 * HALF + 128, :])
    S.emit()
    st.close()
    return nc


def _pair_layout(a):
    a = np.asarray(a)
    rest = a.shape[2:]
    a = a.reshape((32, 2, 64) + rest)
    a = np.moveaxis(a, 0, 2)
    return np.ascontiguousarray(a.reshape((128, 32) + rest))


def _from_pair_layout(a):
    a = a.reshape(2, 64, 32)
    return np.ascontiguousarray(np.moveaxis(a, 2, 0).reshape(64, 64))


def _host_inputs(inp):
    f = np.float32
    xpr = np.asarray(inp["x_prompt"], f)
    xsm = np.asarray(inp["x_sample"], f)
    rep = lambda v: np.ascontiguousarray(np.broadcast_to(np.asarray(v, f)[None, :], (128, np.asarray(v).shape[-1])))
    common = dict(
        lam_r=_pair_layout(np.asarray(inp["ssm_a_re"], f)[0]),
        lam_i=_pair_layout(np.asarray(inp["ssm_a_im"], f)[0]),
        lstep=_pair_layout(np.broadcast_to(np.asarray(inp["ssm_log_step"], f)[0][:, None], (64, 64))),
        b_r=_pair_layout(np.asarray(inp["ssm_b_re"], f)[0]),
        b_i=_pair_layout(np.asarray(inp["ssm_b_im"], f)[0]),
        c_r=_pair_layout(np.transpose(np.asarray(inp["ssm_c_re"], f)[0], (0, 2, 1))),
        c_i=_pair_layout(np.transpose(np.asarray(inp["ssm_c_im"], f)[0], (0, 2, 1))),
        gmix=np.stack([rep(inp["norm_mix"][i]) for i in range(2)]),
        gffn=np.stack([rep(inp["norm_ffn"][i]) for i in range(2)]),
        gfin=rep(inp["norm_final"]),
        dskip=rep(inp["ssm_d"][0]),
        ident=np.eye(128, dtype=f),
        tmask=(np.arange(128)[:, None] // 16 <= np.arange(128)[None, :] // 16).astype(f),
        tri=(np.arange(128)[:, None] <= np.arange(128)[None, :]).astype(f),
        cmask=np.where(np.arange(128)[:, None] <= np.arange(128)[None, :], 0.0, -30000.0).astype(f),
        bft=rep(inp["fox_b_f"][0]),
        w_glu=np.ascontiguousarray(np.asarray(inp["ssm_w_glu"], f)[0]),
        w_up=np.asarray(inp["ffn_w_up"], f), w_gate=np.asarray(inp["ffn_w_gate"], f), w_down=np.asarray(inp["ffn_w_down"], f),
        conv_w=np.asarray(inp["ffn_conv_w"], f), conv_b=np.asarray(inp["ffn_conv_b"], f),
        w_qkvf=np.ascontiguousarray(np.asarray(inp["fox_w_qkvf"], f)[0]),
        w_o=np.ascontiguousarray(np.asarray(inp["fox_w_o"], f)[0]),
    )
    maps = []
    for k in range(8):
        b, half = k // 2, k % 2
        prev = xpr[b, 0:HALF] if half == 1 else np.zeros((HALF, D), f)
        own = xpr[b, half * HALF:(half + 1) * HALF]
        m = dict(common)
        m["xp"] = np.ascontiguousarray(np.concatenate([prev, own], 0))
        m["xs"] = np.ascontiguousarray(xsm[2 * k:2 * k + 2].reshape(128, D))
        m["kcache"] = np.ascontiguousarray(np.asarray(inp["cache_fox_k"], f)[0, 2 * k:2 * k + 2].reshape(2, HALF, D))
        m["vcache"] = np.ascontiguousarray(np.asarray(inp["cache_fox_v"], f)[0, 2 * k:2 * k + 2].reshape(2, HALF, D))
        m["lfcache"] = np.ascontiguousarray(np.asarray(inp["cache_fox_logf"], f)[0, 2 * k:2 * k + 2])
        m["pmk"] = np.full((128, 1), 0.0 if half == 1 else -30000.0, f)
        m["cvst"] = np.ascontiguousarray(np.asarray(inp["state_ffn_conv"], f)[:, 2 * k:2 * k + 2])
        m["h0r"] = np.ascontiguousarray(np.stack([_pair_layout(np.asarray(inp["state_ssm_re"], f)[0, 2 * k + s]) for s in range(2)], 1))
        m["h0i"] = np.ascontiguousarray(np.stack([_pair_layout(np.asarray(inp["state_ssm_im"], f)[0, 2 * k + s]) for s in range(2)], 1))
        maps.append(m)
    return maps


_NC_CACHE = {}


def kernel(**inputs):
    maps = _host_inputs(inputs)
    if "nc" not in _NC_CACHE:
        _NC_CACHE["nc"] = build()
    nc = _NC_CACHE["nc"]
    res = run_bass_kernel_spmd(nc, maps, core_ids=list(range(8)))
    R = res.results
    f = np.float32
    y_prompt = np.zeros((4, 8192, D), f); y_sample = np.zeros((16, 64, D), f)
    ssm_re_p = np.zeros((1, 4, 64, 64), f); ssm_im_p = np.zeros((1, 4, 64, 64), f)
    ssm_re_s = np.zeros((1, 16, 64, 64), f); ssm_im_s = np.zeros((1, 16, 64, 64), f)
    k_p = np.zeros((1, 4, 8192, NH, HD), f); v_p = np.zeros((1, 4, 8192, NH, HD), f); lf_p = np.zeros((1, 4, 8192, NH), f)
    k_s = np.zeros((1, 16, 64, NH, HD), f); v_s = np.zeros((1, 16, 64, NH, HD), f); lf_s = np.zeros((1, 16, 64, NH), f)
    cv_p = np.zeros((2, 4, 2, DFF), f); cv_s = np.zeros((2, 16, 2, DFF), f)
    for k in range(8):
        b, half = k // 2, k % 2
        r = R[k]
        if half == 1:
            ssm_re_p[0, b] = _from_pair_layout(r["o_ssm_p"][0]); ssm_im_p[0, b] = _from_pair_layout(r["o_ssm_p"][1])
        if "o_cvp" in r:
            if half == 1:
                cv_p[:, b] = r["o_cvp"]
            cv_s[:, 2 * k:2 * k + 2] = r["o_cvs"]
        if "o_kp" in r:
            sl = slice(half * HALF, (half + 1) * HALF)
            k_p[0, b, sl] = r["o_kp"].reshape(HALF, NH, HD); v_p[0, b, sl] = r["o_vp"].reshape(HALF, NH, HD); lf_p[0, b, sl] = r["o_lfp"]
            k_s[0, 2 * k:2 * k + 2] = r["o_ks"].reshape(2, 64, NH, HD); v_s[0, 2 * k:2 * k + 2] = r["o_vs"].reshape(2, 64, NH, HD)
            lf_s[0, 2 * k:2 * k + 2] = r["o_lfs"].reshape(2, 64, NH)
        if "o_yp" in r:
            y_prompt[b, half * HALF:(half + 1) * HALF] = r["o_yp"]
            y_sample[2 * k:2 * k + 2] = r["o_ys"].reshape(2, 64, D)
        for s in range(2):
            ssm_re_s[0, 2 * k + s] = _from_pair_layout(r["o_ssm_s"][s, 0]); ssm_im_s[0, 2 * k + s] = _from_pair_layout(r["o_ssm_s"][s, 1])
    kernel.last = R
    return (y_prompt, y_sample, ssm_re_p, ssm_im_p, k_p, v_p, lf_p, cv_p, ssm_re_s, ssm_im_s, k_s, v_s, lf_s, cv_s)
```

```python
import contextlib
import math
import numpy as np
import concourse.bass as bass
import concourse.mybir as mybir
from concourse.bass_utils import run_bass_kernel_spmd

F32 = mybir.dt.float32
BF16 = mybir.dt.bfloat16
AF = mybir.ActivationFunctionType
ALU = mybir.AluOpType
AX = mybir.AxisListType

D = 1024
DFF = 2816
NH = 16
HD = 64
HALF = 4096
NPAIR = 32
EPS = 1e-6
TWO_PI = 2.0 * math.pi

ENGS = ("pe", "act", "dve", "pool", "sp")
NO_SELF_WAIT = ("pe", "sp")
DMA_RING = 8


class Op:
    __slots__ = ("eng", "fn", "deps", "dma", "idx", "needed", "cnt", "slot", "tgt")

    def __init__(self, eng, fn, deps, dma, idx):
        self.eng, self.fn, self.deps, self.dma, self.idx = eng, fn, deps, dma, idx
        self.needed = False
        self.cnt = 0
        self.slot = -1
        self.tgt = 0


class Sched:
    def __init__(self, nc):
        self.nc = nc
        self.ops = {e: [] for e in ENGS}
        self.last_w = {}
        self.readers = {}
        self.ndma = {e: 0 for e in ENGS}
        self.last_slot = {}

    def add(self, eng, fn, reads=(), writes=(), dma=False):
        lst = self.ops[eng]
        idx = len(lst)
        deps = set()
        for r in reads:
            lw = self.last_w.get(r)
            if lw is not None:
                deps.add(lw)
        for w in writes:
            lw = self.last_w.get(w)
            if lw is not None:
                deps.add(lw)
            rd = self.readers.get(w)
            if rd:
                for e, i in rd.items():
                    deps.add((e, i))
        op = Op(eng, fn, deps, dma, idx)
        if dma:
            k = self.ndma[eng]
            self.ndma[eng] = k + 1
            op.slot = k % DMA_RING
            op.tgt = 16 * (k // DMA_RING + 1)
            op.needed = True
            prev = self.last_slot.get((eng, op.slot))
            if prev is not None:
                deps.add((eng, prev))
            self.last_slot[(eng, op.slot)] = idx
        lst.append(op)
        for r in reads:
            self.readers.setdefault(r, {})[eng] = idx
        for w in writes:
            self.last_w[w] = (eng, idx)
            self.readers[w] = {}
        return op

    def pe(self, fn, reads=(), writes=()):
        return self.add("pe", fn, reads, writes)

    def act(self, fn, reads=(), writes=()):
        return self.add("act", fn, reads, writes)

    def dve(self, fn, reads=(), writes=()):
        return self.add("dve", fn, reads, writes)

    def pool(self, fn, reads=(), writes=()):
        return self.add("pool", fn, reads, writes)

    def on(self, eng, fn, reads=(), writes=()):
        return self.add(eng, fn, reads, writes)

    def dma(self, out, in_, reads=(), writes=(), q="sp", **kw):
        return self.add(q, lambda e: e.dma_start(out=out, in_=in_, **kw), reads, writes, dma=True)

    def barrier(self):
        deps = set()
        for e in ENGS:
            lst = self.ops[e]
            seen = set()
            got_last = False
            for j in range(len(lst) - 1, -1, -1):
                o = lst[j]
                if o.dma:
                    if o.slot not in seen:
                        seen.add(o.slot)
                        deps.add((e, j))
                elif not got_last and o.fn is not None:
                    got_last = True
                    deps.add((e, j))
                if got_last and len(seen) >= DMA_RING:
                    break
        for e in ENGS:
            self.ops[e].append(Op(e, None, set(deps), False, len(self.ops[e])))
        self.last_w.clear()
        self.readers.clear()

    def emit(self, final_wait_eng="sp"):
        nc = self.nc
        for e in ENGS:
            for op in self.ops[e]:
                for (de, di) in op.deps:
                    dop = self.ops[de][di]
                    if dop.dma:
                        continue
                    if de == e and e in NO_SELF_WAIT and op.fn is not None:
                        continue
                    if de == e and op.fn is None:
                        continue
                    dop.needed = True
        finals = []
        for e in ENGS:
            if self.ops[e]:
                last = None
                for o in reversed(self.ops[e]):
                    if not o.dma and o.fn is not None:
                        last = o
                        break
                if last is not None:
                    last.needed = True
                    finals.append((e, last.idx))
                seen = {}
                for op in self.ops[e]:
                    if op.dma:
                        seen[op.slot] = op.idx
                for sl, ix in seen.items():
                    finals.append((e, ix))
        for e in ENGS:
            c = 0
            for op in self.ops[e]:
                if op.needed and not op.dma and op.fn is not None:
                    c += 1
                    op.cnt = c
        with contextlib.ExitStack() as st:
            esem = {e: st.enter_context(nc.semaphore("s_" + e)) for e in ENGS}
            dsem = {e: [st.enter_context(nc.semaphore("d_%s%d" % (e, i))) for i in range(DMA_RING)]
                    for e in ENGS if self.ndma[e] > 0}
            block = st.enter_context(nc.Block())
            ops = self.ops

            def run(ename, eng):
                waited = {}
                for op in ops[ename]:
                    for (de, di) in sorted(op.deps):
                        dop = ops[de][di]
                        if dop.dma:
                            key = ("d", de, dop.slot)
                            if waited.get(key, 0) >= dop.tgt:
                                continue
                            waited[key] = dop.tgt
                            eng.wait_ge(dsem[de][dop.slot], dop.tgt)
                        else:
                            if de == ename and (ename in NO_SELF_WAIT or op.fn is None):
                                continue
                            key = ("e", de)
                            if waited.get(key, 0) >= dop.cnt:
                                continue
                            waited[key] = dop.cnt
                            eng.wait_ge(esem[de], dop.cnt)
                    if op.fn is None:
                        continue
                    ins = op.fn(eng)
                    if op.dma:
                        ins.then_inc(dsem[ename][op.slot], 16)
                    elif op.needed:
                        ins.then_inc(esem[ename], 1)
                if ename == final_wait_eng:
                    for (de, di) in finals:
                        dop = ops[de][di]
                        if dop.dma:
                            eng.wait_ge(dsem[de][dop.slot], dop.tgt)
                        elif de != ename:
                            eng.wait_ge(esem[de], dop.cnt)

            @block.tensor
            def _(eng):
                run("pe", eng)

            @block.scalar
            def _(eng):
                run("act", eng)

            @block.vector
            def _(eng):
                run("dve", eng)

            @block.gpsimd
            def _(eng):
                run("pool", eng)

            @block.sync
            def _(eng):
                run("sp", eng)


STAGE = 9


def build(stage=STAGE):
    nc = bass.Bass("TRN2", target_bir_lowering=False)
    S = Sched(nc)
    st = contextlib.ExitStack()

    def din(name, shape, dt=F32):
        return nc.dram_tensor(name, list(shape), dt, kind="ExternalInput").ap()

    def dout(name, shape, dt=F32):
        return nc.dram_tensor(name, list(shape), dt, kind="ExternalOutput").ap()

    def dscr(name, shape, dt=F32):
        return nc.dram_tensor(name, list(shape), dt).ap()

    ARENA_F32 = 53200
    arena = st.enter_context(nc.sbuf_tensor("arena", [128, ARENA_F32], F32))
    aoff = [0]

    def sb(name, shape, dt=F32):
        n = 1
        for d_ in shape[1:]:
            n *= d_
        nf = n if dt == F32 else (n + 1) // 2
        nf = (nf + 7) // 8 * 8
        o = aoff[0]
        assert o + nf <= ARENA_F32, ("arena overflow", name, o, nf)
        aoff[0] = o + nf
        v = arena[:, o:o + nf]
        if dt != F32:
            v = v.bitcast(dt)
        v = v[:, 0:n]
        if len(shape) == 2:
            return v
        letters = "abcdefg"[:len(shape) - 1]
        pat = "p (" + " ".join(letters) + ") -> p " + " ".join(letters)
        kw = {letters[i]: shape[1 + i] for i in range(len(shape) - 2)}
        return v.rearrange(pat, **kw)

    def ps(name, shape, dt=F32):
        return st.enter_context(nc.psum_tensor("p_" + name, list(shape), dt))

    xp = din("xp", [2 * HALF, D])
    xs = din("xs", [128, D])
    h0r = din("h0r", [128, 2, NPAIR])
    h0i = din("h0i", [128, 2, NPAIR])
    lam_r_d = din("lam_r", [128, NPAIR])
    lam_i_d = din("lam_i", [128, NPAIR])
    lstep_d = din("lstep", [128, NPAIR])
    b_r_d = din("b_r", [128, NPAIR, 16])
    b_i_d = din("b_i", [128, NPAIR, 16])
    c_r_d = din("c_r", [128, NPAIR, 16])
    c_i_d = din("c_i", [128, NPAIR, 16])
    gmix_d = din("gmix", [2, 128, D])
    gffn_d = din("gffn", [2, 128, D])
    gfin_d = din("gfin", [128, D])
    dskip_d = din("dskip", [128, D])
    ident_d = din("ident", [128, 128])
    tmask_d = din("tmask", [128, 128])

    o_ssm_p = dout("o_ssm_p", [2, 128, NPAIR])
    o_ssm_s = dout("o_ssm_s", [2, 2, 128, NPAIR])
    NT = 2 * HALF + 128
    gact = dscr("gscr", [NT, D], BF16)

    pbank = [ps("pb%d" % i, [128, 512], F32) for i in range(8)]

    ident = sb("ident", [128, 128]); identb = sb("identb", [128, 128], BF16)
    tmask = sb("tmask", [128, 128])
    S.dma(ident[:], ident_d, writes=["ident"])
    S.dma(tmask[:], tmask_d, writes=["tmask"])
    S.dve(lambda e: e.tensor_copy(out=identb[:], in_=ident[:]), ["ident"], ["identb"])
    negpi = sb("negpi", [128, 1])
    S.pool(lambda e: e.memset(negpi[:], -math.pi), [], ["negpi"])
    tri = sb("tri", [128, 128], BF16); onesb = sb("onesb", [128, 128], BF16); cmask = sb("cmask", [128, 128])
    bft = sb("bft", [128, NH]); pmk = sb("pmk", [128, 1]); one1 = sb("one1", [128, 1])
    cstg = sb("cstg", [128, 128])
    S.dma(cstg[:], din("tri", [128, 128]), writes=["cstg"])
    S.dve(lambda e: e.tensor_copy(out=tri[:], in_=cstg[:]), ["cstg"], ["tri"])
    S.pool(lambda e: e.memset(onesb[:], 1.0), [], ["onesb"])
    S.pool(lambda e: e.memset(one1[:], 1.0), [], ["one1"])
    S.dma(cmask[:], din("cmask", [128, 128]), writes=["cmask"])
    S.dma(bft[:], din("bft", [128, NH]), writes=["bft"])
    S.dma(pmk[:], din("pmk", [128, 1]), writes=["pmk"])
    ssn = sb("ssn", [128, 4]); rvn = sb("rvn", [128, 4])
    mP = aoff[0]
    mP2 = mP

    lam_r = sb("lam_r", [128, NPAIR]); lam_i = sb("lam_i", [128, NPAIR]); dtt = sb("dtt", [128, NPAIR])
    lrdt = sb("lrdt", [128, NPAIR]); lidt = sb("lidt", [128, NPAIR])
    Er = sb("Er", [128, 9, NPAIR]); Ei = sb("Ei", [128, 9, NPAIR])
    Emr = sb("Emr", [128, 9, NPAIR]); Emi = sb("Emi", [128, 9, NPAIR])
    x_t = sb("x_t", [128, 8, D])
    _al = [x_t[:, i, 0:NPAIR * 16].rearrange("p (a h) -> p a h", h=16) for i in range(8)]
    Br, Bi, Cr, Ci, Bbr, Bbi, t16a, t16b = _al
    tA = sb("tA", [128, NPAIR]); tB = sb("tB", [128, NPAIR]); tC = sb("tC", [128, NPAIR]); tD = sb("tD", [128, NPAIR])
    zr = sb("zr", [128, NPAIR]); zi = sb("zi", [128, NPAIR])
    S.dma(lam_r[:], lam_r_d, writes=["lam_r"]); S.dma(lam_i[:], lam_i_d, writes=["lam_i"])
    S.dma(dtt[:], lstep_d, writes=["dtt"])
    S.dma(Br[:], b_r_d, writes=["Br"]); S.dma(Bi[:], b_i_d, writes=["Bi"])
    S.dma(Cr[:], c_r_d, writes=["Cr"]); S.dma(Ci[:], c_i_d, writes=["Ci"])
    S.act(lambda e: e.activation(out=dtt[:], in_=dtt[:], func=AF.Exp), ["dtt"], ["dtt"])
    S.dve(lambda e: e.tensor_tensor(out=lrdt[:], in0=lam_r[:], in1=dtt[:], op=ALU.mult), ["lam_r", "dtt"], ["lrdt"])
    S.dve(lambda e: e.tensor_tensor(out=lidt[:], in0=lam_i[:], in1=dtt[:], op=ALU.mult), ["lam_i", "dtt"], ["lidt"])
    S.dve(lambda e: e.memset(Er[:, 0, :], 1.0), [], ["E"])
    S.dve(lambda e: e.memset(Ei[:, 0, :], 0.0), [], ["E"])
    halfpi = sb("halfpi", [128, 1])
    S.pool(lambda e: e.memset(halfpi[:], 0.5 * math.pi), [], ["halfpi"])
    S.act(lambda e: e.activation(out=tA[:], in_=lidt[:], func=AF.Sin, scale=0.125), ["lidt"], ["tA"])
    S.act(lambda e: e.activation(out=tB[:], in_=lidt[:], func=AF.Sin, scale=-0.125, bias=halfpi[:, 0:1]), ["lidt", "halfpi"], ["tB"])
    for _ in range(3):
        S.dve(lambda e: e.tensor_tensor(out=tC[:], in0=tA[:], in1=tB[:], op=ALU.mult), ["tA", "tB", "tC"], ["tC"])
        S.dve(lambda e: e.tensor_tensor(out=tD[:], in0=tA[:], in1=tA[:], op=ALU.mult), ["tA", "tD"], ["tD"])
        S.dve(lambda e: e.tensor_scalar(out=tA[:], in0=tC[:], scalar1=2.0, scalar2=None, op0=ALU.mult), ["tC", "tA"], ["tA"])
        S.dve(lambda e: e.tensor_scalar(out=tB[:], in0=tD[:], scalar1=-2.0, scalar2=1.0, op0=ALU.mult, op1=ALU.add), ["tD", "tB"], ["tB"])
    S.act(lambda e: e.activation(out=tC[:], in_=lrdt[:], func=AF.Exp), ["lrdt", "tC"], ["tC"])
    S.dve(lambda e: e.tensor_tensor(out=Er[:, 1, :], in0=tB[:], in1=tC[:], op=ALU.mult), ["tB", "tC"], ["E"])
    S.dve(lambda e: e.tensor_tensor(out=Ei[:, 1, :], in0=tA[:], in1=tC[:], op=ALU.mult), ["tA", "tC"], ["E"])
    for k in range(2, 9):
        S.dve(lambda e, k=k: e.tensor_tensor(out=tA[:], in0=Er[:, k - 1, :], in1=Er[:, 1, :], op=ALU.mult), ["E", "tA"], ["tA"])
        S.dve(lambda e, k=k: e.tensor_tensor(out=tB[:], in0=Ei[:, k - 1, :], in1=Ei[:, 1, :], op=ALU.mult), ["E", "tB"], ["tB"])
        S.dve(lambda e, k=k: e.tensor_tensor(out=tC[:], in0=Er[:, k - 1, :], in1=Ei[:, 1, :], op=ALU.mult), ["E", "tC"], ["tC"])
        S.dve(lambda e, k=k: e.tensor_tensor(out=tD[:], in0=Ei[:, k - 1, :], in1=Er[:, 1, :], op=ALU.mult), ["E", "tD"], ["tD"])
        S.dve(lambda e, k=k: e.tensor_tensor(out=Er[:, k, :], in0=tA[:], in1=tB[:], op=ALU.subtract), ["tA", "tB", "E"], ["E"])
        S.dve(lambda e, k=k: e.tensor_tensor(out=Ei[:, k, :], in0=tC[:], in1=tD[:], op=ALU.add), ["tC", "tD", "E"], ["E"])
    for k in range(1, 9):
        S.act(lambda e, k=k: e.activation(out=tC[:], in_=lrdt[:], func=AF.Exp, scale=-2.0 * k), ["lrdt", "tC"], ["tC"])
        S.dve(lambda e, k=k: e.tensor_tensor(out=Emr[:, k, :], in0=Er[:, k, :], in1=tC[:], op=ALU.mult), ["E", "tC"], ["E"])
        S.dve(lambda e, k=k: e.scalar_tensor_tensor(out=Emi[:, k, :], in0=Ei[:, k, :], scalar=-1.0, in1=tC[:], op0=ALU.mult, op1=ALU.mult), ["E", "tC"], ["E"])
    S.dve(lambda e: e.tensor_tensor(out=tA[:], in0=lam_r[:], in1=lam_r[:], op=ALU.mult), ["lam_r", "tA"], ["tA"])
    S.dve(lambda e: e.tensor_tensor(out=tB[:], in0=lam_i[:], in1=lam_i[:], op=ALU.mult), ["lam_i", "tB"], ["tB"])
    S.dve(lambda e: e.tensor_tensor(out=tA[:], in0=tA[:], in1=tB[:], op=ALU.add), ["tA", "tB"], ["tA"])
    S.dve(lambda e: e.reciprocal(out=tA[:], in_=tA[:]), ["tA"], ["tA"])
    S.dve(lambda e: e.tensor_scalar(out=tB[:], in0=Er[:, 1, :], scalar1=-1.0, scalar2=None, op0=ALU.add), ["E", "tB"], ["tB"])
    S.dve(lambda e: e.tensor_tensor(out=tC[:], in0=tB[:], in1=lam_r[:], op=ALU.mult), ["tB", "lam_r", "tC"], ["tC"])
    S.dve(lambda e: e.tensor_tensor(out=tD[:], in0=Ei[:, 1, :], in1=lam_i[:], op=ALU.mult), ["E", "lam_i", "tD"], ["tD"])
    S.dve(lambda e: e.tensor_tensor(out=tC[:], in0=tC[:], in1=tD[:], op=ALU.add), ["tC", "tD"], ["tC"])
    S.dve(lambda e: e.tensor_tensor(out=zr[:], in0=tC[:], in1=tA[:], op=ALU.mult), ["tC", "tA"], ["zr"])
    S.dve(lambda e: e.tensor_tensor(out=tC[:], in0=Ei[:, 1, :], in1=lam_r[:], op=ALU.mult), ["E", "lam_r", "tC"], ["tC"])
    S.dve(lambda e: e.tensor_tensor(out=tD[:], in0=tB[:], in1=lam_i[:], op=ALU.mult), ["tB", "lam_i", "tD"], ["tD"])
    S.dve(lambda e: e.tensor_tensor(out=tC[:], in0=tC[:], in1=tD[:], op=ALU.subtract), ["tC", "tD"], ["tC"])
    S.dve(lambda e: e.tensor_tensor(out=zi[:], in0=tC[:], in1=tA[:], op=ALU.mult), ["tC", "tA"], ["zi"])

    import os
    KPREP = int(os.environ.get("KPREP", "9"))

    def bch(ap2, n=16):
        return ap2.unsqueeze(2).to_broadcast([128, ap2.shape[1], n])

    S.dve(lambda e: e.tensor_tensor(out=t16a[:], in0=Br[:], in1=bch(zr[:]), op=ALU.mult), ["Br", "zr", "t16a"], ["t16a"])
    S.dve(lambda e: e.tensor_tensor(out=t16b[:], in0=Bi[:], in1=bch(zi[:]), op=ALU.mult), ["Bi", "zi", "t16b"], ["t16b"])
    S.dve(lambda e: e.tensor_tensor(out=Bbr[:], in0=t16a[:], in1=t16b[:], op=ALU.subtract), ["t16a", "t16b"], ["Bbr"])
    S.dve(lambda e: e.tensor_tensor(out=t16a[:], in0=Bi[:], in1=bch(zr[:]), op=ALU.mult), ["Bi", "zr", "t16a"], ["t16a"])
    S.dve(lambda e: e.tensor_tensor(out=t16b[:], in0=Br[:], in1=bch(zi[:]), op=ALU.mult), ["Br", "zi", "t16b"], ["t16b"])
    S.dve(lambda e: e.tensor_tensor(out=Bbi[:], in0=t16a[:], in1=t16b[:], op=ALU.add), ["t16a", "t16b"], ["Bbi"])

    WinReT = sb("WinReT", [128, NPAIR, 128], BF16)
    WinImT = sb("WinImT", [128, NPAIR, 128], BF16)
    Toep = sb("Toep", [128, 2 * NPAIR, 128], BF16)
    WoRe = sb("WoRe", [128, 2, NPAIR, 128], BF16)
    WoIm = sb("WoIm", [128, 2, NPAIR, 128], BF16)
    S.pool(lambda e: e.memset(WoRe[:], 0.0), [], ["WoRe"])
    S.pool(lambda e: e.memset(WoIm[:], 0.0), [], ["WoIm"])
    NPC = 2
    XRe = sb("XRe", [128, NPC, 8, 16]); XIm = sb("XIm", [128, NPC, 8, 16])
    WiRe = sb("WiRe", [128, NPC, 8, 16]); WiIm = sb("WiIm", [128, NPC, 8, 16])
    WoReF = sb("WoReF", [128, NPC, 8, 16]); WoImF = sb("WoImF", [128, NPC, 8, 16])
    XReB = sb("XReB", [128, NPC, 128], BF16); XImB = sb("XImB", [128, NPC, 128], BF16)
    pa = sb("pa", [128, NPC, 16]); pb_ = sb("pb_", [128, NPC, 16])

    def cmul_into(outr, outi, ar, ai, br, bi, tagr, tagi, negate_im=False):
        S.dve(lambda e: e.tensor_tensor(out=pa[:], in0=br, in1=bch(ar), op=ALU.mult), ["E", "Bbr", "Bbi", "Cr", "Ci", "pa"], ["pa"])
        S.dve(lambda e: e.tensor_tensor(out=pb_[:], in0=bi, in1=bch(ai), op=ALU.mult), ["E", "Bbr", "Bbi", "Cr", "Ci", "pb_"], ["pb_"])
        S.dve(lambda e: e.tensor_tensor(out=outr, in0=pa[:], in1=pb_[:], op=ALU.subtract), ["pa", "pb_"], [tagr])
        S.dve(lambda e: e.tensor_tensor(out=pa[:], in0=bi, in1=bch(ar), op=ALU.mult), ["E", "Bbr", "Bbi", "Cr", "Ci", "pa"], ["pa"])
        S.dve(lambda e: e.tensor_tensor(out=pb_[:], in0=br, in1=bch(ai), op=ALU.mult), ["E", "Bbr", "Bbi", "Cr", "Ci", "pb_"], ["pb_"])
        if negate_im:
            S.dve(lambda e: e.scalar_tensor_tensor(out=outi, in0=pa[:], scalar=-1.0, in1=pb_[:], op0=ALU.mult, op1=ALU.subtract), ["pa", "pb_"], [tagi])
        else:
            S.dve(lambda e: e.tensor_tensor(out=outi, in0=pa[:], in1=pb_[:], op=ALU.add), ["pa", "pb_"], [tagi])

    for q in range(NPAIR // NPC):
        psl = slice(q * NPC, (q + 1) * NPC)
        for s in range(8):
            cmul_into(XRe[:, :, s, :], XIm[:, :, s, :], Emr[:, s + 1, psl], Emi[:, s + 1, psl], Bbr[:, psl, :], Bbi[:, psl, :], "XRe", "XIm")
            cmul_into(WiRe[:, :, s, :], WiIm[:, :, s, :], Er[:, 7 - s, psl], Ei[:, 7 - s, psl], Bbr[:, psl, :], Bbi[:, psl, :], "WiRe", "WiIm")
            cmul_into(WoReF[:, :, s, :], WoImF[:, :, s, :], Er[:, s + 1, psl], Ei[:, s + 1, psl], Cr[:, psl, :], Ci[:, psl, :], "WoReF", "WoImF", negate_im=True)
        for g2 in range(2):
            rs = slice(64 * g2, 64 * g2 + 64)
            S.act(lambda e, psl=psl, rs=rs, g2=g2: e.copy(out=WoRe[rs, g2, psl, :], in_=WoReF[rs].rearrange("p a s h -> p a (s h)")), ["WoReF", "WoRe"], ["WoRe"])
            S.act(lambda e, psl=psl, rs=rs, g2=g2: e.copy(out=WoIm[rs, g2, psl, :], in_=WoImF[rs].rearrange("p a s h -> p a (s h)")), ["WoImF", "WoIm"], ["WoIm"])
        S.act(lambda e: e.copy(out=XReB[:], in_=XRe[:].rearrange("p a s h -> p a (s h)")), ["XRe"], ["XReB"])
        S.act(lambda e: e.copy(out=XImB[:], in_=XIm[:].rearrange("p a s h -> p a (s h)")), ["XIm"], ["XImB"])
        if KPREP <= 3:
            continue
        for a in range(NPC):
            pair = q * NPC + a
            bk = pbank[a % 2]; bkn0 = 'pb%d' % (a % 2)
            S.pe(lambda e, a=a, bk=bk: e.transpose(out=bk[:, 0:128], in_=WiRe[:, a].rearrange("p s h -> p (s h)"), identity=ident[:]), ["WiRe", "ident"], ["pb%d" % (a % 2)])
            S.pe(lambda e, a=a, bk=bk: e.transpose(out=bk[:, 128:256], in_=WiIm[:, a].rearrange("p s h -> p (s h)"), identity=ident[:]), ["WiIm", "ident"], ["pb%d" % (a % 2)])
            for g2 in range(2 if KPREP >= 5 else 0):
                rs = slice(64 * g2, 64 * g2 + 64)
                cs = slice(128 * g2, 128 * g2 + 128)
                bk = pbank[2 + a % 2]
                S.pe(lambda e, a=a, bk=bk, rs=rs, cs=cs, q=q, g2=g2: e.matmul(bk[:, cs], lhsT=XReB[:, a, :], rhs=WoRe[:, g2, q * NPC + a, :], start=True, stop=False),
                     ["XReB", "WoRe"], ["pb%d" % (2 + a % 2)])
                S.pe(lambda e, a=a, bk=bk, rs=rs, cs=cs, q=q, g2=g2: e.matmul(bk[:, cs], lhsT=XImB[:, a, :], rhs=WoIm[:, g2, q * NPC + a, :], start=False, stop=True),
                     ["XImB", "WoIm"], ["pb%d" % (2 + a % 2)])
            S.act(lambda e, pair=pair, bk=pbank[a % 2]: e.copy(out=WinReT[:, pair, :], in_=bk[:, 0:128]), ["pb%d" % (a % 2)], ["WinReT"])
            S.act(lambda e, pair=pair, bk=pbank[a % 2]: e.copy(out=WinImT[:, pair, :], in_=bk[:, 128:256]), ["pb%d" % (a % 2)], ["WinImT"])
            for g2 in range(2 if KPREP >= 5 else 0):
                S.dve(lambda e, pair=pair, g2=g2, bk2=pbank[2 + a % 2]: e.tensor_tensor(out=Toep[:, 2 * pair + g2, :], in0=bk2[:, 128 * g2:128 + 128 * g2], in1=tmask[:], op=ALU.mult),
                      ["pb%d" % (2 + a % 2), "tmask"], ["Toep"])

    M1 = sb("M1", [128, 2, NPAIR]); M2n = sb("M2n", [128, NPAIR]); M2p = sb("M2p", [128, NPAIR])
    S.dve(lambda e: e.tensor_copy(out=M1[:, 0, :], in_=Er[:, 8, :]), ["E"], ["M1"])
    S.dve(lambda e: e.tensor_copy(out=M1[:, 1, :], in_=Er[:, 8, :]), ["E"], ["M1"])
    S.dve(lambda e: e.tensor_copy(out=M2p[:], in_=Ei[:, 8, :]), ["E"], ["M2"])
    S.dve(lambda e: e.tensor_scalar(out=M2n[:], in0=Ei[:, 8, :], scalar1=-1.0, scalar2=None, op0=ALU.mult), ["E"], ["M2"])

    gmix0 = sb("gmix0", [128, D]); dg = sb("dg", [128, D])
    S.dma(gmix0[:], gmix_d[0], writes=["gmix0"])
    S.dma(dg[:], dskip_d, writes=["dg"])
    S.dve(lambda e: e.tensor_tensor(out=dg[:], in0=dg[:], in1=gmix0[:], op=ALU.mult), ["dg", "gmix0"], ["dg"])

    S.dve(lambda e: e.memset(x_t[:, 0, 0:1], 0.0), ["Br", "Bi", "Cr", "Ci", "Bbr", "Bbi", "t16a", "t16b"], ["x_t"])
    ss = sb("ss", [128, 8]); rinv = sb("rinv", [128, 8])
    ub = sb("ub", [128, 64, 8, 16], BF16)
    UT = sb("UT", [128, 64, 128], BF16)
    SH = sb("SH", [128, 2, NPAIR, 128])
    Hpb = sb("Hpb", [128, 2, NPAIR, 128], BF16)
    carry = sb("carry", [128, 2, NPAIR])
    T1 = {"dve": sb("T1d", [128, 2, NPAIR // 2]), "pool": sb("T1p", [128, 2, NPAIR // 2])}
    T2 = {"dve": sb("T2d", [128, 2, NPAIR // 2]), "pool": sb("T2p", [128, 2, NPAIR // 2])}
    ytmps = [sb("ytmp0", [128, 8, 16]), sb("ytmp1", [128, 8, 16])]
    junk = UT[:].rearrange("p g c -> p (g c)")[:, 0:D]
    Gb = ub[:].rearrange("p g t h -> p (g t h)").rearrange("p (t d) -> p t d", t=8)

    def s5_macro(src_ap, nvalid, mt_tag, seqs, g_out_ap, final_out):
        P = nvalid
        S.dma(x_t[0:P], src_ap.rearrange("(c t) d -> c t d", t=8), writes=["x_t"])
        for t in range(8):
            S.act(lambda e, t=t: e.activation(out=junk[0:P], in_=x_t[0:P, t, :], func=AF.Square, accum_out=ss[0:P, t:t + 1]), ["x_t"], ["UT", "ss"])
        S.dve(lambda e: e.tensor_scalar(out=rinv[0:P], in0=ss[0:P], scalar1=1.0 / D, scalar2=EPS, op0=ALU.mult, op1=ALU.add), ["ss"], ["rinv"])
        S.act(lambda e: e.activation(out=rinv[0:P], in_=rinv[0:P], func=AF.Sqrt), ["rinv"], ["rinv"])
        S.dve(lambda e: e.reciprocal(out=rinv[0:P], in_=rinv[0:P]), ["rinv"], ["rinv"])
        for t in range(8):
            S.dve(lambda e, t=t: e.scalar_tensor_tensor(out=ub[0:P, :, t, :], in0=x_t[0:P, t, :].rearrange("p (g h) -> p g h", h=16), scalar=rinv[0:P, t:t + 1],
                                                        in1=gmix0[0:P].rearrange("p (g h) -> p g h", h=16), op0=ALU.mult, op1=ALU.mult),
                  ["x_t", "rinv", "gmix0", "ub"], ["ub"])
        for g8 in range(8):
            bk = pbank[g8 % 2]; bkn = "pb%d" % (g8 % 2)
            bkb = bk[:].bitcast(BF16)
            for j in range(8):
                g = g8 * 8 + j
                S.pe(lambda e, g=g, j=j, bkb=bkb: e.transpose(out=bkb[:, j * 128:j * 128 + P], in_=ub[0:P, g].rearrange("p t h -> p (t h)"), identity=identb[0:P, 0:P]),
                     ["ub", "identb"], [bkn])
            S.act(lambda e, g8=g8, bkb=bkb: e.copy(out=UT[:, g8 * 8:(g8 + 1) * 8, 0:P], in_=bkb[:, 0:1024].rearrange("p (j c) -> p j c", j=8)[:, :, 0:P]), [bkn], ["UT"])
        for q in range(8):
            bre = pbank[4 + 2 * (q % 2)]; bim = pbank[5 + 2 * (q % 2)]
            nre = "pb%d" % (4 + 2 * (q % 2)); nim = "pb%d" % (5 + 2 * (q % 2))
            for a in range(4):
                pair = q * 4 + a
                for g2 in range(2):
                    g = 2 * pair + g2
                    S.pe(lambda e, a=a, g2=g2, g=g, pair=pair, bre=bre: e.matmul(bre[64 * g2:64 * g2 + 64, a * 128:a * 128 + P], lhsT=WinReT[:, pair, 64 * g2:64 * g2 + 64], rhs=UT[:, g, 0:P], start=True, stop=True),
                         ["WinReT", "UT"], [nre])
                    S.pe(lambda e, a=a, g2=g2, g=g, pair=pair, bim=bim: e.matmul(bim[64 * g2:64 * g2 + 64, a * 128:a * 128 + P], lhsT=WinImT[:, pair, 64 * g2:64 * g2 + 64], rhs=UT[:, g, 0:P], start=True, stop=True),
                         ["WinImT", "UT"], [nim])
            S.act(lambda e, q=q, bre=bre: e.copy(out=SH[:, 0, q * 4:q * 4 + 4, 0:P], in_=bre[:].rearrange("p (a c) -> p a c", a=4)[:, :, 0:P]), [nre], ["SHr"])
            S.dve(lambda e, q=q, bim=bim: e.tensor_copy(out=SH[:, 1, q * 4:q * 4 + 4, 0:P], in_=bim[:].rearrange("p (a c) -> p a c", a=4)[:, :, 0:P]), [nim], ["SHi"])
        for (c0, c1, init) in seqs:
            for ei, en in enumerate(("dve", "pool")):
                hp = slice(ei * 16, ei * 16 + 16)
                t1 = T1[en]; t2 = T2[en]
                if init[0] == "h0":
                    S.dma(carry[:, 0, hp], h0r[:, init[1], hp], writes=["carry" + en], q="sp")
                    S.dma(carry[:, 1, hp], h0i[:, init[1], hp], writes=["carry" + en], q="sp")
                elif init[0] == "zero":
                    S.on(en, lambda e, hp=hp: e.memset(carry[:, :, hp], 0.0), [], ["carry" + en])
                for c in range(c0, c1):
                    if c == c0:
                        prev = carry[:, :, hp]; prev_r = carry[:, 0, hp]; prev_i = carry[:, 1, hp]
                    else:
                        prev = SH[:, :, hp, c - 1]; prev_r = SH[:, 0, hp, c - 1]; prev_i = SH[:, 1, hp, c - 1]
                    rd = ["SH" + en, "carry" + en, "M1", "M2", "SHr", "SHi"]
                    S.on(en, lambda e, prev=prev, hp=hp, c=c: e.tensor_copy(out=Hpb[:, :, hp, c], in_=prev), rd, ["Hpb" + en])
                    S.on(en, lambda e, prev=prev, hp=hp, t1=t1: e.tensor_tensor(out=t1[:], in0=prev, in1=M1[:, :, hp], op=ALU.mult), rd + ["t1" + en], ["t1" + en])
                    S.on(en, lambda e, prev_i=prev_i, hp=hp, t2=t2: e.tensor_tensor(out=t2[:, 0, :], in0=prev_i, in1=M2n[:, hp], op=ALU.mult), rd + ["t2" + en], ["t2" + en])
                    S.on(en, lambda e, prev_r=prev_r, hp=hp, t2=t2: e.tensor_tensor(out=t2[:, 1, :], in0=prev_r, in1=M2p[:, hp], op=ALU.mult), rd + ["t2" + en], ["t2" + en])
                    S.on(en, lambda e, t1=t1, t2=t2: e.tensor_tensor(out=t1[:], in0=t1[:], in1=t2[:], op=ALU.add), ["t1" + en, "t2" + en], ["t1" + en])
                    S.on(en, lambda e, t1=t1, hp=hp, c=c: e.tensor_tensor(out=SH[:, :, hp, c], in0=SH[:, :, hp, c], in1=t1[:], op=ALU.add), ["t1" + en, "SHr", "SHi", "SH" + en], ["SH" + en])
                S.on(en, lambda e, hp=hp, c1=c1: e.tensor_copy(out=carry[:, :, hp], in_=SH[:, :, hp, c1 - 1]), ["SH" + en, "SHr", "SHi"], ["carry" + en])
        for (cl, ore, oim) in final_out:
            S.dma(ore, SH[:, 0, :, cl], reads=["SHdve", "SHpool", "SHr", "SHi"], allow_slow_non_contiguous=True)
            S.dma(oim, SH[:, 1, :, cl], reads=["SHdve", "SHpool", "SHr", "SHi"], allow_slow_non_contiguous=True)
        for g in range(64):
            pair, g2 = g // 2, g % 2
            bk = pbank[2 + g % 2]; bkn = "pb%d" % (2 + g % 2)
            rs = slice(64 * g2, 64 * g2 + 64)
            S.pe(lambda e, g=g, bk=bk: e.matmul(bk[0:P, 0:128], lhsT=UT[:, g, 0:P], rhs=Toep[:, g, :], start=True, stop=False), ["UT", "Toep"], [bkn])
            S.pe(lambda e, pair=pair, rs=rs, bk=bk, g2=g2: e.matmul(bk[0:P, 0:128], lhsT=Hpb[:, 0, pair, 0:P], rhs=WoRe[:, g2, pair, :], start=False, stop=False), ["Hpbdve", "Hpbpool", "WoRe"], [bkn])
            S.pe(lambda e, pair=pair, rs=rs, bk=bk, g2=g2: e.matmul(bk[0:P, 0:128], lhsT=Hpb[:, 1, pair, 0:P], rhs=WoIm[:, g2, pair, :], start=False, stop=True), ["Hpbdve", "Hpbpool", "WoIm"], [bkn])
            dsl = slice(16 * g, 16 * g + 16)
            ytmp = ytmps[g % 2]; yn = "ytmp%d" % (g % 2)
            S.dve(lambda e, dsl=dsl, ytmp=ytmp: e.tensor_tensor(out=ytmp[0:P], in0=x_t[0:P, :, dsl], in1=rinv[0:P].unsqueeze(2).to_broadcast([P, 8, 16]), op=ALU.mult), ["x_t", "rinv"], [yn])
            S.dve(lambda e, dsl=dsl, ytmp=ytmp: e.tensor_tensor(out=ytmp[0:P], in0=ytmp[0:P], in1=dg[0:P, dsl].unsqueeze(1).to_broadcast([P, 8, 16]), op=ALU.mult), [yn, "dg"], [yn])
            S.dve(lambda e, bk=bk, ytmp=ytmp: e.tensor_tensor(out=ytmp[0:P], in0=ytmp[0:P], in1=bk[0:P, 0:128].rearrange("p (t h) -> p t h", t=8), op=ALU.add), [yn, bkn], [yn])
            S.act(lambda e, dsl=dsl, ytmp=ytmp: e.activation(out=Gb[0:P, :, dsl], in_=ytmp[0:P], func=AF.Gelu_apprx_tanh), [yn], ["ub"])
        S.dma(g_out_ap.rearrange("(c t) d -> c t d", t=8), Gb[0:P], reads=["ub"])

    import os
    KST = int(os.environ.get("KSTAGE", "9"))
    for mt in range(8 if KST >= 3 else (1 if KST == 2 else 0)):
        seqs = [(0, 128, ("zero",) if mt == 0 else ("carry",))]
        fin = [(127, o_ssm_p[0], o_ssm_p[1])] if mt == 7 or KST == 2 else []
        s5_macro(xp[mt * 1024:(mt + 1) * 1024, :], 128, mt, seqs, gact[mt * 1024:(mt + 1) * 1024, :], fin)
    if KST >= 1:
      s5_macro(xs, 16, 8, [(0, 8, ("h0", 0)), (8, 16, ("h0", 1))], gact[2 * HALF:2 * HALF + 128, :],
             [(7, o_ssm_s[0, 0], o_ssm_s[0, 1]), (15, o_ssm_s[1, 0], o_ssm_s[1, 1])])


    x1s = dscr("x1s", [NT, D]); x2s = dscr("x2s", [NT, D]); x3s = dscr("x3s", [NT, D])

    def xsrc(t0, n):
        return xp[t0:t0 + n, :] if t0 < 2 * HALF else xs[t0 - 2 * HALF:t0 - 2 * HALF + n, :]

    def load_weight(dst3, src2, nchunks, P, ncol, tag):
        stg = [sb(tag + "_s0", [128, ncol]), sb(tag + "_s1", [128, ncol])]
        engs = ["act", "pool", "dve"]
        for c in range(nchunks):
            b = stg[c % 2]; bn = tag + "_s%d" % (c % 2)
            S.dma(b[0:P], src2[c * P:(c + 1) * P, :], writes=[bn])
            en = engs[c % 3]
            if en == "act":
                S.act(lambda e, b=b, c=c: e.copy(out=dst3[0:P, c, :], in_=b[0:P]), [bn], [tag + str(c)])
            else:
                S.on(en, lambda e, b=b, c=c: e.tensor_copy(out=dst3[0:P, c, :], in_=b[0:P]), [bn], [tag + str(c)])


    def rms_to_bf16(x_ap, P, gain, out_ap, xname, oname, col):
        S.act(lambda e: e.activation(out=out_ap, in_=x_ap, func=AF.Square, accum_out=ssn[0:P, col:col + 1]), [xname], [oname, "ssn%d" % col])
        S.dve(lambda e: e.tensor_scalar(out=rvn[0:P, col:col + 1], in0=ssn[0:P, col:col + 1], scalar1=1.0 / D, scalar2=EPS, op0=ALU.mult, op1=ALU.add), ["ssn%d" % col], ["rvn%d" % col])
        S.act(lambda e: e.activation(out=rvn[0:P, col:col + 1], in_=rvn[0:P, col:col + 1], func=AF.Sqrt), ["rvn%d" % col], ["rvn%d" % col])
        S.dve(lambda e: e.reciprocal(out=rvn[0:P, col:col + 1], in_=rvn[0:P, col:col + 1]), ["rvn%d" % col], ["rvn%d" % col])
        S.dve(lambda e: e.scalar_tensor_tensor(out=out_ap, in0=x_ap, scalar=rvn[0:P, col:col + 1], in1=gain[0:P], op0=ALU.mult, op1=ALU.mult), [xname, "rvn%d" % col, oname], [oname])

    tcount = [0]

    def transpose8(src_ap, P, dstT_ap, sname, dname):
        bi = tcount[0] % 2
        tcount[0] += 1
        bkb = pbank[bi][:].bitcast(BF16)
        bn = "pb%d" % bi
        for dc in range(8):
            S.pe(lambda e, dc=dc: e.transpose(out=bkb[:, dc * 128:dc * 128 + P], in_=src_ap[:, dc * 128:(dc + 1) * 128], identity=identb[0:P, 0:P]), [sname], [bn])
        S.act(lambda e: e.copy(out=dstT_ap, in_=bkb[:, 0:1024].rearrange("p (j c) -> p j c", j=8)[:, :, 0:P]), [bn], [dname])

    S.barrier(); aoff[0] = mP2
    wglu_d = din("w_glu", [D, 2 * D])
    Wglu = sb("Wglu", [128, 8, 2 * D], BF16)
    m1 = aoff[0]
    load_weight(Wglu, wglu_d, 8, 128, 2 * D, "wg")
    S.barrier(); aoff[0] = m1
    Gt = [sb("Gt%d" % i, [128, D], BF16) for i in range(2)]
    Xt = [sb("Xt%d" % i, [128, D]) for i in range(2)]
    GTt = [sb("GTt%d" % i, [128, 8, 128], BF16) for i in range(2)]
    sgb = [sb("sgb%d" % i, [128, 512]) for i in range(2)]
    mmb = [sb("mmb%d" % i, [128, 512]) for i in range(2)]
    tilesB = [(t, 128) for t in range(0, NT, 128)]
    if KST < 4:
        tilesB = []

    def loadB(i):
        t0, P = tilesB[i]; b = i % 2
        S.dma(Gt[b][0:P], gact[t0:t0 + P, :], reads=["gscr"], writes=["Gt%d" % b])
        S.dma(Xt[b][0:P], xsrc(t0, P), writes=["Xt%d" % b])

    if tilesB:
        loadB(0)
    for i, (t0, P) in enumerate(tilesB):
        b = i % 2
        if i + 1 < len(tilesB):
            loadB(i + 1)
        transpose8(Gt[b][0:P], P, GTt[b][:, :, 0:P], "Gt%d" % b, "GTt%d" % b)
        for nh in range(2):
            j = 2 * i + nh
            vb = pbank[2 + 2 * (j % 3)]; gb = pbank[3 + 2 * (j % 3)]
            vn = "pb%d" % (2 + 2 * (j % 3)); gn = "pb%d" % (3 + 2 * (j % 3))
            for dc in range(8):
                S.pe(lambda e, dc=dc, vb=vb, nh=nh, b=b: e.matmul(vb[0:P, :], lhsT=GTt[b][:, dc, 0:P], rhs=Wglu[:, dc, nh * 512:(nh + 1) * 512], start=(dc == 0), stop=(dc == 7)), ["GTt%d" % b], [vn])
            for dc in range(8):
                S.pe(lambda e, dc=dc, gb=gb, nh=nh, b=b: e.matmul(gb[0:P, :], lhsT=GTt[b][:, dc, 0:P], rhs=Wglu[:, dc, D + nh * 512:D + (nh + 1) * 512], start=(dc == 0), stop=(dc == 7)), ["GTt%d" % b], [gn])
            sg = sgb[j % 2]; mm = mmb[j % 2]
            S.act(lambda e, sg=sg, gb=gb: e.activation(out=sg[0:P], in_=gb[0:P, :], func=AF.Sigmoid), [gn], ["sg%d" % (j % 2)])
            S.dve(lambda e, sg=sg, mm=mm, vb=vb: e.tensor_tensor(out=mm[0:P], in0=vb[0:P, :], in1=sg[0:P], op=ALU.mult), [vn, "sg%d" % (j % 2)], ["mm%d" % (j % 2)])
            S.pool(lambda e, mm=mm, b=b, nh=nh: e.tensor_tensor(out=Xt[b][0:P, nh * 512:(nh + 1) * 512], in0=Xt[b][0:P, nh * 512:(nh + 1) * 512], in1=mm[0:P], op=ALU.add), ["mm%d" % (j % 2), "Xt%d" % b], ["Xt%d" % b])
        S.dma(x1s[t0:t0 + P, :], Xt[b][0:P], reads=["Xt%d" % b], writes=["x1s"])

    wup_d = din("w_up", [2, D, DFF]); wgt_d = din("w_gate", [2, D, DFF]); wdn_d = din("w_down", [2, DFF, D])
    cw_d = din("conv_w", [2, 3, DFF]); cb_d = din("conv_b", [2, DFF]); cvst_d = din("cvst", [2, 2, 2, DFF])
    o_cvp = dout("o_cvp", [2, 2, DFF]); o_cvs = dout("o_cvs", [2, 2, 2, DFF])
    o_yp = dout("o_yp", [HALF, D]); o_ys = dout("o_ys", [128, D])
    NFC = DFF // 128

    def ffn_phase(layer, xin, xout, groups, gain_d, final):
        S.barrier(); aoff[0] = mP2
        Wup = sb("Wup", [128, 8, DFF], BF16); Wgt = sb("Wgt", [128, 8, DFF], BF16); Wdn = sb("Wdn", [128, NFC, D], BF16)
        cw = sb("cw", [128, NFC, 3]); cbb = sb("cbb", [128, NFC]); hist = sb("hist", [128, NFC, 2])
        gain = sb("gain", [128, D]); gfin = sb("gfinal", [128, D])
        m1 = aoff[0]
        load_weight(Wup, wup_d[layer], 8, 128, DFF, "wu")
        aoff[0] = m1
        S.barrier()
        load_weight(Wgt, wgt_d[layer], 8, 128, DFF, "wt")
        aoff[0] = m1
        S.barrier()
        load_weight(Wdn, wdn_d[layer], NFC, 128, D, "wd")
        for j in range(3):
            S.dma(cw[:, :, j], cw_d[layer, j].rearrange("(fc f) -> f fc", f=128), writes=["cw"], allow_slow_non_contiguous=True)
        S.dma(cbb[:], cb_d[layer].rearrange("(fc f) -> f fc", f=128), writes=["cbb"], allow_slow_non_contiguous=True)
        S.dma(gain[:], gain_d, writes=["gain"])
        S.dma(gfin[:], gfin_d, writes=["gfin"])
        S.barrier(); aoff[0] = m1
        Xg = [sb("Xg%d" % i, [128, 2, D]) for i in range(2)]
        hsb = sb("hsb", [128, D], BF16)
        hsT = [sb("hsT%d" % i, [128, 8, 256], BF16) for i in range(2)]
        hT = [sb("hT%d" % i, [128, NFC, 256], BF16) for i in range(2)]
        abuf = [sb("abuf%d" % i, [128, 258]) for i in range(3)]
        cbuf = [sb("cbuf%d" % i, [128, 256]) for i in range(3)]
        ybuf = sb("ybuf", [128, D])

        def tiles_of(g):
            n = g["n"]
            return [(0, 128), (1, 128)] if n == 256 else [(0, n)]

        def load(gi):
            g = groups[gi]; b = gi % 2
            if g["n"] == 256:
                S.dma(Xg[b][:], xin[g["t0"]:g["t0"] + 256, :].rearrange("(i p) d -> p i d", i=2), writes=["Xg%d" % b])
            else:
                S.dma(Xg[b][0:g["n"], 0, :], xin[g["t0"]:g["t0"] + g["n"], :], writes=["Xg%d" % b])

        def upgate(gi):
            g = groups[gi]; b = gi % 2; N = g["n"]
            if g["hist"] == "zero":
                S.pool(lambda e: e.memset(hist[:], 0.0), [], ["hist"])
            elif g["hist"] != "carry":
                for j in range(2):
                    S.dma(hist[:, :, j], cvst_d[layer, g["hist"], j].rearrange("(fc f) -> f fc", f=128), writes=["hist"], allow_slow_non_contiguous=True)
            for (ti, P) in tiles_of(g):
                rms_to_bf16(Xg[b][0:P, ti, :], P, gain, hsb[0:P], "Xg%d" % b, "hsb", ti)
                transpose8(hsb[0:P], P, hsT[b][:, :, ti * 128:ti * 128 + P], "hsb", "hsT%d" % b)
            for fc in range(NFC):
                ug = pbank[2 + fc % 3]; un = "pb%d" % (2 + fc % 3)
                for kc in range(8):
                    S.pe(lambda e, kc=kc, fc=fc, ug=ug: e.matmul(ug[:, 0:N], lhsT=Wup[:, kc, fc * 128:(fc + 1) * 128], rhs=hsT[b][:, kc, 0:N], start=(kc == 0), stop=(kc == 7)), ["hsT%d" % b], [un])
                for kc in range(8):
                    S.pe(lambda e, kc=kc, fc=fc, ug=ug: e.matmul(ug[:, 256:256 + N], lhsT=Wgt[:, kc, fc * 128:(fc + 1) * 128], rhs=hsT[b][:, kc, 0:N], start=(kc == 0), stop=(kc == 7)), ["hsT%d" % b], [un])
                ac = abuf[fc % 3]; an = "ab%d" % (fc % 3); ct = cbuf[fc % 3]; cn = "cb%d" % (fc % 3)
                S.pool(lambda e, ac=ac, fc=fc: e.tensor_copy(out=ac[:, 0:2], in_=hist[:, fc, :]), ["hist", an], [an])
                S.act(lambda e, ac=ac, ug=ug: e.copy(out=ac[:, 2:2 + N], in_=ug[:, 0:N]), [un, an], [an])
                S.act(lambda e, ct=ct, ug=ug, fc=fc: e.activation(out=ct[:, 0:N], in_=ug[:, 0:N], func=AF.Identity, scale=cw[:, fc, 2:3], bias=cbb[:, fc:fc + 1]), [un, "cw", "cbb", cn], [cn])
                S.dve(lambda e, ct=ct, ac=ac, fc=fc: e.scalar_tensor_tensor(out=ct[:, 0:N], in0=ac[:, 1:1 + N], scalar=cw[:, fc, 1:2], in1=ct[:, 0:N], op0=ALU.mult, op1=ALU.add), [an, cn, "cw"], [cn])
                S.dve(lambda e, ct=ct, ac=ac, fc=fc: e.scalar_tensor_tensor(out=ct[:, 0:N], in0=ac[:, 0:N], scalar=cw[:, fc, 0:1], in1=ct[:, 0:N], op0=ALU.mult, op1=ALU.add), [an, cn, "cw"], [cn])
                S.pool(lambda e, ac=ac, fc=fc: e.tensor_copy(out=hist[:, fc, :], in_=ac[:, N:N + 2]), [an, "hist"], ["hist"])
                S.act(lambda e, ct=ct: e.activation(out=ct[:, 0:N], in_=ct[:, 0:N], func=AF.Gelu_apprx_tanh), [cn], [cn])
                S.dve(lambda e, ct=ct, ug=ug, fc=fc: e.tensor_tensor(out=hT[b][:, fc, 0:N], in0=ct[:, 0:N], in1=ug[:, 256:256 + N], op=ALU.mult), [cn, un, "hT%d" % b], ["hT%d" % b])
            if g.get("conv_out") is not None:
                for j in range(2):
                    S.dma(g["conv_out"][j].rearrange("(fc f) -> f fc", f=128), hist[:, :, j], reads=["hist"], allow_slow_non_contiguous=True)

        dcount = [0]

        def down(gi):
            g = groups[gi]; b = gi % 2
            if not g["store"]:
                return
            for (ti, P) in tiles_of(g):
                for half in range(2):
                    k = dcount[0] % 3; dcount[0] += 1
                    ob = pbank[5 + k]; on_ = "pb%d" % (5 + k)
                    for fc in range(NFC):
                        S.pe(lambda e, fc=fc, ob=ob, ti=ti, half=half: e.matmul(ob[0:P, :], lhsT=hT[b][:, fc, ti * 128:ti * 128 + P], rhs=Wdn[:, fc, half * 512:(half + 1) * 512], start=(fc == 0), stop=(fc == NFC - 1)), ["hT%d" % b], [on_])
                    S.dve(lambda e, ob=ob, ti=ti, half=half: e.tensor_tensor(out=Xg[b][0:P, ti, half * 512:(half + 1) * 512], in0=Xg[b][0:P, ti, half * 512:(half + 1) * 512], in1=ob[0:P, :], op=ALU.add), [on_, "Xg%d" % b], ["Xg%d" % b])
                t0 = g["t0"] + ti * 128
                if final:
                    dst = g["yout"][ti * 128:ti * 128 + P, :]
                    S.act(lambda e, ti=ti: e.activation(out=ybuf[0:P], in_=Xg[b][0:P, ti, :], func=AF.Square, accum_out=ssn[0:P, 2:3]), ["Xg%d" % b, "ybuf"], ["ybuf", "ssn2"])
                    S.dve(lambda e: e.tensor_scalar(out=rvn[0:P, 2:3], in0=ssn[0:P, 2:3], scalar1=1.0 / D, scalar2=EPS, op0=ALU.mult, op1=ALU.add), ["ssn2"], ["rvn2"])
                    S.act(lambda e: e.activation(out=rvn[0:P, 2:3], in_=rvn[0:P, 2:3], func=AF.Sqrt), ["rvn2"], ["rvn2"])
                    S.dve(lambda e: e.reciprocal(out=rvn[0:P, 2:3], in_=rvn[0:P, 2:3]), ["rvn2"], ["rvn2"])
                    S.dve(lambda e, ti=ti: e.scalar_tensor_tensor(out=ybuf[0:P], in0=Xg[b][0:P, ti, :], scalar=rvn[0:P, 2:3], in1=gfin[0:P], op0=ALU.mult, op1=ALU.mult), ["Xg%d" % b, "rvn2", "ybuf"], ["ybuf"])
                    S.dma(dst, ybuf[0:P], reads=["ybuf"])
                else:
                    S.dma(xout[t0:t0 + P, :], Xg[b][0:P, ti, :], reads=["Xg%d" % b], writes=["xout"])

        if groups:
            load(0)
        for gi in range(len(groups)):
            upgate(gi)
            if gi > 0:
                down(gi - 1)
            if gi + 1 < len(groups):
                load(gi + 1)
        if groups:
            down(len(groups) - 1)

    groups0 = []
    for gi in range(2 * HALF // 256):
        groups0.append(dict(t0=gi * 256, n=256, hist="zero" if gi == 0 else "carry", store=True,
                            conv_out=o_cvp[0] if gi == 2 * HALF // 256 - 1 else None))
    for sq in range(2):
        groups0.append(dict(t0=2 * HALF + 64 * sq, n=64, hist=sq, store=True, conv_out=o_cvs[0, sq]))
    if KST >= 5:
        ffn_phase(0, x1s, x2s, groups0, gffn_d[0], False)


    wq_d = din("w_qkvf", [D, 3 * D + NH])
    kc_d = din("kcache", [2, HALF, D]); vc_d = din("vcache", [2, HALF, D]); lfc_d = din("lfcache", [2, HALF, NH])
    o_kp = dout("o_kp", [HALF, D]); o_vp = dout("o_vp", [HALF, D]); o_lfp = dout("o_lfp", [HALF, NH])
    o_ks = dout("o_ks", [128, D]); o_vs = dout("o_vs", [128, D]); o_lfs = dout("o_lfs", [128, NH])
    kts = dscr("kts", [NH, HD, NT], BF16); qts = dscr("qts", [NH, HD, NT], BF16); vsb = dscr("vsb", [NT, D], BF16)
    crow = dscr("crow", [NH, NT]); ccol = dscr("ccol", [NT, NH])
    kcT = dscr("kcT", [2, NH, HD, HALF], BF16); vcb = dscr("vcb", [2, HALF, D], BF16); ccc = dscr("ccc", [2, HALF, NH])

    def phase_d():
        S.barrier(); aoff[0] = mP2
        NQ = 3 * D + NH
        Wq = sb("Wq", [128, 8, NQ], BF16)
        gain = sb("gainq", [128, D])
        Lacc = sb("Lacc", [128, NH])
        m1 = aoff[0]
        load_weight(Wq, wq_d, 8, 128, NQ, "wq")
        S.dma(gain[:], gmix_d[1], writes=["gainq"])
        S.barrier(); aoff[0] = m1
        Xq = [sb("Xq%d" % i, [128, D]) for i in range(2)]
        hsq = sb("hsq", [128, D], BF16)
        hsTq = [sb("hsTq%d" % i, [128, 8, 128], BF16) for i in range(2)]
        Qb = [sb("Qb%d" % i, [128, D], BF16) for i in range(2)]
        Kf = [sb("Kf%d" % i, [128, D]) for i in range(2)]
        Kb = [sb("Kb%d" % i, [128, D], BF16) for i in range(2)]
        Vf = [sb("Vf%d" % i, [128, D]) for i in range(2)]
        Vb = [sb("Vb%d" % i, [128, D], BF16) for i in range(2)]
        QTt = [sb("QTt%d" % i, [128, 8, 128], BF16) for i in range(2)]
        KTt = [sb("KTt%d" % i, [128, 8, 128], BF16) for i in range(2)]
        zf = [sb("zf%d" % i, [128, NH]) for i in range(2)]
        lgf = [sb("lgf%d" % i, [128, NH]) for i in range(2)]
        r1 = sb("r1", [128, NH]); L3 = sb("L3", [128, 3, NH], BF16); La3 = sb("La3", [128, 3, NH], BF16); ra = sb("ra", [128, NH])
        csb = [sb("csb%d" % i, [128, NH]) for i in range(2)]
        cTs = [sb("cTs%d" % i, [NH, 128]) for i in range(2)]

        def split3(src, dst3, tmp, P, sname, dname, tname):
            S.dve(lambda e: e.tensor_copy(out=dst3[0:P, 0, :], in_=src[0:P]), [sname, dname], [dname])
            S.dve(lambda e: e.tensor_tensor(out=tmp[0:P], in0=src[0:P], in1=dst3[0:P, 0, :], op=ALU.subtract), [sname, dname, tname], [tname])
            S.dve(lambda e: e.tensor_copy(out=dst3[0:P, 1, :], in_=tmp[0:P]), [tname, dname], [dname])
            S.dve(lambda e: e.tensor_tensor(out=tmp[0:P], in0=tmp[0:P], in1=dst3[0:P, 1, :], op=ALU.subtract), [tname, dname], [tname])
            S.dve(lambda e: e.tensor_copy(out=dst3[0:P, 2, :], in_=tmp[0:P]), [tname, dname], [dname])

        def cumsum_path(lg, lgn, P, b, ccol_dst, crow_dst):
            split3(lg, L3, r1, P, lgn, "L3", "r1")
            split3(Lacc, La3, ra, 128, "Lacc", "La3", "ra")
            cb = pbank[3]
            for j in range(3):
                S.pe(lambda e, j=j: e.matmul(cb[0:P, 0:NH], lhsT=tri[0:P, 0:P], rhs=L3[0:P, j, :], start=(j == 0), stop=False), ["L3", "tri"], ["pb3"])
            for j in range(3):
                S.pe(lambda e, j=j: e.matmul(cb[0:P, 0:NH], lhsT=onesb[:, 0:P], rhs=La3[:, j, :], start=False, stop=(j == 2)), ["La3", "onesb"], ["pb3"])
            S.dve(lambda e: e.tensor_tensor(out=Lacc[0:P], in0=Lacc[0:P], in1=lg[0:P], op=ALU.add), ["Lacc", lgn, "La3"], ["Lacc"])
            S.act(lambda e: e.copy(out=csb[b][0:P], in_=cb[0:P, 0:NH]), ["pb3"], ["csb%d" % b])
            S.dma(ccol_dst, csb[b][0:P], reads=["csb%d" % b], writes=["ccol"])
            if crow_dst is not None:
                bi = tcount[0] % 2; tcount[0] += 1
                tb = pbank[bi]
                S.pe(lambda e: e.transpose(out=tb[0:NH, 0:P], in_=csb[b][0:P, :], identity=ident[0:P, 0:P]), ["csb%d" % b, "ident"], ["pb%d" % bi])
                S.act(lambda e: e.copy(out=cTs[b][0:NH, 0:P], in_=tb[0:NH, 0:P]), ["pb%d" % bi], ["cTs%d" % b])
                S.dma(crow_dst, cTs[b][0:NH, 0:P], reads=["cTs%d" % b], writes=["crow"])

        def kt_dst(scr, t0, P):
            return scr.rearrange("(hp h2) d t -> (h2 d) hp t", h2=2)[:, :, t0:t0 + P]

        tiles = []
        for i in range(2 * HALF // 128):
            tiles.append(dict(kind="full", t0=i * 128, P=128, reset=(i == 0)))
        for sq in range(2):
            for j in range(HALF // 128):
                tiles.append(dict(kind="cache", sq=sq, j=j, P=128, reset=(j == 0)))
            tiles.append(dict(kind="full", t0=2 * HALF + 64 * sq, P=64, reset=False))

        def load(i):
            tl = tiles[i]; b = i % 2; P = tl["P"]
            if tl["kind"] == "full":
                S.dma(Xq[b][0:P], x2s[tl["t0"]:tl["t0"] + P, :], writes=["Xq%d" % b])
            else:
                sq, j = tl["sq"], tl["j"]
                S.dma(Kf[b][:], kc_d[sq, j * 128:(j + 1) * 128, :], writes=["Kf%d" % b])
                S.dma(Vf[b][:], vc_d[sq, j * 128:(j + 1) * 128, :], writes=["Vf%d" % b])
                S.dma(lgf[b][:], lfc_d[sq, j * 128:(j + 1) * 128, :], writes=["lgf%d" % b])

        def do_tile(i, tl):
            b = i % 2; P = tl["P"]
            if i + 1 < len(tiles):
                load(i + 1)
            if tl["reset"]:
                S.dve(lambda e: e.memset(Lacc[:], 0.0), ["Lacc"], ["Lacc"])
            if tl["kind"] == "cache":
                sq, j = tl["sq"], tl["j"]
                S.pool(lambda e, b=b: e.tensor_copy(out=Kb[b][:], in_=Kf[b][:]), ["Kf%d" % b], ["Kb%d" % b])
                S.pool(lambda e, b=b: e.tensor_copy(out=Vb[b][:], in_=Vf[b][:]), ["Vf%d" % b], ["Vb%d" % b])
                transpose8(Kb[b][:], 128, KTt[b][:], "Kb%d" % b, "KTt%d" % b)
                S.dma(kt_dst(kcT[sq], j * 128, 128), KTt[b][:], reads=["KTt%d" % b], writes=["kcT"])
                S.dma(vcb[sq, j * 128:(j + 1) * 128, :], Vb[b][:], reads=["Vb%d" % b], writes=["vcb"])
                cumsum_path(lgf[b], "lgf%d" % b, 128, b, ccc[sq, j * 128:(j + 1) * 128, :], None)
                return
            t0 = tl["t0"]
            rms_to_bf16(Xq[b][0:P], P, gain, hsq[0:P], "Xq%d" % b, "hsq", 3)
            transpose8(hsq[0:P], P, hsTq[b][:, :, 0:P], "hsq", "hsTq%d" % b)
            hT_ = hsTq[b]; hn = "hsTq%d" % b

            def mm(bank, cols, c0, n):
                for kc in range(8):
                    S.pe(lambda e, kc=kc: e.matmul(pbank[bank][0:P, cols], lhsT=hT_[:, kc, 0:P], rhs=Wq[:, kc, c0:c0 + n], start=(kc == 0), stop=(kc == 7)), [hn], ["pb%d" % bank])
            mm(2, slice(0, NH), 3 * D, NH)
            mm(3, slice(0, 512), 0, 512); mm(4, slice(0, 512), 512, 512)
            mm(5, slice(0, 512), D, 512); mm(6, slice(0, 512), D + 512, 512)
            mm(7, slice(0, 512), 2 * D, 512)
            S.dve(lambda e, b=b: e.tensor_tensor(out=zf[b][0:P], in0=pbank[2][0:P, 0:NH], in1=bft[0:P], op=ALU.add), ["pb2", "bft"], ["zf%d" % b])
            mm(2, slice(0, 512), 2 * D + 512, 512)
            S.act(lambda e, b=b: e.activation(out=zf[b][0:P], in_=zf[b][0:P], func=AF.Exp, scale=-1.0), ["zf%d" % b], ["zf%d" % b])
            S.act(lambda e, b=b: e.activation(out=zf[b][0:P], in_=zf[b][0:P], func=AF.Ln, bias=one1[0:P, 0:1]), ["zf%d" % b, "one1"], ["zf%d" % b])
            S.dve(lambda e, b=b: e.tensor_scalar(out=lgf[b][0:P], in0=zf[b][0:P], scalar1=-1.0, scalar2=None, op0=ALU.mult), ["zf%d" % b], ["lgf%d" % b])
            for hf in range(2):
                S.act(lambda e, hf=hf, b=b: e.activation(out=Qb[b][0:P, hf * 512:(hf + 1) * 512], in_=pbank[3 + hf][0:P, :], func=AF.Copy, scale=HD ** -0.5), ["pb%d" % (3 + hf)], ["Qb%d" % b])
                S.dve(lambda e, hf=hf, b=b: e.tensor_copy(out=Kf[b][0:P, hf * 512:(hf + 1) * 512], in_=pbank[5 + hf][0:P, :]), ["pb%d" % (5 + hf)], ["Kf%d" % b])
            S.act(lambda e, b=b: e.copy(out=Vf[b][0:P, 0:512], in_=pbank[7][0:P, :]), ["pb7"], ["Vf%d" % b])
            S.act(lambda e, b=b: e.copy(out=Vf[b][0:P, 512:1024], in_=pbank[2][0:P, :]), ["pb2"], ["Vf%d" % b])
            S.pool(lambda e, b=b: e.tensor_copy(out=Kb[b][0:P], in_=Kf[b][0:P]), ["Kf%d" % b], ["Kb%d" % b])
            S.pool(lambda e, b=b: e.tensor_copy(out=Vb[b][0:P], in_=Vf[b][0:P]), ["Vf%d" % b], ["Vb%d" % b])
            transpose8(Qb[b][0:P], P, QTt[b][:, :, 0:P], "Qb%d" % b, "QTt%d" % b)
            transpose8(Kb[b][0:P], P, KTt[b][:, :, 0:P], "Kb%d" % b, "KTt%d" % b)
            S.dma(kt_dst(qts, t0, P), QTt[b][:, :, 0:P], reads=["QTt%d" % b], writes=["qts"])
            S.dma(kt_dst(kts, t0, P), KTt[b][:, :, 0:P], reads=["KTt%d" % b], writes=["kts"])
            S.dma(vsb[t0:t0 + P, :], Vb[b][0:P], reads=["Vb%d" % b], writes=["vsb"])
            if t0 >= 2 * HALF:
                o0 = t0 - 2 * HALF
                S.dma(o_ks[o0:o0 + P, :], Kf[b][0:P], reads=["Kf%d" % b]); S.dma(o_vs[o0:o0 + P, :], Vf[b][0:P], reads=["Vf%d" % b])
                S.dma(o_lfs[o0:o0 + P, :], lgf[b][0:P], reads=["lgf%d" % b])
            elif t0 >= HALF:
                o0 = t0 - HALF
                S.dma(o_kp[o0:o0 + P, :], Kf[b][0:P], reads=["Kf%d" % b]); S.dma(o_vp[o0:o0 + P, :], Vf[b][0:P], reads=["Vf%d" % b])
                S.dma(o_lfp[o0:o0 + P, :], lgf[b][0:P], reads=["lgf%d" % b])
            cumsum_path(lgf[b], "lgf%d" % b, P, b, ccol[t0:t0 + P, :], crow[:, t0:t0 + P])

        load(0)
        for i, tl in enumerate(tiles):
            do_tile(i, tl)

    if KST >= 6:
        phase_d()


    ats = dscr("ats", [NH, HD, NT], BF16)
    QS0 = HALF - 512

    def phase_e():
        S.barrier(); aoff[0] = mP2
        NKB = 2 * HALF // 128
        KA = [sb("KA%d" % i, [128, 2 * HALF], BF16) for i in range(2)]
        VA = [sb("VA%d" % i, [128, NKB, 65], BF16) for i in range(2)]
        CK = [sb("CK%d" % i, [128, NKB]) for i in range(2)]
        QA = [sb("QA%d" % i, [128, 512], BF16) for i in range(2)]
        crq = [sb("crq%d" % i, [128, 512]) for i in range(2)]
        cref = [sb("cref%d" % i, [128, 1]) for i in range(2)]
        cbias = [sb("cbias%d" % i, [128, NKB]) for i in range(2)]
        dq = sb("dq", [128, 512]); hq = sb("hq", [128, 512], BF16)
        PT = [sb("PT%d" % i, [128, 512], BF16) for i in range(4)]
        Osb = [sb("Osb%d" % i, [128, 512]) for i in range(2)]
        rt = sb("rt", [128, 512]); rt2 = sb("rt2", [128, 512]); rl3 = sb("rl3", [128, 3, 512], BF16)
        ATb = [sb("ATb%d" % i, [128, 512], BF16) for i in range(2)]
        for i in range(2):
            S.pool(lambda e, i=i: e.memset(KA[i][32:33, :], 1.0), [], ["KA%d" % i])
            S.pool(lambda e, i=i: e.memset(KA[i][64:65, :], 1.0), [], ["KA%d" % i])
            S.pool(lambda e, i=i: e.memset(VA[i][:, :, 64:65], 1.0), [], ["VA%d" % i])
            S.pool(lambda e, i=i: e.memset(QA[i][:], 0.0), [], ["QA%d" % i])
        sel = sb("sel", [128, 64], BF16)
        S.pool(lambda e: e.memset(sel[:], 0.0), [], ["sel"])
        S.pool(lambda e: e.memset(sel[64:65, :], 1.0), ["sel"], ["sel"])
        S.pool(lambda e: e.memset(rl3[:], 0.0), [], ["rl3"])
        cnt = {"s": 0, "q": 0, "p": 0}
        pending = []

        def rows67(dst, src, c0, c1, name, wname):
            S.dma(dst[0:32, c0:c1], src[0:32, :], writes=[wname])
            S.dma(dst[33:64, c0:c1], src[32:63, :], writes=[wname])
            S.dma(dst[65:66, c0:c1], src[63:64, :], writes=[wname])

        def attend(h, hb, q_t0, nq, blocks, pm_nb, out_t0):
            qb = cnt["q"] % 2; cnt["q"] += 1
            qn = "QA%d" % qb
            rows67(QA[qb], qts[h, :, q_t0:q_t0 + nq], 0, nq, qn, qn)
            S.dma(crq[qb][32:33, 0:nq], crow[h:h + 1, q_t0:q_t0 + nq], writes=["crq%d" % qb])
            S.dma(crq[qb][64:65, 0:nq], crow[h:h + 1, q_t0:q_t0 + nq], writes=["crq%d" % qb])
            S.dma(cref[qb][:], crow[h:h + 1, q_t0:q_t0 + 1].partition_broadcast(128), writes=["cref%d" % qb])
            S.dve(lambda e: e.tensor_scalar(out=dq[64:65, 0:nq], in0=crq[qb][64:65, 0:nq], scalar1=crq[qb][64:65, 0:1], scalar2=None, op0=ALU.subtract), ["crq%d" % qb, "dq64"], ["dq64"])
            S.dve(lambda e: e.tensor_copy(out=QA[qb][64:65, 0:nq], in_=dq[64:65, 0:nq]), ["dq64", qn], [qn])
            S.dve(lambda e: e.tensor_scalar(out=dq[32:33, 0:nq], in0=crq[qb][32:33, 0:nq], scalar1=crq[qb][32:33, 0:1], scalar2=None, op0=ALU.subtract), ["crq%d" % qb, "dq32"], ["dq32"])
            S.dve(lambda e: e.tensor_copy(out=hq[32:33, 0:nq], in_=dq[32:33, 0:nq]), ["dq32", "hq"], ["hq"])
            S.dve(lambda e: e.tensor_tensor(out=dq[32:33, 0:nq], in0=dq[32:33, 0:nq], in1=hq[32:33, 0:nq], op=ALU.subtract), ["dq32", "hq"], ["dq32"])
            S.dve(lambda e: e.tensor_copy(out=QA[qb][32:33, 0:nq], in_=dq[32:33, 0:nq]), ["dq32", qn], [qn])
            nbk = max(bk for (bk, nk, c0) in blocks) + 1
            S.dve(lambda e: e.tensor_scalar(out=cbias[qb][:, 0:nbk], in0=CK[hb][:, 0:nbk], scalar1=-1.0, scalar2=cref[qb][:, 0:1], op0=ALU.mult, op1=ALU.add), ["CK%d" % hb, "cref%d" % qb, "cbias%d" % qb], ["cbias%d" % qb])
            if pm_nb > 0:
                S.dve(lambda e: e.tensor_scalar(out=cbias[qb][:, 0:pm_nb], in0=cbias[qb][:, 0:pm_nb], scalar1=pmk[:, 0:1], scalar2=None, op0=ALU.add), ["cbias%d" % qb, "pmk"], ["cbias%d" % qb])
            ob = pbank[5 + qb]; obn = "pb%d" % (5 + qb)
            KEL = int(os.environ.get("KE_LVL", "9"))
            if KEL < 2:
                return
            nb_ = len(blocks)
            meta = {}

            def qk(bi_):
                bk, nk, c0 = blocks[bi_]
                si = cnt["s"] % 3; cnt["s"] += 1
                pi = cnt["p"] % 4; cnt["p"] += 1
                sbk = pbank[2 + si]; sn = "pb%d" % (2 + si)
                cc = 0 if c0 is None else c0
                meta[bi_] = (pi, cc)
                S.pe(lambda e: e.matmul(sbk[0:nk, cc:nq], lhsT=KA[hb][0:66, bk * 128:bk * 128 + nk], rhs=QA[qb][0:66, cc:nq], start=True, stop=True), ["KA%d" % hb, qn], [sn])
                if c0 is not None:
                    w = min(128, nq - cc)
                    S.dve(lambda e: e.tensor_tensor(out=sbk[0:nk, cc:cc + w], in0=sbk[0:nk, cc:cc + w], in1=cmask[0:nk, 0:w], op=ALU.add), [sn, "cmask"], [sn])
                S.act(lambda e: e.activation(out=PT[pi][0:nk, cc:nq], in_=sbk[0:nk, cc:nq], func=AF.Exp, bias=cbias[qb][0:nk, bk:bk + 1]), [sn, "cbias%d" % qb], ["PT%d" % pi])

            def pv(bi_):
                bk, nk, c0 = blocks[bi_]
                pi, cc = meta[bi_]
                S.pe(lambda e: e.matmul(ob[0:65, cc:nq], lhsT=VA[hb][0:nk, bk, 0:65], rhs=PT[pi][0:nk, cc:nq], start=(bi_ == 0), stop=(bi_ == nb_ - 1)), ["VA%d" % hb, "PT%d" % pi], [obn])

            LOOK = 2
            for i_ in range(nb_ + LOOK):
                if i_ < nb_:
                    qk(i_)
                if i_ == LOOK and pending:
                    pending.pop()()
                if i_ - LOOK >= 0 and KEL >= 3:
                    pv(i_ - LOOK)
            if pending:
                pending.pop()()
            if KEL < 4:
                return
            pending.append(lambda: finish(qb, nq, ob, obn, h, out_t0))

        def finish(qb, nq, ob, obn, h, out_t0):
            S.act(lambda e: e.copy(out=Osb[qb][0:65, 0:nq], in_=ob[0:65, 0:nq]), [obn], ["Osb%d" % qb])
            S.dve(lambda e: e.reciprocal(out=rt[64:65, 0:nq], in_=Osb[qb][64:65, 0:nq]), ["Osb%d" % qb, "rt"], ["rt"])
            S.dve(lambda e: e.tensor_copy(out=rl3[64:65, 0, 0:nq], in_=rt[64:65, 0:nq]), ["rt", "rl3"], ["rl3"])
            S.dve(lambda e: e.tensor_tensor(out=rt2[64:65, 0:nq], in0=rt[64:65, 0:nq], in1=rl3[64:65, 0, 0:nq], op=ALU.subtract), ["rt", "rl3", "rt2"], ["rt2"])
            S.dve(lambda e: e.tensor_copy(out=rl3[64:65, 1, 0:nq], in_=rt2[64:65, 0:nq]), ["rt2", "rl3"], ["rl3"])
            S.dve(lambda e: e.tensor_tensor(out=rt2[64:65, 0:nq], in0=rt2[64:65, 0:nq], in1=rl3[64:65, 1, 0:nq], op=ALU.subtract), ["rt2", "rl3"], ["rt2"])
            S.dve(lambda e: e.tensor_copy(out=rl3[64:65, 2, 0:nq], in_=rt2[64:65, 0:nq]), ["rt2", "rl3"], ["rl3"])
            bc = pbank[7]
            for j in range(3):
                S.pe(lambda e, j=j: e.matmul(bc[0:64, 0:nq], lhsT=sel[0:65, 0:64], rhs=rl3[0:65, j, 0:nq], start=(j == 0), stop=(j == 2)), ["rl3", "sel"], ["pb7"])
            S.dve(lambda e: e.tensor_tensor(out=ATb[qb][0:64, 0:nq], in0=Osb[qb][0:64, 0:nq], in1=bc[0:64, 0:nq], op=ALU.mult), ["Osb%d" % qb, "pb7", "ATb%d" % qb], ["ATb%d" % qb])
            S.dma(ats[h, :, out_t0:out_t0 + nq], ATb[qb][0:64, 0:nq], reads=["ATb%d" % qb], writes=["ats"])

        hcount = [0]
        def load_head(h, hb):
            rows67(KA[hb], kts[h, :, 0:2 * HALF], 0, 2 * HALF, "k", "KA%d" % hb)
            S.dma(VA[hb][:, :, 0:64], vsb[0:2 * HALF, h * HD:(h + 1) * HD].rearrange("(kb p) d -> p kb d", p=128), writes=["VA%d" % hb])
            S.dma(CK[hb][:, :], ccol[0:2 * HALF, h:h + 1].rearrange("(kb p) o -> p (kb o)", p=128), writes=["CK%d" % hb], allow_slow_non_contiguous=True)

        NHE = int(os.environ.get('KE_H', NH)) if KST >= 7 else 0
        if NHE:
            load_head(0, 0)
        for h in range(NHE):
            hb = hcount[0] % 2; hcount[0] += 1
            if h + 1 < NHE:
                load_head(h + 1, (hb + 1) % 2)
            for j in range(9):
                qs = QS0 + 512 * j
                nfull = qs // 128
                blocks = [(kb, 128, None) for kb in range(nfull)] + [(nfull + d_, 128, 128 * d_) for d_ in range(4)]
                attend(h, hb, qs, 512, blocks, (QS0 // 128) if j == 0 else (HALF // 128), qs)
        for sq in range(int(os.environ.get('KE_S', 2)) if KST >= 7 else 0):
            tn = 2 * HALF + 64 * sq
            for h in range(NH):
                hb = hcount[0] % 2; hcount[0] += 1
                rows67(KA[hb], kcT[sq, h], 0, HALF, "k", "KA%d" % hb)
                rows67(KA[hb], kts[h, :, tn:tn + 64], HALF, HALF + 64, "k", "KA%d" % hb)
                S.dma(VA[hb][:, 0:32, 0:64], vcb[sq, :, h * HD:(h + 1) * HD].rearrange("(kb p) d -> p kb d", p=128), writes=["VA%d" % hb])
                S.dma(VA[hb][0:64, 32, 0:64], vsb[tn:tn + 64, h * HD:(h + 1) * HD], writes=["VA%d" % hb])
                S.dma(CK[hb][:, 0:32], ccc[sq, :, h:h + 1].rearrange("(kb p) o -> p (kb o)", p=128), writes=["CK%d" % hb], allow_slow_non_contiguous=True)
                S.dma(CK[hb][0:64, 32:33], ccol[tn:tn + 64, h:h + 1], writes=["CK%d" % hb], allow_slow_non_contiguous=True)
                blocks = [(kb, 128, None) for kb in range(32)] + [(32, 128, 0)]
                attend(h, hb, tn, 64, blocks, 0, tn)
        if pending:
            pending.pop()()

    if KST >= 7:
        phase_e()

    wo_d = din("w_o", [D, D])

    def phase_f():
        S.barrier(); aoff[0] = mP2
        Wo = sb("Wo", [128, 8, D], BF16)
        m1 = aoff[0]
        load_weight(Wo, wo_d, 8, 128, D, "wo")
        S.barrier(); aoff[0] = m1
        ATt = [sb("ATt%d" % i, [128, 8, 128], BF16) for i in range(2)]
        Xf = [sb("Xf%d" % i, [128, D]) for i in range(2)]
        tiles = [(t, 128) for t in range(QS0, NT, 128)]

        def load(i):
            t0, P = tiles[i]; b = i % 2
            for h2 in range(2):
                S.dma(ATt[b][64 * h2:64 * h2 + 64, :, 0:P], ats[:, :, t0:t0 + P].rearrange("(hp h2) d t -> h2 d hp t", h2=2)[h2], writes=["ATt%d" % b])
            S.dma(Xf[b][0:P], x2s[t0:t0 + P, :], writes=["Xf%d" % b])

        def do(i):
            t0, P = tiles[i]; b = i % 2
            if i + 1 < len(tiles):
                load(i + 1)
            for half in range(2 if not os.environ.get("KF_NOMM") else 0):
                k = (2 * i + half) % 4
                ob = pbank[2 + k]; on_ = "pb%d" % (2 + k)
                for h in range(8):
                    S.pe(lambda e, h=h, half=half, ob=ob: e.matmul(ob[0:P, :], lhsT=ATt[b][:, h, 0:P], rhs=Wo[:, h, half * 512:(half + 1) * 512], start=(h == 0), stop=(h == 7)), ["ATt%d" % b], [on_])
                S.dve(lambda e, half=half, ob=ob: e.tensor_tensor(out=Xf[b][0:P, half * 512:(half + 1) * 512], in0=Xf[b][0:P, half * 512:(half + 1) * 512], in1=ob[0:P, :], op=ALU.add), [on_, "Xf%d" % b], ["Xf%d" % b])
            S.dma(x3s[t0:t0 + P, :], Xf[b][0:P], reads=["Xf%d" % b], writes=["x3s"])

        load(0)
        for i in range(len(tiles)):
            do(i)

    if KST >= 8:
        phase_f()
        groups1 = [dict(t0=HALF - 256, n=256, hist="zero", store=False, conv_out=None)]
        for gi in range(HALF // 256):
            groups1.append(dict(t0=HALF + gi * 256, n=256, hist="carry", store=True, yout=o_yp[gi * 256:(gi + 1) * 256, :],
                                conv_out=o_cvp[1] if gi == HALF // 256 - 1 else None))
        for sq in range(2):
            groups1.append(dict(t0=2 * HALF + 64 * sq, n=64, hist=sq, store=True, yout=o_ys[64 * sq:64 * sq + 64, :], conv_out=o_cvs[1, sq]))
        if not os.environ.get("KSKIPG"):
            ffn_phase(1, x3s, None, groups1, gffn_d[1], True)

    if os.environ.get("KDBG"):
        dbgG = dout("dbgG", [128, D], BF16); dbgX1 = dout("dbgX1", [128, D]); dbgX2 = dout("dbgX2", [128, D])
        S.barrier()
        S.dma(dbgG, gact[2 * HALF:2 * HALF + 128, :])
        S.dma(dbgX1, x1s[2 * HALF:2 * HALF + 128, :])
        S.dma(dbgX2, x2s[2 * HALF:2 * HALF + 128, :])
    S.emit()
    st.close()
    return nc


def _pair_layout(a):
    a = np.asarray(a)
    rest = a.shape[2:]
    a = a.reshape((32, 2, 64) + rest)
    a = np.moveaxis(a, 0, 2)
    return np.ascontiguousarray(a.reshape((128, 32) + rest))


def _from_pair_layout(a):
    a = a.reshape(2, 64, 32)
    return np.ascontiguousarray(np.moveaxis(a, 2, 0).reshape(64, 64))


def _host_inputs(inp):
    f = np.float32
    xpr = np.asarray(inp["x_prompt"], f)
    xsm = np.asarray(inp["x_sample"], f)
    rep = lambda v: np.ascontiguousarray(np.broadcast_to(np.asarray(v, f)[None, :], (128, np.asarray(v).shape[-1])))
    common = dict(
        lam_r=_pair_layout(np.asarray(inp["ssm_a_re"], f)[0]),
        lam_i=_pair_layout(np.asarray(inp["ssm_a_im"], f)[0]),
        lstep=_pair_layout(np.broadcast_to(np.asarray(inp["ssm_log_step"], f)[0][:, None], (64, 64))),
        b_r=_pair_layout(np.asarray(inp["ssm_b_re"], f)[0]),
        b_i=_pair_layout(np.asarray(inp["ssm_b_im"], f)[0]),
        c_r=_pair_layout(np.transpose(np.asarray(inp["ssm_c_re"], f)[0], (0, 2, 1))),
        c_i=_pair_layout(np.transpose(np.asarray(inp["ssm_c_im"], f)[0], (0, 2, 1))),
        gmix=np.stack([rep(inp["norm_mix"][i]) for i in range(2)]),
        gffn=np.stack([rep(inp["norm_ffn"][i]) for i in range(2)]),
        gfin=rep(inp["norm_final"]),
        dskip=rep(inp["ssm_d"][0]),
        ident=np.eye(128, dtype=f),
        tmask=(np.arange(128)[:, None] // 16 <= np.arange(128)[None, :] // 16).astype(f),
        tri=(np.arange(128)[:, None] <= np.arange(128)[None, :]).astype(f),
        cmask=np.where(np.arange(128)[:, None] <= np.arange(128)[None, :], 0.0, -30000.0).astype(f),
        bft=rep(inp["fox_b_f"][0]),
        w_glu=np.ascontiguousarray(np.asarray(inp["ssm_w_glu"], f)[0]),
        w_up=np.asarray(inp["ffn_w_up"], f), w_gate=np.asarray(inp["ffn_w_gate"], f), w_down=np.asarray(inp["ffn_w_down"], f),
        conv_w=np.asarray(inp["ffn_conv_w"], f), conv_b=np.asarray(inp["ffn_conv_b"], f),
        w_qkvf=np.ascontiguousarray(np.asarray(inp["fox_w_qkvf"], f)[0]),
        w_o=np.ascontiguousarray(np.asarray(inp["fox_w_o"], f)[0]),
    )
    maps = []
    for k in range(8):
        b, half = k // 2, k % 2
        prev = xpr[b, 0:HALF] if half == 1 else np.zeros((HALF, D), f)
        own = xpr[b, half * HALF:(half + 1) * HALF]
        m = dict(common)
        m["xp"] = np.ascontiguousarray(np.concatenate([prev, own], 0))
        m["xs"] = np.ascontiguousarray(xsm[2 * k:2 * k + 2].reshape(128, D))
        m["kcache"] = np.ascontiguousarray(np.asarray(inp["cache_fox_k"], f)[0, 2 * k:2 * k + 2].reshape(2, HALF, D))
        m["vcache"] = np.ascontiguousarray(np.asarray(inp["cache_fox_v"], f)[0, 2 * k:2 * k + 2].reshape(2, HALF, D))
        m["lfcache"] = np.ascontiguousarray(np.asarray(inp["cache_fox_logf"], f)[0, 2 * k:2 * k + 2])
        m["pmk"] = np.full((128, 1), 0.0 if half == 1 else -30000.0, f)
        m["cvst"] = np.ascontiguousarray(np.asarray(inp["state_ffn_conv"], f)[:, 2 * k:2 * k + 2])
        m["h0r"] = np.ascontiguousarray(np.stack([_pair_layout(np.asarray(inp["state_ssm_re"], f)[0, 2 * k + s]) for s in range(2)], 1))
        m["h0i"] = np.ascontiguousarray(np.stack([_pair_layout(np.asarray(inp["state_ssm_im"], f)[0, 2 * k + s]) for s in range(2)], 1))
        maps.append(m)
    return maps


_NC_CACHE = {}


def kernel(**inputs):
    maps = _host_inputs(inputs)
    if "nc" not in _NC_CACHE:
        _NC_CACHE["nc"] = build()
    nc = _NC_CACHE["nc"]
    res = run_bass_kernel_spmd(nc, maps, core_ids=list(range(8)))
    R = res.results
    f = np.float32
    y_prompt = np.zeros((4, 8192, D), f); y_sample = np.zeros((16, 64, D), f)
    ssm_re_p = np.zeros((1, 4, 64, 64), f); ssm_im_p = np.zeros((1, 4, 64, 64), f)
    ssm_re_s = np.zeros((1, 16, 64, 64), f); ssm_im_s = np.zeros((1, 16, 64, 64), f)
    k_p = np.zeros((1, 4, 8192, NH, HD), f); v_p = np.zeros((1, 4, 8192, NH, HD), f); lf_p = np.zeros((1, 4, 8192, NH), f)
    k_s = np.zeros((1, 16, 64, NH, HD), f); v_s = np.zeros((1, 16, 64, NH, HD), f); lf_s = np.zeros((1, 16, 64, NH), f)
    cv_p = np.zeros((2, 4, 2, DFF), f); cv_s = np.zeros((2, 16, 2, DFF), f)
    for k in range(8):
        b, half = k // 2, k % 2
        r = R[k]
        if half == 1:
            ssm_re_p[0, b] = _from_pair_layout(r["o_ssm_p"][0]); ssm_im_p[0, b] = _from_pair_layout(r["o_ssm_p"][1])
        if "o_cvp" in r:
            if half == 1:
                cv_p[:, b] = r["o_cvp"]
            cv_s[:, 2 * k:2 * k + 2] = r["o_cvs"]
        if "o_kp" in r:
            sl = slice(half * HALF, (half + 1) * HALF)
            k_p[0, b, sl] = r["o_kp"].reshape(HALF, NH, HD); v_p[0, b, sl] = r["o_vp"].reshape(HALF, NH, HD); lf_p[0, b, sl] = r["o_lfp"]
            k_s[0, 2 * k:2 * k + 2] = r["o_ks"].reshape(2, 64, NH, HD); v_s[0, 2 * k:2 * k + 2] = r["o_vs"].reshape(2, 64, NH, HD)
            lf_s[0, 2 * k:2 * k + 2] = r["o_lfs"].reshape(2, 64, NH)
        if "o_yp" in r:
            y_prompt[b, half * HALF:(half + 1) * HALF] = r["o_yp"]
            y_sample[2 * k:2 * k + 2] = r["o_ys"].reshape(2, 64, D)
        for s in range(2):
            ssm_re_s[0, 2 * k + s] = _from_pair_layout(r["o_ssm_s"][s, 0]); ssm_im_s[0, 2 * k + s] = _from_pair_layout(r["o_ssm_s"][s, 1])
    kernel.last = R
    return (y_prompt, y_sample, ssm_re_p, ssm_im_p, k_p, v_p, lf_p, cv_p, ssm_re_s, ssm_im_s, k_s, v_s, lf_s, cv_s)
```

```python
import contextlib
import math
import numpy as np
import concourse.bass as bass
import concourse.mybir as mybir
from concourse.bass_utils import run_bass_kernel_spmd

F32 = mybir.dt.float32
BF16 = mybir.dt.bfloat16
AF = mybir.ActivationFunctionType
ALU = mybir.AluOpType
AX = mybir.AxisListType

D = 1024
DFF = 2816
NH = 16
HD = 64
HALF = 4096
NPAIR = 32
EPS = 1e-6
TWO_PI = 2.0 * math.pi

ENGS = ("pe", "act", "dve", "pool", "sp")
NO_SELF_WAIT = ("pe", "sp")
DMA_RING = 8


class Op:
    __slots__ = ("eng", "fn", "deps", "dma", "idx", "needed", "cnt", "slot", "tgt")

    def __init__(self, eng, fn, deps, dma, idx):
        self.eng, self.fn, self.deps, self.dma, self.idx = eng, fn, deps, dma, idx
        self.needed = False
        self.cnt = 0
        self.slot = -1
        self.tgt = 0


class Sched:
    def __init__(self, nc):
        self.nc = nc
        self.ops = {e: [] for e in ENGS}
        self.last_w = {}
        self.readers = {}
        self.ndma = {e: 0 for e in ENGS}
        self.last_slot = {}

    def add(self, eng, fn, reads=(), writes=(), dma=False):
        lst = self.ops[eng]
        idx = len(lst)
        deps = set()
        for r in reads:
            lw = self.last_w.get(r)
            if lw is not None:
                deps.add(lw)
        for w in writes:
            lw = self.last_w.get(w)
            if lw is not None:
                deps.add(lw)
            rd = self.readers.get(w)
            if rd:
                for e, i in rd.items():
                    deps.add((e, i))
        op = Op(eng, fn, deps, dma, idx)
        if dma:
            k = self.ndma[eng]
            self.ndma[eng] = k + 1
            op.slot = k % DMA_RING
            op.tgt = 16 * (k // DMA_RING + 1)
            op.needed = True
            prev = self.last_slot.get((eng, op.slot))
            if prev is not None:
                deps.add((eng, prev))
            self.last_slot[(eng, op.slot)] = idx
        lst.append(op)
        for r in reads:
            self.readers.setdefault(r, {})[eng] = idx
        for w in writes:
            self.last_w[w] = (eng, idx)
            self.readers[w] = {}
        return op

    def pe(self, fn, reads=(), writes=()):
        return self.add("pe", fn, reads, writes)

    def act(self, fn, reads=(), writes=()):
        return self.add("act", fn, reads, writes)

    def dve(self, fn, reads=(), writes=()):
        return self.add("dve", fn, reads, writes)

    def pool(self, fn, reads=(), writes=()):
        return self.add("pool", fn, reads, writes)

    def on(self, eng, fn, reads=(), writes=()):
        return self.add(eng, fn, reads, writes)

    def dma(self, out, in_, reads=(), writes=(), q="sp", **kw):
        return self.add(q, lambda e: e.dma_start(out=out, in_=in_, **kw), reads, writes, dma=True)

    def barrier(self):
        deps = set()
        for e in ENGS:
            lst = self.ops[e]
            seen = set()
            got_last = False
            for j in range(len(lst) - 1, -1, -1):
                o = lst[j]
                if o.dma:
                    if o.slot not in seen:
                        seen.add(o.slot)
                        deps.add((e, j))
                elif not got_last and o.fn is not None:
                    got_last = True
                    deps.add((e, j))
                if got_last and len(seen) >= DMA_RING:
                    break
        for e in ENGS:
            self.ops[e].append(Op(e, None, set(deps), False, len(self.ops[e])))
        self.last_w.clear()
        self.readers.clear()

    def emit(self, final_wait_eng="sp"):
        nc = self.nc
        for e in ENGS:
            for op in self.ops[e]:
                for (de, di) in op.deps:
                    dop = self.ops[de][di]
                    if dop.dma:
                        continue
                    if de == e and e in NO_SELF_WAIT and op.fn is not None:
                        continue
                    if de == e and op.fn is None:
                        continue
                    dop.needed = True
        finals = []
        for e in ENGS:
            if self.ops[e]:
                last = None
                for o in reversed(self.ops[e]):
                    if not o.dma and o.fn is not None:
                        last = o
                        break
                if last is not None:
                    last.needed = True
                    finals.append((e, last.idx))
                seen = {}
                for op in self.ops[e]:
                    if op.dma:
                        seen[op.slot] = op.idx
                for sl, ix in seen.items():
                    finals.append((e, ix))
        for e in ENGS:
            c = 0
            for op in self.ops[e]:
                if op.needed and not op.dma and op.fn is not None:
                    c += 1
                    op.cnt = c
        with contextlib.ExitStack() as st:
            esem = {e: st.enter_context(nc.semaphore("s_" + e)) for e in ENGS}
            dsem = {e: [st.enter_context(nc.semaphore("d_%s%d" % (e, i))) for i in range(DMA_RING)]
                    for e in ENGS if self.ndma[e] > 0}
            block = st.enter_context(nc.Block())
            ops = self.ops

            def run(ename, eng):
                waited = {}
                for op in ops[ename]:
                    for (de, di) in sorted(op.deps):
                        dop = ops[de][di]
                        if dop.dma:
                            key = ("d", de, dop.slot)
                            if waited.get(key, 0) >= dop.tgt:
                                continue
                            waited[key] = dop.tgt
                            eng.wait_ge(dsem[de][dop.slot], dop.tgt)
                        else:
                            if de == ename and (ename in NO_SELF_WAIT or op.fn is None):
                                continue
                            key = ("e", de)
                            if waited.get(key, 0) >= dop.cnt:
                                continue
                            waited[key] = dop.cnt
                            eng.wait_ge(esem[de], dop.cnt)
                    if op.fn is None:
                        continue
                    ins = op.fn(eng)
                    if op.dma:
                        ins.then_inc(dsem[ename][op.slot], 16)
                    elif op.needed:
                        ins.then_inc(esem[ename], 1)
                if ename == final_wait_eng:
                    for (de, di) in finals:
                        dop = ops[de][di]
                        if dop.dma:
                            eng.wait_ge(dsem[de][dop.slot], dop.tgt)
                        elif de != ename:
                            eng.wait_ge(esem[de], dop.cnt)

            @block.tensor
            def _(eng):
                run("pe", eng)

            @block.scalar
            def _(eng):
                run("act", eng)

            @block.vector
            def _(eng):
                run("dve", eng)

            @block.gpsimd
            def _(eng):
                run("pool", eng)

            @block.sync
            def _(eng):
                run("sp", eng)


STAGE = 9


def build(stage=STAGE):
    nc = bass.Bass("TRN2", target_bir_lowering=False)
    S = Sched(nc)
    st = contextlib.ExitStack()

    def din(name, shape, dt=F32):
        return nc.dram_tensor(name, list(shape), dt, kind="ExternalInput").ap()

    def dout(name, shape, dt=F32):
        return nc.dram_tensor(name, list(shape), dt, kind="ExternalOutput").ap()

    def dscr(name, shape, dt=F32):
        return nc.dram_tensor(name, list(shape), dt).ap()

    ARENA_F32 = 53200
    arena = st.enter_context(nc.sbuf_tensor("arena", [128, ARENA_F32], F32))
    aoff = [0]

    def sb(name, shape, dt=F32):
        n = 1
        for d_ in shape[1:]:
            n *= d_
        nf = n if dt == F32 else (n + 1) // 2
        nf = (nf + 7) // 8 * 8
        o = aoff[0]
        assert o + nf <= ARENA_F32, ("arena overflow", name, o, nf)
        aoff[0] = o + nf
        v = arena[:, o:o + nf]
        if dt != F32:
            v = v.bitcast(dt)
        v = v[:, 0:n]
        if len(shape) == 2:
            return v
        letters = "abcdefg"[:len(shape) - 1]
        pat = "p (" + " ".join(letters) + ") -> p " + " ".join(letters)
        kw = {letters[i]: shape[1 + i] for i in range(len(shape) - 2)}
        return v.rearrange(pat, **kw)

    def ps(name, shape, dt=F32):
        return st.enter_context(nc.psum_tensor("p_" + name, list(shape), dt))

    xp = din("xp", [2 * HALF, D])
    xs = din("xs", [128, D])
    h0r = din("h0r", [128, 2, NPAIR])
    h0i = din("h0i", [128, 2, NPAIR])
    lam_r_d = din("lam_r", [128, NPAIR])
    lam_i_d = din("lam_i", [128, NPAIR])
    lstep_d = din("lstep", [128, NPAIR])
    b_r_d = din("b_r", [128, NPAIR, 16])
    b_i_d = din("b_i", [128, NPAIR, 16])
    c_r_d = din("c_r", [128, NPAIR, 16])
    c_i_d = din("c_i", [128, NPAIR, 16])
    gmix_d = din("gmix", [2, 128, D])
    gffn_d = din("gffn", [2, 128, D])
    gfin_d = din("gfin", [128, D])
    dskip_d = din("dskip", [128, D])
    ident_d = din("ident", [128, 128])
    tmask_d = din("tmask", [128, 128])

    o_ssm_p = dout("o_ssm_p", [2, 128, NPAIR])
    o_ssm_s = dout("o_ssm_s", [2, 2, 128, NPAIR])
    NT = 2 * HALF + 128
    gact = dscr("gscr", [NT, D], BF16)

    pbank = [ps("pb%d" % i, [128, 512], F32) for i in range(8)]

    ident = sb("ident", [128, 128]); identb = sb("identb", [128, 128], BF16)
    tmask = sb("tmask", [128, 128])
    S.dma(ident[:], ident_d, writes=["ident"])
    S.dma(tmask[:], tmask_d, writes=["tmask"])
    S.dve(lambda e: e.tensor_copy(out=identb[:], in_=ident[:]), ["ident"], ["identb"])
    negpi = sb("negpi", [128, 1])
    S.pool(lambda e: e.memset(negpi[:], -math.pi), [], ["negpi"])
    tri = sb("tri", [128, 128], BF16); onesb = sb("onesb", [128, 128], BF16); cmask = sb("cmask", [128, 128])
    bft = sb("bft", [128, NH]); pmk = sb("pmk", [128, 1]); one1 = sb("one1", [128, 1])
    cstg = sb("cstg", [128, 128])
    S.dma(cstg[:], din("tri", [128, 128]), writes=["cstg"])
    S.dve(lambda e: e.tensor_copy(out=tri[:], in_=cstg[:]), ["cstg"], ["tri"])
    S.pool(lambda e: e.memset(onesb[:], 1.0), [], ["onesb"])
    S.pool(lambda e: e.memset(one1[:], 1.0), [], ["one1"])
    S.dma(cmask[:], din("cmask", [128, 128]), writes=["cmask"])
    S.dma(bft[:], din("bft", [128, NH]), writes=["bft"])
    S.dma(pmk[:], din("pmk", [128, 1]), writes=["pmk"])
    ssn = sb("ssn", [128, 4]); rvn = sb("rvn", [128, 4])
    mP = aoff[0]
    mP2 = mP

    lam_r = sb("lam_r", [128, NPAIR]); lam_i = sb("lam_i", [128, NPAIR]); dtt = sb("dtt", [128, NPAIR])
    lrdt = sb("lrdt", [128, NPAIR]); lidt = sb("lidt", [128, NPAIR])
    Er = sb("Er", [128, 9, NPAIR]); Ei = sb("Ei", [128, 9, NPAIR])
    Emr = sb("Emr", [128, 9, NPAIR]); Emi = sb("Emi", [128, 9, NPAIR])
    x_t = sb("x_t", [128, 8, D])
    _al = [x_t[:, i, 0:NPAIR * 16].rearrange("p (a h) -> p a h", h=16) for i in range(8)]
    Br, Bi, Cr, Ci, Bbr, Bbi, t16a, t16b = _al
    tA = sb("tA", [128, NPAIR]); tB = sb("tB", [128, NPAIR]); tC = sb("tC", [128, NPAIR]); tD = sb("tD", [128, NPAIR])
    zr = sb("zr", [128, NPAIR]); zi = sb("zi", [128, NPAIR])
    S.dma(lam_r[:], lam_r_d, writes=["lam_r"]); S.dma(lam_i[:], lam_i_d, writes=["lam_i"])
    S.dma(dtt[:], lstep_d, writes=["dtt"])
    S.dma(Br[:], b_r_d, writes=["Br"]); S.dma(Bi[:], b_i_d, writes=["Bi"])
    S.dma(Cr[:], c_r_d, writes=["Cr"]); S.dma(Ci[:], c_i_d, writes=["Ci"])
    S.act(lambda e: e.activation(out=dtt[:], in_=dtt[:], func=AF.Exp), ["dtt"], ["dtt"])
    S.dve(lambda e: e.tensor_tensor(out=lrdt[:], in0=lam_r[:], in1=dtt[:], op=ALU.mult), ["lam_r", "dtt"], ["lrdt"])
    S.dve(lambda e: e.tensor_tensor(out=lidt[:], in0=lam_i[:], in1=dtt[:], op=ALU.mult), ["lam_i", "dtt"], ["lidt"])
    S.dve(lambda e: e.memset(Er[:, 0, :], 1.0), [], ["E"])
    S.dve(lambda e: e.memset(Ei[:, 0, :], 0.0), [], ["E"])
    halfpi = sb("halfpi", [128, 1])
    S.pool(lambda e: e.memset(halfpi[:], 0.5 * math.pi), [], ["halfpi"])
    S.act(lambda e: e.activation(out=tA[:], in_=lidt[:], func=AF.Sin, scale=0.125), ["lidt"], ["tA"])
    S.act(lambda e: e.activation(out=tB[:], in_=lidt[:], func=AF.Sin, scale=-0.125, bias=halfpi[:, 0:1]), ["lidt", "halfpi"], ["tB"])
    for _ in range(3):
        S.dve(lambda e: e.tensor_tensor(out=tC[:], in0=tA[:], in1=tB[:], op=ALU.mult), ["tA", "tB", "tC"], ["tC"])
        S.dve(lambda e: e.tensor_tensor(out=tD[:], in0=tA[:], in1=tA[:], op=ALU.mult), ["tA", "tD"], ["tD"])
        S.dve(lambda e: e.tensor_scalar(out=tA[:], in0=tC[:], scalar1=2.0, scalar2=None, op0=ALU.mult), ["tC", "tA"], ["tA"])
        S.dve(lambda e: e.tensor_scalar(out=tB[:], in0=tD[:], scalar1=-2.0, scalar2=1.0, op0=ALU.mult, op1=ALU.add), ["tD", "tB"], ["tB"])
    S.act(lambda e: e.activation(out=tC[:], in_=lrdt[:], func=AF.Exp), ["lrdt", "tC"], ["tC"])
    S.dve(lambda e: e.tensor_tensor(out=Er[:, 1, :], in0=tB[:], in1=tC[:], op=ALU.mult), ["tB", "tC"], ["E"])
    S.dve(lambda e: e.tensor_tensor(out=Ei[:, 1, :], in0=tA[:], in1=tC[:], op=ALU.mult), ["tA", "tC"], ["E"])
    for k in range(2, 9):
        S.dve(lambda e, k=k: e.tensor_tensor(out=tA[:], in0=Er[:, k - 1, :], in1=Er[:, 1, :], op=ALU.mult), ["E", "tA"], ["tA"])
        S.dve(lambda e, k=k: e.tensor_tensor(out=tB[:], in0=Ei[:, k - 1, :], in1=Ei[:, 1, :], op=ALU.mult), ["E", "tB"], ["tB"])
        S.dve(lambda e, k=k: e.tensor_tensor(out=tC[:], in0=Er[:, k - 1, :], in1=Ei[:, 1, :], op=ALU.mult), ["E", "tC"], ["tC"])
        S.dve(lambda e, k=k: e.tensor_tensor(out=tD[:], in0=Ei[:, k - 1, :], in1=Er[:, 1, :], op=ALU.mult), ["E", "tD"], ["tD"])
        S.dve(lambda e, k=k: e.tensor_tensor(out=Er[:, k, :], in0=tA[:], in1=tB[:], op=ALU.subtract), ["tA", "tB", "E"], ["E"])
        S.dve(lambda e, k=k: e.tensor_tensor(out=Ei[:, k, :], in0=tC[:], in1=tD[:], op=ALU.add), ["tC", "tD", "E"], ["E"])
    for k in range(1, 9):
        S.act(lambda e, k=k: e.activation(out=tC[:], in_=lrdt[:], func=AF.Exp, scale=-2.0 * k), ["lrdt", "tC"], ["tC"])
        S.dve(lambda e, k=k: e.tensor_tensor(out=Emr[:, k, :], in0=Er[:, k, :], in1=tC[:], op=ALU.mult), ["E", "tC"], ["E"])
        S.dve(lambda e, k=k: e.scalar_tensor_tensor(out=Emi[:, k, :], in0=Ei[:, k, :], scalar=-1.0, in1=tC[:], op0=ALU.mult, op1=ALU.mult), ["E", "tC"], ["E"])
    S.dve(lambda e: e.tensor_tensor(out=tA[:], in0=lam_r[:], in1=lam_r[:], op=ALU.mult), ["lam_r", "tA"], ["tA"])
    S.dve(lambda e: e.tensor_tensor(out=tB[:], in0=lam_i[:], in1=lam_i[:], op=ALU.mult), ["lam_i", "tB"], ["tB"])
    S.dve(lambda e: e.tensor_tensor(out=tA[:], in0=tA[:], in1=tB[:], op=ALU.add), ["tA", "tB"], ["tA"])
    S.dve(lambda e: e.reciprocal(out=tA[:], in_=tA[:]), ["tA"], ["tA"])
    S.dve(lambda e: e.tensor_scalar(out=tB[:], in0=Er[:, 1, :], scalar1=-1.0, scalar2=None, op0=ALU.add), ["E", "tB"], ["tB"])
    S.dve(lambda e: e.tensor_tensor(out=tC[:], in0=tB[:], in1=lam_r[:], op=ALU.mult), ["tB", "lam_r", "tC"], ["tC"])
    S.dve(lambda e: e.tensor_tensor(out=tD[:], in0=Ei[:, 1, :], in1=lam_i[:], op=ALU.mult), ["E", "lam_i", "tD"], ["tD"])
    S.dve(lambda e: e.tensor_tensor(out=tC[:], in0=tC[:], in1=tD[:], op=ALU.add), ["tC", "tD"], ["tC"])
    S.dve(lambda e: e.tensor_tensor(out=zr[:], in0=tC[:], in1=tA[:], op=ALU.mult), ["tC", "tA"], ["zr"])
    S.dve(lambda e: e.tensor_tensor(out=tC[:], in0=Ei[:, 1, :], in1=lam_r[:], op=ALU.mult), ["E", "lam_r", "tC"], ["tC"])
    S.dve(lambda e: e.tensor_tensor(out=tD[:], in0=tB[:], in1=lam_i[:], op=ALU.mult), ["tB", "lam_i", "tD"], ["tD"])
    S.dve(lambda e: e.tensor_tensor(out=tC[:], in0=tC[:], in1=tD[:], op=ALU.subtract), ["tC", "tD"], ["tC"])
    S.dve(lambda e: e.tensor_tensor(out=zi[:], in0=tC[:], in1=tA[:], op=ALU.mult), ["tC", "tA"], ["zi"])

    import os
    KPREP = int(os.environ.get("KPREP", "9"))

    def bch(ap2, n=16):
        return ap2.unsqueeze(2).to_broadcast([128, ap2.shape[1], n])

    S.dve(lambda e: e.tensor_tensor(out=t16a[:], in0=Br[:], in1=bch(zr[:]), op=ALU.mult), ["Br", "zr", "t16a"], ["t16a"])
    S.dve(lambda e: e.tensor_tensor(out=t16b[:], in0=Bi[:], in1=bch(zi[:]), op=ALU.mult), ["Bi", "zi", "t16b"], ["t16b"])
    S.dve(lambda e: e.tensor_tensor(out=Bbr[:], in0=t16a[:], in1=t16b[:], op=ALU.subtract), ["t16a", "t16b"], ["Bbr"])
    S.dve(lambda e: e.tensor_tensor(out=t16a[:], in0=Bi[:], in1=bch(zr[:]), op=ALU.mult), ["Bi", "zr", "t16a"], ["t16a"])
    S.dve(lambda e: e.tensor_tensor(out=t16b[:], in0=Br[:], in1=bch(zi[:]), op=ALU.mult), ["Br", "zi", "t16b"], ["t16b"])
    S.dve(lambda e: e.tensor_tensor(out=Bbi[:], in0=t16a[:], in1=t16b[:], op=ALU.add), ["t16a", "t16b"], ["Bbi"])

    WinReT = sb("WinReT", [128, NPAIR, 128], BF16)
    WinImT = sb("WinImT", [128, NPAIR, 128], BF16)
    Toep = sb("Toep", [128, 2 * NPAIR, 128], BF16)
    WoRe = sb("WoRe", [128, 2, NPAIR, 128], BF16)
    WoIm = sb("WoIm", [128, 2, NPAIR, 128], BF16)
    S.pool(lambda e: e.memset(WoRe[:], 0.0), [], ["WoRe"])
    S.pool(lambda e: e.memset(WoIm[:], 0.0), [], ["WoIm"])
    NPC = 2
    XRe = sb("XRe", [128, NPC, 8, 16]); XIm = sb("XIm", [128, NPC, 8, 16])
    WiRe = sb("WiRe", [128, NPC, 8, 16]); WiIm = sb("WiIm", [128, NPC, 8, 16])
    WoReF = sb("WoReF", [128, NPC, 8, 16]); WoImF = sb("WoImF", [128, NPC, 8, 16])
    XReB = sb("XReB", [128, NPC, 128], BF16); XImB = sb("XImB", [128, NPC, 128], BF16)
    pa = sb("pa", [128, NPC, 16]); pb_ = sb("pb_", [128, NPC, 16])

    def cmul_into(outr, outi, ar, ai, br, bi, tagr, tagi, negate_im=False):
        S.dve(lambda e: e.tensor_tensor(out=pa[:], in0=br, in1=bch(ar), op=ALU.mult), ["E", "Bbr", "Bbi", "Cr", "Ci", "pa"], ["pa"])
        S.dve(lambda e: e.tensor_tensor(out=pb_[:], in0=bi, in1=bch(ai), op=ALU.mult), ["E", "Bbr", "Bbi", "Cr", "Ci", "pb_"], ["pb_"])
        S.dve(lambda e: e.tensor_tensor(out=outr, in0=pa[:], in1=pb_[:], op=ALU.subtract), ["pa", "pb_"], [tagr])
        S.dve(lambda e: e.tensor_tensor(out=pa[:], in0=bi, in1=bch(ar), op=ALU.mult), ["E", "Bbr", "Bbi", "Cr", "Ci", "pa"], ["pa"])
        S.dve(lambda e: e.tensor_tensor(out=pb_[:], in0=br, in1=bch(ai), op=ALU.mult), ["E", "Bbr", "Bbi", "Cr", "Ci", "pb_"], ["pb_"])
        if negate_im:
            S.dve(lambda e: e.scalar_tensor_tensor(out=outi, in0=pa[:], scalar=-1.0, in1=pb_[:], op0=ALU.mult, op1=ALU.subtract), ["pa", "pb_"], [tagi])
        else:
            S.dve(lambda e: e.tensor_tensor(out=outi, in0=pa[:], in1=pb_[:], op=ALU.add), ["pa", "pb_"], [tagi])

    for q in range(NPAIR // NPC):
        psl = slice(q * NPC, (q + 1) * NPC)
        for s in range(8):
            cmul_into(XRe[:, :, s, :], XIm[:, :, s, :], Emr[:, s + 1, psl], Emi[:, s + 1, psl], Bbr[:, psl, :], Bbi[:, psl, :], "XRe", "XIm")
            cmul_into(WiRe[:, :, s, :], WiIm[:, :, s, :], Er[:, 7 - s, psl], Ei[:, 7 - s, psl], Bbr[:, psl, :], Bbi[:, psl, :], "WiRe", "WiIm")
            cmul_into(WoReF[:, :, s, :], WoImF[:, :, s, :], Er[:, s + 1, psl], Ei[:, s + 1, psl], Cr[:, psl, :], Ci[:, psl, :], "WoReF", "WoImF", negate_im=True)
        for g2 in range(2):
            rs = slice(64 * g2, 64 * g2 + 64)
            S.act(lambda e, psl=psl, rs=rs, g2=g2: e.copy(out=WoRe[rs, g2, psl, :], in_=WoReF[rs].rearrange("p a s h -> p a (s h)")), ["WoReF", "WoRe"], ["WoRe"])
            S.act(lambda e, psl=psl, rs=rs, g2=g2: e.copy(out=WoIm[rs, g2, psl, :], in_=WoImF[rs].rearrange("p a s h -> p a (s h)")), ["WoImF", "WoIm"], ["WoIm"])
        S.act(lambda e: e.copy(out=XReB[:], in_=XRe[:].rearrange("p a s h -> p a (s h)")), ["XRe"], ["XReB"])
        S.act(lambda e: e.copy(out=XImB[:], in_=XIm[:].rearrange("p a s h -> p a (s h)")), ["XIm"], ["XImB"])
        if KPREP <= 3:
            continue
        for a in range(NPC):
            pair = q * NPC + a
            bk = pbank[a % 2]; bkn0 = 'pb%d' % (a % 2)
            S.pe(lambda e, a=a, bk=bk: e.transpose(out=bk[:, 0:128], in_=WiRe[:, a].rearrange("p s h -> p (s h)"), identity=ident[:]), ["WiRe", "ident"], ["pb%d" % (a % 2)])
            S.pe(lambda e, a=a, bk=bk: e.transpose(out=bk[:, 128:256], in_=WiIm[:, a].rearrange("p s h -> p (s h)"), identity=ident[:]), ["WiIm", "ident"], ["pb%d" % (a % 2)])
            for g2 in range(2 if KPREP >= 5 else 0):
                rs = slice(64 * g2, 64 * g2 + 64)
                cs = slice(128 * g2, 128 * g2 + 128)
                bk = pbank[2 + a % 2]
                S.pe(lambda e, a=a, bk=bk, rs=rs, cs=cs, q=q, g2=g2: e.matmul(bk[:, cs], lhsT=XReB[:, a, :], rhs=WoRe[:, g2, q * NPC + a, :], start=True, stop=False),
                     ["XReB", "WoRe"], ["pb%d" % (2 + a % 2)])
                S.pe(lambda e, a=a, bk=bk, rs=rs, cs=cs, q=q, g2=g2: e.matmul(bk[:, cs], lhsT=XImB[:, a, :], rhs=WoIm[:, g2, q * NPC + a, :], start=False, stop=True),
                     ["XImB", "WoIm"], ["pb%d" % (2 + a % 2)])
            S.act(lambda e, pair=pair, bk=pbank[a % 2]: e.copy(out=WinReT[:, pair, :], in_=bk[:, 0:128]), ["pb%d" % (a % 2)], ["WinReT"])
            S.act(lambda e, pair=pair, bk=pbank[a % 2]: e.copy(out=WinImT[:, pair, :], in_=bk[:, 128:256]), ["pb%d" % (a % 2)], ["WinImT"])
            for g2 in range(2 if KPREP >= 5 else 0):
                S.dve(lambda e, pair=pair, g2=g2, bk2=pbank[2 + a % 2]: e.tensor_tensor(out=Toep[:, 2 * pair + g2, :], in0=bk2[:, 128 * g2:128 + 128 * g2], in1=tmask[:], op=ALU.mult),
                      ["pb%d" % (2 + a % 2), "tmask"], ["Toep"])

    M1 = sb("M1", [128, 2, NPAIR]); M2n = sb("M2n", [128, NPAIR]); M2p = sb("M2p", [128, NPAIR])
    S.dve(lambda e: e.tensor_copy(out=M1[:, 0, :], in_=Er[:, 8, :]), ["E"], ["M1"])
    S.dve(lambda e: e.tensor_copy(out=M1[:, 1, :], in_=Er[:, 8, :]), ["E"], ["M1"])
    S.dve(lambda e: e.tensor_copy(out=M2p[:], in_=Ei[:, 8, :]), ["E"], ["M2"])
    S.dve(lambda e: e.tensor_scalar(out=M2n[:], in0=Ei[:, 8, :], scalar1=-1.0, scalar2=None, op0=ALU.mult), ["E"], ["M2"])

    gmix0 = sb("gmix0", [128, D]); dg = sb("dg", [128, D])
    S.dma(gmix0[:], gmix_d[0], writes=["gmix0"])
    S.dma(dg[:], dskip_d, writes=["dg"])
    S.dve(lambda e: e.tensor_tensor(out=dg[:], in0=dg[:], in1=gmix0[:], op=ALU.mult), ["dg", "gmix0"], ["dg"])

    S.dve(lambda e: e.memset(x_t[:, 0, 0:1], 0.0), ["Br", "Bi", "Cr", "Ci", "Bbr", "Bbi", "t16a", "t16b"], ["x_t"])
    ss = sb("ss", [128, 8]); rinv = sb("rinv", [128, 8])
    ub = sb("ub", [128, 64, 8, 16], BF16)
    UT = sb("UT", [128, 64, 128], BF16)
    SH = sb("SH", [128, 2, NPAIR, 128])
    Hpb = sb("Hpb", [128, 2, NPAIR, 128], BF16)
    carry = sb("carry", [128, 2, NPAIR])
    T1 = {"dve": sb("T1d", [128, 2, NPAIR // 2]), "pool": sb("T1p", [128, 2, NPAIR // 2])}
    T2 = {"dve": sb("T2d", [128, 2, NPAIR // 2]), "pool": sb("T2p", [128, 2, NPAIR // 2])}
    ytmps = [sb("ytmp0", [128, 8, 16]), sb("ytmp1", [128, 8, 16])]
    junk = UT[:].rearrange("p g c -> p (g c)")[:, 0:D]
    Gb = ub[:].rearrange("p g t h -> p (g t h)").rearrange("p (t d) -> p t d", t=8)

    def s5_macro(src_ap, nvalid, mt_tag, seqs, g_out_ap, final_out):
        P = nvalid
        S.dma(x_t[0:P], src_ap.rearrange("(c t) d -> c t d", t=8), writes=["x_t"])
        for t in range(8):
            S.act(lambda e, t=t: e.activation(out=junk[0:P], in_=x_t[0:P, t, :], func=AF.Square, accum_out=ss[0:P, t:t + 1]), ["x_t"], ["UT", "ss"])
        S.dve(lambda e: e.tensor_scalar(out=rinv[0:P], in0=ss[0:P], scalar1=1.0 / D, scalar2=EPS, op0=ALU.mult, op1=ALU.add), ["ss"], ["rinv"])
        S.act(lambda e: e.activation(out=rinv[0:P], in_=rinv[0:P], func=AF.Sqrt), ["rinv"], ["rinv"])
        S.dve(lambda e: e.reciprocal(out=rinv[0:P], in_=rinv[0:P]), ["rinv"], ["rinv"])
        for t in range(8):
            S.dve(lambda e, t=t: e.scalar_tensor_tensor(out=ub[0:P, :, t, :], in0=x_t[0:P, t, :].rearrange("p (g h) -> p g h", h=16), scalar=rinv[0:P, t:t + 1],
                                                        in1=gmix0[0:P].rearrange("p (g h) -> p g h", h=16), op0=ALU.mult, op1=ALU.mult),
                  ["x_t", "rinv", "gmix0", "ub"], ["ub"])
        for g8 in range(8):
            bk = pbank[g8 % 2]; bkn = "pb%d" % (g8 % 2)
            bkb = bk[:].bitcast(BF16)
            for j in range(8):
                g = g8 * 8 + j
                S.pe(lambda e, g=g, j=j, bkb=bkb: e.transpose(out=bkb[:, j * 128:j * 128 + P], in_=ub[0:P, g].rearrange("p t h -> p (t h)"), identity=identb[0:P, 0:P]),
                     ["ub", "identb"], [bkn])
            S.act(lambda e, g8=g8, bkb=bkb: e.copy(out=UT[:, g8 * 8:(g8 + 1) * 8, 0:P], in_=bkb[:, 0:1024].rearrange("p (j c) -> p j c", j=8)[:, :, 0:P]), [bkn], ["UT"])
        for q in range(8):
            bre = pbank[4 + 2 * (q % 2)]; bim = pbank[5 + 2 * (q % 2)]
            nre = "pb%d" % (4 + 2 * (q % 2)); nim = "pb%d" % (5 + 2 * (q % 2))
            for a in range(4):
                pair = q * 4 + a
                for g2 in range(2):
                    g = 2 * pair + g2
                    S.pe(lambda e, a=a, g2=g2, g=g, pair=pair, bre=bre: e.matmul(bre[64 * g2:64 * g2 + 64, a * 128:a * 128 + P], lhsT=WinReT[:, pair, 64 * g2:64 * g2 + 64], rhs=UT[:, g, 0:P], start=True, stop=True),
                         ["WinReT", "UT"], [nre])
                    S.pe(lambda e, a=a, g2=g2, g=g, pair=pair, bim=bim: e.matmul(bim[64 * g2:64 * g2 + 64, a * 128:a * 128 + P], lhsT=WinImT[:, pair, 64 * g2:64 * g2 + 64], rhs=UT[:, g, 0:P], start=True, stop=True),
                         ["WinImT", "UT"], [nim])
            S.act(lambda e, q=q, bre=bre: e.copy(out=SH[:, 0, q * 4:q * 4 + 4, 0:P], in_=bre[:].rearrange("p (a c) -> p a c", a=4)[:, :, 0:P]), [nre], ["SHr"])
            S.dve(lambda e, q=q, bim=bim: e.tensor_copy(out=SH[:, 1, q * 4:q * 4 + 4, 0:P], in_=bim[:].rearrange("p (a c) -> p a c", a=4)[:, :, 0:P]), [nim], ["SHi"])
        for (c0, c1, init) in seqs:
            for ei, en in enumerate(("dve", "pool")):
                hp = slice(ei * 16, ei * 16 + 16)
                t1 = T1[en]; t2 = T2[en]
                if init[0] == "h0":
                    S.dma(carry[:, 0, hp], h0r[:, init[1], hp], writes=["carry" + en], q="sp")
                    S.dma(carry[:, 1, hp], h0i[:, init[1], hp], writes=["carry" + en], q="sp")
                elif init[0] == "zero":
                    S.on(en, lambda e, hp=hp: e.memset(carry[:, :, hp], 0.0), [], ["carry" + en])
                for c in range(c0, c1):
                    if c == c0:
                        prev = carry[:, :, hp]; prev_r = carry[:, 0, hp]; prev_i = carry[:, 1, hp]
                    else:
                        prev = SH[:, :, hp, c - 1]; prev_r = SH[:, 0, hp, c - 1]; prev_i = SH[:, 1, hp, c - 1]
                    rd = ["SH" + en, "carry" + en, "M1", "M2", "SHr", "SHi"]
                    if c == c0:
                        S.on(en, lambda e, prev=prev, hp=hp, c=c: e.tensor_copy(out=Hpb[:, :, hp, c], in_=prev), rd, ["Hpb" + en])
                    S.on(en, lambda e, prev=prev, hp=hp, t1=t1: e.tensor_tensor(out=t1[:], in0=prev, in1=M1[:, :, hp], op=ALU.mult), rd + ["t1" + en], ["t1" + en])
                    S.on(en, lambda e, prev_i=prev_i, hp=hp, t2=t2: e.tensor_tensor(out=t2[:, 0, :], in0=prev_i, in1=M2n[:, hp], op=ALU.mult), rd + ["t2" + en], ["t2" + en])
                    S.on(en, lambda e, prev_r=prev_r, hp=hp, t2=t2: e.tensor_tensor(out=t2[:, 1, :], in0=prev_r, in1=M2p[:, hp], op=ALU.mult), rd + ["t2" + en], ["t2" + en])
                    S.on(en, lambda e, t1=t1, t2=t2: e.tensor_tensor(out=t1[:], in0=t1[:], in1=t2[:], op=ALU.add), ["t1" + en, "t2" + en], ["t1" + en])
                    S.on(en, lambda e, t1=t1, hp=hp, c=c: e.tensor_tensor(out=SH[:, :, hp, c], in0=SH[:, :, hp, c], in1=t1[:], op=ALU.add), ["t1" + en, "SHr", "SHi", "SH" + en], ["SH" + en])
                for ri in range(2):
                    S.on(en, lambda e, hp=hp, c0=c0, c1=c1, ri=ri: e.tensor_copy(out=Hpb[:, ri, hp, c0 + 1:c1], in_=SH[:, ri, hp, c0:c1 - 1]), ["SH" + en, "SHr", "SHi"], ["Hpb" + en])
                S.on(en, lambda e, hp=hp, c1=c1: e.tensor_copy(out=carry[:, :, hp], in_=SH[:, :, hp, c1 - 1]), ["SH" + en, "SHr", "SHi"], ["carry" + en])
        for (cl, ore, oim) in final_out:
            S.dma(ore, SH[:, 0, :, cl], reads=["SHdve", "SHpool", "SHr", "SHi"], allow_slow_non_contiguous=True)
            S.dma(oim, SH[:, 1, :, cl], reads=["SHdve", "SHpool", "SHr", "SHi"], allow_slow_non_contiguous=True)
        for g in range(64):
            pair, g2 = g // 2, g % 2
            bk = pbank[2 + g % 2]; bkn = "pb%d" % (2 + g % 2)
            rs = slice(64 * g2, 64 * g2 + 64)
            S.pe(lambda e, g=g, bk=bk: e.matmul(bk[0:P, 0:128], lhsT=UT[:, g, 0:P], rhs=Toep[:, g, :], start=True, stop=False), ["UT", "Toep"], [bkn])
            S.pe(lambda e, pair=pair, rs=rs, bk=bk, g2=g2: e.matmul(bk[0:P, 0:128], lhsT=Hpb[:, 0, pair, 0:P], rhs=WoRe[:, g2, pair, :], start=False, stop=False), ["Hpbdve", "Hpbpool", "WoRe"], [bkn])
            S.pe(lambda e, pair=pair, rs=rs, bk=bk, g2=g2: e.matmul(bk[0:P, 0:128], lhsT=Hpb[:, 1, pair, 0:P], rhs=WoIm[:, g2, pair, :], start=False, stop=True), ["Hpbdve", "Hpbpool", "WoIm"], [bkn])
            dsl = slice(16 * g, 16 * g + 16)
            ytmp = ytmps[g % 2]; yn = "ytmp%d" % (g % 2)
            S.dve(lambda e, dsl=dsl, ytmp=ytmp: e.tensor_tensor(out=ytmp[0:P], in0=x_t[0:P, :, dsl], in1=rinv[0:P].unsqueeze(2).to_broadcast([P, 8, 16]), op=ALU.mult), ["x_t", "rinv"], [yn])
            S.dve(lambda e, dsl=dsl, ytmp=ytmp: e.tensor_tensor(out=ytmp[0:P], in0=ytmp[0:P], in1=dg[0:P, dsl].unsqueeze(1).to_broadcast([P, 8, 16]), op=ALU.mult), [yn, "dg"], [yn])
            S.dve(lambda e, bk=bk, ytmp=ytmp: e.tensor_tensor(out=ytmp[0:P], in0=ytmp[0:P], in1=bk[0:P, 0:128].rearrange("p (t h) -> p t h", t=8), op=ALU.add), [yn, bkn], [yn])
            S.act(lambda e, dsl=dsl, ytmp=ytmp: e.activation(out=Gb[0:P, :, dsl], in_=ytmp[0:P], func=AF.Gelu_apprx_tanh), [yn], ["ub"])
        S.dma(g_out_ap.rearrange("(c t) d -> c t d", t=8), Gb[0:P], reads=["ub"])

    import os
    KST = int(os.environ.get("KSTAGE", "9"))
    for mt in range(8 if KST >= 3 else (1 if KST == 2 else 0)):
        seqs = [(0, 128, ("zero",) if mt == 0 else ("carry",))]
        fin = [(127, o_ssm_p[0], o_ssm_p[1])] if mt == 7 or KST == 2 else []
        s5_macro(xp[mt * 1024:(mt + 1) * 1024, :], 128, mt, seqs, gact[mt * 1024:(mt + 1) * 1024, :], fin)
    if KST >= 1:
      s5_macro(xs, 16, 8, [(0, 8, ("h0", 0)), (8, 16, ("h0", 1))], gact[2 * HALF:2 * HALF + 128, :],
             [(7, o_ssm_s[0, 0], o_ssm_s[0, 1]), (15, o_ssm_s[1, 0], o_ssm_s[1, 1])])


    x1s = dscr("x1s", [NT, D]); x2s = dscr("x2s", [NT, D]); x3s = dscr("x3s", [NT, D])

    def xsrc(t0, n):
        return xp[t0:t0 + n, :] if t0 < 2 * HALF else xs[t0 - 2 * HALF:t0 - 2 * HALF + n, :]

    def load_weight(dst3, src2, nchunks, P, ncol, tag):
        stg = [sb(tag + "_s0", [128, ncol]), sb(tag + "_s1", [128, ncol])]
        engs = ["act", "pool", "dve"]
        for c in range(nchunks):
            b = stg[c % 2]; bn = tag + "_s%d" % (c % 2)
            S.dma(b[0:P], src2[c * P:(c + 1) * P, :], writes=[bn])
            en = engs[c % 3]
            if en == "act":
                S.act(lambda e, b=b, c=c: e.copy(out=dst3[0:P, c, :], in_=b[0:P]), [bn], [tag + str(c)])
            else:
                S.on(en, lambda e, b=b, c=c: e.tensor_copy(out=dst3[0:P, c, :], in_=b[0:P]), [bn], [tag + str(c)])


    def rms_to_bf16(x_ap, P, gain, out_ap, xname, oname, col):
        S.act(lambda e: e.activation(out=out_ap, in_=x_ap, func=AF.Square, accum_out=ssn[0:P, col:col + 1]), [xname], [oname, "ssn%d" % col])
        S.dve(lambda e: e.tensor_scalar(out=rvn[0:P, col:col + 1], in0=ssn[0:P, col:col + 1], scalar1=1.0 / D, scalar2=EPS, op0=ALU.mult, op1=ALU.add), ["ssn%d" % col], ["rvn%d" % col])
        S.act(lambda e: e.activation(out=rvn[0:P, col:col + 1], in_=rvn[0:P, col:col + 1], func=AF.Sqrt), ["rvn%d" % col], ["rvn%d" % col])
        S.dve(lambda e: e.reciprocal(out=rvn[0:P, col:col + 1], in_=rvn[0:P, col:col + 1]), ["rvn%d" % col], ["rvn%d" % col])
        S.dve(lambda e: e.scalar_tensor_tensor(out=out_ap, in0=x_ap, scalar=rvn[0:P, col:col + 1], in1=gain[0:P], op0=ALU.mult, op1=ALU.mult), [xname, "rvn%d" % col, oname], [oname])

    tcount = [0]

    def transpose8(src_ap, P, dstT_ap, sname, dname):
        bi = tcount[0] % 2
        tcount[0] += 1
        bkb = pbank[bi][:].bitcast(BF16)
        bn = "pb%d" % bi
        for dc in range(8):
            S.pe(lambda e, dc=dc: e.transpose(out=bkb[:, dc * 128:dc * 128 + P], in_=src_ap[:, dc * 128:(dc + 1) * 128], identity=identb[0:P, 0:P]), [sname], [bn])
        S.act(lambda e: e.copy(out=dstT_ap, in_=bkb[:, 0:1024].rearrange("p (j c) -> p j c", j=8)[:, :, 0:P]), [bn], [dname])

    S.barrier(); aoff[0] = mP2
    wglu_d = din("w_glu", [D, 2 * D])
    Wglu = sb("Wglu", [128, 8, 2 * D], BF16)
    m1 = aoff[0]
    load_weight(Wglu, wglu_d, 8, 128, 2 * D, "wg")
    S.barrier(); aoff[0] = m1
    Gt = [sb("Gt%d" % i, [128, D], BF16) for i in range(2)]
    Xt = [sb("Xt%d" % i, [128, D]) for i in range(2)]
    GTt = [sb("GTt%d" % i, [128, 8, 128], BF16) for i in range(2)]
    sgb = [sb("sgb%d" % i, [128, 512]) for i in range(2)]
    mmb = [sb("mmb%d" % i, [128, 512]) for i in range(2)]
    tilesB = [(t, 128) for t in range(0, NT, 128)]
    if KST < 4:
        tilesB = []

    def loadB(i):
        t0, P = tilesB[i]; b = i % 2
        S.dma(Gt[b][0:P], gact[t0:t0 + P, :], reads=["gscr"], writes=["Gt%d" % b])
        S.dma(Xt[b][0:P], xsrc(t0, P), writes=["Xt%d" % b])

    if tilesB:
        loadB(0)
    for i, (t0, P) in enumerate(tilesB):
        b = i % 2
        if i + 1 < len(tilesB):
            loadB(i + 1)
        transpose8(Gt[b][0:P], P, GTt[b][:, :, 0:P], "Gt%d" % b, "GTt%d" % b)
        for nh in range(2):
            j = 2 * i + nh
            vb = pbank[2 + 2 * (j % 3)]; gb = pbank[3 + 2 * (j % 3)]
            vn = "pb%d" % (2 + 2 * (j % 3)); gn = "pb%d" % (3 + 2 * (j % 3))
            for dc in range(8):
                S.pe(lambda e, dc=dc, vb=vb, nh=nh, b=b: e.matmul(vb[0:P, :], lhsT=GTt[b][:, dc, 0:P], rhs=Wglu[:, dc, nh * 512:(nh + 1) * 512], start=(dc == 0), stop=(dc == 7)), ["GTt%d" % b], [vn])
            for dc in range(8):
                S.pe(lambda e, dc=dc, gb=gb, nh=nh, b=b: e.matmul(gb[0:P, :], lhsT=GTt[b][:, dc, 0:P], rhs=Wglu[:, dc, D + nh * 512:D + (nh + 1) * 512], start=(dc == 0), stop=(dc == 7)), ["GTt%d" % b], [gn])
            sg = sgb[j % 2]; mm = mmb[j % 2]
            S.act(lambda e, sg=sg, gb=gb: e.activation(out=sg[0:P], in_=gb[0:P, :], func=AF.Sigmoid), [gn], ["sg%d" % (j % 2)])
            S.dve(lambda e, sg=sg, mm=mm, vb=vb: e.tensor_tensor(out=mm[0:P], in0=vb[0:P, :], in1=sg[0:P], op=ALU.mult), [vn, "sg%d" % (j % 2)], ["mm%d" % (j % 2)])
            S.pool(lambda e, mm=mm, b=b, nh=nh: e.tensor_tensor(out=Xt[b][0:P, nh * 512:(nh + 1) * 512], in0=Xt[b][0:P, nh * 512:(nh + 1) * 512], in1=mm[0:P], op=ALU.add), ["mm%d" % (j % 2), "Xt%d" % b], ["Xt%d" % b])
        S.dma(x1s[t0:t0 + P, :], Xt[b][0:P], reads=["Xt%d" % b], writes=["x1s"])

    wup_d = din("w_up", [2, D, DFF]); wgt_d = din("w_gate", [2, D, DFF]); wdn_d = din("w_down", [2, DFF, D])
    cw_d = din("conv_w", [2, 3, DFF]); cb_d = din("conv_b", [2, DFF]); cvst_d = din("cvst", [2, 2, 2, DFF])
    o_cvp = dout("o_cvp", [2, 2, DFF]); o_cvs = dout("o_cvs", [2, 2, 2, DFF])
    o_yp = dout("o_yp", [HALF, D]); o_ys = dout("o_ys", [128, D])
    NFC = DFF // 128

    def ffn_phase(layer, xin, xout, groups, gain_d, final):
        S.barrier(); aoff[0] = mP2
        Wup = sb("Wup", [128, 8, DFF], BF16); Wgt = sb("Wgt", [128, 8, DFF], BF16); Wdn = sb("Wdn", [128, NFC, D], BF16)
        cw = sb("cw", [128, NFC, 3]); cbb = sb("cbb", [128, NFC]); hist = sb("hist", [128, NFC, 2])
        gain = sb("gain", [128, D]); gfin = sb("gfinal", [128, D])
        m1 = aoff[0]
        load_weight(Wup, wup_d[layer], 8, 128, DFF, "wu")
        aoff[0] = m1
        S.barrier()
        load_weight(Wgt, wgt_d[layer], 8, 128, DFF, "wt")
        aoff[0] = m1
        S.barrier()
        load_weight(Wdn, wdn_d[layer], NFC, 128, D, "wd")
        for j in range(3):
            S.dma(cw[:, :, j], cw_d[layer, j].rearrange("(fc f) -> f fc", f=128), writes=["cw"], allow_slow_non_contiguous=True)
        S.dma(cbb[:], cb_d[layer].rearrange("(fc f) -> f fc", f=128), writes=["cbb"], allow_slow_non_contiguous=True)
        S.dma(gain[:], gain_d, writes=["gain"])
        S.dma(gfin[:], gfin_d, writes=["gfin"])
        S.barrier(); aoff[0] = m1
        Xg = [sb("Xg%d" % i, [128, 2, D]) for i in range(2)]
        hsb = sb("hsb", [128, D], BF16)
        hsT = [sb("hsT%d" % i, [128, 8, 256], BF16) for i in range(2)]
        hT = [sb("hT%d" % i, [128, NFC, 256], BF16) for i in range(2)]
        abuf = [sb("abuf%d" % i, [128, 258]) for i in range(3)]
        cbuf = [sb("cbuf%d" % i, [128, 256]) for i in range(3)]
        ybuf = sb("ybuf", [128, D])

        def tiles_of(g):
            n = g["n"]
            return [(0, 128), (1, 128)] if n == 256 else [(0, n)]

        def load(gi):
            g = groups[gi]; b = gi % 2
            if g["n"] == 256:
                S.dma(Xg[b][:], xin[g["t0"]:g["t0"] + 256, :].rearrange("(i p) d -> p i d", i=2), writes=["Xg%d" % b])
            else:
                S.dma(Xg[b][0:g["n"], 0, :], xin[g["t0"]:g["t0"] + g["n"], :], writes=["Xg%d" % b])

        def upgate(gi):
            g = groups[gi]; b = gi % 2; N = g["n"]
            if g["hist"] == "zero":
                S.pool(lambda e: e.memset(hist[:], 0.0), [], ["hist"])
            elif g["hist"] != "carry":
                for j in range(2):
                    S.dma(hist[:, :, j], cvst_d[layer, g["hist"], j].rearrange("(fc f) -> f fc", f=128), writes=["hist"], allow_slow_non_contiguous=True)
            for (ti, P) in tiles_of(g):
                rms_to_bf16(Xg[b][0:P, ti, :], P, gain, hsb[0:P], "Xg%d" % b, "hsb", ti)
                transpose8(hsb[0:P], P, hsT[b][:, :, ti * 128:ti * 128 + P], "hsb", "hsT%d" % b)
            for fc in range(NFC):
                ug = pbank[2 + fc % 3]; un = "pb%d" % (2 + fc % 3)
                for kc in range(8):
                    S.pe(lambda e, kc=kc, fc=fc, ug=ug: e.matmul(ug[:, 0:N], lhsT=Wup[:, kc, fc * 128:(fc + 1) * 128], rhs=hsT[b][:, kc, 0:N], start=(kc == 0), stop=(kc == 7)), ["hsT%d" % b], [un])
                for kc in range(8):
                    S.pe(lambda e, kc=kc, fc=fc, ug=ug: e.matmul(ug[:, 256:256 + N], lhsT=Wgt[:, kc, fc * 128:(fc + 1) * 128], rhs=hsT[b][:, kc, 0:N], start=(kc == 0), stop=(kc == 7)), ["hsT%d" % b], [un])
                ac = abuf[fc % 3]; an = "ab%d" % (fc % 3); ct = cbuf[fc % 3]; cn = "cb%d" % (fc % 3)
                S.pool(lambda e, ac=ac, fc=fc: e.tensor_copy(out=ac[:, 0:2], in_=hist[:, fc, :]), ["hist", an], [an])
                S.act(lambda e, ac=ac, ug=ug: e.copy(out=ac[:, 2:2 + N], in_=ug[:, 0:N]), [un, an], [an])
                S.pool(lambda e, ct=ct, ac=ac, fc=fc: e.tensor_scalar(out=ct[:, 0:N], in0=ac[:, 2:2 + N], scalar1=cw[:, fc, 2:3], scalar2=cbb[:, fc:fc + 1], op0=ALU.mult, op1=ALU.add), [an, "cw", "cbb", cn], [cn])
                S.dve(lambda e, ct=ct, ac=ac, fc=fc: e.scalar_tensor_tensor(out=ct[:, 0:N], in0=ac[:, 1:1 + N], scalar=cw[:, fc, 1:2], in1=ct[:, 0:N], op0=ALU.mult, op1=ALU.add), [an, cn, "cw"], [cn])
                S.dve(lambda e, ct=ct, ac=ac, fc=fc: e.scalar_tensor_tensor(out=ct[:, 0:N], in0=ac[:, 0:N], scalar=cw[:, fc, 0:1], in1=ct[:, 0:N], op0=ALU.mult, op1=ALU.add), [an, cn, "cw"], [cn])
                S.pool(lambda e, ac=ac, fc=fc: e.tensor_copy(out=hist[:, fc, :], in_=ac[:, N:N + 2]), [an, "hist"], ["hist"])
                S.act(lambda e, ct=ct: e.activation(out=ct[:, 0:N], in_=ct[:, 0:N], func=AF.Gelu_apprx_tanh), [cn], [cn])
                S.dve(lambda e, ct=ct, ug=ug, fc=fc: e.tensor_tensor(out=hT[b][:, fc, 0:N], in0=ct[:, 0:N], in1=ug[:, 256:256 + N], op=ALU.mult), [cn, un, "hT%d" % b], ["hT%d" % b])
            if g.get("conv_out") is not None:
                for j in range(2):
                    S.dma(g["conv_out"][j].rearrange("(fc f) -> f fc", f=128), hist[:, :, j], reads=["hist"], allow_slow_non_contiguous=True)

        dcount = [0]

        def down(gi):
            g = groups[gi]; b = gi % 2
            if not g["store"]:
                return
            for (ti, P) in tiles_of(g):
                for half in range(2):
                    k = dcount[0] % 3; dcount[0] += 1
                    ob = pbank[5 + k]; on_ = "pb%d" % (5 + k)
                    for fc in range(NFC):
                        S.pe(lambda e, fc=fc, ob=ob, ti=ti, half=half: e.matmul(ob[0:P, :], lhsT=hT[b][:, fc, ti * 128:ti * 128 + P], rhs=Wdn[:, fc, half * 512:(half + 1) * 512], start=(fc == 0), stop=(fc == NFC - 1)), ["hT%d" % b], [on_])
                    S.dve(lambda e, ob=ob, ti=ti, half=half: e.tensor_tensor(out=Xg[b][0:P, ti, half * 512:(half + 1) * 512], in0=Xg[b][0:P, ti, half * 512:(half + 1) * 512], in1=ob[0:P, :], op=ALU.add), [on_, "Xg%d" % b], ["Xg%d" % b])
                t0 = g["t0"] + ti * 128
                if final:
                    dst = g["yout"][ti * 128:ti * 128 + P, :]
                    S.act(lambda e, ti=ti: e.activation(out=ybuf[0:P], in_=Xg[b][0:P, ti, :], func=AF.Square, accum_out=ssn[0:P, 2:3]), ["Xg%d" % b, "ybuf"], ["ybuf", "ssn2"])
                    S.dve(lambda e: e.tensor_scalar(out=rvn[0:P, 2:3], in0=ssn[0:P, 2:3], scalar1=1.0 / D, scalar2=EPS, op0=ALU.mult, op1=ALU.add), ["ssn2"], ["rvn2"])
                    S.act(lambda e: e.activation(out=rvn[0:P, 2:3], in_=rvn[0:P, 2:3], func=AF.Sqrt), ["rvn2"], ["rvn2"])
                    S.dve(lambda e: e.reciprocal(out=rvn[0:P, 2:3], in_=rvn[0:P, 2:3]), ["rvn2"], ["rvn2"])
                    S.dve(lambda e, ti=ti: e.scalar_tensor_tensor(out=ybuf[0:P], in0=Xg[b][0:P, ti, :], scalar=rvn[0:P, 2:3], in1=gfin[0:P], op0=ALU.mult, op1=ALU.mult), ["Xg%d" % b, "rvn2", "ybuf"], ["ybuf"])
                    S.dma(dst, ybuf[0:P], reads=["ybuf"])
                else:
                    S.dma(xout[t0:t0 + P, :], Xg[b][0:P, ti, :], reads=["Xg%d" % b], writes=["xout"])

        if groups:
            load(0)
        for gi in range(len(groups)):
            upgate(gi)
            if gi > 0:
                down(gi - 1)
            if gi + 1 < len(groups):
                load(gi + 1)
        if groups:
            down(len(groups) - 1)

    groups0 = []
    for gi in range(2 * HALF // 256):
        groups0.append(dict(t0=gi * 256, n=256, hist="zero" if gi == 0 else "carry", store=True,
                            conv_out=o_cvp[0] if gi == 2 * HALF // 256 - 1 else None))
    for sq in range(2):
        groups0.append(dict(t0=2 * HALF + 64 * sq, n=64, hist=sq, store=True, conv_out=o_cvs[0, sq]))
    if KST >= 5:
        ffn_phase(0, x1s, x2s, groups0, gffn_d[0], False)


    wq_d = din("w_qkvf", [D, 3 * D + NH])
    kc_d = din("kcache", [2, HALF, D]); vc_d = din("vcache", [2, HALF, D]); lfc_d = din("lfcache", [2, HALF, NH])
    o_kp = dout("o_kp", [HALF, D]); o_vp = dout("o_vp", [HALF, D]); o_lfp = dout("o_lfp", [HALF, NH])
    o_ks = dout("o_ks", [128, D]); o_vs = dout("o_vs", [128, D]); o_lfs = dout("o_lfs", [128, NH])
    kts = dscr("kts", [NH, HD, NT], BF16); qts = dscr("qts", [NH, HD, NT], BF16); vsb = dscr("vsb", [NT, D], BF16)
    crow = dscr("crow", [NH, NT]); ccol = dscr("ccol", [NT, NH])
    kcT = dscr("kcT", [2, NH, HD, HALF], BF16); vcb = dscr("vcb", [2, HALF, D], BF16); ccc = dscr("ccc", [2, HALF, NH])

    def phase_d():
        S.barrier(); aoff[0] = mP2
        NQ = 3 * D + NH
        Wq = sb("Wq", [128, 8, NQ], BF16)
        gain = sb("gainq", [128, D])
        Lacc = sb("Lacc", [128, NH])
        m1 = aoff[0]
        load_weight(Wq, wq_d, 8, 128, NQ, "wq")
        S.dma(gain[:], gmix_d[1], writes=["gainq"])
        S.barrier(); aoff[0] = m1
        Xq = [sb("Xq%d" % i, [128, D]) for i in range(2)]
        hsq = sb("hsq", [128, D], BF16)
        hsTq = [sb("hsTq%d" % i, [128, 8, 128], BF16) for i in range(2)]
        Qb = [sb("Qb%d" % i, [128, D], BF16) for i in range(2)]
        Kf = [sb("Kf%d" % i, [128, D]) for i in range(2)]
        Kb = [sb("Kb%d" % i, [128, D], BF16) for i in range(2)]
        Vf = [sb("Vf%d" % i, [128, D]) for i in range(2)]
        Vb = [sb("Vb%d" % i, [128, D], BF16) for i in range(2)]
        QTt = [sb("QTt%d" % i, [128, 8, 128], BF16) for i in range(2)]
        KTt = [sb("KTt%d" % i, [128, 8, 128], BF16) for i in range(2)]
        zf = [sb("zf%d" % i, [128, NH]) for i in range(2)]
        lgf = [sb("lgf%d" % i, [128, NH]) for i in range(2)]
        r1 = sb("r1", [128, NH]); L3 = sb("L3", [128, 3, NH], BF16); La3 = sb("La3", [128, 3, NH], BF16); ra = sb("ra", [128, NH])
        csb = [sb("csb%d" % i, [128, NH]) for i in range(2)]
        cTs = [sb("cTs%d" % i, [NH, 128]) for i in range(2)]

        def split3(src, dst3, tmp, P, sname, dname, tname):
            S.dve(lambda e: e.tensor_copy(out=dst3[0:P, 0, :], in_=src[0:P]), [sname, dname], [dname])
            S.dve(lambda e: e.tensor_tensor(out=tmp[0:P], in0=src[0:P], in1=dst3[0:P, 0, :], op=ALU.subtract), [sname, dname, tname], [tname])
            S.dve(lambda e: e.tensor_copy(out=dst3[0:P, 1, :], in_=tmp[0:P]), [tname, dname], [dname])
            S.dve(lambda e: e.tensor_tensor(out=tmp[0:P], in0=tmp[0:P], in1=dst3[0:P, 1, :], op=ALU.subtract), [tname, dname], [tname])
            S.dve(lambda e: e.tensor_copy(out=dst3[0:P, 2, :], in_=tmp[0:P]), [tname, dname], [dname])

        def cumsum_path(lg, lgn, P, b, ccol_dst, crow_dst):
            split3(lg, L3, r1, P, lgn, "L3", "r1")
            split3(Lacc, La3, ra, 128, "Lacc", "La3", "ra")
            cb = pbank[3]
            for j in range(3):
                S.pe(lambda e, j=j: e.matmul(cb[0:P, 0:NH], lhsT=tri[0:P, 0:P], rhs=L3[0:P, j, :], start=(j == 0), stop=False), ["L3", "tri"], ["pb3"])
            for j in range(3):
                S.pe(lambda e, j=j: e.matmul(cb[0:P, 0:NH], lhsT=onesb[:, 0:P], rhs=La3[:, j, :], start=False, stop=(j == 2)), ["La3", "onesb"], ["pb3"])
            S.dve(lambda e: e.tensor_tensor(out=Lacc[0:P], in0=Lacc[0:P], in1=lg[0:P], op=ALU.add), ["Lacc", lgn, "La3"], ["Lacc"])
            S.act(lambda e: e.copy(out=csb[b][0:P], in_=cb[0:P, 0:NH]), ["pb3"], ["csb%d" % b])
            S.dma(ccol_dst, csb[b][0:P], reads=["csb%d" % b], writes=["ccol"])
            if crow_dst is not None:
                bi = tcount[0] % 2; tcount[0] += 1
                tb = pbank[bi]
                S.pe(lambda e: e.transpose(out=tb[0:NH, 0:P], in_=csb[b][0:P, :], identity=ident[0:P, 0:P]), ["csb%d" % b, "ident"], ["pb%d" % bi])
                S.act(lambda e: e.copy(out=cTs[b][0:NH, 0:P], in_=tb[0:NH, 0:P]), ["pb%d" % bi], ["cTs%d" % b])
                S.dma(crow_dst, cTs[b][0:NH, 0:P], reads=["cTs%d" % b], writes=["crow"])

        def kt_dst(scr, t0, P):
            return scr.rearrange("(hp h2) d t -> (h2 d) hp t", h2=2)[:, :, t0:t0 + P]

        tiles = []
        for i in range(2 * HALF // 128):
            tiles.append(dict(kind="full", t0=i * 128, P=128, reset=(i == 0)))
        for sq in range(2):
            for j in range(HALF // 128):
                tiles.append(dict(kind="cache", sq=sq, j=j, P=128, reset=(j == 0)))
            tiles.append(dict(kind="full", t0=2 * HALF + 64 * sq, P=64, reset=False))

        def load(i):
            tl = tiles[i]; b = i % 2; P = tl["P"]
            if tl["kind"] == "full":
                S.dma(Xq[b][0:P], x2s[tl["t0"]:tl["t0"] + P, :], writes=["Xq%d" % b])
            else:
                sq, j = tl["sq"], tl["j"]
                S.dma(Kf[b][:], kc_d[sq, j * 128:(j + 1) * 128, :], writes=["Kf%d" % b])
                S.dma(Vf[b][:], vc_d[sq, j * 128:(j + 1) * 128, :], writes=["Vf%d" % b])
                S.dma(lgf[b][:], lfc_d[sq, j * 128:(j + 1) * 128, :], writes=["lgf%d" % b])

        def do_tile(i, tl):
            b = i % 2; P = tl["P"]
            if i + 1 < len(tiles):
                load(i + 1)
            if tl["reset"]:
                S.dve(lambda e: e.memset(Lacc[:], 0.0), ["Lacc"], ["Lacc"])
            if tl["kind"] == "cache":
                sq, j = tl["sq"], tl["j"]
                S.pool(lambda e, b=b: e.tensor_copy(out=Kb[b][:], in_=Kf[b][:]), ["Kf%d" % b], ["Kb%d" % b])
                S.pool(lambda e, b=b: e.tensor_copy(out=Vb[b][:], in_=Vf[b][:]), ["Vf%d" % b], ["Vb%d" % b])
                transpose8(Kb[b][:], 128, KTt[b][:], "Kb%d" % b, "KTt%d" % b)
                S.dma(kt_dst(kcT[sq], j * 128, 128), KTt[b][:], reads=["KTt%d" % b], writes=["kcT"])
                S.dma(vcb[sq, j * 128:(j + 1) * 128, :], Vb[b][:], reads=["Vb%d" % b], writes=["vcb"])
                cumsum_path(lgf[b], "lgf%d" % b, 128, b, ccc[sq, j * 128:(j + 1) * 128, :], None)
                return
            t0 = tl["t0"]
            rms_to_bf16(Xq[b][0:P], P, gain, hsq[0:P], "Xq%d" % b, "hsq", 3)
            transpose8(hsq[0:P], P, hsTq[b][:, :, 0:P], "hsq", "hsTq%d" % b)
            hT_ = hsTq[b]; hn = "hsTq%d" % b

            def mm(bank, cols, c0, n):
                for kc in range(8):
                    S.pe(lambda e, kc=kc: e.matmul(pbank[bank][0:P, cols], lhsT=hT_[:, kc, 0:P], rhs=Wq[:, kc, c0:c0 + n], start=(kc == 0), stop=(kc == 7)), [hn], ["pb%d" % bank])
            mm(2, slice(0, NH), 3 * D, NH)
            mm(3, slice(0, 512), 0, 512); mm(4, slice(0, 512), 512, 512)
            mm(5, slice(0, 512), D, 512); mm(6, slice(0, 512), D + 512, 512)
            mm(7, slice(0, 512), 2 * D, 512)
            S.dve(lambda e, b=b: e.tensor_tensor(out=zf[b][0:P], in0=pbank[2][0:P, 0:NH], in1=bft[0:P], op=ALU.add), ["pb2", "bft"], ["zf%d" % b])
            mm(2, slice(0, 512), 2 * D + 512, 512)
            S.act(lambda e, b=b: e.activation(out=zf[b][0:P], in_=zf[b][0:P], func=AF.Exp, scale=-1.0), ["zf%d" % b], ["zf%d" % b])
            S.act(lambda e, b=b: e.activation(out=zf[b][0:P], in_=zf[b][0:P], func=AF.Ln, bias=one1[0:P, 0:1]), ["zf%d" % b, "one1"], ["zf%d" % b])
            S.dve(lambda e, b=b: e.tensor_scalar(out=lgf[b][0:P], in0=zf[b][0:P], scalar1=-1.0, scalar2=None, op0=ALU.mult), ["zf%d" % b], ["lgf%d" % b])
            for hf in range(2):
                S.act(lambda e, hf=hf, b=b: e.activation(out=Qb[b][0:P, hf * 512:(hf + 1) * 512], in_=pbank[3 + hf][0:P, :], func=AF.Copy, scale=HD ** -0.5), ["pb%d" % (3 + hf)], ["Qb%d" % b])
                S.dve(lambda e, hf=hf, b=b: e.tensor_copy(out=Kf[b][0:P, hf * 512:(hf + 1) * 512], in_=pbank[5 + hf][0:P, :]), ["pb%d" % (5 + hf)], ["Kf%d" % b])
            S.act(lambda e, b=b: e.copy(out=Vf[b][0:P, 0:512], in_=pbank[7][0:P, :]), ["pb7"], ["Vf%d" % b])
            S.act(lambda e, b=b: e.copy(out=Vf[b][0:P, 512:1024], in_=pbank[2][0:P, :]), ["pb2"], ["Vf%d" % b])
            S.pool(lambda e, b=b: e.tensor_copy(out=Kb[b][0:P], in_=Kf[b][0:P]), ["Kf%d" % b], ["Kb%d" % b])
            S.pool(lambda e, b=b: e.tensor_copy(out=Vb[b][0:P], in_=Vf[b][0:P]), ["Vf%d" % b], ["Vb%d" % b])
            transpose8(Qb[b][0:P], P, QTt[b][:, :, 0:P], "Qb%d" % b, "QTt%d" % b)
            transpose8(Kb[b][0:P], P, KTt[b][:, :, 0:P], "Kb%d" % b, "KTt%d" % b)
            S.dma(kt_dst(qts, t0, P), QTt[b][:, :, 0:P], reads=["QTt%d" % b], writes=["qts"])
            S.dma(kt_dst(kts, t0, P), KTt[b][:, :, 0:P], reads=["KTt%d" % b], writes=["kts"])
            S.dma(vsb[t0:t0 + P, :], Vb[b][0:P], reads=["Vb%d" % b], writes=["vsb"])
            if t0 >= 2 * HALF:
                o0 = t0 - 2 * HALF
                S.dma(o_ks[o0:o0 + P, :], Kf[b][0:P], reads=["Kf%d" % b]); S.dma(o_vs[o0:o0 + P, :], Vf[b][0:P], reads=["Vf%d" % b])
                S.dma(o_lfs[o0:o0 + P, :], lgf[b][0:P], reads=["lgf%d" % b])
            elif t0 >= HALF:
                o0 = t0 - HALF
                S.dma(o_kp[o0:o0 + P, :], Kf[b][0:P], reads=["Kf%d" % b]); S.dma(o_vp[o0:o0 + P, :], Vf[b][0:P], reads=["Vf%d" % b])
                S.dma(o_lfp[o0:o0 + P, :], lgf[b][0:P], reads=["lgf%d" % b])
            cumsum_path(lgf[b], "lgf%d" % b, P, b, ccol[t0:t0 + P, :], crow[:, t0:t0 + P])

        load(0)
        for i, tl in enumerate(tiles):
            do_tile(i, tl)

    if KST >= 6:
        phase_d()


    ats = dscr("ats", [NH, HD, NT], BF16)
    QS0 = HALF - 512

    def phase_e():
        S.barrier(); aoff[0] = mP2
        NKB = 2 * HALF // 128
        KA = [sb("KA%d" % i, [128, 2 * HALF], BF16) for i in range(2)]
        VA = [sb("VA%d" % i, [128, NKB, 65], BF16) for i in range(2)]
        CK = [sb("CK%d" % i, [128, NKB]) for i in range(2)]
        QA = [sb("QA%d" % i, [128, 512], BF16) for i in range(2)]
        crq = [sb("crq%d" % i, [128, 512]) for i in range(2)]
        cref = [sb("cref%d" % i, [128, 1]) for i in range(2)]
        cbias = [sb("cbias%d" % i, [128, NKB]) for i in range(2)]
        dq = sb("dq", [128, 512]); hq = sb("hq", [128, 512], BF16)
        PT = [sb("PT%d" % i, [128, 512], BF16) for i in range(4)]
        Osb = [sb("Osb%d" % i, [128, 512]) for i in range(2)]
        rt = sb("rt", [128, 512]); rt2 = sb("rt2", [128, 512]); rl3 = sb("rl3", [128, 3, 512], BF16)
        ATb = [sb("ATb%d" % i, [128, 512], BF16) for i in range(2)]
        for i in range(2):
            S.pool(lambda e, i=i: e.memset(KA[i][32:33, :], 1.0), [], ["KA%d" % i])
            S.pool(lambda e, i=i: e.memset(KA[i][64:65, :], 1.0), [], ["KA%d" % i])
            S.pool(lambda e, i=i: e.memset(VA[i][:, :, 64:65], 1.0), [], ["VA%d" % i])
            S.pool(lambda e, i=i: e.memset(QA[i][:], 0.0), [], ["QA%d" % i])
        sel = sb("sel", [128, 64], BF16)
        S.pool(lambda e: e.memset(sel[:], 0.0), [], ["sel"])
        S.pool(lambda e: e.memset(sel[64:65, :], 1.0), ["sel"], ["sel"])
        S.pool(lambda e: e.memset(rl3[:], 0.0), [], ["rl3"])
        cnt = {"s": 0, "q": 0, "p": 0}
        pending = []

        def rows67(dst, src, c0, c1, name, wname):
            S.dma(dst[0:32, c0:c1], src[0:32, :], writes=[wname])
            S.dma(dst[33:64, c0:c1], src[32:63, :], writes=[wname])
            S.dma(dst[65:66, c0:c1], src[63:64, :], writes=[wname])

        def attend(h, hb, q_t0, nq, blocks, pm_nb, out_t0):
            qb = cnt["q"] % 2; cnt["q"] += 1
            qn = "QA%d" % qb
            rows67(QA[qb], qts[h, :, q_t0:q_t0 + nq], 0, nq, qn, qn)
            S.dma(crq[qb][32:33, 0:nq], crow[h:h + 1, q_t0:q_t0 + nq], writes=["crq%d" % qb])
            S.dma(crq[qb][64:65, 0:nq], crow[h:h + 1, q_t0:q_t0 + nq], writes=["crq%d" % qb])
            S.dma(cref[qb][:], crow[h:h + 1, q_t0:q_t0 + 1].partition_broadcast(128), writes=["cref%d" % qb])
            S.dve(lambda e: e.tensor_scalar(out=dq[64:65, 0:nq], in0=crq[qb][64:65, 0:nq], scalar1=crq[qb][64:65, 0:1], scalar2=None, op0=ALU.subtract), ["crq%d" % qb, "dq64"], ["dq64"])
            S.dve(lambda e: e.tensor_copy(out=QA[qb][64:65, 0:nq], in_=dq[64:65, 0:nq]), ["dq64", qn], [qn])
            S.dve(lambda e: e.tensor_scalar(out=dq[32:33, 0:nq], in0=crq[qb][32:33, 0:nq], scalar1=crq[qb][32:33, 0:1], scalar2=None, op0=ALU.subtract), ["crq%d" % qb, "dq32"], ["dq32"])
            S.dve(lambda e: e.tensor_copy(out=hq[32:33, 0:nq], in_=dq[32:33, 0:nq]), ["dq32", "hq"], ["hq"])
            S.dve(lambda e: e.tensor_tensor(out=dq[32:33, 0:nq], in0=dq[32:33, 0:nq], in1=hq[32:33, 0:nq], op=ALU.subtract), ["dq32", "hq"], ["dq32"])
            S.dve(lambda e: e.tensor_copy(out=QA[qb][32:33, 0:nq], in_=dq[32:33, 0:nq]), ["dq32", qn], [qn])
            nbk = max(bk for (bk, nk, c0) in blocks) + 1
            S.dve(lambda e: e.tensor_scalar(out=cbias[qb][:, 0:nbk], in0=CK[hb][:, 0:nbk], scalar1=-1.0, scalar2=cref[qb][:, 0:1], op0=ALU.mult, op1=ALU.add), ["CK%d" % hb, "cref%d" % qb, "cbias%d" % qb], ["cbias%d" % qb])
            if pm_nb > 0:
                S.dve(lambda e: e.tensor_scalar(out=cbias[qb][:, 0:pm_nb], in0=cbias[qb][:, 0:pm_nb], scalar1=pmk[:, 0:1], scalar2=None, op0=ALU.add), ["cbias%d" % qb, "pmk"], ["cbias%d" % qb])
            ob = pbank[5 + qb]; obn = "pb%d" % (5 + qb)
            KEL = int(os.environ.get("KE_LVL", "9"))
            if KEL < 2:
                return
            nb_ = len(blocks)
            meta = {}

            def qk(bi_):
                bk, nk, c0 = blocks[bi_]
                si = cnt["s"] % 3; cnt["s"] += 1
                pi = cnt["p"] % 4; cnt["p"] += 1
                sbk = pbank[2 + si]; sn = "pb%d" % (2 + si)
                cc = 0 if c0 is None else c0
                meta[bi_] = (pi, cc)
                S.pe(lambda e: e.matmul(sbk[0:nk, cc:nq], lhsT=KA[hb][0:66, bk * 128:bk * 128 + nk], rhs=QA[qb][0:66, cc:nq], start=True, stop=True), ["KA%d" % hb, qn], [sn])
                if c0 is not None:
                    w = min(128, nq - cc)
                    S.dve(lambda e: e.tensor_tensor(out=sbk[0:nk, cc:cc + w], in0=sbk[0:nk, cc:cc + w], in1=cmask[0:nk, 0:w], op=ALU.add), [sn, "cmask"], [sn])
                S.act(lambda e: e.activation(out=PT[pi][0:nk, cc:nq], in_=sbk[0:nk, cc:nq], func=AF.Exp, bias=cbias[qb][0:nk, bk:bk + 1]), [sn, "cbias%d" % qb], ["PT%d" % pi])

            def pv(bi_):
                bk, nk, c0 = blocks[bi_]
                pi, cc = meta[bi_]
                S.pe(lambda e: e.matmul(ob[0:65, cc:nq], lhsT=VA[hb][0:nk, bk, 0:65], rhs=PT[pi][0:nk, cc:nq], start=(bi_ == 0), stop=(bi_ == nb_ - 1)), ["VA%d" % hb, "PT%d" % pi], [obn])

            LOOK = 2
            for i_ in range(nb_ + LOOK):
                if i_ < nb_:
                    qk(i_)
                if i_ == LOOK and pending:
                    pending.pop()()
                if i_ - LOOK >= 0 and KEL >= 3:
                    pv(i_ - LOOK)
            if pending:
                pending.pop()()
            if KEL < 4:
                return
            pending.append(lambda: finish(qb, nq, ob, obn, h, out_t0))

        def finish(qb, nq, ob, obn, h, out_t0):
            S.act(lambda e: e.copy(out=Osb[qb][0:65, 0:nq], in_=ob[0:65, 0:nq]), [obn], ["Osb%d" % qb])
            S.dve(lambda e: e.reciprocal(out=rt[64:65, 0:nq], in_=Osb[qb][64:65, 0:nq]), ["Osb%d" % qb, "rt"], ["rt"])
            S.dve(lambda e: e.tensor_copy(out=rl3[64:65, 0, 0:nq], in_=rt[64:65, 0:nq]), ["rt", "rl3"], ["rl3"])
            S.dve(lambda e: e.tensor_tensor(out=rt2[64:65, 0:nq], in0=rt[64:65, 0:nq], in1=rl3[64:65, 0, 0:nq], op=ALU.subtract), ["rt", "rl3", "rt2"], ["rt2"])
            S.dve(lambda e: e.tensor_copy(out=rl3[64:65, 1, 0:nq], in_=rt2[64:65, 0:nq]), ["rt2", "rl3"], ["rl3"])
            S.dve(lambda e: e.tensor_tensor(out=rt2[64:65, 0:nq], in0=rt2[64:65, 0:nq], in1=rl3[64:65, 1, 0:nq], op=ALU.subtract), ["rt2", "rl3"], ["rt2"])
            S.dve(lambda e: e.tensor_copy(out=rl3[64:65, 2, 0:nq], in_=rt2[64:65, 0:nq]), ["rt2", "rl3"], ["rl3"])
            bc = pbank[7]
            for j in range(3):
                S.pe(lambda e, j=j: e.matmul(bc[0:64, 0:nq], lhsT=sel[0:65, 0:64], rhs=rl3[0:65, j, 0:nq], start=(j == 0), stop=(j == 2)), ["rl3", "sel"], ["pb7"])
            S.dve(lambda e: e.tensor_tensor(out=ATb[qb][0:64, 0:nq], in0=Osb[qb][0:64, 0:nq], in1=bc[0:64, 0:nq], op=ALU.mult), ["Osb%d" % qb, "pb7", "ATb%d" % qb], ["ATb%d" % qb])
            S.dma(ats[h, :, out_t0:out_t0 + nq], ATb[qb][0:64, 0:nq], reads=["ATb%d" % qb], writes=["ats"])

        hcount = [0]
        def load_head(h, hb):
            rows67(KA[hb], kts[h, :, 0:2 * HALF], 0, 2 * HALF, "k", "KA%d" % hb)
            S.dma(VA[hb][:, :, 0:64], vsb[0:2 * HALF, h * HD:(h + 1) * HD].rearrange("(kb p) d -> p kb d", p=128), writes=["VA%d" % hb])
            S.dma(CK[hb][:, :], ccol[0:2 * HALF, h:h + 1].rearrange("(kb p) o -> p (kb o)", p=128), writes=["CK%d" % hb], allow_slow_non_contiguous=True)

        NHE = int(os.environ.get('KE_H', NH)) if KST >= 7 else 0
        if NHE:
            load_head(0, 0)
        for h in range(NHE):
            hb = hcount[0] % 2; hcount[0] += 1
            if h + 1 < NHE:
                load_head(h + 1, (hb + 1) % 2)
            for j in range(9):
                qs = QS0 + 512 * j
                nfull = qs // 128
                blocks = [(kb, 128, None) for kb in range(nfull)] + [(nfull + d_, 128, 128 * d_) for d_ in range(4)]
                attend(h, hb, qs, 512, blocks, (QS0 // 128) if j == 0 else (HALF // 128), qs)
        for sq in range(int(os.environ.get('KE_S', 2)) if KST >= 7 else 0):
            tn = 2 * HALF + 64 * sq
            for h in range(NH):
                hb = hcount[0] % 2; hcount[0] += 1
                rows67(KA[hb], kcT[sq, h], 0, HALF, "k", "KA%d" % hb)
                rows67(KA[hb], kts[h, :, tn:tn + 64], HALF, HALF + 64, "k", "KA%d" % hb)
                S.dma(VA[hb][:, 0:32, 0:64], vcb[sq, :, h * HD:(h + 1) * HD].rearrange("(kb p) d -> p kb d", p=128), writes=["VA%d" % hb])
                S.dma(VA[hb][0:64, 32, 0:64], vsb[tn:tn + 64, h * HD:(h + 1) * HD], writes=["VA%d" % hb])
                S.dma(CK[hb][:, 0:32], ccc[sq, :, h:h + 1].rearrange("(kb p) o -> p (kb o)", p=128), writes=["CK%d" % hb], allow_slow_non_contiguous=True)
                S.dma(CK[hb][0:64, 32:33], ccol[tn:tn + 64, h:h + 1], writes=["CK%d" % hb], allow_slow_non_contiguous=True)
                blocks = [(kb, 128, None) for kb in range(32)] + [(32, 128, 0)]
                attend(h, hb, tn, 64, blocks, 0, tn)
        if pending:
            pending.pop()()

    if KST >= 7:
        phase_e()

    wo_d = din("w_o", [D, D])

    def phase_f():
        S.barrier(); aoff[0] = mP2
        Wo = sb("Wo", [128, 8, D], BF16)
        m1 = aoff[0]
        load_weight(Wo, wo_d, 8, 128, D, "wo")
        S.barrier(); aoff[0] = m1
        ATt = [sb("ATt%d" % i, [128, 8, 512], BF16) for i in range(2)]
        Xf = [sb("Xf%d" % i, [128, 4, D]) for i in range(2)]
        groups = [(t, 512) for t in range(QS0, 2 * HALF, 512)] + [(2 * HALF, 128)]

        def load(i):
            t0, n = groups[i]; b = i % 2
            for h2 in range(2):
                S.dma(ATt[b][64 * h2:64 * h2 + 64, :, 0:n], ats[:, :, t0:t0 + n].rearrange("(hp h2) d t -> h2 d hp t", h2=2)[h2], writes=["ATt%d" % b])
            S.dma(Xf[b][:, 0:n // 128, :], x2s[t0:t0 + n, :].rearrange("(i p) d -> p i d", p=128), writes=["Xf%d" % b])

        fcount = [0]

        def do(i):
            t0, n = groups[i]; b = i % 2
            if i + 1 < len(groups):
                load(i + 1)
            for ti in range(n // 128):
                for half in range(2):
                    k = fcount[0] % 4; fcount[0] += 1
                    ob = pbank[2 + k]; on_ = "pb%d" % (2 + k)
                    for h in range(8):
                        S.pe(lambda e, h=h, half=half, ob=ob, ti=ti: e.matmul(ob[:, :], lhsT=ATt[b][:, h, ti * 128:(ti + 1) * 128], rhs=Wo[:, h, half * 512:(half + 1) * 512], start=(h == 0), stop=(h == 7)), ["ATt%d" % b], [on_])
                    S.dve(lambda e, half=half, ob=ob, ti=ti: e.tensor_tensor(out=Xf[b][:, ti, half * 512:(half + 1) * 512], in0=Xf[b][:, ti, half * 512:(half + 1) * 512], in1=ob[:, :], op=ALU.add), [on_, "Xf%d" % b], ["Xf%d" % b])
            S.dma(x3s[t0:t0 + n, :].rearrange("(i p) d -> p i d", p=128), Xf[b][:, 0:n // 128, :], reads=["Xf%d" % b], writes=["x3s"])

        load(0)
        for i in range(len(groups)):
            do(i)

    if KST >= 8:
        phase_f()
        groups1 = [dict(t0=HALF - 256, n=256, hist="zero", store=False, conv_out=None)]
        for gi in range(HALF // 256):
            groups1.append(dict(t0=HALF + gi * 256, n=256, hist="carry", store=True, yout=o_yp[gi * 256:(gi + 1) * 256, :],
                                conv_out=o_cvp[1] if gi == HALF // 256 - 1 else None))
        for sq in range(2):
            groups1.append(dict(t0=2 * HALF + 64 * sq, n=64, hist=sq, store=True, yout=o_ys[64 * sq:64 * sq + 64, :], conv_out=o_cvs[1, sq]))
        if not os.environ.get("KSKIPG"):
            ffn_phase(1, x3s, None, groups1, gffn_d[1], True)

    if os.environ.get("KDBG"):
        dbgG = dout("dbgG", [128, D], BF16); dbgX1 = dout("dbgX1", [128, D]); dbgX2 = dout("dbgX2", [128, D])
        S.barrier()
        S.dma(dbgG, gact[2 * HALF:2 * HALF + 128, :])
        S.dma(dbgX1, x1s[2 * HALF:2 * HALF + 128, :])
        S.dma(dbgX2, x2s[2 * HALF:2 * HALF + 128, :])
    S.emit()
    st.close()
    return nc


def _pair_layout(a):
    a = np.asarray(a)
    rest = a.shape[2:]
    a = a.reshape((32, 2, 64) + rest)
    a = np.moveaxis(a, 0, 2)
    return np.ascontiguousarray(a.reshape((128, 32) + rest))


def _from_pair_layout(a):
    a = a.reshape(2, 64, 32)
    return np.ascontiguousarray(np.moveaxis(a, 2, 0).reshape(64, 64))


def _host_inputs(inp):
    f = np.float32
    xpr = np.asarray(inp["x_prompt"], f)
    xsm = np.asarray(inp["x_sample"], f)
    rep = lambda v: np.ascontiguousarray(np.broadcast_to(np.asarray(v, f)[None, :], (128, np.asarray(v).shape[-1])))
    common = dict(
        lam_r=_pair_layout(np.asarray(inp["ssm_a_re"], f)[0]),
        lam_i=_pair_layout(np.asarray(inp["ssm_a_im"], f)[0]),
        lstep=_pair_layout(np.broadcast_to(np.asarray(inp["ssm_log_step"], f)[0][:, None], (64, 64))),
        b_r=_pair_layout(np.asarray(inp["ssm_b_re"], f)[0]),
        b_i=_pair_layout(np.asarray(inp["ssm_b_im"], f)[0]),
        c_r=_pair_layout(np.transpose(np.asarray(inp["ssm_c_re"], f)[0], (0, 2, 1))),
        c_i=_pair_layout(np.transpose(np.asarray(inp["ssm_c_im"], f)[0], (0, 2, 1))),
        gmix=np.stack([rep(inp["norm_mix"][i]) for i in range(2)]),
        gffn=np.stack([rep(inp["norm_ffn"][i]) for i in range(2)]),
        gfin=rep(inp["norm_final"]),
        dskip=rep(inp["ssm_d"][0]),
        ident=np.eye(128, dtype=f),
        tmask=(np.arange(128)[:, None] // 16 <= np.arange(128)[None, :] // 16).astype(f),
        tri=(np.arange(128)[:, None] <= np.arange(128)[None, :]).astype(f),
        cmask=np.where(np.arange(128)[:, None] <= np.arange(128)[None, :], 0.0, -30000.0).astype(f),
        bft=rep(inp["fox_b_f"][0]),
        w_glu=np.ascontiguousarray(np.asarray(inp["ssm_w_glu"], f)[0]),
        w_up=np.asarray(inp["ffn_w_up"], f), w_gate=np.asarray(inp["ffn_w_gate"], f), w_down=np.asarray(inp["ffn_w_down"], f),
        conv_w=np.asarray(inp["ffn_conv_w"], f), conv_b=np.asarray(inp["ffn_conv_b"], f),
        w_qkvf=np.ascontiguousarray(np.asarray(inp["fox_w_qkvf"], f)[0]),
        w_o=np.ascontiguousarray(np.asarray(inp["fox_w_o"], f)[0]),
    )
    maps = []
    for k in range(8):
        b, half = k // 2, k % 2
        prev = xpr[b, 0:HALF] if half == 1 else np.zeros((HALF, D), f)
        own = xpr[b, half * HALF:(half + 1) * HALF]
        m = dict(common)
        m["xp"] = np.ascontiguousarray(np.concatenate([prev, own], 0))
        m["xs"] = np.ascontiguousarray(xsm[2 * k:2 * k + 2].reshape(128, D))
        m["kcache"] = np.ascontiguousarray(np.asarray(inp["cache_fox_k"], f)[0, 2 * k:2 * k + 2].reshape(2, HALF, D))
        m["vcache"] = np.ascontiguousarray(np.asarray(inp["cache_fox_v"], f)[0, 2 * k:2 * k + 2].reshape(2, HALF, D))
        m["lfcache"] = np.ascontiguousarray(np.asarray(inp["cache_fox_logf"], f)[0, 2 * k:2 * k + 2])
        m["pmk"] = np.full((128, 1), 0.0 if half == 1 else -30000.0, f)
        m["cvst"] = np.ascontiguousarray(np.asarray(inp["state_ffn_conv"], f)[:, 2 * k:2 * k + 2])
        m["h0r"] = np.ascontiguousarray(np.stack([_pair_layout(np.asarray(inp["state_ssm_re"], f)[0, 2 * k + s]) for s in range(2)], 1))
        m["h0i"] = np.ascontiguousarray(np.stack([_pair_layout(np.asarray(inp["state_ssm_im"], f)[0, 2 * k + s]) for s in range(2)], 1))
        maps.append(m)
    return maps


_NC_CACHE = {}


def kernel(**inputs):
    maps = _host_inputs(inputs)
    if "nc" not in _NC_CACHE:
        _NC_CACHE["nc"] = build()
    nc = _NC_CACHE["nc"]
    res = run_bass_kernel_spmd(nc, maps, core_ids=list(range(8)))
    R = res.results
    f = np.float32
    y_prompt = np.zeros((4, 8192, D), f); y_sample = np.zeros((16, 64, D), f)
    ssm_re_p = np.zeros((1, 4, 64, 64), f); ssm_im_p = np.zeros((1, 4, 64, 64), f)
    ssm_re_s = np.zeros((1, 16, 64, 64), f); ssm_im_s = np.zeros((1, 16, 64, 64), f)
    k_p = np.zeros((1, 4, 8192, NH, HD), f); v_p = np.zeros((1, 4, 8192, NH, HD), f); lf_p = np.zeros((1, 4, 8192, NH), f)
    k_s = np.zeros((1, 16, 64, NH, HD), f); v_s = np.zeros((1, 16, 64, NH, HD), f); lf_s = np.zeros((1, 16, 64, NH), f)
    cv_p = np.zeros((2, 4, 2, DFF), f); cv_s = np.zeros((2, 16, 2, DFF), f)
    for k in range(8):
        b, half = k // 2, k % 2
        r = R[k]
        if half == 1:
            ssm_re_p[0, b] = _from_pair_layout(r["o_ssm_p"][0]); ssm_im_p[0, b] = _from_pair_layout(r["o_ssm_p"][1])
        if "o_cvp" in r:
            if half == 1:
                cv_p[:, b] = r["o_cvp"]
            cv_s[:, 2 * k:2 * k + 2] = r["o_cvs"]
        if "o_kp" in r:
            sl = slice(half * HALF, (half + 1) * HALF)
            k_p[0, b, sl] = r["o_kp"].reshape(HALF, NH, HD); v_p[0, b, sl] = r["o_vp"].reshape(HALF, NH, HD); lf_p[0, b, sl] = r["o_lfp"]
            k_s[0, 2 * k:2 * k + 2] = r["o_ks"].reshape(2, 64, NH, HD); v_s[0, 2 * k:2 * k + 2] = r["o_vs"].reshape(2, 64, NH, HD)
            lf_s[0, 2 * k:2 * k + 2] = r["o_lfs"].reshape(2, 64, NH)
        if "o_yp" in r:
            y_prompt[b, half * HALF:(half + 1) * HALF] = r["o_yp"]
            y_sample[2 * k:2 * k + 2] = r["o_ys"].reshape(2, 64, D)
        for s in range(2):
            ssm_re_s[0, 2 * k + s] = _from_pair_layout(r["o_ssm_s"][s, 0]); ssm_im_s[0, 2 * k + s] = _from_pair_layout(r["o_ssm_s"][s, 1])
    kernel.last = R
    return (y_prompt, y_sample, ssm_re_p, ssm_im_p, k_p, v_p, lf_p, cv_p, ssm_re_s, ssm_im_s, k_s, v_s, lf_s, cv_s)
```

```python
import contextlib
import math
import numpy as np
import concourse.bass as bass
import concourse.mybir as mybir
from concourse.bass_utils import run_bass_kernel_spmd

F32 = mybir.dt.float32
BF16 = mybir.dt.bfloat16
AF = mybir.ActivationFunctionType
ALU = mybir.AluOpType
AX = mybir.AxisListType

D = 1024
DFF = 2816
NH = 16
HD = 64
HALF = 4096
NPAIR = 32
EPS = 1e-6
TWO_PI = 2.0 * math.pi

ENGS = ("pe", "act", "dve", "pool", "sp")
NO_SELF_WAIT = ("pe", "sp")
DMA_RING = 8


class Op:
    __slots__ = ("eng", "fn", "deps", "dma", "idx", "needed", "cnt", "slot", "tgt")

    def __init__(self, eng, fn, deps, dma, idx):
        self.eng, self.fn, self.deps, self.dma, self.idx = eng, fn, deps, dma, idx
        self.needed = False
        self.cnt = 0
        self.slot = -1
        self.tgt = 0


class Sched:
    def __init__(self, nc):
        self.nc = nc
        self.ops = {e: [] for e in ENGS}
        self.last_w = {}
        self.readers = {}
        self.ndma = {e: 0 for e in ENGS}
        self.last_slot = {}

    def add(self, eng, fn, reads=(), writes=(), dma=False):
        lst = self.ops[eng]
        idx = len(lst)
        deps = set()
        for r in reads:
            lw = self.last_w.get(r)
            if lw is not None:
                deps.add(lw)
        for w in writes:
            lw = self.last_w.get(w)
            if lw is not None:
                deps.add(lw)
            rd = self.readers.get(w)
            if rd:
                for e, i in rd.items():
                    deps.add((e, i))
        op = Op(eng, fn, deps, dma, idx)
        if dma:
            k = self.ndma[eng]
            self.ndma[eng] = k + 1
            op.slot = k % DMA_RING
            op.tgt = 16 * (k // DMA_RING + 1)
            op.needed = True
            prev = self.last_slot.get((eng, op.slot))
            if prev is not None:
                deps.add((eng, prev))
            self.last_slot[(eng, op.slot)] = idx
        lst.append(op)
        for r in reads:
            self.readers.setdefault(r, {})[eng] = idx
        for w in writes:
            self.last_w[w] = (eng, idx)
            self.readers[w] = {}
        return op

    def pe(self, fn, reads=(), writes=()):
        return self.add("pe", fn, reads, writes)

    def act(self, fn, reads=(), writes=()):
        return self.add("act", fn, reads, writes)

    def dve(self, fn, reads=(), writes=()):
        return self.add("dve", fn, reads, writes)

    def pool(self, fn, reads=(), writes=()):
        return self.add("pool", fn, reads, writes)

    def on(self, eng, fn, reads=(), writes=()):
        return self.add(eng, fn, reads, writes)

    def dma(self, out, in_, reads=(), writes=(), q="sp", **kw):
        return self.add(q, lambda e: e.dma_start(out=out, in_=in_, **kw), reads, writes, dma=True)

    def barrier(self):
        deps = set()
        for e in ENGS:
            lst = self.ops[e]
            seen = set()
            got_last = False
            for j in range(len(lst) - 1, -1, -1):
                o = lst[j]
                if o.dma:
                    if o.slot not in seen:
                        seen.add(o.slot)
                        deps.add((e, j))
                elif not got_last and o.fn is not None:
                    got_last = True
                    deps.add((e, j))
                if got_last and len(seen) >= DMA_RING:
                    break
        for e in ENGS:
            self.ops[e].append(Op(e, None, set(deps), False, len(self.ops[e])))
        self.last_w.clear()
        self.readers.clear()

    def emit(self, final_wait_eng="sp"):
        nc = self.nc
        for e in ENGS:
            for op in self.ops[e]:
                for (de, di) in op.deps:
                    dop = self.ops[de][di]
                    if dop.dma:
                        continue
                    if de == e and e in NO_SELF_WAIT and op.fn is not None:
                        continue
                    if de == e and op.fn is None:
                        continue
                    dop.needed = True
        finals = []
        for e in ENGS:
            if self.ops[e]:
                last = None
                for o in reversed(self.ops[e]):
                    if not o.dma and o.fn is not None:
                        last = o
                        break
                if last is not None:
                    last.needed = True
                    finals.append((e, last.idx))
                seen = {}
                for op in self.ops[e]:
                    if op.dma:
                        seen[op.slot] = op.idx
                for sl, ix in seen.items():
                    finals.append((e, ix))
        for e in ENGS:
            c = 0
            for op in self.ops[e]:
                if op.needed and not op.dma and op.fn is not None:
                    c += 1
                    op.cnt = c
        with contextlib.ExitStack() as st:
            esem = {e: st.enter_context(nc.semaphore("s_" + e)) for e in ENGS}
            dsem = {e: [st.enter_context(nc.semaphore("d_%s%d" % (e, i))) for i in range(DMA_RING)]
                    for e in ENGS if self.ndma[e] > 0}
            block = st.enter_context(nc.Block())
            ops = self.ops

            def run(ename, eng):
                waited = {}
                for op in ops[ename]:
                    for (de, di) in sorted(op.deps):
                        dop = ops[de][di]
                        if dop.dma:
                            key = ("d", de, dop.slot)
                            if waited.get(key, 0) >= dop.tgt:
                                continue
                            waited[key] = dop.tgt
                            eng.wait_ge(dsem[de][dop.slot], dop.tgt)
                        else:
                            if de == ename and (ename in NO_SELF_WAIT or op.fn is None):
                                continue
                            key = ("e", de)
                            if waited.get(key, 0) >= dop.cnt:
                                continue
                            waited[key] = dop.cnt
                            eng.wait_ge(esem[de], dop.cnt)
                    if op.fn is None:
                        continue
                    ins = op.fn(eng)
                    if op.dma:
                        ins.then_inc(dsem[ename][op.slot], 16)
                    elif op.needed:
                        ins.then_inc(esem[ename], 1)
                if ename == final_wait_eng:
                    for (de, di) in finals:
                        dop = ops[de][di]
                        if dop.dma:
                            eng.wait_ge(dsem[de][dop.slot], dop.tgt)
                        elif de != ename:
                            eng.wait_ge(esem[de], dop.cnt)

            @block.tensor
            def _(eng):
                run("pe", eng)

            @block.scalar
            def _(eng):
                run("act", eng)

            @block.vector
            def _(eng):
                run("dve", eng)

            @block.gpsimd
            def _(eng):
                run("pool", eng)

            @block.sync
            def _(eng):
                run("sp", eng)


STAGE = 9


def build(stage=STAGE):
    nc = bass.Bass("TRN2", target_bir_lowering=False)
    S = Sched(nc)
    st = contextlib.ExitStack()

    def din(name, shape, dt=F32):
        return nc.dram_tensor(name, list(shape), dt, kind="ExternalInput").ap()

    def dout(name, shape, dt=F32):
        return nc.dram_tensor(name, list(shape), dt, kind="ExternalOutput").ap()

    def dscr(name, shape, dt=F32):
        return nc.dram_tensor(name, list(shape), dt).ap()

    ARENA_F32 = 53200
    arena = st.enter_context(nc.sbuf_tensor("arena", [128, ARENA_F32], F32))
    aoff = [0]

    def sb(name, shape, dt=F32):
        n = 1
        for d_ in shape[1:]:
            n *= d_
        nf = n if dt == F32 else (n + 1) // 2
        nf = (nf + 7) // 8 * 8
        o = aoff[0]
        assert o + nf <= ARENA_F32, ("arena overflow", name, o, nf)
        aoff[0] = o + nf
        v = arena[:, o:o + nf]
        if dt != F32:
            v = v.bitcast(dt)
        v = v[:, 0:n]
        if len(shape) == 2:
            return v
        letters = "abcdefg"[:len(shape) - 1]
        pat = "p (" + " ".join(letters) + ") -> p " + " ".join(letters)
        kw = {letters[i]: shape[1 + i] for i in range(len(shape) - 2)}
        return v.rearrange(pat, **kw)

    def ps(name, shape, dt=F32):
        return st.enter_context(nc.psum_tensor("p_" + name, list(shape), dt))

    xp = din("xp", [2 * HALF, D])
    xs = din("xs", [128, D])
    h0r = din("h0r", [128, 2, NPAIR])
    h0i = din("h0i", [128, 2, NPAIR])
    lam_r_d = din("lam_r", [128, NPAIR])
    lam_i_d = din("lam_i", [128, NPAIR])
    lstep_d = din("lstep", [128, NPAIR])
    b_r_d = din("b_r", [128, NPAIR, 16])
    b_i_d = din("b_i", [128, NPAIR, 16])
    c_r_d = din("c_r", [128, NPAIR, 16])
    c_i_d = din("c_i", [128, NPAIR, 16])
    gmix_d = din("gmix", [2, 128, D])
    gffn_d = din("gffn", [2, 128, D])
    gfin_d = din("gfin", [128, D])
    dskip_d = din("dskip", [128, D])
    ident_d = din("ident", [128, 128])
    tmask_d = din("tmask", [128, 128])

    o_ssm_p = dout("o_ssm_p", [2, 128, NPAIR])
    o_ssm_s = dout("o_ssm_s", [2, 2, 128, NPAIR])
    NT = 2 * HALF + 128
    gact = dscr("gscr", [NT, D], BF16)

    pbank = [ps("pb%d" % i, [128, 512], F32) for i in range(8)]

    ident = sb("ident", [128, 128]); identb = sb("identb", [128, 128], BF16)
    tmask = sb("tmask", [128, 128])
    S.dma(ident[:], ident_d, writes=["ident"])
    S.dma(tmask[:], tmask_d, writes=["tmask"])
    S.dve(lambda e: e.tensor_copy(out=identb[:], in_=ident[:]), ["ident"], ["identb"])
    negpi = sb("negpi", [128, 1])
    S.pool(lambda e: e.memset(negpi[:], -math.pi), [], ["negpi"])
    tri = sb("tri", [128, 128], BF16); onesb = sb("onesb", [128, 128], BF16); cmask = sb("cmask", [128, 128])
    bft = sb("bft", [128, NH]); pmk = sb("pmk", [128, 1]); one1 = sb("one1", [128, 1])
    cstg = sb("cstg", [128, 128])
    S.dma(cstg[:], din("tri", [128, 128]), writes=["cstg"])
    S.dve(lambda e: e.tensor_copy(out=tri[:], in_=cstg[:]), ["cstg"], ["tri"])
    S.pool(lambda e: e.memset(onesb[:], 1.0), [], ["onesb"])
    S.pool(lambda e: e.memset(one1[:], 1.0), [], ["one1"])
    S.dma(cmask[:], din("cmask", [128, 128]), writes=["cmask"])
    S.dma(bft[:], din("bft", [128, NH]), writes=["bft"])
    S.dma(pmk[:], din("pmk", [128, 1]), writes=["pmk"])
    ssn = sb("ssn", [128, 4]); rvn = sb("rvn", [128, 4])
    mP = aoff[0]
    mP2 = mP

    lam_r = sb("lam_r", [128, NPAIR]); lam_i = sb("lam_i", [128, NPAIR]); dtt = sb("dtt", [128, NPAIR])
    lrdt = sb("lrdt", [128, NPAIR]); lidt = sb("lidt", [128, NPAIR])
    Er = sb("Er", [128, 9, NPAIR]); Ei = sb("Ei", [128, 9, NPAIR])
    Emr = sb("Emr", [128, 9, NPAIR]); Emi = sb("Emi", [128, 9, NPAIR])
    x_t = sb("x_t", [128, 8, D])
    _al = [x_t[:, i, 0:NPAIR * 16].rearrange("p (a h) -> p a h", h=16) for i in range(8)]
    Br, Bi, Cr, Ci, Bbr, Bbi, t16a, t16b = _al
    tA = sb("tA", [128, NPAIR]); tB = sb("tB", [128, NPAIR]); tC = sb("tC", [128, NPAIR]); tD = sb("tD", [128, NPAIR])
    zr = sb("zr", [128, NPAIR]); zi = sb("zi", [128, NPAIR])
    S.dma(lam_r[:], lam_r_d, writes=["lam_r"]); S.dma(lam_i[:], lam_i_d, writes=["lam_i"])
    S.dma(dtt[:], lstep_d, writes=["dtt"])
    S.dma(Br[:], b_r_d, writes=["Br"]); S.dma(Bi[:], b_i_d, writes=["Bi"])
    S.dma(Cr[:], c_r_d, writes=["Cr"]); S.dma(Ci[:], c_i_d, writes=["Ci"])
    S.act(lambda e: e.activation(out=dtt[:], in_=dtt[:], func=AF.Exp), ["dtt"], ["dtt"])
    S.dve(lambda e: e.tensor_tensor(out=lrdt[:], in0=lam_r[:], in1=dtt[:], op=ALU.mult), ["lam_r", "dtt"], ["lrdt"])
    S.dve(lambda e: e.tensor_tensor(out=lidt[:], in0=lam_i[:], in1=dtt[:], op=ALU.mult), ["lam_i", "dtt"], ["lidt"])
    S.dve(lambda e: e.memset(Er[:, 0, :], 1.0), [], ["E"])
    S.dve(lambda e: e.memset(Ei[:, 0, :], 0.0), [], ["E"])
    halfpi = sb("halfpi", [128, 1])
    S.pool(lambda e: e.memset(halfpi[:], 0.5 * math.pi), [], ["halfpi"])
    S.act(lambda e: e.activation(out=tA[:], in_=lidt[:], func=AF.Sin, scale=0.125), ["lidt"], ["tA"])
    S.act(lambda e: e.activation(out=tB[:], in_=lidt[:], func=AF.Sin, scale=-0.125, bias=halfpi[:, 0:1]), ["lidt", "halfpi"], ["tB"])
    for _ in range(3):
        S.dve(lambda e: e.tensor_tensor(out=tC[:], in0=tA[:], in1=tB[:], op=ALU.mult), ["tA", "tB", "tC"], ["tC"])
        S.dve(lambda e: e.tensor_tensor(out=tD[:], in0=tA[:], in1=tA[:], op=ALU.mult), ["tA", "tD"], ["tD"])
        S.dve(lambda e: e.tensor_scalar(out=tA[:], in0=tC[:], scalar1=2.0, scalar2=None, op0=ALU.mult), ["tC", "tA"], ["tA"])
        S.dve(lambda e: e.tensor_scalar(out=tB[:], in0=tD[:], scalar1=-2.0, scalar2=1.0, op0=ALU.mult, op1=ALU.add), ["tD", "tB"], ["tB"])
    S.act(lambda e: e.activation(out=tC[:], in_=lrdt[:], func=AF.Exp), ["lrdt", "tC"], ["tC"])
    S.dve(lambda e: e.tensor_tensor(out=Er[:, 1, :], in0=tB[:], in1=tC[:], op=ALU.mult), ["tB", "tC"], ["E"])
    S.dve(lambda e: e.tensor_tensor(out=Ei[:, 1, :], in0=tA[:], in1=tC[:], op=ALU.mult), ["tA", "tC"], ["E"])
    for k in range(2, 9):
        S.dve(lambda e, k=k: e.tensor_tensor(out=tA[:], in0=Er[:, k - 1, :], in1=Er[:, 1, :], op=ALU.mult), ["E", "tA"], ["tA"])
        S.dve(lambda e, k=k: e.tensor_tensor(out=tB[:], in0=Ei[:, k - 1, :], in1=Ei[:, 1, :], op=ALU.mult), ["E", "tB"], ["tB"])
        S.dve(lambda e, k=k: e.tensor_tensor(out=tC[:], in0=Er[:, k - 1, :], in1=Ei[:, 1, :], op=ALU.mult), ["E", "tC"], ["tC"])
        S.dve(lambda e, k=k: e.tensor_tensor(out=tD[:], in0=Ei[:, k - 1, :], in1=Er[:, 1, :], op=ALU.mult), ["E", "tD"], ["tD"])
        S.dve(lambda e, k=k: e.tensor_tensor(out=Er[:, k, :], in0=tA[:], in1=tB[:], op=ALU.subtract), ["tA", "tB", "E"], ["E"])
        S.dve(lambda e, k=k: e.tensor_tensor(out=Ei[:, k, :], in0=tC[:], in1=tD[:], op=ALU.add), ["tC", "tD", "E"], ["E"])
    for k in range(1, 9):
        S.act(lambda e, k=k: e.activation(out=tC[:], in_=lrdt[:], func=AF.Exp, scale=-2.0 * k), ["lrdt", "tC"], ["tC"])
        S.dve(lambda e, k=k: e.tensor_tensor(out=Emr[:, k, :], in0=Er[:, k, :], in1=tC[:], op=ALU.mult), ["E", "tC"], ["E"])
        S.dve(lambda e, k=k: e.scalar_tensor_tensor(out=Emi[:, k, :], in0=Ei[:, k, :], scalar=-1.0, in1=tC[:], op0=ALU.mult, op1=ALU.mult), ["E", "tC"], ["E"])
    S.dve(lambda e: e.tensor_tensor(out=tA[:], in0=lam_r[:], in1=lam_r[:], op=ALU.mult), ["lam_r", "tA"], ["tA"])
    S.dve(lambda e: e.tensor_tensor(out=tB[:], in0=lam_i[:], in1=lam_i[:], op=ALU.mult), ["lam_i", "tB"], ["tB"])
    S.dve(lambda e: e.tensor_tensor(out=tA[:], in0=tA[:], in1=tB[:], op=ALU.add), ["tA", "tB"], ["tA"])
    S.dve(lambda e: e.reciprocal(out=tA[:], in_=tA[:]), ["tA"], ["tA"])
    S.dve(lambda e: e.tensor_scalar(out=tB[:], in0=Er[:, 1, :], scalar1=-1.0, scalar2=None, op0=ALU.add), ["E", "tB"], ["tB"])
    S.dve(lambda e: e.tensor_tensor(out=tC[:], in0=tB[:], in1=lam_r[:], op=ALU.mult), ["tB", "lam_r", "tC"], ["tC"])
    S.dve(lambda e: e.tensor_tensor(out=tD[:], in0=Ei[:, 1, :], in1=lam_i[:], op=ALU.mult), ["E", "lam_i", "tD"], ["tD"])
    S.dve(lambda e: e.tensor_tensor(out=tC[:], in0=tC[:], in1=tD[:], op=ALU.add), ["tC", "tD"], ["tC"])
    S.dve(lambda e: e.tensor_tensor(out=zr[:], in0=tC[:], in1=tA[:], op=ALU.mult), ["tC", "tA"], ["zr"])
    S.dve(lambda e: e.tensor_tensor(out=tC[:], in0=Ei[:, 1, :], in1=lam_r[:], op=ALU.mult), ["E", "lam_r", "tC"], ["tC"])
    S.dve(lambda e: e.tensor_tensor(out=tD[:], in0=tB[:], in1=lam_i[:], op=ALU.mult), ["tB", "lam_i", "tD"], ["tD"])
    S.dve(lambda e: e.tensor_tensor(out=tC[:], in0=tC[:], in1=tD[:], op=ALU.subtract), ["tC", "tD"], ["tC"])
    S.dve(lambda e: e.tensor_tensor(out=zi[:], in0=tC[:], in1=tA[:], op=ALU.mult), ["tC", "tA"], ["zi"])

    import os
    KPREP = int(os.environ.get("KPREP", "9"))

    def bch(ap2, n=16):
        return ap2.unsqueeze(2).to_broadcast([128, ap2.shape[1], n])

    S.dve(lambda e: e.tensor_tensor(out=t16a[:], in0=Br[:], in1=bch(zr[:]), op=ALU.mult), ["Br", "zr", "t16a"], ["t16a"])
    S.dve(lambda e: e.tensor_tensor(out=t16b[:], in0=Bi[:], in1=bch(zi[:]), op=ALU.mult), ["Bi", "zi", "t16b"], ["t16b"])
    S.dve(lambda e: e.tensor_tensor(out=Bbr[:], in0=t16a[:], in1=t16b[:], op=ALU.subtract), ["t16a", "t16b"], ["Bbr"])
    S.dve(lambda e: e.tensor_tensor(out=t16a[:], in0=Bi[:], in1=bch(zr[:]), op=ALU.mult), ["Bi", "zr", "t16a"], ["t16a"])
    S.dve(lambda e: e.tensor_tensor(out=t16b[:], in0=Br[:], in1=bch(zi[:]), op=ALU.mult), ["Br", "zi", "t16b"], ["t16b"])
    S.dve(lambda e: e.tensor_tensor(out=Bbi[:], in0=t16a[:], in1=t16b[:], op=ALU.add), ["t16a", "t16b"], ["Bbi"])

    WinReT = sb("WinReT", [128, NPAIR, 128], BF16)
    WinImT = sb("WinImT", [128, NPAIR, 128], BF16)
    Toep = sb("Toep", [128, 2 * NPAIR, 128], BF16)
    WoRe = sb("WoRe", [128, 2, NPAIR, 128], BF16)
    WoIm = sb("WoIm", [128, 2, NPAIR, 128], BF16)
    S.pool(lambda e: e.memset(WoRe[:], 0.0), [], ["WoRe"])
    S.pool(lambda e: e.memset(WoIm[:], 0.0), [], ["WoIm"])
    NPC = 2
    XRe = sb("XRe", [128, NPC, 8, 16]); XIm = sb("XIm", [128, NPC, 8, 16])
    WiRe = sb("WiRe", [128, NPC, 8, 16]); WiIm = sb("WiIm", [128, NPC, 8, 16])
    WoReF = sb("WoReF", [128, NPC, 8, 16]); WoImF = sb("WoImF", [128, NPC, 8, 16])
    XReB = sb("XReB", [128, NPC, 128], BF16); XImB = sb("XImB", [128, NPC, 128], BF16)
    pa = sb("pa", [128, NPC, 16]); pb_ = sb("pb_", [128, NPC, 16])

    def cmul_into(outr, outi, ar, ai, br, bi, tagr, tagi, negate_im=False):
        S.dve(lambda e: e.tensor_tensor(out=pa[:], in0=br, in1=bch(ar), op=ALU.mult), ["E", "Bbr", "Bbi", "Cr", "Ci", "pa"], ["pa"])
        S.dve(lambda e: e.tensor_tensor(out=pb_[:], in0=bi, in1=bch(ai), op=ALU.mult), ["E", "Bbr", "Bbi", "Cr", "Ci", "pb_"], ["pb_"])
        S.dve(lambda e: e.tensor_tensor(out=outr, in0=pa[:], in1=pb_[:], op=ALU.subtract), ["pa", "pb_"], [tagr])
        S.dve(lambda e: e.tensor_tensor(out=pa[:], in0=bi, in1=bch(ar), op=ALU.mult), ["E", "Bbr", "Bbi", "Cr", "Ci", "pa"], ["pa"])
        S.dve(lambda e: e.tensor_tensor(out=pb_[:], in0=br, in1=bch(ai), op=ALU.mult), ["E", "Bbr", "Bbi", "Cr", "Ci", "pb_"], ["pb_"])
        if negate_im:
            S.dve(lambda e: e.scalar_tensor_tensor(out=outi, in0=pa[:], scalar=-1.0, in1=pb_[:], op0=ALU.mult, op1=ALU.subtract), ["pa", "pb_"], [tagi])
        else:
            S.dve(lambda e: e.tensor_tensor(out=outi, in0=pa[:], in1=pb_[:], op=ALU.add), ["pa", "pb_"], [tagi])

    for q in range(NPAIR // NPC):
        psl = slice(q * NPC, (q + 1) * NPC)
        for s in range(8):
            cmul_into(XRe[:, :, s, :], XIm[:, :, s, :], Emr[:, s + 1, psl], Emi[:, s + 1, psl], Bbr[:, psl, :], Bbi[:, psl, :], "XRe", "XIm")
            cmul_into(WiRe[:, :, s, :], WiIm[:, :, s, :], Er[:, 7 - s, psl], Ei[:, 7 - s, psl], Bbr[:, psl, :], Bbi[:, psl, :], "WiRe", "WiIm")
            cmul_into(WoReF[:, :, s, :], WoImF[:, :, s, :], Er[:, s + 1, psl], Ei[:, s + 1, psl], Cr[:, psl, :], Ci[:, psl, :], "WoReF", "WoImF", negate_im=True)
        for g2 in range(2):
            rs = slice(64 * g2, 64 * g2 + 64)
            S.act(lambda e, psl=psl, rs=rs, g2=g2: e.copy(out=WoRe[rs, g2, psl, :], in_=WoReF[rs].rearrange("p a s h -> p a (s h)")), ["WoReF", "WoRe"], ["WoRe"])
            S.act(lambda e, psl=psl, rs=rs, g2=g2: e.copy(out=WoIm[rs, g2, psl, :], in_=WoImF[rs].rearrange("p a s h -> p a (s h)")), ["WoImF", "WoIm"], ["WoIm"])
        S.act(lambda e: e.copy(out=XReB[:], in_=XRe[:].rearrange("p a s h -> p a (s h)")), ["XRe"], ["XReB"])
        S.act(lambda e: e.copy(out=XImB[:], in_=XIm[:].rearrange("p a s h -> p a (s h)")), ["XIm"], ["XImB"])
        if KPREP <= 3:
            continue
        for a in range(NPC):
            pair = q * NPC + a
            bk = pbank[a % 2]; bkn0 = 'pb%d' % (a % 2)
            S.pe(lambda e, a=a, bk=bk: e.transpose(out=bk[:, 0:128], in_=WiRe[:, a].rearrange("p s h -> p (s h)"), identity=ident[:]), ["WiRe", "ident"], ["pb%d" % (a % 2)])
            S.pe(lambda e, a=a, bk=bk: e.transpose(out=bk[:, 128:256], in_=WiIm[:, a].rearrange("p s h -> p (s h)"), identity=ident[:]), ["WiIm", "ident"], ["pb%d" % (a % 2)])
            for g2 in range(2 if KPREP >= 5 else 0):
                rs = slice(64 * g2, 64 * g2 + 64)
                cs = slice(128 * g2, 128 * g2 + 128)
                bk = pbank[2 + a % 2]
                S.pe(lambda e, a=a, bk=bk, rs=rs, cs=cs, q=q, g2=g2: e.matmul(bk[:, cs], lhsT=XReB[:, a, :], rhs=WoRe[:, g2, q * NPC + a, :], start=True, stop=False),
                     ["XReB", "WoRe"], ["pb%d" % (2 + a % 2)])
                S.pe(lambda e, a=a, bk=bk, rs=rs, cs=cs, q=q, g2=g2: e.matmul(bk[:, cs], lhsT=XImB[:, a, :], rhs=WoIm[:, g2, q * NPC + a, :], start=False, stop=True),
                     ["XImB", "WoIm"], ["pb%d" % (2 + a % 2)])
            S.act(lambda e, pair=pair, bk=pbank[a % 2]: e.copy(out=WinReT[:, pair, :], in_=bk[:, 0:128]), ["pb%d" % (a % 2)], ["WinReT"])
            S.act(lambda e, pair=pair, bk=pbank[a % 2]: e.copy(out=WinImT[:, pair, :], in_=bk[:, 128:256]), ["pb%d" % (a % 2)], ["WinImT"])
            for g2 in range(2 if KPREP >= 5 else 0):
                S.dve(lambda e, pair=pair, g2=g2, bk2=pbank[2 + a % 2]: e.tensor_tensor(out=Toep[:, 2 * pair + g2, :], in0=bk2[:, 128 * g2:128 + 128 * g2], in1=tmask[:], op=ALU.mult),
                      ["pb%d" % (2 + a % 2), "tmask"], ["Toep"])

    M1 = sb("M1", [128, 2, NPAIR]); M2n = sb("M2n", [128, NPAIR]); M2p = sb("M2p", [128, NPAIR])
    S.dve(lambda e: e.tensor_copy(out=M1[:, 0, :], in_=Er[:, 8, :]), ["E"], ["M1"])
    S.dve(lambda e: e.tensor_copy(out=M1[:, 1, :], in_=Er[:, 8, :]), ["E"], ["M1"])
    S.dve(lambda e: e.tensor_copy(out=M2p[:], in_=Ei[:, 8, :]), ["E"], ["M2"])
    S.dve(lambda e: e.tensor_scalar(out=M2n[:], in0=Ei[:, 8, :], scalar1=-1.0, scalar2=None, op0=ALU.mult), ["E"], ["M2"])

    gmix0 = sb("gmix0", [128, D]); dg = sb("dg", [128, D])
    S.dma(gmix0[:], gmix_d[0], writes=["gmix0"])
    S.dma(dg[:], dskip_d, writes=["dg"])
    S.dve(lambda e: e.tensor_tensor(out=dg[:], in0=dg[:], in1=gmix0[:], op=ALU.mult), ["dg", "gmix0"], ["dg"])

    S.dve(lambda e: e.memset(x_t[:, 0, 0:1], 0.0), ["Br", "Bi", "Cr", "Ci", "Bbr", "Bbi", "t16a", "t16b"], ["x_t"])
    ss = sb("ss", [128, 8]); rinv = sb("rinv", [128, 8])
    ub = sb("ub", [128, 64, 8, 16], BF16)
    UT = sb("UT", [128, 64, 128], BF16)
    SH = sb("SH", [128, 2, NPAIR, 128])
    Hpb = sb("Hpb", [128, 2, NPAIR, 128], BF16)
    carry = sb("carry", [128, 2, NPAIR])
    T1 = {"dve": sb("T1d", [128, 2, NPAIR // 2]), "pool": sb("T1p", [128, 2, NPAIR // 2])}
    T2 = {"dve": sb("T2d", [128, 2, NPAIR // 2]), "pool": sb("T2p", [128, 2, NPAIR // 2])}
    ytmps = [sb("ytmp0", [128, 8, 16]), sb("ytmp1", [128, 8, 16])]
    junk = UT[:].rearrange("p g c -> p (g c)")[:, 0:D]
    Gb = ub[:].rearrange("p g t h -> p (g t h)").rearrange("p (t d) -> p t d", t=8)

    def s5_macro(src_ap, nvalid, mt_tag, seqs, g_out_ap, final_out):
        P = nvalid
        S.dma(x_t[0:P], src_ap.rearrange("(c t) d -> c t d", t=8), writes=["x_t"])
        for t in range(8):
            S.act(lambda e, t=t: e.activation(out=junk[0:P], in_=x_t[0:P, t, :], func=AF.Square, accum_out=ss[0:P, t:t + 1]), ["x_t"], ["UT", "ss"])
        S.dve(lambda e: e.tensor_scalar(out=rinv[0:P], in0=ss[0:P], scalar1=1.0 / D, scalar2=EPS, op0=ALU.mult, op1=ALU.add), ["ss"], ["rinv"])
        S.act(lambda e: e.activation(out=rinv[0:P], in_=rinv[0:P], func=AF.Sqrt), ["rinv"], ["rinv"])
        S.dve(lambda e: e.reciprocal(out=rinv[0:P], in_=rinv[0:P]), ["rinv"], ["rinv"])
        for t in range(8):
            S.dve(lambda e, t=t: e.scalar_tensor_tensor(out=ub[0:P, :, t, :], in0=x_t[0:P, t, :].rearrange("p (g h) -> p g h", h=16), scalar=rinv[0:P, t:t + 1],
                                                        in1=gmix0[0:P].rearrange("p (g h) -> p g h", h=16), op0=ALU.mult, op1=ALU.mult),
                  ["x_t", "rinv", "gmix0", "ub"], ["ub"])
        for g8 in range(8):
            bk = pbank[g8 % 2]; bkn = "pb%d" % (g8 % 2)
            bkb = bk[:].bitcast(BF16)
            for j in range(8):
                g = g8 * 8 + j
                S.pe(lambda e, g=g, j=j, bkb=bkb: e.transpose(out=bkb[:, j * 128:j * 128 + P], in_=ub[0:P, g].rearrange("p t h -> p (t h)"), identity=identb[0:P, 0:P]),
                     ["ub", "identb"], [bkn])
            S.act(lambda e, g8=g8, bkb=bkb: e.copy(out=UT[:, g8 * 8:(g8 + 1) * 8, 0:P], in_=bkb[:, 0:1024].rearrange("p (j c) -> p j c", j=8)[:, :, 0:P]), [bkn], ["UT"])
        for q in range(8):
            bre = pbank[4 + 2 * (q % 2)]; bim = pbank[5 + 2 * (q % 2)]
            nre = "pb%d" % (4 + 2 * (q % 2)); nim = "pb%d" % (5 + 2 * (q % 2))
            for a in range(4):
                pair = q * 4 + a
                for g2 in range(2):
                    g = 2 * pair + g2
                    S.pe(lambda e, a=a, g2=g2, g=g, pair=pair, bre=bre: e.matmul(bre[64 * g2:64 * g2 + 64, a * 128:a * 128 + P], lhsT=WinReT[:, pair, 64 * g2:64 * g2 + 64], rhs=UT[:, g, 0:P], start=True, stop=True),
                         ["WinReT", "UT"], [nre])
                    S.pe(lambda e, a=a, g2=g2, g=g, pair=pair, bim=bim: e.matmul(bim[64 * g2:64 * g2 + 64, a * 128:a * 128 + P], lhsT=WinImT[:, pair, 64 * g2:64 * g2 + 64], rhs=UT[:, g, 0:P], start=True, stop=True),
                         ["WinImT", "UT"], [nim])
            S.act(lambda e, q=q, bre=bre: e.copy(out=SH[:, 0, q * 4:q * 4 + 4, 0:P], in_=bre[:].rearrange("p (a c) -> p a c", a=4)[:, :, 0:P]), [nre], ["SHr"])
            S.dve(lambda e, q=q, bim=bim: e.tensor_copy(out=SH[:, 1, q * 4:q * 4 + 4, 0:P], in_=bim[:].rearrange("p (a c) -> p a c", a=4)[:, :, 0:P]), [nim], ["SHi"])
        for (c0, c1, init) in seqs:
            for ei, en in enumerate(("dve", "pool")):
                hp = slice(ei * 16, ei * 16 + 16)
                t1 = T1[en]; t2 = T2[en]
                if init[0] == "h0":
                    S.dma(carry[:, 0, hp], h0r[:, init[1], hp], writes=["carry" + en], q="sp")
                    S.dma(carry[:, 1, hp], h0i[:, init[1], hp], writes=["carry" + en], q="sp")
                elif init[0] == "zero":
                    S.on(en, lambda e, hp=hp: e.memset(carry[:, :, hp], 0.0), [], ["carry" + en])
                for c in range(c0, c1):
                    if c == c0:
                        prev = carry[:, :, hp]; prev_r = carry[:, 0, hp]; prev_i = carry[:, 1, hp]
                    else:
                        prev = SH[:, :, hp, c - 1]; prev_r = SH[:, 0, hp, c - 1]; prev_i = SH[:, 1, hp, c - 1]
                    rd = ["SH" + en, "carry" + en, "M1", "M2", "SHr", "SHi"]
                    if c == c0:
                        S.on(en, lambda e, prev=prev, hp=hp, c=c: e.tensor_copy(out=Hpb[:, :, hp, c], in_=prev), rd, ["Hpb" + en])
                    S.on(en, lambda e, prev=prev, hp=hp, t1=t1: e.tensor_tensor(out=t1[:], in0=prev, in1=M1[:, :, hp], op=ALU.mult), rd + ["t1" + en], ["t1" + en])
                    S.on(en, lambda e, prev_i=prev_i, hp=hp, t2=t2: e.tensor_tensor(out=t2[:, 0, :], in0=prev_i, in1=M2n[:, hp], op=ALU.mult), rd + ["t2" + en], ["t2" + en])
                    S.on(en, lambda e, prev_r=prev_r, hp=hp, t2=t2: e.tensor_tensor(out=t2[:, 1, :], in0=prev_r, in1=M2p[:, hp], op=ALU.mult), rd + ["t2" + en], ["t2" + en])
                    S.on(en, lambda e, t1=t1, t2=t2: e.tensor_tensor(out=t1[:], in0=t1[:], in1=t2[:], op=ALU.add), ["t1" + en, "t2" + en], ["t1" + en])
                    S.on(en, lambda e, t1=t1, hp=hp, c=c: e.tensor_tensor(out=SH[:, :, hp, c], in0=SH[:, :, hp, c], in1=t1[:], op=ALU.add), ["t1" + en, "SHr", "SHi", "SH" + en], ["SH" + en])
                for ri in range(2):
                    S.on(en, lambda e, hp=hp, c0=c0, c1=c1, ri=ri: e.tensor_copy(out=Hpb[:, ri, hp, c0 + 1:c1], in_=SH[:, ri, hp, c0:c1 - 1]), ["SH" + en, "SHr", "SHi"], ["Hpb" + en])
                S.on(en, lambda e, hp=hp, c1=c1: e.tensor_copy(out=carry[:, :, hp], in_=SH[:, :, hp, c1 - 1]), ["SH" + en, "SHr", "SHi"], ["carry" + en])
        for (cl, ore, oim) in final_out:
            S.dma(ore, SH[:, 0, :, cl], reads=["SHdve", "SHpool", "SHr", "SHi"], allow_slow_non_contiguous=True)
            S.dma(oim, SH[:, 1, :, cl], reads=["SHdve", "SHpool", "SHr", "SHi"], allow_slow_non_contiguous=True)
        for g in range(64):
            pair, g2 = g // 2, g % 2
            bk = pbank[2 + g % 2]; bkn = "pb%d" % (2 + g % 2)
            rs = slice(64 * g2, 64 * g2 + 64)
            S.pe(lambda e, g=g, bk=bk: e.matmul(bk[0:P, 0:128], lhsT=UT[:, g, 0:P], rhs=Toep[:, g, :], start=True, stop=False), ["UT", "Toep"], [bkn])
            S.pe(lambda e, pair=pair, rs=rs, bk=bk, g2=g2: e.matmul(bk[0:P, 0:128], lhsT=Hpb[:, 0, pair, 0:P], rhs=WoRe[:, g2, pair, :], start=False, stop=False), ["Hpbdve", "Hpbpool", "WoRe"], [bkn])
            S.pe(lambda e, pair=pair, rs=rs, bk=bk, g2=g2: e.matmul(bk[0:P, 0:128], lhsT=Hpb[:, 1, pair, 0:P], rhs=WoIm[:, g2, pair, :], start=False, stop=True), ["Hpbdve", "Hpbpool", "WoIm"], [bkn])
            dsl = slice(16 * g, 16 * g + 16)
            ytmp = ytmps[g % 2]; yn = "ytmp%d" % (g % 2)
            S.dve(lambda e, dsl=dsl, ytmp=ytmp: e.tensor_tensor(out=ytmp[0:P], in0=x_t[0:P, :, dsl], in1=rinv[0:P].unsqueeze(2).to_broadcast([P, 8, 16]), op=ALU.mult), ["x_t", "rinv"], [yn])
            S.dve(lambda e, dsl=dsl, ytmp=ytmp: e.tensor_tensor(out=ytmp[0:P], in0=ytmp[0:P], in1=dg[0:P, dsl].unsqueeze(1).to_broadcast([P, 8, 16]), op=ALU.mult), [yn, "dg"], [yn])
            S.dve(lambda e, bk=bk, ytmp=ytmp: e.tensor_tensor(out=ytmp[0:P], in0=ytmp[0:P], in1=bk[0:P, 0:128].rearrange("p (t h) -> p t h", t=8), op=ALU.add), [yn, bkn], [yn])
            S.act(lambda e, dsl=dsl, ytmp=ytmp: e.activation(out=Gb[0:P, :, dsl], in_=ytmp[0:P], func=AF.Gelu_apprx_tanh), [yn], ["ub"])
        S.dma(g_out_ap.rearrange("(c t) d -> c t d", t=8), Gb[0:P], reads=["ub"])

    import os
    KST = int(os.environ.get("KSTAGE", "9"))
    for mt in range(8 if KST >= 3 else (1 if KST == 2 else 0)):
        seqs = [(0, 128, ("zero",) if mt == 0 else ("carry",))]
        fin = [(127, o_ssm_p[0], o_ssm_p[1])] if mt == 7 or KST == 2 else []
        s5_macro(xp[mt * 1024:(mt + 1) * 1024, :], 128, mt, seqs, gact[mt * 1024:(mt + 1) * 1024, :], fin)
    if KST >= 1:
      s5_macro(xs, 16, 8, [(0, 8, ("h0", 0)), (8, 16, ("h0", 1))], gact[2 * HALF:2 * HALF + 128, :],
             [(7, o_ssm_s[0, 0], o_ssm_s[0, 1]), (15, o_ssm_s[1, 0], o_ssm_s[1, 1])])


    x1s = dscr("x1s", [NT, D]); x2s = dscr("x2s", [NT, D]); x3s = dscr("x3s", [NT, D])

    def xsrc(t0, n):
        return xp[t0:t0 + n, :] if t0 < 2 * HALF else xs[t0 - 2 * HALF:t0 - 2 * HALF + n, :]

    def load_weight(dst3, src2, nchunks, P, ncol, tag):
        stg = [sb(tag + "_s0", [128, ncol]), sb(tag + "_s1", [128, ncol])]
        engs = ["act", "pool", "dve"]
        for c in range(nchunks):
            b = stg[c % 2]; bn = tag + "_s%d" % (c % 2)
            S.dma(b[0:P], src2[c * P:(c + 1) * P, :], writes=[bn])
            en = engs[c % 3]
            if en == "act":
                S.act(lambda e, b=b, c=c: e.copy(out=dst3[0:P, c, :], in_=b[0:P]), [bn], [tag + str(c)])
            else:
                S.on(en, lambda e, b=b, c=c: e.tensor_copy(out=dst3[0:P, c, :], in_=b[0:P]), [bn], [tag + str(c)])


    def rms_to_bf16(x_ap, P, gain, out_ap, xname, oname, col):
        S.act(lambda e: e.activation(out=out_ap, in_=x_ap, func=AF.Square, accum_out=ssn[0:P, col:col + 1]), [xname], [oname, "ssn%d" % col])
        S.dve(lambda e: e.tensor_scalar(out=rvn[0:P, col:col + 1], in0=ssn[0:P, col:col + 1], scalar1=1.0 / D, scalar2=EPS, op0=ALU.mult, op1=ALU.add), ["ssn%d" % col], ["rvn%d" % col])
        S.act(lambda e: e.activation(out=rvn[0:P, col:col + 1], in_=rvn[0:P, col:col + 1], func=AF.Sqrt), ["rvn%d" % col], ["rvn%d" % col])
        S.dve(lambda e: e.reciprocal(out=rvn[0:P, col:col + 1], in_=rvn[0:P, col:col + 1]), ["rvn%d" % col], ["rvn%d" % col])
        S.dve(lambda e: e.scalar_tensor_tensor(out=out_ap, in0=x_ap, scalar=rvn[0:P, col:col + 1], in1=gain[0:P], op0=ALU.mult, op1=ALU.mult), [xname, "rvn%d" % col, oname], [oname])

    tcount = [0]
    tforce = [None]

    def transpose8(src_ap, P, dstT_ap, sname, dname):
        bi = tcount[0] % 2 if tforce[0] is None else tforce[0]
        tcount[0] += 1
        bkb = pbank[bi][:].bitcast(BF16)
        bn = "pb%d" % bi
        for dc in range(8):
            S.pe(lambda e, dc=dc: e.transpose(out=bkb[:, dc * 128:dc * 128 + P], in_=src_ap[:, dc * 128:(dc + 1) * 128], identity=identb[0:P, 0:P]), [sname], [bn])
        S.act(lambda e: e.copy(out=dstT_ap, in_=bkb[:, 0:1024].rearrange("p (j c) -> p j c", j=8)[:, :, 0:P]), [bn], [dname])

    S.barrier(); aoff[0] = mP2
    wglu_d = din("w_glu", [D, 2 * D])
    Wglu = sb("Wglu", [128, 8, 2 * D], BF16)
    m1 = aoff[0]
    load_weight(Wglu, wglu_d, 8, 128, 2 * D, "wg")
    S.barrier(); aoff[0] = m1
    Gt = [sb("Gt%d" % i, [128, D], BF16) for i in range(2)]
    Xt = [sb("Xt%d" % i, [128, D]) for i in range(2)]
    GTt = [sb("GTt%d" % i, [128, 8, 128], BF16) for i in range(2)]
    sgb = [sb("sgb%d" % i, [128, 512]) for i in range(2)]
    mmb = [sb("mmb%d" % i, [128, 512]) for i in range(2)]
    tilesB = [(t, 128) for t in range(0, NT, 128)]
    if KST < 4:
        tilesB = []

    def loadB(i):
        t0, P = tilesB[i]; b = i % 2
        S.dma(Gt[b][0:P], gact[t0:t0 + P, :], reads=["gscr"], writes=["Gt%d" % b])
        S.dma(Xt[b][0:P], xsrc(t0, P), writes=["Xt%d" % b])

    if tilesB:
        loadB(0)
    for i, (t0, P) in enumerate(tilesB):
        b = i % 2
        if i + 1 < len(tilesB):
            loadB(i + 1)
        transpose8(Gt[b][0:P], P, GTt[b][:, :, 0:P], "Gt%d" % b, "GTt%d" % b)
        for nh in range(2):
            j = 2 * i + nh
            vb = pbank[2 + 2 * (j % 3)]; gb = pbank[3 + 2 * (j % 3)]
            vn = "pb%d" % (2 + 2 * (j % 3)); gn = "pb%d" % (3 + 2 * (j % 3))
            for dc in range(8):
                S.pe(lambda e, dc=dc, vb=vb, nh=nh, b=b: e.matmul(vb[0:P, :], lhsT=GTt[b][:, dc, 0:P], rhs=Wglu[:, dc, nh * 512:(nh + 1) * 512], start=(dc == 0), stop=(dc == 7)), ["GTt%d" % b], [vn])
            for dc in range(8):
                S.pe(lambda e, dc=dc, gb=gb, nh=nh, b=b: e.matmul(gb[0:P, :], lhsT=GTt[b][:, dc, 0:P], rhs=Wglu[:, dc, D + nh * 512:D + (nh + 1) * 512], start=(dc == 0), stop=(dc == 7)), ["GTt%d" % b], [gn])
            sg = sgb[j % 2]; mm = mmb[j % 2]
            S.act(lambda e, sg=sg, gb=gb: e.activation(out=sg[0:P], in_=gb[0:P, :], func=AF.Sigmoid), [gn], ["sg%d" % (j % 2)])
            S.dve(lambda e, sg=sg, mm=mm, vb=vb: e.tensor_tensor(out=mm[0:P], in0=vb[0:P, :], in1=sg[0:P], op=ALU.mult), [vn, "sg%d" % (j % 2)], ["mm%d" % (j % 2)])
            S.pool(lambda e, mm=mm, b=b, nh=nh: e.tensor_tensor(out=Xt[b][0:P, nh * 512:(nh + 1) * 512], in0=Xt[b][0:P, nh * 512:(nh + 1) * 512], in1=mm[0:P], op=ALU.add), ["mm%d" % (j % 2), "Xt%d" % b], ["Xt%d" % b])
        S.dma(x1s[t0:t0 + P, :], Xt[b][0:P], reads=["Xt%d" % b], writes=["x1s"])

    wup_d = din("w_up", [2, D, DFF]); wgt_d = din("w_gate", [2, D, DFF]); wdn_d = din("w_down", [2, DFF, D])
    cw_d = din("conv_w", [2, 3, DFF]); cb_d = din("conv_b", [2, DFF]); cvst_d = din("cvst", [2, 2, 2, DFF])
    o_cvp = dout("o_cvp", [2, 2, DFF]); o_cvs = dout("o_cvs", [2, 2, 2, DFF])
    o_yp = dout("o_yp", [HALF, D]); o_ys = dout("o_ys", [128, D])
    NFC = DFF // 128

    def ffn_phase(layer, xin, xout, groups, gain_d, final):
        S.barrier(); aoff[0] = mP2
        tforce[0] = 0
        Wup = sb("Wup", [128, 8, DFF], BF16); Wgt = sb("Wgt", [128, 8, DFF], BF16); Wdn = sb("Wdn", [128, NFC, D], BF16)
        cw = sb("cw", [128, NFC, 3]); cbb = sb("cbb", [128, NFC]); hist = sb("hist", [128, NFC, 2])
        gain = sb("gain", [128, D]); gfin = sb("gfinal", [128, D])
        m1 = aoff[0]
        load_weight(Wup, wup_d[layer], 8, 128, DFF, "wu")
        aoff[0] = m1
        S.barrier()
        load_weight(Wgt, wgt_d[layer], 8, 128, DFF, "wt")
        aoff[0] = m1
        S.barrier()
        load_weight(Wdn, wdn_d[layer], NFC, 128, D, "wd")
        for j in range(3):
            S.dma(cw[:, :, j], cw_d[layer, j].rearrange("(fc f) -> f fc", f=128), writes=["cw"], allow_slow_non_contiguous=True)
        S.dma(cbb[:], cb_d[layer].rearrange("(fc f) -> f fc", f=128), writes=["cbb"], allow_slow_non_contiguous=True)
        S.dma(gain[:], gain_d, writes=["gain"])
        S.dma(gfin[:], gfin_d, writes=["gfin"])
        S.barrier(); aoff[0] = m1
        Xg = [sb("Xg%d" % i, [128, 2, D]) for i in range(2)]
        hsb = sb("hsb", [128, D], BF16)
        hsT = [sb("hsT%d" % i, [128, 8, 256], BF16) for i in range(2)]
        hT = [sb("hT%d" % i, [128, NFC, 256], BF16) for i in range(2)]
        abuf = [sb("abuf%d" % i, [128, 258]) for i in range(4)]
        cbuf = [sb("cbuf%d" % i, [128, 256]) for i in range(4)]
        ybuf = sb("ybuf", [128, D])

        def tiles_of(g):
            n = g["n"]
            return [(0, 128), (1, 128)] if n == 256 else [(0, n)]

        def load(gi):
            g = groups[gi]; b = gi % 2
            if g["n"] == 256:
                S.dma(Xg[b][:], xin[g["t0"]:g["t0"] + 256, :].rearrange("(i p) d -> p i d", i=2), writes=["Xg%d" % b])
            else:
                S.dma(Xg[b][0:g["n"], 0, :], xin[g["t0"]:g["t0"] + g["n"], :], writes=["Xg%d" % b])

        def upgate(gi):
            g = groups[gi]; b = gi % 2; N = g["n"]
            if g["hist"] == "zero":
                S.pool(lambda e: e.memset(hist[:], 0.0), [], ["hist"])
            elif g["hist"] != "carry":
                for j in range(2):
                    S.dma(hist[:, :, j], cvst_d[layer, g["hist"], j].rearrange("(fc f) -> f fc", f=128), writes=["hist"], allow_slow_non_contiguous=True)
            for (ti, P) in tiles_of(g):
                rms_to_bf16(Xg[b][0:P, ti, :], P, gain, hsb[0:P], "Xg%d" % b, "hsb", ti)
                transpose8(hsb[0:P], P, hsT[b][:, :, ti * 128:ti * 128 + P], "hsb", "hsT%d" % b)
            for fc in range(NFC):
                ug = pbank[1 + fc % 4]; un = "pbu%d" % (1 + fc % 4); gn_ = un
                for kc in range(8):
                    S.pe(lambda e, kc=kc, fc=fc, ug=ug: e.matmul(ug[:, 0:N], lhsT=Wup[:, kc, fc * 128:(fc + 1) * 128], rhs=hsT[b][:, kc, 0:N], start=(kc == 0), stop=(kc == 7)), ["hsT%d" % b], [un])
                for kc in range(8):
                    S.pe(lambda e, kc=kc, fc=fc, ug=ug: e.matmul(ug[:, 256:256 + N], lhsT=Wgt[:, kc, fc * 128:(fc + 1) * 128], rhs=hsT[b][:, kc, 0:N], start=(kc == 0), stop=(kc == 7)), ["hsT%d" % b], [gn_])
                ac = abuf[fc % 4]; an = "ab%d" % (fc % 4); ct = cbuf[fc % 4]; cn = "cb%d" % (fc % 4)
                S.pool(lambda e, ac=ac, fc=fc: e.tensor_copy(out=ac[:, 0:2], in_=hist[:, fc, :]), ["hist", an], [an])
                S.act(lambda e, ac=ac, ug=ug: e.copy(out=ac[:, 2:2 + N], in_=ug[:, 0:N]), [un, an], [an])
                S.pool(lambda e, ct=ct, ac=ac, fc=fc: e.tensor_scalar(out=ct[:, 0:N], in0=ac[:, 2:2 + N], scalar1=cw[:, fc, 2:3], scalar2=cbb[:, fc:fc + 1], op0=ALU.mult, op1=ALU.add), [an, "cw", "cbb", cn], [cn])
                S.dve(lambda e, ct=ct, ac=ac, fc=fc: e.scalar_tensor_tensor(out=ct[:, 0:N], in0=ac[:, 1:1 + N], scalar=cw[:, fc, 1:2], in1=ct[:, 0:N], op0=ALU.mult, op1=ALU.add), [an, cn, "cw"], [cn])
                S.dve(lambda e, ct=ct, ac=ac, fc=fc: e.scalar_tensor_tensor(out=ct[:, 0:N], in0=ac[:, 0:N], scalar=cw[:, fc, 0:1], in1=ct[:, 0:N], op0=ALU.mult, op1=ALU.add), [an, cn, "cw"], [cn])
                S.pool(lambda e, ac=ac, fc=fc: e.tensor_copy(out=hist[:, fc, :], in_=ac[:, N:N + 2]), [an, "hist"], ["hist"])
                S.act(lambda e, ct=ct: e.activation(out=ct[:, 0:N], in_=ct[:, 0:N], func=AF.Gelu_apprx_tanh), [cn], [cn])
                S.dve(lambda e, ct=ct, ug=ug, fc=fc: e.tensor_tensor(out=hT[b][:, fc, 0:N], in0=ct[:, 0:N], in1=ug[:, 256:256 + N], op=ALU.mult), [cn, gn_, "hT%d" % b], ["hT%d" % b])
            if g.get("conv_out") is not None:
                for j in range(2):
                    S.dma(g["conv_out"][j].rearrange("(fc f) -> f fc", f=128), hist[:, :, j], reads=["hist"], allow_slow_non_contiguous=True)

        dcount = [0]

        def down(gi):
            g = groups[gi]; b = gi % 2
            if not g["store"]:
                return
            for (ti, P) in tiles_of(g):
                for half in range(2):
                    k = dcount[0] % 3; dcount[0] += 1
                    ob = pbank[5 + k]; on_ = "pb%d" % (5 + k)
                    for fc in range(NFC):
                        S.pe(lambda e, fc=fc, ob=ob, ti=ti, half=half: e.matmul(ob[0:P, :], lhsT=hT[b][:, fc, ti * 128:ti * 128 + P], rhs=Wdn[:, fc, half * 512:(half + 1) * 512], start=(fc == 0), stop=(fc == NFC - 1)), ["hT%d" % b], [on_])
                    S.dve(lambda e, ob=ob, ti=ti, half=half: e.tensor_tensor(out=Xg[b][0:P, ti, half * 512:(half + 1) * 512], in0=Xg[b][0:P, ti, half * 512:(half + 1) * 512], in1=ob[0:P, :], op=ALU.add), [on_, "Xg%d" % b], ["Xg%d" % b])
                t0 = g["t0"] + ti * 128
                if final:
                    dst = g["yout"][ti * 128:ti * 128 + P, :]
                    S.act(lambda e, ti=ti: e.activation(out=ybuf[0:P], in_=Xg[b][0:P, ti, :], func=AF.Square, accum_out=ssn[0:P, 2:3]), ["Xg%d" % b, "ybuf"], ["ybuf", "ssn2"])
                    S.dve(lambda e: e.tensor_scalar(out=rvn[0:P, 2:3], in0=ssn[0:P, 2:3], scalar1=1.0 / D, scalar2=EPS, op0=ALU.mult, op1=ALU.add), ["ssn2"], ["rvn2"])
                    S.act(lambda e: e.activation(out=rvn[0:P, 2:3], in_=rvn[0:P, 2:3], func=AF.Sqrt), ["rvn2"], ["rvn2"])
                    S.dve(lambda e: e.reciprocal(out=rvn[0:P, 2:3], in_=rvn[0:P, 2:3]), ["rvn2"], ["rvn2"])
                    S.dve(lambda e, ti=ti: e.scalar_tensor_tensor(out=ybuf[0:P], in0=Xg[b][0:P, ti, :], scalar=rvn[0:P, 2:3], in1=gfin[0:P], op0=ALU.mult, op1=ALU.mult), ["Xg%d" % b, "rvn2", "ybuf"], ["ybuf"])
                    S.dma(dst, ybuf[0:P], reads=["ybuf"])
                else:
                    S.dma(xout[t0:t0 + P, :], Xg[b][0:P, ti, :], reads=["Xg%d" % b], writes=["xout"])

        if groups:
            load(0)
        for gi in range(len(groups)):
            upgate(gi)
            if gi > 0:
                down(gi - 1)
            if gi + 1 < len(groups):
                load(gi + 1)
        if groups:
            down(len(groups) - 1)
        tforce[0] = None

    groups0 = []
    for gi in range(2 * HALF // 256):
        groups0.append(dict(t0=gi * 256, n=256, hist="zero" if gi == 0 else "carry", store=True,
                            conv_out=o_cvp[0] if gi == 2 * HALF // 256 - 1 else None))
    for sq in range(2):
        groups0.append(dict(t0=2 * HALF + 64 * sq, n=64, hist=sq, store=True, conv_out=o_cvs[0, sq]))
    if KST >= 5:
        ffn_phase(0, x1s, x2s, groups0, gffn_d[0], False)


    wq_d = din("w_qkvf", [D, 3 * D + NH])
    kc_d = din("kcache", [2, HALF, D]); vc_d = din("vcache", [2, HALF, D]); lfc_d = din("lfcache", [2, HALF, NH])
    o_kp = dout("o_kp", [HALF, D]); o_vp = dout("o_vp", [HALF, D]); o_lfp = dout("o_lfp", [HALF, NH])
    o_ks = dout("o_ks", [128, D]); o_vs = dout("o_vs", [128, D]); o_lfs = dout("o_lfs", [128, NH])
    kts = dscr("kts", [NH, HD, NT], BF16); qts = dscr("qts", [NH, HD, NT], BF16); vsb = dscr("vsb", [NT, D], BF16)
    crow = dscr("crow", [NH, NT]); ccol = dscr("ccol", [NT, NH])
    kcT = dscr("kcT", [2, NH, HD, HALF], BF16); vcb = dscr("vcb", [2, HALF, D], BF16); ccc = dscr("ccc", [2, HALF, NH])

    def phase_d():
        S.barrier(); aoff[0] = mP2
        NQ = 3 * D + NH
        Wq = sb("Wq", [128, 8, NQ], BF16)
        gain = sb("gainq", [128, D])
        Lacc = sb("Lacc", [128, NH])
        m1 = aoff[0]
        load_weight(Wq, wq_d, 8, 128, NQ, "wq")
        S.dma(gain[:], gmix_d[1], writes=["gainq"])
        S.barrier(); aoff[0] = m1
        Xq = [sb("Xq%d" % i, [128, D]) for i in range(2)]
        hsq = sb("hsq", [128, D], BF16)
        hsTq = [sb("hsTq%d" % i, [128, 8, 128], BF16) for i in range(2)]
        Qb = [sb("Qb%d" % i, [128, D], BF16) for i in range(2)]
        Kf = [sb("Kf%d" % i, [128, D]) for i in range(2)]
        Kb = [sb("Kb%d" % i, [128, D], BF16) for i in range(2)]
        Vf = [sb("Vf%d" % i, [128, D]) for i in range(2)]
        Vb = [sb("Vb%d" % i, [128, D], BF16) for i in range(2)]
        QTt = [sb("QTt%d" % i, [128, 8, 128], BF16) for i in range(2)]
        KTt = [sb("KTt%d" % i, [128, 8, 128], BF16) for i in range(2)]
        zf = [sb("zf%d" % i, [128, NH]) for i in range(2)]
        lgf = [sb("lgf%d" % i, [128, NH]) for i in range(2)]
        r1 = sb("r1", [128, NH]); L3 = sb("L3", [128, 3, NH], BF16); La3 = sb("La3", [128, 3, NH], BF16); ra = sb("ra", [128, NH])
        csb = [sb("csb%d" % i, [128, NH]) for i in range(2)]
        cTs = [sb("cTs%d" % i, [NH, 128]) for i in range(2)]

        def split3(src, dst3, tmp, P, sname, dname, tname):
            S.dve(lambda e: e.tensor_copy(out=dst3[0:P, 0, :], in_=src[0:P]), [sname, dname], [dname])
            S.dve(lambda e: e.tensor_tensor(out=tmp[0:P], in0=src[0:P], in1=dst3[0:P, 0, :], op=ALU.subtract), [sname, dname, tname], [tname])
            S.dve(lambda e: e.tensor_copy(out=dst3[0:P, 1, :], in_=tmp[0:P]), [tname, dname], [dname])
            S.dve(lambda e: e.tensor_tensor(out=tmp[0:P], in0=tmp[0:P], in1=dst3[0:P, 1, :], op=ALU.subtract), [tname, dname], [tname])
            S.dve(lambda e: e.tensor_copy(out=dst3[0:P, 2, :], in_=tmp[0:P]), [tname, dname], [dname])

        def cumsum_path(lg, lgn, P, b, ccol_dst, crow_dst):
            split3(lg, L3, r1, P, lgn, "L3", "r1")
            split3(Lacc, La3, ra, 128, "Lacc", "La3", "ra")
            cb = pbank[3]
            for j in range(3):
                S.pe(lambda e, j=j: e.matmul(cb[0:P, 0:NH], lhsT=tri[0:P, 0:P], rhs=L3[0:P, j, :], start=(j == 0), stop=False), ["L3", "tri"], ["pb3"])
            for j in range(3):
                S.pe(lambda e, j=j: e.matmul(cb[0:P, 0:NH], lhsT=onesb[:, 0:P], rhs=La3[:, j, :], start=False, stop=(j == 2)), ["La3", "onesb"], ["pb3"])
            S.dve(lambda e: e.tensor_tensor(out=Lacc[0:P], in0=Lacc[0:P], in1=lg[0:P], op=ALU.add), ["Lacc", lgn, "La3"], ["Lacc"])
            S.act(lambda e: e.copy(out=csb[b][0:P], in_=cb[0:P, 0:NH]), ["pb3"], ["csb%d" % b])
            S.dma(ccol_dst, csb[b][0:P], reads=["csb%d" % b], writes=["ccol"])
            if crow_dst is not None:
                bi = tcount[0] % 2; tcount[0] += 1
                tb = pbank[bi]
                S.pe(lambda e: e.transpose(out=tb[0:NH, 0:P], in_=csb[b][0:P, :], identity=ident[0:P, 0:P]), ["csb%d" % b, "ident"], ["pb%d" % bi])
                S.act(lambda e: e.copy(out=cTs[b][0:NH, 0:P], in_=tb[0:NH, 0:P]), ["pb%d" % bi], ["cTs%d" % b])
                S.dma(crow_dst, cTs[b][0:NH, 0:P], reads=["cTs%d" % b], writes=["crow"])

        def kt_dst(scr, t0, P):
            return scr.rearrange("(hp h2) d t -> (h2 d) hp t", h2=2)[:, :, t0:t0 + P]

        tiles = []
        for i in range(2 * HALF // 128):
            tiles.append(dict(kind="full", t0=i * 128, P=128, reset=(i == 0)))
        for sq in range(2):
            for j in range(HALF // 128):
                tiles.append(dict(kind="cache", sq=sq, j=j, P=128, reset=(j == 0)))
            tiles.append(dict(kind="full", t0=2 * HALF + 64 * sq, P=64, reset=False))

        def load(i):
            tl = tiles[i]; b = i % 2; P = tl["P"]
            if tl["kind"] == "full":
                S.dma(Xq[b][0:P], x2s[tl["t0"]:tl["t0"] + P, :], writes=["Xq%d" % b])
            else:
                sq, j = tl["sq"], tl["j"]
                S.dma(Kf[b][:], kc_d[sq, j * 128:(j + 1) * 128, :], writes=["Kf%d" % b])
                S.dma(Vf[b][:], vc_d[sq, j * 128:(j + 1) * 128, :], writes=["Vf%d" % b])
                S.dma(lgf[b][:], lfc_d[sq, j * 128:(j + 1) * 128, :], writes=["lgf%d" % b])

        def do_tile(i, tl):
            b = i % 2; P = tl["P"]
            if i + 1 < len(tiles):
                load(i + 1)
            if tl["reset"]:
                S.dve(lambda e: e.memset(Lacc[:], 0.0), ["Lacc"], ["Lacc"])
            if tl["kind"] == "cache":
                sq, j = tl["sq"], tl["j"]
                S.pool(lambda e, b=b: e.tensor_copy(out=Kb[b][:], in_=Kf[b][:]), ["Kf%d" % b], ["Kb%d" % b])
                S.pool(lambda e, b=b: e.tensor_copy(out=Vb[b][:], in_=Vf[b][:]), ["Vf%d" % b], ["Vb%d" % b])
                transpose8(Kb[b][:], 128, KTt[b][:], "Kb%d" % b, "KTt%d" % b)
                S.dma(kt_dst(kcT[sq], j * 128, 128), KTt[b][:], reads=["KTt%d" % b], writes=["kcT"])
                S.dma(vcb[sq, j * 128:(j + 1) * 128, :], Vb[b][:], reads=["Vb%d" % b], writes=["vcb"])
                cumsum_path(lgf[b], "lgf%d" % b, 128, b, ccc[sq, j * 128:(j + 1) * 128, :], None)
                return
            t0 = tl["t0"]
            rms_to_bf16(Xq[b][0:P], P, gain, hsq[0:P], "Xq%d" % b, "hsq", 3)
            transpose8(hsq[0:P], P, hsTq[b][:, :, 0:P], "hsq", "hsTq%d" % b)
            hT_ = hsTq[b]; hn = "hsTq%d" % b

            def mm(bank, cols, c0, n):
                for kc in range(8):
                    S.pe(lambda e, kc=kc: e.matmul(pbank[bank][0:P, cols], lhsT=hT_[:, kc, 0:P], rhs=Wq[:, kc, c0:c0 + n], start=(kc == 0), stop=(kc == 7)), [hn], ["pb%d" % bank])
            mm(2, slice(0, NH), 3 * D, NH)
            mm(3, slice(0, 512), 0, 512); mm(4, slice(0, 512), 512, 512)
            mm(5, slice(0, 512), D, 512); mm(6, slice(0, 512), D + 512, 512)
            mm(7, slice(0, 512), 2 * D, 512)
            S.dve(lambda e, b=b: e.tensor_tensor(out=zf[b][0:P], in0=pbank[2][0:P, 0:NH], in1=bft[0:P], op=ALU.add), ["pb2", "bft"], ["zf%d" % b])
            mm(2, slice(0, 512), 2 * D + 512, 512)
            S.act(lambda e, b=b: e.activation(out=zf[b][0:P], in_=zf[b][0:P], func=AF.Exp, scale=-1.0), ["zf%d" % b], ["zf%d" % b])
            S.act(lambda e, b=b: e.activation(out=zf[b][0:P], in_=zf[b][0:P], func=AF.Ln, bias=one1[0:P, 0:1]), ["zf%d" % b, "one1"], ["zf%d" % b])
            S.dve(lambda e, b=b: e.tensor_scalar(out=lgf[b][0:P], in0=zf[b][0:P], scalar1=-1.0, scalar2=None, op0=ALU.mult), ["zf%d" % b], ["lgf%d" % b])
            for hf in range(2):
                S.act(lambda e, hf=hf, b=b: e.activation(out=Qb[b][0:P, hf * 512:(hf + 1) * 512], in_=pbank[3 + hf][0:P, :], func=AF.Copy, scale=HD ** -0.5), ["pb%d" % (3 + hf)], ["Qb%d" % b])
                S.dve(lambda e, hf=hf, b=b: e.tensor_copy(out=Kf[b][0:P, hf * 512:(hf + 1) * 512], in_=pbank[5 + hf][0:P, :]), ["pb%d" % (5 + hf)], ["Kf%d" % b])
            S.act(lambda e, b=b: e.copy(out=Vf[b][0:P, 0:512], in_=pbank[7][0:P, :]), ["pb7"], ["Vf%d" % b])
            S.act(lambda e, b=b: e.copy(out=Vf[b][0:P, 512:1024], in_=pbank[2][0:P, :]), ["pb2"], ["Vf%d" % b])
            S.pool(lambda e, b=b: e.tensor_copy(out=Kb[b][0:P], in_=Kf[b][0:P]), ["Kf%d" % b], ["Kb%d" % b])
            S.pool(lambda e, b=b: e.tensor_copy(out=Vb[b][0:P], in_=Vf[b][0:P]), ["Vf%d" % b], ["Vb%d" % b])
            transpose8(Qb[b][0:P], P, QTt[b][:, :, 0:P], "Qb%d" % b, "QTt%d" % b)
            transpose8(Kb[b][0:P], P, KTt[b][:, :, 0:P], "Kb%d" % b, "KTt%d" % b)
            S.dma(kt_dst(qts, t0, P), QTt[b][:, :, 0:P], reads=["QTt%d" % b], writes=["qts"])
            S.dma(kt_dst(kts, t0, P), KTt[b][:, :, 0:P], reads=["KTt%d" % b], writes=["kts"])
            S.dma(vsb[t0:t0 + P, :], Vb[b][0:P], reads=["Vb%d" % b], writes=["vsb"])
            if t0 >= 2 * HALF:
                o0 = t0 - 2 * HALF
                S.dma(o_ks[o0:o0 + P, :], Kf[b][0:P], reads=["Kf%d" % b]); S.dma(o_vs[o0:o0 + P, :], Vf[b][0:P], reads=["Vf%d" % b])
                S.dma(o_lfs[o0:o0 + P, :], lgf[b][0:P], reads=["lgf%d" % b])
            elif t0 >= HALF:
                o0 = t0 - HALF
                S.dma(o_kp[o0:o0 + P, :], Kf[b][0:P], reads=["Kf%d" % b]); S.dma(o_vp[o0:o0 + P, :], Vf[b][0:P], reads=["Vf%d" % b])
                S.dma(o_lfp[o0:o0 + P, :], lgf[b][0:P], reads=["lgf%d" % b])
            cumsum_path(lgf[b], "lgf%d" % b, P, b, ccol[t0:t0 + P, :], crow[:, t0:t0 + P])

        load(0)
        for i, tl in enumerate(tiles):
            do_tile(i, tl)

    if KST >= 6:
        phase_d()


    ats = dscr("ats", [NH, HD, NT], BF16)
    QS0 = HALF - 512

    def phase_e():
        S.barrier(); aoff[0] = mP2
        NKB = 2 * HALF // 128
        KA = [sb("KA%d" % i, [128, 2 * HALF], BF16) for i in range(2)]
        VA = [sb("VA%d" % i, [128, NKB, 65], BF16) for i in range(2)]
        CK = [sb("CK%d" % i, [128, NKB]) for i in range(2)]
        QA = [sb("QA%d" % i, [128, 512], BF16) for i in range(2)]
        crq = [sb("crq%d" % i, [128, 512]) for i in range(2)]
        cref = [sb("cref%d" % i, [128, 1]) for i in range(2)]
        cbias = [sb("cbias%d" % i, [128, NKB]) for i in range(2)]
        dq = sb("dq", [128, 512]); hq = sb("hq", [128, 512], BF16)
        PT = [sb("PT%d" % i, [128, 512], BF16) for i in range(4)]
        Osb = [sb("Osb%d" % i, [128, 512]) for i in range(2)]
        rt = sb("rt", [128, 512]); rt2 = sb("rt2", [128, 512]); rl3 = sb("rl3", [128, 3, 512], BF16)
        ATb = [sb("ATb%d" % i, [128, 512], BF16) for i in range(2)]
        for i in range(2):
            S.pool(lambda e, i=i: e.memset(KA[i][32:33, :], 1.0), [], ["KA%d" % i])
            S.pool(lambda e, i=i: e.memset(KA[i][64:65, :], 1.0), [], ["KA%d" % i])
            S.pool(lambda e, i=i: e.memset(VA[i][:, :, 64:65], 1.0), [], ["VA%d" % i])
            S.pool(lambda e, i=i: e.memset(QA[i][:], 0.0), [], ["QA%d" % i])
        sel = sb("sel", [128, 64], BF16)
        S.pool(lambda e: e.memset(sel[:], 0.0), [], ["sel"])
        S.pool(lambda e: e.memset(sel[64:65, :], 1.0), ["sel"], ["sel"])
        S.pool(lambda e: e.memset(rl3[:], 0.0), [], ["rl3"])
        cnt = {"s": 0, "q": 0, "p": 0}
        pending = []

        def rows67(dst, src, c0, c1, name, wname):
            S.dma(dst[0:32, c0:c1], src[0:32, :], writes=[wname])
            S.dma(dst[33:64, c0:c1], src[32:63, :], writes=[wname])
            S.dma(dst[65:66, c0:c1], src[63:64, :], writes=[wname])

        def attend(h, hb, q_t0, nq, blocks, pm_nb, out_t0):
            qb = cnt["q"] % 2; cnt["q"] += 1
            qn = "QA%d" % qb
            rows67(QA[qb], qts[h, :, q_t0:q_t0 + nq], 0, nq, qn, qn)
            S.dma(crq[qb][32:33, 0:nq], crow[h:h + 1, q_t0:q_t0 + nq], writes=["crq%d" % qb])
            S.dma(crq[qb][64:65, 0:nq], crow[h:h + 1, q_t0:q_t0 + nq], writes=["crq%d" % qb])
            S.dma(cref[qb][:], crow[h:h + 1, q_t0:q_t0 + 1].partition_broadcast(128), writes=["cref%d" % qb])
            S.dve(lambda e: e.tensor_scalar(out=dq[64:65, 0:nq], in0=crq[qb][64:65, 0:nq], scalar1=crq[qb][64:65, 0:1], scalar2=None, op0=ALU.subtract), ["crq%d" % qb, "dq64"], ["dq64"])
            S.dve(lambda e: e.tensor_copy(out=QA[qb][64:65, 0:nq], in_=dq[64:65, 0:nq]), ["dq64", qn], [qn])
            S.dve(lambda e: e.tensor_scalar(out=dq[32:33, 0:nq], in0=crq[qb][32:33, 0:nq], scalar1=crq[qb][32:33, 0:1], scalar2=None, op0=ALU.subtract), ["crq%d" % qb, "dq32"], ["dq32"])
            S.dve(lambda e: e.tensor_copy(out=hq[32:33, 0:nq], in_=dq[32:33, 0:nq]), ["dq32", "hq"], ["hq"])
            S.dve(lambda e: e.tensor_tensor(out=dq[32:33, 0:nq], in0=dq[32:33, 0:nq], in1=hq[32:33, 0:nq], op=ALU.subtract), ["dq32", "hq"], ["dq32"])
            S.dve(lambda e: e.tensor_copy(out=QA[qb][32:33, 0:nq], in_=dq[32:33, 0:nq]), ["dq32", qn], [qn])
            nbk = max(bk for (bk, nk, c0) in blocks) + 1
            S.dve(lambda e: e.tensor_scalar(out=cbias[qb][:, 0:nbk], in0=CK[hb][:, 0:nbk], scalar1=-1.0, scalar2=cref[qb][:, 0:1], op0=ALU.mult, op1=ALU.add), ["CK%d" % hb, "cref%d" % qb, "cbias%d" % qb], ["cbias%d" % qb])
            if pm_nb > 0:
                S.dve(lambda e: e.tensor_scalar(out=cbias[qb][:, 0:pm_nb], in0=cbias[qb][:, 0:pm_nb], scalar1=pmk[:, 0:1], scalar2=None, op0=ALU.add), ["cbias%d" % qb, "pmk"], ["cbias%d" % qb])
            ob = pbank[5 + qb]; obn = "pb%d" % (5 + qb)
            KEL = int(os.environ.get("KE_LVL", "9"))
            if KEL < 2:
                return
            nb_ = len(blocks)
            meta = {}

            def qk(bi_):
                bk, nk, c0 = blocks[bi_]
                si = cnt["s"] % 3; cnt["s"] += 1
                pi = cnt["p"] % 4; cnt["p"] += 1
                sbk = pbank[2 + si]; sn = "pb%d" % (2 + si)
                cc = 0 if c0 is None else c0
                meta[bi_] = (pi, cc)
                S.pe(lambda e: e.matmul(sbk[0:nk, cc:nq], lhsT=KA[hb][0:66, bk * 128:bk * 128 + nk], rhs=QA[qb][0:66, cc:nq], start=True, stop=True), ["KA%d" % hb, qn], [sn])
                if c0 is not None:
                    w = min(128, nq - cc)
                    S.dve(lambda e: e.tensor_tensor(out=sbk[0:nk, cc:cc + w], in0=sbk[0:nk, cc:cc + w], in1=cmask[0:nk, 0:w], op=ALU.add), [sn, "cmask"], [sn])
                S.act(lambda e: e.activation(out=PT[pi][0:nk, cc:nq], in_=sbk[0:nk, cc:nq], func=AF.Exp, bias=cbias[qb][0:nk, bk:bk + 1]), [sn, "cbias%d" % qb], ["PT%d" % pi])

            def pv(bi_):
                bk, nk, c0 = blocks[bi_]
                pi, cc = meta[bi_]
                S.pe(lambda e: e.matmul(ob[0:65, cc:nq], lhsT=VA[hb][0:nk, bk, 0:65], rhs=PT[pi][0:nk, cc:nq], start=(bi_ == 0), stop=(bi_ == nb_ - 1)), ["VA%d" % hb, "PT%d" % pi], [obn])

            LOOK = 2
            for i_ in range(nb_ + LOOK):
                if i_ < nb_:
                    qk(i_)
                if i_ == LOOK and pending:
                    pending.pop()()
                if i_ - LOOK >= 0 and KEL >= 3:
                    pv(i_ - LOOK)
            if pending:
                pending.pop()()
            if KEL < 4:
                return
            pending.append(lambda: finish(qb, nq, ob, obn, h, out_t0))

        def finish(qb, nq, ob, obn, h, out_t0):
            S.act(lambda e: e.copy(out=Osb[qb][0:65, 0:nq], in_=ob[0:65, 0:nq]), [obn], ["Osb%d" % qb])
            S.dve(lambda e: e.reciprocal(out=rt[64:65, 0:nq], in_=Osb[qb][64:65, 0:nq]), ["Osb%d" % qb, "rt"], ["rt"])
            S.dve(lambda e: e.tensor_copy(out=rl3[64:65, 0, 0:nq], in_=rt[64:65, 0:nq]), ["rt", "rl3"], ["rl3"])
            S.dve(lambda e: e.tensor_tensor(out=rt2[64:65, 0:nq], in0=rt[64:65, 0:nq], in1=rl3[64:65, 0, 0:nq], op=ALU.subtract), ["rt", "rl3", "rt2"], ["rt2"])
            S.dve(lambda e: e.tensor_copy(out=rl3[64:65, 1, 0:nq], in_=rt2[64:65, 0:nq]), ["rt2", "rl3"], ["rl3"])
            S.dve(lambda e: e.tensor_tensor(out=rt2[64:65, 0:nq], in0=rt2[64:65, 0:nq], in1=rl3[64:65, 1, 0:nq], op=ALU.subtract), ["rt2", "rl3"], ["rt2"])
            S.dve(lambda e: e.tensor_copy(out=rl3[64:65, 2, 0:nq], in_=rt2[64:65, 0:nq]), ["rt2", "rl3"], ["rl3"])
            bc = pbank[7]
            for j in range(3):
                S.pe(lambda e, j=j: e.matmul(bc[0:64, 0:nq], lhsT=sel[0:65, 0:64], rhs=rl3[0:65, j, 0:nq], start=(j == 0), stop=(j == 2)), ["rl3", "sel"], ["pb7"])
            S.dve(lambda e: e.tensor_tensor(out=ATb[qb][0:64, 0:nq], in0=Osb[qb][0:64, 0:nq], in1=bc[0:64, 0:nq], op=ALU.mult), ["Osb%d" % qb, "pb7", "ATb%d" % qb], ["ATb%d" % qb])
            S.dma(ats[h, :, out_t0:out_t0 + nq], ATb[qb][0:64, 0:nq], reads=["ATb%d" % qb], writes=["ats"])

        hcount = [0]
        def load_head(h, hb):
            rows67(KA[hb], kts[h, :, 0:2 * HALF], 0, 2 * HALF, "k", "KA%d" % hb)
            S.dma(VA[hb][:, :, 0:64], vsb[0:2 * HALF, h * HD:(h + 1) * HD].rearrange("(kb p) d -> p kb d", p=128), writes=["VA%d" % hb])
            S.dma(CK[hb][:, :], ccol[0:2 * HALF, h:h + 1].rearrange("(kb p) o -> p (kb o)", p=128), writes=["CK%d" % hb], allow_slow_non_contiguous=True)

        NHE = int(os.environ.get('KE_H', NH)) if KST >= 7 else 0
        if NHE:
            load_head(0, 0)
        for h in range(NHE):
            hb = hcount[0] % 2; hcount[0] += 1
            if h + 1 < NHE:
                load_head(h + 1, (hb + 1) % 2)
            for j in range(9):
                qs = QS0 + 512 * j
                nfull = qs // 128
                blocks = [(kb, 128, None) for kb in range(nfull)] + [(nfull + d_, 128, 128 * d_) for d_ in range(4)]
                attend(h, hb, qs, 512, blocks, (QS0 // 128) if j == 0 else (HALF // 128), qs)
        for sq in range(int(os.environ.get('KE_S', 2)) if KST >= 7 else 0):
            tn = 2 * HALF + 64 * sq
            for h in range(NH):
                hb = hcount[0] % 2; hcount[0] += 1
                rows67(KA[hb], kcT[sq, h], 0, HALF, "k", "KA%d" % hb)
                rows67(KA[hb], kts[h, :, tn:tn + 64], HALF, HALF + 64, "k", "KA%d" % hb)
                S.dma(VA[hb][:, 0:32, 0:64], vcb[sq, :, h * HD:(h + 1) * HD].rearrange("(kb p) d -> p kb d", p=128), writes=["VA%d" % hb])
                S.dma(VA[hb][0:64, 32, 0:64], vsb[tn:tn + 64, h * HD:(h + 1) * HD], writes=["VA%d" % hb])
                S.dma(CK[hb][:, 0:32], ccc[sq, :, h:h + 1].rearrange("(kb p) o -> p (kb o)", p=128), writes=["CK%d" % hb], allow_slow_non_contiguous=True)
                S.dma(CK[hb][0:64, 32:33], ccol[tn:tn + 64, h:h + 1], writes=["CK%d" % hb], allow_slow_non_contiguous=True)
                blocks = [(kb, 128, None) for kb in range(32)] + [(32, 128, 0)]
                attend(h, hb, tn, 64, blocks, 0, tn)
        if pending:
            pending.pop()()

    if KST >= 7:
        phase_e()

    wo_d = din("w_o", [D, D])

    def phase_f():
        S.barrier(); aoff[0] = mP2
        Wo = sb("Wo", [128, 8, D], BF16)
        m1 = aoff[0]
        load_weight(Wo, wo_d, 8, 128, D, "wo")
        S.barrier(); aoff[0] = m1
        ATt = [sb("ATt%d" % i, [128, 8, 512], BF16) for i in range(2)]
        Xf = [sb("Xf%d" % i, [128, 4, D]) for i in range(2)]
        groups = [(t, 512) for t in range(QS0, 2 * HALF, 512)] + [(2 * HALF, 128)]

        def load(i):
            t0, n = groups[i]; b = i % 2
            for h2 in range(2):
                S.dma(ATt[b][64 * h2:64 * h2 + 64, :, 0:n], ats[:, :, t0:t0 + n].rearrange("(hp h2) d t -> h2 d hp t", h2=2)[h2], writes=["ATt%d" % b])
            S.dma(Xf[b][:, 0:n // 128, :], x2s[t0:t0 + n, :].rearrange("(i p) d -> p i d", p=128), writes=["Xf%d" % b])

        fcount = [0]

        def do(i):
            t0, n = groups[i]; b = i % 2
            if i + 1 < len(groups):
                load(i + 1)
            for ti in range(n // 128):
                for half in range(2):
                    k = fcount[0] % 4; fcount[0] += 1
                    ob = pbank[2 + k]; on_ = "pb%d" % (2 + k)
                    for h in range(8):
                        S.pe(lambda e, h=h, half=half, ob=ob, ti=ti: e.matmul(ob[:, :], lhsT=ATt[b][:, h, ti * 128:(ti + 1) * 128], rhs=Wo[:, h, half * 512:(half + 1) * 512], start=(h == 0), stop=(h == 7)), ["ATt%d" % b], [on_])
                    S.dve(lambda e, half=half, ob=ob, ti=ti: e.tensor_tensor(out=Xf[b][:, ti, half * 512:(half + 1) * 512], in0=Xf[b][:, ti, half * 512:(half + 1) * 512], in1=ob[:, :], op=ALU.add), [on_, "Xf%d" % b], ["Xf%d" % b])
            S.dma(x3s[t0:t0 + n, :].rearrange("(i p) d -> p i d", p=128), Xf[b][:, 0:n // 128, :], reads=["Xf%d" % b], writes=["x3s"])

        load(0)
        for i in range(len(groups)):
            do(i)

    if KST >= 8:
        phase_f()
        groups1 = [dict(t0=HALF - 256, n=256, hist="zero", store=False, conv_out=None)]
        for gi in range(HALF // 256):
            groups1.append(dict(t0=HALF + gi * 256, n=256, hist="carry", store=True, yout=o_yp[gi * 256:(gi + 1) * 256, :],
                                conv_out=o_cvp[1] if gi == HALF // 256 - 1 else None))
        for sq in range(2):
            groups1.append(dict(t0=2 * HALF + 64 * sq, n=64, hist=sq, store=True, yout=o_ys[64 * sq:64 * sq + 64, :], conv_out=o_cvs[1, sq]))
        if not os.environ.get("KSKIPG"):
            ffn_phase(1, x3s, None, groups1, gffn_d[1], True)

    if os.environ.get("KDBG"):
        dbgG = dout("dbgG", [128, D], BF16); dbgX1 = dout("dbgX1", [128, D]); dbgX2 = dout("dbgX2", [128, D])
        S.barrier()
        S.dma(dbgG, gact[2 * HALF:2 * HALF + 128, :])
        S.dma(dbgX1, x1s[2 * HALF:2 * HALF + 128, :])
        S.dma(dbgX2, x2s[2 * HALF:2 * HALF + 128, :])
    S.emit()
    st.close()
    return nc


def _pair_layout(a):
    a = np.asarray(a)
    rest = a.shape[2:]
    a = a.reshape((32, 2, 64) + rest)
    a = np.moveaxis(a, 0, 2)
    return np.ascontiguousarray(a.reshape((128, 32) + rest))


def _from_pair_layout(a):
    a = a.reshape(2, 64, 32)
    return np.ascontiguousarray(np.moveaxis(a, 2, 0).reshape(64, 64))


def _host_inputs(inp):
    f = np.float32
    xpr = np.asarray(inp["x_prompt"], f)
    xsm = np.asarray(inp["x_sample"], f)
    rep = lambda v: np.ascontiguousarray(np.broadcast_to(np.asarray(v, f)[None, :], (128, np.asarray(v).shape[-1])))
    common = dict(
        lam_r=_pair_layout(np.asarray(inp["ssm_a_re"], f)[0]),
        lam_i=_pair_layout(np.asarray(inp["ssm_a_im"], f)[0]),
        lstep=_pair_layout(np.broadcast_to(np.asarray(inp["ssm_log_step"], f)[0][:, None], (64, 64))),
        b_r=_pair_layout(np.asarray(inp["ssm_b_re"], f)[0]),
        b_i=_pair_layout(np.asarray(inp["ssm_b_im"], f)[0]),
        c_r=_pair_layout(np.transpose(np.asarray(inp["ssm_c_re"], f)[0], (0, 2, 1))),
        c_i=_pair_layout(np.transpose(np.asarray(inp["ssm_c_im"], f)[0], (0, 2, 1))),
        gmix=np.stack([rep(inp["norm_mix"][i]) for i in range(2)]),
        gffn=np.stack([rep(inp["norm_ffn"][i]) for i in range(2)]),
        gfin=rep(inp["norm_final"]),
        dskip=rep(inp["ssm_d"][0]),
        ident=np.eye(128, dtype=f),
        tmask=(np.arange(128)[:, None] // 16 <= np.arange(128)[None, :] // 16).astype(f),
        tri=(np.arange(128)[:, None] <= np.arange(128)[None, :]).astype(f),
        cmask=np.where(np.arange(128)[:, None] <= np.arange(128)[None, :], 0.0, -30000.0).astype(f),
        bft=rep(inp["fox_b_f"][0]),
        w_glu=np.ascontiguousarray(np.asarray(inp["ssm_w_glu"], f)[0]),
        w_up=np.asarray(inp["ffn_w_up"], f), w_gate=np.asarray(inp["ffn_w_gate"], f), w_down=np.asarray(inp["ffn_w_down"], f),
        conv_w=np.asarray(inp["ffn_conv_w"], f), conv_b=np.asarray(inp["ffn_conv_b"], f),
        w_qkvf=np.ascontiguousarray(np.asarray(inp["fox_w_qkvf"], f)[0]),
        w_o=np.ascontiguousarray(np.asarray(inp["fox_w_o"], f)[0]),
    )
    maps = []
    for k in range(8):
        b, half = k // 2, k % 2
        prev = xpr[b, 0:HALF] if half == 1 else np.zeros((HALF, D), f)
        own = xpr[b, half * HALF:(half + 1) * HALF]
        m = dict(common)
        m["xp"] = np.ascontiguousarray(np.concatenate([prev, own], 0))
        m["xs"] = np.ascontiguousarray(xsm[2 * k:2 * k + 2].reshape(128, D))
        m["kcache"] = np.ascontiguousarray(np.asarray(inp["cache_fox_k"], f)[0, 2 * k:2 * k + 2].reshape(2, HALF, D))
        m["vcache"] = np.ascontiguousarray(np.asarray(inp["cache_fox_v"], f)[0, 2 * k:2 * k + 2].reshape(2, HALF, D))
        m["lfcache"] = np.ascontiguousarray(np.asarray(inp["cache_fox_logf"], f)[0, 2 * k:2 * k + 2])
        m["pmk"] = np.full((128, 1), 0.0 if half == 1 else -30000.0, f)
        m["cvst"] = np.ascontiguousarray(np.asarray(inp["state_ffn_conv"], f)[:, 2 * k:2 * k + 2])
        m["h0r"] = np.ascontiguousarray(np.stack([_pair_layout(np.asarray(inp["state_ssm_re"], f)[0, 2 * k + s]) for s in range(2)], 1))
        m["h0i"] = np.ascontiguousarray(np.stack([_pair_layout(np.asarray(inp["state_ssm_im"], f)[0, 2 * k + s]) for s in range(2)], 1))
        maps.append(m)
    return maps


_NC_CACHE = {}


def kernel(**inputs):
    maps = _host_inputs(inputs)
    if "nc" not in _NC_CACHE:
        _NC_CACHE["nc"] = build()
    nc = _NC_CACHE["nc"]
    res = run_bass_kernel_spmd(nc, maps, core_ids=list(range(8)))
    R = res.results
    f = np.float32
    y_prompt = np.zeros((4, 8192, D), f); y_sample = np.zeros((16, 64, D), f)
    ssm_re_p = np.zeros((1, 4, 64, 64), f); ssm_im_p = np.zeros((1, 4, 64, 64), f)
    ssm_re_s = np.zeros((1, 16, 64, 64), f); ssm_im_s = np.zeros((1, 16, 64, 64), f)
    k_p = np.zeros((1, 4, 8192, NH, HD), f); v_p = np.zeros((1, 4, 8192, NH, HD), f); lf_p = np.zeros((1, 4, 8192, NH), f)
    k_s = np.zeros((1, 16, 64, NH, HD), f); v_s = np.zeros((1, 16, 64, NH, HD), f); lf_s = np.zeros((1, 16, 64, NH), f)
    cv_p = np.zeros((2, 4, 2, DFF), f); cv_s = np.zeros((2, 16, 2, DFF), f)
    for k in range(8):
        b, half = k // 2, k % 2
        r = R[k]
        if half == 1:
            ssm_re_p[0, b] = _from_pair_layout(r["o_ssm_p"][0]); ssm_im_p[0, b] = _from_pair_layout(r["o_ssm_p"][1])
        if "o_cvp" in r:
            if half == 1:
                cv_p[:, b] = r["o_cvp"]
            cv_s[:, 2 * k:2 * k + 2] = r["o_cvs"]
        if "o_kp" in r:
            sl = slice(half * HALF, (half + 1) * HALF)
            k_p[0, b, sl] = r["o_kp"].reshape(HALF, NH, HD); v_p[0, b, sl] = r["o_vp"].reshape(HALF, NH, HD); lf_p[0, b, sl] = r["o_lfp"]
            k_s[0, 2 * k:2 * k + 2] = r["o_ks"].reshape(2, 64, NH, HD); v_s[0, 2 * k:2 * k + 2] = r["o_vs"].reshape(2, 64, NH, HD)
            lf_s[0, 2 * k:2 * k + 2] = r["o_lfs"].reshape(2, 64, NH)
        if "o_yp" in r:
            y_prompt[b, half * HALF:(half + 1) * HALF] = r["o_yp"]
            y_sample[2 * k:2 * k + 2] = r["o_ys"].reshape(2, 64, D)
        for s in range(2):
            ssm_re_s[0, 2 * k + s] = _from_pair_layout(r["o_ssm_s"][s, 0]); ssm_im_s[0, 2 * k + s] = _from_pair_layout(r["o_ssm_s"][s, 1])
    kernel.last = R
    return (y_prompt, y_sample, ssm_re_p, ssm_im_p, k_p, v_p, lf_p, cv_p, ssm_re_s, ssm_im_s, k_s, v_s, lf_s, cv_s)
```

```python
import contextlib
import math
import numpy as np
import concourse.bass as bass
import concourse.mybir as mybir
from concourse.bass_utils import run_bass_kernel_spmd

F32 = mybir.dt.float32
BF16 = mybir.dt.bfloat16
AF = mybir.ActivationFunctionType
ALU = mybir.AluOpType
AX = mybir.AxisListType

D = 1024
DFF = 2816
NH = 16
HD = 64
HALF = 4096
NPAIR = 32
EPS = 1e-6
TWO_PI = 2.0 * math.pi

ENGS = ("pe", "act", "dve", "pool", "sp")
NO_SELF_WAIT = ("pe", "sp")
DMA_RING = 8


class Op:
    __slots__ = ("eng", "fn", "deps", "dma", "idx", "needed", "cnt", "slot", "tgt")

    def __init__(self, eng, fn, deps, dma, idx):
        self.eng, self.fn, self.deps, self.dma, self.idx = eng, fn, deps, dma, idx
        self.needed = False
        self.cnt = 0
        self.slot = -1
        self.tgt = 0


class Sched:
    def __init__(self, nc):
        self.nc = nc
        self.ops = {e: [] for e in ENGS}
        self.last_w = {}
        self.readers = {}
        self.ndma = {e: 0 for e in ENGS}
        self.last_slot = {}

    def add(self, eng, fn, reads=(), writes=(), dma=False):
        lst = self.ops[eng]
        idx = len(lst)
        deps = set()
        for r in reads:
            lw = self.last_w.get(r)
            if lw is not None:
                deps.add(lw)
        for w in writes:
            lw = self.last_w.get(w)
            if lw is not None:
                deps.add(lw)
            rd = self.readers.get(w)
            if rd:
                for e, i in rd.items():
                    deps.add((e, i))
        op = Op(eng, fn, deps, dma, idx)
        if dma:
            k = self.ndma[eng]
            self.ndma[eng] = k + 1
            op.slot = k % DMA_RING
            op.tgt = 16 * (k // DMA_RING + 1)
            op.needed = True
            prev = self.last_slot.get((eng, op.slot))
            if prev is not None:
                deps.add((eng, prev))
            self.last_slot[(eng, op.slot)] = idx
        lst.append(op)
        for r in reads:
            self.readers.setdefault(r, {})[eng] = idx
        for w in writes:
            self.last_w[w] = (eng, idx)
            self.readers[w] = {}
        return op

    def pe(self, fn, reads=(), writes=()):
        return self.add("pe", fn, reads, writes)

    def act(self, fn, reads=(), writes=()):
        return self.add("act", fn, reads, writes)

    def dve(self, fn, reads=(), writes=()):
        return self.add("dve", fn, reads, writes)

    def pool(self, fn, reads=(), writes=()):
        return self.add("pool", fn, reads, writes)

    def on(self, eng, fn, reads=(), writes=()):
        return self.add(eng, fn, reads, writes)

    def dma(self, out, in_, reads=(), writes=(), q="sp", **kw):
        return self.add(q, lambda e: e.dma_start(out=out, in_=in_, **kw), reads, writes, dma=True)

    def barrier(self):
        deps = set()
        for e in ENGS:
            lst = self.ops[e]
            seen = set()
            got_last = False
            for j in range(len(lst) - 1, -1, -1):
                o = lst[j]
                if o.dma:
                    if o.slot not in seen:
                        seen.add(o.slot)
                        deps.add((e, j))
                elif not got_last and o.fn is not None:
                    got_last = True
                    deps.add((e, j))
                if got_last and len(seen) >= DMA_RING:
                    break
        for e in ENGS:
            self.ops[e].append(Op(e, None, set(deps), False, len(self.ops[e])))
        self.last_w.clear()
        self.readers.clear()

    def emit(self, final_wait_eng="sp"):
        nc = self.nc
        for e in ENGS:
            for op in self.ops[e]:
                for (de, di) in op.deps:
                    dop = self.ops[de][di]
                    if dop.dma:
                        continue
                    if de == e and e in NO_SELF_WAIT and op.fn is not None:
                        continue
                    if de == e and op.fn is None:
                        continue
                    dop.needed = True
        finals = []
        for e in ENGS:
            if self.ops[e]:
                last = None
                for o in reversed(self.ops[e]):
                    if not o.dma and o.fn is not None:
                        last = o
                        break
                if last is not None:
                    last.needed = True
                    finals.append((e, last.idx))
                seen = {}
                for op in self.ops[e]:
                    if op.dma:
                        seen[op.slot] = op.idx
                for sl, ix in seen.items():
                    finals.append((e, ix))
        for e in ENGS:
            c = 0
            for op in self.ops[e]:
                if op.needed and not op.dma and op.fn is not None:
                    c += 1
                    op.cnt = c
        with contextlib.ExitStack() as st:
            esem = {e: st.enter_context(nc.semaphore("s_" + e)) for e in ENGS}
            dsem = {e: [st.enter_context(nc.semaphore("d_%s%d" % (e, i))) for i in range(DMA_RING)]
                    for e in ENGS if self.ndma[e] > 0}
            block = st.enter_context(nc.Block())
            ops = self.ops

            def run(ename, eng):
                waited = {}
                for op in ops[ename]:
                    for (de, di) in sorted(op.deps):
                        dop = ops[de][di]
                        if dop.dma:
                            key = ("d", de, dop.slot)
                            if waited.get(key, 0) >= dop.tgt:
                                continue
                            waited[key] = dop.tgt
                            eng.wait_ge(dsem[de][dop.slot], dop.tgt)
                        else:
                            if de == ename and (ename in NO_SELF_WAIT or op.fn is None):
                                continue
                            key = ("e", de)
                            if waited.get(key, 0) >= dop.cnt:
                                continue
                            waited[key] = dop.cnt
                            eng.wait_ge(esem[de], dop.cnt)
                    if op.fn is None:
                        continue
                    ins = op.fn(eng)
                    if op.dma:
                        ins.then_inc(dsem[ename][op.slot], 16)
                    elif op.needed:
                        ins.then_inc(esem[ename], 1)
                if ename == final_wait_eng:
                    for (de, di) in finals:
                        dop = ops[de][di]
                        if dop.dma:
                            eng.wait_ge(dsem[de][dop.slot], dop.tgt)
                        elif de != ename:
                            eng.wait_ge(esem[de], dop.cnt)

            @block.tensor
            def _(eng):
                run("pe", eng)

            @block.scalar
            def _(eng):
                run("act", eng)

            @block.vector
            def _(eng):
                run("dve", eng)

            @block.gpsimd
            def _(eng):
                run("pool", eng)

            @block.sync
            def _(eng):
                run("sp", eng)


STAGE = 9


def build(stage=STAGE):
    nc = bass.Bass("TRN2", target_bir_lowering=False)
    S = Sched(nc)
    st = contextlib.ExitStack()

    def din(name, shape, dt=F32):
        return nc.dram_tensor(name, list(shape), dt, kind="ExternalInput").ap()

    def dout(name, shape, dt=F32):
        return nc.dram_tensor(name, list(shape), dt, kind="ExternalOutput").ap()

    def dscr(name, shape, dt=F32):
        return nc.dram_tensor(name, list(shape), dt).ap()

    ARENA_F32 = 53200
    arena = st.enter_context(nc.sbuf_tensor("arena", [128, ARENA_F32], F32))
    aoff = [0]

    def sb(name, shape, dt=F32):
        n = 1
        for d_ in shape[1:]:
            n *= d_
        nf = n if dt == F32 else (n + 1) // 2
        nf = (nf + 7) // 8 * 8
        o = aoff[0]
        assert o + nf <= ARENA_F32, ("arena overflow", name, o, nf)
        aoff[0] = o + nf
        v = arena[:, o:o + nf]
        if dt != F32:
            v = v.bitcast(dt)
        v = v[:, 0:n]
        if len(shape) == 2:
            return v
        letters = "abcdefg"[:len(shape) - 1]
        pat = "p (" + " ".join(letters) + ") -> p " + " ".join(letters)
        kw = {letters[i]: shape[1 + i] for i in range(len(shape) - 2)}
        return v.rearrange(pat, **kw)

    def ps(name, shape, dt=F32):
        return st.enter_context(nc.psum_tensor("p_" + name, list(shape), dt))

    xp = din("xp", [2 * HALF, D])
    xs = din("xs", [128, D])
    h0r = din("h0r", [128, 2, NPAIR])
    h0i = din("h0i", [128, 2, NPAIR])
    lam_r_d = din("lam_r", [128, NPAIR])
    lam_i_d = din("lam_i", [128, NPAIR])
    lstep_d = din("lstep", [128, NPAIR])
    b_r_d = din("b_r", [128, NPAIR, 16])
    b_i_d = din("b_i", [128, NPAIR, 16])
    c_r_d = din("c_r", [128, NPAIR, 16])
    c_i_d = din("c_i", [128, NPAIR, 16])
    gmix_d = din("gmix", [2, 128, D])
    gffn_d = din("gffn", [2, 128, D])
    gfin_d = din("gfin", [128, D])
    dskip_d = din("dskip", [128, D])
    ident_d = din("ident", [128, 128])
    tmask_d = din("tmask", [128, 128])

    o_ssm_p = dout("o_ssm_p", [2, 128, NPAIR])
    o_ssm_s = dout("o_ssm_s", [2, 2, 128, NPAIR])
    NT = 2 * HALF + 128
    gact = dscr("gscr", [NT, D], BF16)

    pbank = [ps("pb%d" % i, [128, 512], F32) for i in range(8)]

    ident = sb("ident", [128, 128]); identb = sb("identb", [128, 128], BF16)
    tmask = sb("tmask", [128, 128])
    S.dma(ident[:], ident_d, writes=["ident"])
    S.dma(tmask[:], tmask_d, writes=["tmask"])
    S.dve(lambda e: e.tensor_copy(out=identb[:], in_=ident[:]), ["ident"], ["identb"])
    negpi = sb("negpi", [128, 1])
    S.pool(lambda e: e.memset(negpi[:], -math.pi), [], ["negpi"])
    tri = sb("tri", [128, 128], BF16); onesb = sb("onesb", [128, 128], BF16); cmask = sb("cmask", [128, 128])
    bft = sb("bft", [128, NH]); pmk = sb("pmk", [128, 1]); one1 = sb("one1", [128, 1])
    cstg = sb("cstg", [128, 128])
    S.dma(cstg[:], din("tri", [128, 128]), writes=["cstg"])
    S.dve(lambda e: e.tensor_copy(out=tri[:], in_=cstg[:]), ["cstg"], ["tri"])
    S.pool(lambda e: e.memset(onesb[:], 1.0), [], ["onesb"])
    S.pool(lambda e: e.memset(one1[:], 1.0), [], ["one1"])
    S.dma(cmask[:], din("cmask", [128, 128]), writes=["cmask"])
    S.dma(bft[:], din("bft", [128, NH]), writes=["bft"])
    S.dma(pmk[:], din("pmk", [128, 1]), writes=["pmk"])
    ssn = sb("ssn", [128, 4]); rvn = sb("rvn", [128, 4])
    mP = aoff[0]
    mP2 = mP

    lam_r = sb("lam_r", [128, NPAIR]); lam_i = sb("lam_i", [128, NPAIR]); dtt = sb("dtt", [128, NPAIR])
    lrdt = sb("lrdt", [128, NPAIR]); lidt = sb("lidt", [128, NPAIR])
    Er = sb("Er", [128, 9, NPAIR]); Ei = sb("Ei", [128, 9, NPAIR])
    Emr = sb("Emr", [128, 9, NPAIR]); Emi = sb("Emi", [128, 9, NPAIR])
    x_t = sb("x_t", [128, 8, D])
    _al = [x_t[:, i, 0:NPAIR * 16].rearrange("p (a h) -> p a h", h=16) for i in range(8)]
    Br, Bi, Cr, Ci, Bbr, Bbi, t16a, t16b = _al
    tA = sb("tA", [128, NPAIR]); tB = sb("tB", [128, NPAIR]); tC = sb("tC", [128, NPAIR]); tD = sb("tD", [128, NPAIR])
    zr = sb("zr", [128, NPAIR]); zi = sb("zi", [128, NPAIR])
    S.dma(lam_r[:], lam_r_d, writes=["lam_r"]); S.dma(lam_i[:], lam_i_d, writes=["lam_i"])
    S.dma(dtt[:], lstep_d, writes=["dtt"])
    S.dma(Br[:], b_r_d, writes=["Br"]); S.dma(Bi[:], b_i_d, writes=["Bi"])
    S.dma(Cr[:], c_r_d, writes=["Cr"]); S.dma(Ci[:], c_i_d, writes=["Ci"])
    S.act(lambda e: e.activation(out=dtt[:], in_=dtt[:], func=AF.Exp), ["dtt"], ["dtt"])
    S.dve(lambda e: e.tensor_tensor(out=lrdt[:], in0=lam_r[:], in1=dtt[:], op=ALU.mult), ["lam_r", "dtt"], ["lrdt"])
    S.dve(lambda e: e.tensor_tensor(out=lidt[:], in0=lam_i[:], in1=dtt[:], op=ALU.mult), ["lam_i", "dtt"], ["lidt"])
    S.dve(lambda e: e.memset(Er[:, 0, :], 1.0), [], ["E"])
    S.dve(lambda e: e.memset(Ei[:, 0, :], 0.0), [], ["E"])
    halfpi = sb("halfpi", [128, 1])
    S.pool(lambda e: e.memset(halfpi[:], 0.5 * math.pi), [], ["halfpi"])
    S.act(lambda e: e.activation(out=tA[:], in_=lidt[:], func=AF.Sin, scale=0.125), ["lidt"], ["tA"])
    S.act(lambda e: e.activation(out=tB[:], in_=lidt[:], func=AF.Sin, scale=-0.125, bias=halfpi[:, 0:1]), ["lidt", "halfpi"], ["tB"])
    for _ in range(3):
        S.dve(lambda e: e.tensor_tensor(out=tC[:], in0=tA[:], in1=tB[:], op=ALU.mult), ["tA", "tB", "tC"], ["tC"])
        S.dve(lambda e: e.tensor_tensor(out=tD[:], in0=tA[:], in1=tA[:], op=ALU.mult), ["tA", "tD"], ["tD"])
        S.dve(lambda e: e.tensor_scalar(out=tA[:], in0=tC[:], scalar1=2.0, scalar2=None, op0=ALU.mult), ["tC", "tA"], ["tA"])
        S.dve(lambda e: e.tensor_scalar(out=tB[:], in0=tD[:], scalar1=-2.0, scalar2=1.0, op0=ALU.mult, op1=ALU.add), ["tD", "tB"], ["tB"])
    S.act(lambda e: e.activation(out=tC[:], in_=lrdt[:], func=AF.Exp), ["lrdt", "tC"], ["tC"])
    S.dve(lambda e: e.tensor_tensor(out=Er[:, 1, :], in0=tB[:], in1=tC[:], op=ALU.mult), ["tB", "tC"], ["E"])
    S.dve(lambda e: e.tensor_tensor(out=Ei[:, 1, :], in0=tA[:], in1=tC[:], op=ALU.mult), ["tA", "tC"], ["E"])
    for k in range(2, 9):
        S.dve(lambda e, k=k: e.tensor_tensor(out=tA[:], in0=Er[:, k - 1, :], in1=Er[:, 1, :], op=ALU.mult), ["E", "tA"], ["tA"])
        S.dve(lambda e, k=k: e.tensor_tensor(out=tB[:], in0=Ei[:, k - 1, :], in1=Ei[:, 1, :], op=ALU.mult), ["E", "tB"], ["tB"])
        S.dve(lambda e, k=k: e.tensor_tensor(out=tC[:], in0=Er[:, k - 1, :], in1=Ei[:, 1, :], op=ALU.mult), ["E", "tC"], ["tC"])
        S.dve(lambda e, k=k: e.tensor_tensor(out=tD[:], in0=Ei[:, k - 1, :], in1=Er[:, 1, :], op=ALU.mult), ["E", "tD"], ["tD"])
        S.dve(lambda e, k=k: e.tensor_tensor(out=Er[:, k, :], in0=tA[:], in1=tB[:], op=ALU.subtract), ["tA", "tB", "E"], ["E"])
        S.dve(lambda e, k=k: e.tensor_tensor(out=Ei[:, k, :], in0=tC[:], in1=tD[:], op=ALU.add), ["tC", "tD", "E"], ["E"])
    for k in range(1, 9):
        S.act(lambda e, k=k: e.activation(out=tC[:], in_=lrdt[:], func=AF.Exp, scale=-2.0 * k), ["lrdt", "tC"], ["tC"])
        S.dve(lambda e, k=k: e.tensor_tensor(out=Emr[:, k, :], in0=Er[:, k, :], in1=tC[:], op=ALU.mult), ["E", "tC"], ["E"])
        S.dve(lambda e, k=k: e.scalar_tensor_tensor(out=Emi[:, k, :], in0=Ei[:, k, :], scalar=-1.0, in1=tC[:], op0=ALU.mult, op1=ALU.mult), ["E", "tC"], ["E"])
    S.dve(lambda e: e.tensor_tensor(out=tA[:], in0=lam_r[:], in1=lam_r[:], op=ALU.mult), ["lam_r", "tA"], ["tA"])
    S.dve(lambda e: e.tensor_tensor(out=tB[:], in0=lam_i[:], in1=lam_i[:], op=ALU.mult), ["lam_i", "tB"], ["tB"])
    S.dve(lambda e: e.tensor_tensor(out=tA[:], in0=tA[:], in1=tB[:], op=ALU.add), ["tA", "tB"], ["tA"])
    S.dve(lambda e: e.reciprocal(out=tA[:], in_=tA[:]), ["tA"], ["tA"])
    S.dve(lambda e: e.tensor_scalar(out=tB[:], in0=Er[:, 1, :], scalar1=-1.0, scalar2=None, op0=ALU.add), ["E", "tB"], ["tB"])
    S.dve(lambda e: e.tensor_tensor(out=tC[:], in0=tB[:], in1=lam_r[:], op=ALU.mult), ["tB", "lam_r", "tC"], ["tC"])
    S.dve(lambda e: e.tensor_tensor(out=tD[:], in0=Ei[:, 1, :], in1=lam_i[:], op=ALU.mult), ["E", "lam_i", "tD"], ["tD"])
    S.dve(lambda e: e.tensor_tensor(out=tC[:], in0=tC[:], in1=tD[:], op=ALU.add), ["tC", "tD"], ["tC"])
    S.dve(lambda e: e.tensor_tensor(out=zr[:], in0=tC[:], in1=tA[:], op=ALU.mult), ["tC", "tA"], ["zr"])
    S.dve(lambda e: e.tensor_tensor(out=tC[:], in0=Ei[:, 1, :], in1=lam_r[:], op=ALU.mult), ["E", "lam_r", "tC"], ["tC"])
    S.dve(lambda e: e.tensor_tensor(out=tD[:], in0=tB[:], in1=lam_i[:], op=ALU.mult), ["tB", "lam_i", "tD"], ["tD"])
    S.dve(lambda e: e.tensor_tensor(out=tC[:], in0=tC[:], in1=tD[:], op=ALU.subtract), ["tC", "tD"], ["tC"])
    S.dve(lambda e: e.tensor_tensor(out=zi[:], in0=tC[:], in1=tA[:], op=ALU.mult), ["tC", "tA"], ["zi"])

    import os
    KPREP = int(os.environ.get("KPREP", "9"))

    def bch(ap2, n=16):
        return ap2.unsqueeze(2).to_broadcast([128, ap2.shape[1], n])

    S.dve(lambda e: e.tensor_tensor(out=t16a[:], in0=Br[:], in1=bch(zr[:]), op=ALU.mult), ["Br", "zr", "t16a"], ["t16a"])
    S.dve(lambda e: e.tensor_tensor(out=t16b[:], in0=Bi[:], in1=bch(zi[:]), op=ALU.mult), ["Bi", "zi", "t16b"], ["t16b"])
    S.dve(lambda e: e.tensor_tensor(out=Bbr[:], in0=t16a[:], in1=t16b[:], op=ALU.subtract), ["t16a", "t16b"], ["Bbr"])
    S.dve(lambda e: e.tensor_tensor(out=t16a[:], in0=Bi[:], in1=bch(zr[:]), op=ALU.mult), ["Bi", "zr", "t16a"], ["t16a"])
    S.dve(lambda e: e.tensor_tensor(out=t16b[:], in0=Br[:], in1=bch(zi[:]), op=ALU.mult), ["Br", "zi", "t16b"], ["t16b"])
    S.dve(lambda e: e.tensor_tensor(out=Bbi[:], in0=t16a[:], in1=t16b[:], op=ALU.add), ["t16a", "t16b"], ["Bbi"])

    WinReT = sb("WinReT", [128, NPAIR, 128], BF16)
    WinImT = sb("WinImT", [128, NPAIR, 128], BF16)
    Toep = sb("Toep", [128, 2 * NPAIR, 128], BF16)
    WoRe = sb("WoRe", [128, 2, NPAIR, 128], BF16)
    WoIm = sb("WoIm", [128, 2, NPAIR, 128], BF16)
    S.pool(lambda e: e.memset(WoRe[:], 0.0), [], ["WoRe"])
    S.pool(lambda e: e.memset(WoIm[:], 0.0), [], ["WoIm"])
    NPC = 2
    XRe = sb("XRe", [128, NPC, 8, 16]); XIm = sb("XIm", [128, NPC, 8, 16])
    WiRe = sb("WiRe", [128, NPC, 8, 16]); WiIm = sb("WiIm", [128, NPC, 8, 16])
    WoReF = sb("WoReF", [128, NPC, 8, 16]); WoImF = sb("WoImF", [128, NPC, 8, 16])
    XReB = sb("XReB", [128, NPC, 128], BF16); XImB = sb("XImB", [128, NPC, 128], BF16)
    pa = sb("pa", [128, NPC, 16]); pb_ = sb("pb_", [128, NPC, 16])

    def cmul_into(outr, outi, ar, ai, br, bi, tagr, tagi, negate_im=False):
        S.dve(lambda e: e.tensor_tensor(out=pa[:], in0=br, in1=bch(ar), op=ALU.mult), ["E", "Bbr", "Bbi", "Cr", "Ci", "pa"], ["pa"])
        S.dve(lambda e: e.tensor_tensor(out=pb_[:], in0=bi, in1=bch(ai), op=ALU.mult), ["E", "Bbr", "Bbi", "Cr", "Ci", "pb_"], ["pb_"])
        S.dve(lambda e: e.tensor_tensor(out=outr, in0=pa[:], in1=pb_[:], op=ALU.subtract), ["pa", "pb_"], [tagr])
        S.dve(lambda e: e.tensor_tensor(out=pa[:], in0=bi, in1=bch(ar), op=ALU.mult), ["E", "Bbr", "Bbi", "Cr", "Ci", "pa"], ["pa"])
        S.dve(lambda e: e.tensor_tensor(out=pb_[:], in0=br, in1=bch(ai), op=ALU.mult), ["E", "Bbr", "Bbi", "Cr", "Ci", "pb_"], ["pb_"])
        if negate_im:
            S.dve(lambda e: e.scalar_tensor_tensor(out=outi, in0=pa[:], scalar=-1.0, in1=pb_[:], op0=ALU.mult, op1=ALU.subtract), ["pa", "pb_"], [tagi])
        else:
            S.dve(lambda e: e.tensor_tensor(out=outi, in0=pa[:], in1=pb_[:], op=ALU.add), ["pa", "pb_"], [tagi])

    for q in range(NPAIR // NPC):
        psl = slice(q * NPC, (q + 1) * NPC)
        for s in range(8):
            cmul_into(XRe[:, :, s, :], XIm[:, :, s, :], Emr[:, s + 1, psl], Emi[:, s + 1, psl], Bbr[:, psl, :], Bbi[:, psl, :], "XRe", "XIm")
            cmul_into(WiRe[:, :, s, :], WiIm[:, :, s, :], Er[:, 7 - s, psl], Ei[:, 7 - s, psl], Bbr[:, psl, :], Bbi[:, psl, :], "WiRe", "WiIm")
            cmul_into(WoReF[:, :, s, :], WoImF[:, :, s, :], Er[:, s + 1, psl], Ei[:, s + 1, psl], Cr[:, psl, :], Ci[:, psl, :], "WoReF", "WoImF", negate_im=True)
        for g2 in range(2):
            rs = slice(64 * g2, 64 * g2 + 64)
            S.act(lambda e, psl=psl, rs=rs, g2=g2: e.copy(out=WoRe[rs, g2, psl, :], in_=WoReF[rs].rearrange("p a s h -> p a (s h)")), ["WoReF", "WoRe"], ["WoRe"])
            S.act(lambda e, psl=psl, rs=rs, g2=g2: e.copy(out=WoIm[rs, g2, psl, :], in_=WoImF[rs].rearrange("p a s h -> p a (s h)")), ["WoImF", "WoIm"], ["WoIm"])
        S.act(lambda e: e.copy(out=XReB[:], in_=XRe[:].rearrange("p a s h -> p a (s h)")), ["XRe"], ["XReB"])
        S.act(lambda e: e.copy(out=XImB[:], in_=XIm[:].rearrange("p a s h -> p a (s h)")), ["XIm"], ["XImB"])
        if KPREP <= 3:
            continue
        for a in range(NPC):
            pair = q * NPC + a
            bk = pbank[a % 2]; bkn0 = 'pb%d' % (a % 2)
            S.pe(lambda e, a=a, bk=bk: e.transpose(out=bk[:, 0:128], in_=WiRe[:, a].rearrange("p s h -> p (s h)"), identity=ident[:]), ["WiRe", "ident"], ["pb%d" % (a % 2)])
            S.pe(lambda e, a=a, bk=bk: e.transpose(out=bk[:, 128:256], in_=WiIm[:, a].rearrange("p s h -> p (s h)"), identity=ident[:]), ["WiIm", "ident"], ["pb%d" % (a % 2)])
            for g2 in range(2 if KPREP >= 5 else 0):
                rs = slice(64 * g2, 64 * g2 + 64)
                cs = slice(128 * g2, 128 * g2 + 128)
                bk = pbank[2 + a % 2]
                S.pe(lambda e, a=a, bk=bk, rs=rs, cs=cs, q=q, g2=g2: e.matmul(bk[:, cs], lhsT=XReB[:, a, :], rhs=WoRe[:, g2, q * NPC + a, :], start=True, stop=False),
                     ["XReB", "WoRe"], ["pb%d" % (2 + a % 2)])
                S.pe(lambda e, a=a, bk=bk, rs=rs, cs=cs, q=q, g2=g2: e.matmul(bk[:, cs], lhsT=XImB[:, a, :], rhs=WoIm[:, g2, q * NPC + a, :], start=False, stop=True),
                     ["XImB", "WoIm"], ["pb%d" % (2 + a % 2)])
            S.act(lambda e, pair=pair, bk=pbank[a % 2]: e.copy(out=WinReT[:, pair, :], in_=bk[:, 0:128]), ["pb%d" % (a % 2)], ["WinReT"])
            S.act(lambda e, pair=pair, bk=pbank[a % 2]: e.copy(out=WinImT[:, pair, :], in_=bk[:, 128:256]), ["pb%d" % (a % 2)], ["WinImT"])
            for g2 in range(2 if KPREP >= 5 else 0):
                S.dve(lambda e, pair=pair, g2=g2, bk2=pbank[2 + a % 2]: e.tensor_tensor(out=Toep[:, 2 * pair + g2, :], in0=bk2[:, 128 * g2:128 + 128 * g2], in1=tmask[:], op=ALU.mult),
                      ["pb%d" % (2 + a % 2), "tmask"], ["Toep"])

    M1 = sb("M1", [128, 2, NPAIR]); M2n = sb("M2n", [128, NPAIR]); M2p = sb("M2p", [128, NPAIR])
    S.dve(lambda e: e.tensor_copy(out=M1[:, 0, :], in_=Er[:, 8, :]), ["E"], ["M1"])
    S.dve(lambda e: e.tensor_copy(out=M1[:, 1, :], in_=Er[:, 8, :]), ["E"], ["M1"])
    S.dve(lambda e: e.tensor_copy(out=M2p[:], in_=Ei[:, 8, :]), ["E"], ["M2"])
    S.dve(lambda e: e.tensor_scalar(out=M2n[:], in0=Ei[:, 8, :], scalar1=-1.0, scalar2=None, op0=ALU.mult), ["E"], ["M2"])

    gmix0 = sb("gmix0", [128, D]); dg = sb("dg", [128, D])
    S.dma(gmix0[:], gmix_d[0], writes=["gmix0"])
    S.dma(dg[:], dskip_d, writes=["dg"])
    S.dve(lambda e: e.tensor_tensor(out=dg[:], in0=dg[:], in1=gmix0[:], op=ALU.mult), ["dg", "gmix0"], ["dg"])

    S.dve(lambda e: e.memset(x_t[:, 0, 0:1], 0.0), ["Br", "Bi", "Cr", "Ci", "Bbr", "Bbi", "t16a", "t16b"], ["x_t"])
    ss = sb("ss", [128, 8]); rinv = sb("rinv", [128, 8])
    ub = sb("ub", [128, 64, 8, 16], BF16)
    UT = sb("UT", [128, 64, 128], BF16)
    SH = sb("SH", [128, 2, NPAIR, 128])
    Hpb = sb("Hpb", [128, 2, NPAIR, 128], BF16)
    carry = sb("carry", [128, 2, NPAIR])
    T1 = {"dve": sb("T1d", [128, 2, NPAIR // 2]), "pool": sb("T1p", [128, 2, NPAIR // 2])}
    T2 = {"dve": sb("T2d", [128, 2, NPAIR // 2]), "pool": sb("T2p", [128, 2, NPAIR // 2])}
    ytmps = [sb("ytmp0", [128, 8, 16]), sb("ytmp1", [128, 8, 16])]
    junk = UT[:].rearrange("p g c -> p (g c)")[:, 0:D]
    Gb = ub[:].rearrange("p g t h -> p (g t h)").rearrange("p (t d) -> p t d", t=8)

    def s5_macro(src_ap, nvalid, mt_tag, seqs, g_out_ap, final_out):
        P = nvalid
        S.dma(x_t[0:P], src_ap.rearrange("(c t) d -> c t d", t=8), writes=["x_t"])
        for t in range(8):
            S.act(lambda e, t=t: e.activation(out=junk[0:P], in_=x_t[0:P, t, :], func=AF.Square, accum_out=ss[0:P, t:t + 1]), ["x_t"], ["UT", "ss"])
        S.dve(lambda e: e.tensor_scalar(out=rinv[0:P], in0=ss[0:P], scalar1=1.0 / D, scalar2=EPS, op0=ALU.mult, op1=ALU.add), ["ss"], ["rinv"])
        S.act(lambda e: e.activation(out=rinv[0:P], in_=rinv[0:P], func=AF.Sqrt), ["rinv"], ["rinv"])
        S.dve(lambda e: e.reciprocal(out=rinv[0:P], in_=rinv[0:P]), ["rinv"], ["rinv"])
        for t in range(8):
            S.dve(lambda e, t=t: e.scalar_tensor_tensor(out=ub[0:P, :, t, :], in0=x_t[0:P, t, :].rearrange("p (g h) -> p g h", h=16), scalar=rinv[0:P, t:t + 1],
                                                        in1=gmix0[0:P].rearrange("p (g h) -> p g h", h=16), op0=ALU.mult, op1=ALU.mult),
                  ["x_t", "rinv", "gmix0", "ub"], ["ub"])
        for g8 in range(8):
            bk = pbank[g8 % 2]; bkn = "pb%d" % (g8 % 2)
            bkb = bk[:].bitcast(BF16)
            for j in range(8):
                g = g8 * 8 + j
                S.pe(lambda e, g=g, j=j, bkb=bkb: e.transpose(out=bkb[:, j * 128:j * 128 + P], in_=ub[0:P, g].rearrange("p t h -> p (t h)"), identity=identb[0:P, 0:P]),
                     ["ub", "identb"], [bkn])
            S.act(lambda e, g8=g8, bkb=bkb: e.copy(out=UT[:, g8 * 8:(g8 + 1) * 8, 0:P], in_=bkb[:, 0:1024].rearrange("p (j c) -> p j c", j=8)[:, :, 0:P]), [bkn], ["UT"])
        for q in range(8):
            bre = pbank[4 + 2 * (q % 2)]; bim = pbank[5 + 2 * (q % 2)]
            nre = "pb%d" % (4 + 2 * (q % 2)); nim = "pb%d" % (5 + 2 * (q % 2))
            for a in range(4):
                pair = q * 4 + a
                for g2 in range(2):
                    g = 2 * pair + g2
                    S.pe(lambda e, a=a, g2=g2, g=g, pair=pair, bre=bre: e.matmul(bre[64 * g2:64 * g2 + 64, a * 128:a * 128 + P], lhsT=WinReT[:, pair, 64 * g2:64 * g2 + 64], rhs=UT[:, g, 0:P], start=True, stop=True),
                         ["WinReT", "UT"], [nre])
                    S.pe(lambda e, a=a, g2=g2, g=g, pair=pair, bim=bim: e.matmul(bim[64 * g2:64 * g2 + 64, a * 128:a * 128 + P], lhsT=WinImT[:, pair, 64 * g2:64 * g2 + 64], rhs=UT[:, g, 0:P], start=True, stop=True),
                         ["WinImT", "UT"], [nim])
            S.act(lambda e, q=q, bre=bre: e.copy(out=SH[:, 0, q * 4:q * 4 + 4, 0:P], in_=bre[:].rearrange("p (a c) -> p a c", a=4)[:, :, 0:P]), [nre], ["SHr"])
            S.dve(lambda e, q=q, bim=bim: e.tensor_copy(out=SH[:, 1, q * 4:q * 4 + 4, 0:P], in_=bim[:].rearrange("p (a c) -> p a c", a=4)[:, :, 0:P]), [nim], ["SHi"])
        for (c0, c1, init) in seqs:
            for ei, en in enumerate(("dve", "pool")):
                hp = slice(ei * 16, ei * 16 + 16)
                t1 = T1[en]; t2 = T2[en]
                if init[0] == "h0":
                    S.dma(carry[:, 0, hp], h0r[:, init[1], hp], writes=["carry" + en], q="sp")
                    S.dma(carry[:, 1, hp], h0i[:, init[1], hp], writes=["carry" + en], q="sp")
                elif init[0] == "zero":
                    S.on(en, lambda e, hp=hp: e.memset(carry[:, :, hp], 0.0), [], ["carry" + en])
                for c in range(c0, c1):
                    if c == c0:
                        prev = carry[:, :, hp]; prev_r = carry[:, 0, hp]; prev_i = carry[:, 1, hp]
                    else:
                        prev = SH[:, :, hp, c - 1]; prev_r = SH[:, 0, hp, c - 1]; prev_i = SH[:, 1, hp, c - 1]
                    rd = ["SH" + en, "carry" + en, "M1", "M2", "SHr", "SHi"]
                    if c == c0:
                        S.on(en, lambda e, prev=prev, hp=hp, c=c: e.tensor_copy(out=Hpb[:, :, hp, c], in_=prev), rd, ["Hpb" + en])
                    S.on(en, lambda e, prev=prev, hp=hp, t1=t1: e.tensor_tensor(out=t1[:], in0=prev, in1=M1[:, :, hp], op=ALU.mult), rd + ["t1" + en], ["t1" + en])
                    S.on(en, lambda e, prev_i=prev_i, hp=hp, t2=t2: e.tensor_tensor(out=t2[:, 0, :], in0=prev_i, in1=M2n[:, hp], op=ALU.mult), rd + ["t2" + en], ["t2" + en])
                    S.on(en, lambda e, prev_r=prev_r, hp=hp, t2=t2: e.tensor_tensor(out=t2[:, 1, :], in0=prev_r, in1=M2p[:, hp], op=ALU.mult), rd + ["t2" + en], ["t2" + en])
                    S.on(en, lambda e, t1=t1, t2=t2: e.tensor_tensor(out=t1[:], in0=t1[:], in1=t2[:], op=ALU.add), ["t1" + en, "t2" + en], ["t1" + en])
                    S.on(en, lambda e, t1=t1, hp=hp, c=c: e.tensor_tensor(out=SH[:, :, hp, c], in0=SH[:, :, hp, c], in1=t1[:], op=ALU.add), ["t1" + en, "SHr", "SHi", "SH" + en], ["SH" + en])
                for ri in range(2):
                    S.on(en, lambda e, hp=hp, c0=c0, c1=c1, ri=ri: e.tensor_copy(out=Hpb[:, ri, hp, c0 + 1:c1], in_=SH[:, ri, hp, c0:c1 - 1]), ["SH" + en, "SHr", "SHi"], ["Hpb" + en])
                S.on(en, lambda e, hp=hp, c1=c1: e.tensor_copy(out=carry[:, :, hp], in_=SH[:, :, hp, c1 - 1]), ["SH" + en, "SHr", "SHi"], ["carry" + en])
        for (cl, ore, oim) in final_out:
            S.dma(ore, SH[:, 0, :, cl], reads=["SHdve", "SHpool", "SHr", "SHi"], allow_slow_non_contiguous=True)
            S.dma(oim, SH[:, 1, :, cl], reads=["SHdve", "SHpool", "SHr", "SHi"], allow_slow_non_contiguous=True)
        for g in range(64):
            pair, g2 = g // 2, g % 2
            bk = pbank[2 + g % 2]; bkn = "pb%d" % (2 + g % 2)
            rs = slice(64 * g2, 64 * g2 + 64)
            S.pe(lambda e, g=g, bk=bk: e.matmul(bk[0:P, 0:128], lhsT=UT[:, g, 0:P], rhs=Toep[:, g, :], start=True, stop=False), ["UT", "Toep"], [bkn])
            S.pe(lambda e, pair=pair, rs=rs, bk=bk, g2=g2: e.matmul(bk[0:P, 0:128], lhsT=Hpb[:, 0, pair, 0:P], rhs=WoRe[:, g2, pair, :], start=False, stop=False), ["Hpbdve", "Hpbpool", "WoRe"], [bkn])
            S.pe(lambda e, pair=pair, rs=rs, bk=bk, g2=g2: e.matmul(bk[0:P, 0:128], lhsT=Hpb[:, 1, pair, 0:P], rhs=WoIm[:, g2, pair, :], start=False, stop=True), ["Hpbdve", "Hpbpool", "WoIm"], [bkn])
            dsl = slice(16 * g, 16 * g + 16)
            ytmp = ytmps[g % 2]; yn = "ytmp%d" % (g % 2)
            S.dve(lambda e, dsl=dsl, ytmp=ytmp: e.tensor_tensor(out=ytmp[0:P], in0=x_t[0:P, :, dsl], in1=rinv[0:P].unsqueeze(2).to_broadcast([P, 8, 16]), op=ALU.mult), ["x_t", "rinv"], [yn])
            S.dve(lambda e, dsl=dsl, ytmp=ytmp: e.tensor_tensor(out=ytmp[0:P], in0=ytmp[0:P], in1=dg[0:P, dsl].unsqueeze(1).to_broadcast([P, 8, 16]), op=ALU.mult), [yn, "dg"], [yn])
            S.dve(lambda e, bk=bk, ytmp=ytmp: e.tensor_tensor(out=ytmp[0:P], in0=ytmp[0:P], in1=bk[0:P, 0:128].rearrange("p (t h) -> p t h", t=8), op=ALU.add), [yn, bkn], [yn])
            S.act(lambda e, dsl=dsl, ytmp=ytmp: e.activation(out=Gb[0:P, :, dsl], in_=ytmp[0:P], func=AF.Gelu_apprx_tanh), [yn], ["ub"])
        S.dma(g_out_ap.rearrange("(c t) d -> c t d", t=8), Gb[0:P], reads=["ub"])

    import os
    KST = int(os.environ.get("KSTAGE", "9"))
    for mt in range(8 if KST >= 3 else (1 if KST == 2 else 0)):
        seqs = [(0, 128, ("zero",) if mt == 0 else ("carry",))]
        fin = [(127, o_ssm_p[0], o_ssm_p[1])] if mt == 7 or KST == 2 else []
        s5_macro(xp[mt * 1024:(mt + 1) * 1024, :], 128, mt, seqs, gact[mt * 1024:(mt + 1) * 1024, :], fin)
    if KST >= 1:
      s5_macro(xs, 16, 8, [(0, 8, ("h0", 0)), (8, 16, ("h0", 1))], gact[2 * HALF:2 * HALF + 128, :],
             [(7, o_ssm_s[0, 0], o_ssm_s[0, 1]), (15, o_ssm_s[1, 0], o_ssm_s[1, 1])])


    x1s = dscr("x1s", [NT, D]); x2s = dscr("x2s", [NT, D]); x3s = dscr("x3s", [NT, D])

    def xsrc(t0, n):
        return xp[t0:t0 + n, :] if t0 < 2 * HALF else xs[t0 - 2 * HALF:t0 - 2 * HALF + n, :]

    def load_weight(dst3, src2, nchunks, P, ncol, tag):
        stg = [sb(tag + "_s0", [128, ncol]), sb(tag + "_s1", [128, ncol])]
        engs = ["act", "pool", "dve"]
        for c in range(nchunks):
            b = stg[c % 2]; bn = tag + "_s%d" % (c % 2)
            S.dma(b[0:P], src2[c * P:(c + 1) * P, :], writes=[bn])
            en = engs[c % 3]
            if en == "act":
                S.act(lambda e, b=b, c=c: e.copy(out=dst3[0:P, c, :], in_=b[0:P]), [bn], [tag + str(c)])
            else:
                S.on(en, lambda e, b=b, c=c: e.tensor_copy(out=dst3[0:P, c, :], in_=b[0:P]), [bn], [tag + str(c)])


    def rms_to_bf16(x_ap, P, gain, out_ap, xname, oname, col):
        S.act(lambda e: e.activation(out=out_ap, in_=x_ap, func=AF.Square, accum_out=ssn[0:P, col:col + 1]), [xname], [oname, "ssn%d" % col])
        S.dve(lambda e: e.tensor_scalar(out=rvn[0:P, col:col + 1], in0=ssn[0:P, col:col + 1], scalar1=1.0 / D, scalar2=EPS, op0=ALU.mult, op1=ALU.add), ["ssn%d" % col], ["rvn%d" % col])
        S.act(lambda e: e.activation(out=rvn[0:P, col:col + 1], in_=rvn[0:P, col:col + 1], func=AF.Sqrt), ["rvn%d" % col], ["rvn%d" % col])
        S.dve(lambda e: e.reciprocal(out=rvn[0:P, col:col + 1], in_=rvn[0:P, col:col + 1]), ["rvn%d" % col], ["rvn%d" % col])
        S.dve(lambda e: e.scalar_tensor_tensor(out=out_ap, in0=x_ap, scalar=rvn[0:P, col:col + 1], in1=gain[0:P], op0=ALU.mult, op1=ALU.mult), [xname, "rvn%d" % col, oname], [oname])

    tcount = [0]
    tforce = [None]

    def transpose8(src_ap, P, dstT_ap, sname, dname):
        bi = tcount[0] % 2 if tforce[0] is None else tforce[0]
        tcount[0] += 1
        bkb = pbank[bi][:].bitcast(BF16)
        bn = "pb%d" % bi
        for dc in range(8):
            S.pe(lambda e, dc=dc: e.transpose(out=bkb[:, dc * 128:dc * 128 + P], in_=src_ap[:, dc * 128:(dc + 1) * 128], identity=identb[0:P, 0:P]), [sname], [bn])
        S.act(lambda e: e.copy(out=dstT_ap, in_=bkb[:, 0:1024].rearrange("p (j c) -> p j c", j=8)[:, :, 0:P]), [bn], [dname])

    S.barrier(); aoff[0] = mP2
    wglu_d = din("w_glu", [D, 2 * D])
    Wglu = sb("Wglu", [128, 8, 2 * D], BF16)
    m1 = aoff[0]
    load_weight(Wglu, wglu_d, 8, 128, 2 * D, "wg")
    S.barrier(); aoff[0] = m1
    Gt = [sb("Gt%d" % i, [128, D], BF16) for i in range(2)]
    Xt = [sb("Xt%d" % i, [128, D]) for i in range(2)]
    GTt = [sb("GTt%d" % i, [128, 8, 128], BF16) for i in range(2)]
    sgb = [sb("sgb%d" % i, [128, 512]) for i in range(2)]
    mmb = [sb("mmb%d" % i, [128, 512]) for i in range(2)]
    tilesB = [(t, 128) for t in range(0, NT, 128)]
    if KST < 4:
        tilesB = []

    def loadB(i):
        t0, P = tilesB[i]; b = i % 2
        S.dma(Gt[b][0:P], gact[t0:t0 + P, :], reads=["gscr"], writes=["Gt%d" % b])
        S.dma(Xt[b][0:P], xsrc(t0, P), writes=["Xt%d" % b])

    if tilesB:
        loadB(0)
    for i, (t0, P) in enumerate(tilesB):
        b = i % 2
        if i + 1 < len(tilesB):
            loadB(i + 1)
        transpose8(Gt[b][0:P], P, GTt[b][:, :, 0:P], "Gt%d" % b, "GTt%d" % b)
        for nh in range(2):
            j = 2 * i + nh
            vb = pbank[2 + 2 * (j % 3)]; gb = pbank[3 + 2 * (j % 3)]
            vn = "pb%d" % (2 + 2 * (j % 3)); gn = "pb%d" % (3 + 2 * (j % 3))
            for dc in range(8):
                S.pe(lambda e, dc=dc, vb=vb, nh=nh, b=b: e.matmul(vb[0:P, :], lhsT=GTt[b][:, dc, 0:P], rhs=Wglu[:, dc, nh * 512:(nh + 1) * 512], start=(dc == 0), stop=(dc == 7)), ["GTt%d" % b], [vn])
            for dc in range(8):
                S.pe(lambda e, dc=dc, gb=gb, nh=nh, b=b: e.matmul(gb[0:P, :], lhsT=GTt[b][:, dc, 0:P], rhs=Wglu[:, dc, D + nh * 512:D + (nh + 1) * 512], start=(dc == 0), stop=(dc == 7)), ["GTt%d" % b], [gn])
            sg = sgb[j % 2]; mm = mmb[j % 2]
            S.act(lambda e, sg=sg, gb=gb: e.activation(out=sg[0:P], in_=gb[0:P, :], func=AF.Sigmoid), [gn], ["sg%d" % (j % 2)])
            S.dve(lambda e, sg=sg, mm=mm, vb=vb: e.tensor_tensor(out=mm[0:P], in0=vb[0:P, :], in1=sg[0:P], op=ALU.mult), [vn, "sg%d" % (j % 2)], ["mm%d" % (j % 2)])
            S.pool(lambda e, mm=mm, b=b, nh=nh: e.tensor_tensor(out=Xt[b][0:P, nh * 512:(nh + 1) * 512], in0=Xt[b][0:P, nh * 512:(nh + 1) * 512], in1=mm[0:P], op=ALU.add), ["mm%d" % (j % 2), "Xt%d" % b], ["Xt%d" % b])
        S.dma(x1s[t0:t0 + P, :], Xt[b][0:P], reads=["Xt%d" % b], writes=["x1s"])

    wup_d = din("w_up", [2, D, DFF]); wgt_d = din("w_gate", [2, D, DFF]); wdn_d = din("w_down", [2, DFF, D])
    cw_d = din("conv_w", [2, 3, DFF]); cb_d = din("conv_b", [2, DFF]); cvst_d = din("cvst", [2, 2, 2, DFF])
    o_cvp = dout("o_cvp", [2, 2, DFF]); o_cvs = dout("o_cvs", [2, 2, 2, DFF])
    o_yp = dout("o_yp", [HALF, D]); o_ys = dout("o_ys", [128, D])
    NFC = DFF // 128

    def ffn_phase(layer, xin, xout, groups, gain_d, final):
        S.barrier(); aoff[0] = mP2
        tforce[0] = 0
        Wup = sb("Wup", [128, 8, DFF], BF16); Wgt = sb("Wgt", [128, 8, DFF], BF16); Wdn = sb("Wdn", [128, NFC, D], BF16)
        cw = sb("cw", [128, NFC, 3]); cbb = sb("cbb", [128, NFC]); hist = sb("hist", [128, NFC, 2])
        gain = sb("gain", [128, D]); gfin = sb("gfinal", [128, D])
        m1 = aoff[0]
        load_weight(Wup, wup_d[layer], 8, 128, DFF, "wu")
        aoff[0] = m1
        S.barrier()
        load_weight(Wgt, wgt_d[layer], 8, 128, DFF, "wt")
        aoff[0] = m1
        S.barrier()
        load_weight(Wdn, wdn_d[layer], NFC, 128, D, "wd")
        for j in range(3):
            S.dma(cw[:, :, j], cw_d[layer, j].rearrange("(fc f) -> f fc", f=128), writes=["cw"], allow_slow_non_contiguous=True)
        S.dma(cbb[:], cb_d[layer].rearrange("(fc f) -> f fc", f=128), writes=["cbb"], allow_slow_non_contiguous=True)
        S.dma(gain[:], gain_d, writes=["gain"])
        S.dma(gfin[:], gfin_d, writes=["gfin"])
        S.barrier(); aoff[0] = m1
        Xg = [sb("Xg%d" % i, [128, 2, D]) for i in range(2)]
        hsb = sb("hsb", [128, D], BF16)
        hsT = [sb("hsT%d" % i, [128, 8, 256], BF16) for i in range(2)]
        hT = [sb("hT%d" % i, [128, NFC, 256], BF16) for i in range(2)]
        abuf = [sb("abuf%d" % i, [128, 258]) for i in range(4)]
        cbuf = [sb("cbuf%d" % i, [128, 256]) for i in range(4)]
        ybuf = sb("ybuf", [128, D])

        def tiles_of(g):
            n = g["n"]
            return [(0, 128), (1, 128)] if n == 256 else [(0, n)]

        def load(gi):
            g = groups[gi]; b = gi % 2
            if g["n"] == 256:
                S.dma(Xg[b][:], xin[g["t0"]:g["t0"] + 256, :].rearrange("(i p) d -> p i d", i=2), writes=["Xg%d" % b])
            else:
                S.dma(Xg[b][0:g["n"], 0, :], xin[g["t0"]:g["t0"] + g["n"], :], writes=["Xg%d" % b])

        def upgate(gi):
            g = groups[gi]; b = gi % 2; N = g["n"]
            if g["hist"] == "zero":
                S.pool(lambda e: e.memset(hist[:], 0.0), [], ["hist"])
            elif g["hist"] != "carry":
                for j in range(2):
                    S.dma(hist[:, :, j], cvst_d[layer, g["hist"], j].rearrange("(fc f) -> f fc", f=128), writes=["hist"], allow_slow_non_contiguous=True)
            for (ti, P) in tiles_of(g):
                rms_to_bf16(Xg[b][0:P, ti, :], P, gain, hsb[0:P], "Xg%d" % b, "hsb", ti)
                transpose8(hsb[0:P], P, hsT[b][:, :, ti * 128:ti * 128 + P], "hsb", "hsT%d" % b)
            deferred = []
            for fc in range(NFC):
                ug = pbank[1 + fc % 4]; un = "pbu%d" % (1 + fc % 4); gn_ = un
                for kc in range(8):
                    S.pe(lambda e, kc=kc, fc=fc, ug=ug: e.matmul(ug[:, 0:N], lhsT=Wup[:, kc, fc * 128:(fc + 1) * 128], rhs=hsT[b][:, kc, 0:N], start=(kc == 0), stop=(kc == 7)), ["hsT%d" % b], [un])
                for kc in range(8):
                    S.pe(lambda e, kc=kc, fc=fc, ug=ug: e.matmul(ug[:, 256:256 + N], lhsT=Wgt[:, kc, fc * 128:(fc + 1) * 128], rhs=hsT[b][:, kc, 0:N], start=(kc == 0), stop=(kc == 7)), ["hsT%d" % b], [gn_])
                ac = abuf[fc % 4]; an = "ab%d" % (fc % 4); ct = cbuf[fc % 4]; cn = "cb%d" % (fc % 4)
                S.pool(lambda e, ac=ac, fc=fc: e.tensor_copy(out=ac[:, 0:2], in_=hist[:, fc, :]), ["hist", an], [an])
                S.act(lambda e, ac=ac, ug=ug: e.copy(out=ac[:, 2:2 + N], in_=ug[:, 0:N]), [un, an], [an])
                S.pool(lambda e, ct=ct, ac=ac, fc=fc: e.tensor_scalar(out=ct[:, 0:N], in0=ac[:, 2:2 + N], scalar1=cw[:, fc, 2:3], scalar2=cbb[:, fc:fc + 1], op0=ALU.mult, op1=ALU.add), [an, "cw", "cbb", cn], [cn])
                S.dve(lambda e, ct=ct, ac=ac, fc=fc: e.scalar_tensor_tensor(out=ct[:, 0:N], in0=ac[:, 1:1 + N], scalar=cw[:, fc, 1:2], in1=ct[:, 0:N], op0=ALU.mult, op1=ALU.add), [an, cn, "cw"], [cn])
                S.dve(lambda e, ct=ct, ac=ac, fc=fc: e.scalar_tensor_tensor(out=ct[:, 0:N], in0=ac[:, 0:N], scalar=cw[:, fc, 0:1], in1=ct[:, 0:N], op0=ALU.mult, op1=ALU.add), [an, cn, "cw"], [cn])
                S.pool(lambda e, ac=ac, fc=fc: e.tensor_copy(out=hist[:, fc, :], in_=ac[:, N:N + 2]), [an, "hist"], ["hist"])
                if deferred:
                    deferred.pop()()

                def tail(ct=ct, ug=ug, fc=fc, cn=cn, gn_=gn_):
                    S.act(lambda e: e.activation(out=ct[:, 0:N], in_=ct[:, 0:N], func=AF.Gelu_apprx_tanh), [cn], [cn])
                    S.dve(lambda e: e.tensor_tensor(out=hT[b][:, fc, 0:N], in0=ct[:, 0:N], in1=ug[:, 256:256 + N], op=ALU.mult), [cn, gn_, "hT%d" % b], ["hT%d" % b])
                deferred.append(tail)
            if deferred:
                deferred.pop()()
            if g.get("conv_out") is not None:
                for j in range(2):
                    S.dma(g["conv_out"][j].rearrange("(fc f) -> f fc", f=128), hist[:, :, j], reads=["hist"], allow_slow_non_contiguous=True)

        dcount = [0]

        def down(gi):
            g = groups[gi]; b = gi % 2
            if not g["store"]:
                return
            for (ti, P) in tiles_of(g):
                for half in range(2):
                    k = dcount[0] % 3; dcount[0] += 1
                    ob = pbank[5 + k]; on_ = "pb%d" % (5 + k)
                    for fc in range(NFC):
                        S.pe(lambda e, fc=fc, ob=ob, ti=ti, half=half: e.matmul(ob[0:P, :], lhsT=hT[b][:, fc, ti * 128:ti * 128 + P], rhs=Wdn[:, fc, half * 512:(half + 1) * 512], start=(fc == 0), stop=(fc == NFC - 1)), ["hT%d" % b], [on_])
                    S.dve(lambda e, ob=ob, ti=ti, half=half: e.tensor_tensor(out=Xg[b][0:P, ti, half * 512:(half + 1) * 512], in0=Xg[b][0:P, ti, half * 512:(half + 1) * 512], in1=ob[0:P, :], op=ALU.add), [on_, "Xg%d" % b], ["Xg%d" % b])
                t0 = g["t0"] + ti * 128
                if final:
                    dst = g["yout"][ti * 128:ti * 128 + P, :]
                    S.act(lambda e, ti=ti: e.activation(out=ybuf[0:P], in_=Xg[b][0:P, ti, :], func=AF.Square, accum_out=ssn[0:P, 2:3]), ["Xg%d" % b, "ybuf"], ["ybuf", "ssn2"])
                    S.dve(lambda e: e.tensor_scalar(out=rvn[0:P, 2:3], in0=ssn[0:P, 2:3], scalar1=1.0 / D, scalar2=EPS, op0=ALU.mult, op1=ALU.add), ["ssn2"], ["rvn2"])
                    S.act(lambda e: e.activation(out=rvn[0:P, 2:3], in_=rvn[0:P, 2:3], func=AF.Sqrt), ["rvn2"], ["rvn2"])
                    S.dve(lambda e: e.reciprocal(out=rvn[0:P, 2:3], in_=rvn[0:P, 2:3]), ["rvn2"], ["rvn2"])
                    S.dve(lambda e, ti=ti: e.scalar_tensor_tensor(out=ybuf[0:P], in0=Xg[b][0:P, ti, :], scalar=rvn[0:P, 2:3], in1=gfin[0:P], op0=ALU.mult, op1=ALU.mult), ["Xg%d" % b, "rvn2", "ybuf"], ["ybuf"])
                    S.dma(dst, ybuf[0:P], reads=["ybuf"])
                else:
                    S.dma(xout[t0:t0 + P, :], Xg[b][0:P, ti, :], reads=["Xg%d" % b], writes=["xout"])

        if groups:
            load(0)
        for gi in range(len(groups)):
            upgate(gi)
            if gi > 0:
                down(gi - 1)
            if gi + 1 < len(groups):
                load(gi + 1)
        if groups:
            down(len(groups) - 1)
        tforce[0] = None

    groups0 = []
    for gi in range(2 * HALF // 256):
        groups0.append(dict(t0=gi * 256, n=256, hist="zero" if gi == 0 else "carry", store=True,
                            conv_out=o_cvp[0] if gi == 2 * HALF // 256 - 1 else None))
    for sq in range(2):
        groups0.append(dict(t0=2 * HALF + 64 * sq, n=64, hist=sq, store=True, conv_out=o_cvs[0, sq]))
    if KST >= 5:
        ffn_phase(0, x1s, x2s, groups0, gffn_d[0], False)


    wq_d = din("w_qkvf", [D, 3 * D + NH])
    kc_d = din("kcache", [2, HALF, D]); vc_d = din("vcache", [2, HALF, D]); lfc_d = din("lfcache", [2, HALF, NH])
    o_kp = dout("o_kp", [HALF, D]); o_vp = dout("o_vp", [HALF, D]); o_lfp = dout("o_lfp", [HALF, NH])
    o_ks = dout("o_ks", [128, D]); o_vs = dout("o_vs", [128, D]); o_lfs = dout("o_lfs", [128, NH])
    kts = dscr("kts", [NH, HD, NT], BF16); qts = dscr("qts", [NH, HD, NT], BF16); vsb = dscr("vsb", [NT, D], BF16)
    crow = dscr("crow", [NH, NT]); ccol = dscr("ccol", [NT, NH])
    kcT = dscr("kcT", [2, NH, HD, HALF], BF16); vcb = dscr("vcb", [2, HALF, D], BF16); ccc = dscr("ccc", [2, HALF, NH])

    def phase_d():
        S.barrier(); aoff[0] = mP2
        NQ = 3 * D + NH
        Wq = sb("Wq", [128, 8, NQ], BF16)
        gain = sb("gainq", [128, D])
        Lacc = sb("Lacc", [128, NH])
        m1 = aoff[0]
        load_weight(Wq, wq_d, 8, 128, NQ, "wq")
        S.dma(gain[:], gmix_d[1], writes=["gainq"])
        S.barrier(); aoff[0] = m1
        Xq = [sb("Xq%d" % i, [128, D]) for i in range(2)]
        hsq = sb("hsq", [128, D], BF16)
        hsTq = [sb("hsTq%d" % i, [128, 8, 128], BF16) for i in range(2)]
        Qb = [sb("Qb%d" % i, [128, D], BF16) for i in range(2)]
        Kf = [sb("Kf%d" % i, [128, D]) for i in range(2)]
        Kb = [sb("Kb%d" % i, [128, D], BF16) for i in range(2)]
        Vf = [sb("Vf%d" % i, [128, D]) for i in range(2)]
        Vb = [sb("Vb%d" % i, [128, D], BF16) for i in range(2)]
        QTt = [sb("QTt%d" % i, [128, 8, 128], BF16) for i in range(2)]
        KTt = [sb("KTt%d" % i, [128, 8, 128], BF16) for i in range(2)]
        zf = [sb("zf%d" % i, [128, NH]) for i in range(2)]
        lgf = [sb("lgf%d" % i, [128, NH]) for i in range(2)]
        r1 = sb("r1", [128, NH]); L3 = sb("L3", [128, 3, NH], BF16); La3 = sb("La3", [128, 3, NH], BF16); ra = sb("ra", [128, NH])
        csb = [sb("csb%d" % i, [128, NH]) for i in range(2)]
        cTs = [sb("cTs%d" % i, [NH, 128]) for i in range(2)]

        def split3(src, dst3, tmp, P, sname, dname, tname):
            S.dve(lambda e: e.tensor_copy(out=dst3[0:P, 0, :], in_=src[0:P]), [sname, dname], [dname])
            S.dve(lambda e: e.tensor_tensor(out=tmp[0:P], in0=src[0:P], in1=dst3[0:P, 0, :], op=ALU.subtract), [sname, dname, tname], [tname])
            S.dve(lambda e: e.tensor_copy(out=dst3[0:P, 1, :], in_=tmp[0:P]), [tname, dname], [dname])
            S.dve(lambda e: e.tensor_tensor(out=tmp[0:P], in0=tmp[0:P], in1=dst3[0:P, 1, :], op=ALU.subtract), [tname, dname], [tname])
            S.dve(lambda e: e.tensor_copy(out=dst3[0:P, 2, :], in_=tmp[0:P]), [tname, dname], [dname])

        def cumsum_path(lg, lgn, P, b, ccol_dst, crow_dst):
            split3(lg, L3, r1, P, lgn, "L3", "r1")
            split3(Lacc, La3, ra, 128, "Lacc", "La3", "ra")
            cb = pbank[3]
            for j in range(3):
                S.pe(lambda e, j=j: e.matmul(cb[0:P, 0:NH], lhsT=tri[0:P, 0:P], rhs=L3[0:P, j, :], start=(j == 0), stop=False), ["L3", "tri"], ["pb3"])
            for j in range(3):
                S.pe(lambda e, j=j: e.matmul(cb[0:P, 0:NH], lhsT=onesb[:, 0:P], rhs=La3[:, j, :], start=False, stop=(j == 2)), ["La3", "onesb"], ["pb3"])
            S.dve(lambda e: e.tensor_tensor(out=Lacc[0:P], in0=Lacc[0:P], in1=lg[0:P], op=ALU.add), ["Lacc", lgn, "La3"], ["Lacc"])
            S.act(lambda e: e.copy(out=csb[b][0:P], in_=cb[0:P, 0:NH]), ["pb3"], ["csb%d" % b])
            S.dma(ccol_dst, csb[b][0:P], reads=["csb%d" % b], writes=["ccol"])
            if crow_dst is not None:
                bi = tcount[0] % 2; tcount[0] += 1
                tb = pbank[bi]
                S.pe(lambda e: e.transpose(out=tb[0:NH, 0:P], in_=csb[b][0:P, :], identity=ident[0:P, 0:P]), ["csb%d" % b, "ident"], ["pb%d" % bi])
                S.act(lambda e: e.copy(out=cTs[b][0:NH, 0:P], in_=tb[0:NH, 0:P]), ["pb%d" % bi], ["cTs%d" % b])
                S.dma(crow_dst, cTs[b][0:NH, 0:P], reads=["cTs%d" % b], writes=["crow"])

        def kt_dst(scr, t0, P):
            return scr.rearrange("(hp h2) d t -> (h2 d) hp t", h2=2)[:, :, t0:t0 + P]

        tiles = []
        for i in range(2 * HALF // 128):
            tiles.append(dict(kind="full", t0=i * 128, P=128, reset=(i == 0)))
        for sq in range(2):
            for j in range(HALF // 128):
                tiles.append(dict(kind="cache", sq=sq, j=j, P=128, reset=(j == 0)))
            tiles.append(dict(kind="full", t0=2 * HALF + 64 * sq, P=64, reset=False))

        def load(i):
            tl = tiles[i]; b = i % 2; P = tl["P"]
            if tl["kind"] == "full":
                S.dma(Xq[b][0:P], x2s[tl["t0"]:tl["t0"] + P, :], writes=["Xq%d" % b])
            else:
                sq, j = tl["sq"], tl["j"]
                S.dma(Kf[b][:], kc_d[sq, j * 128:(j + 1) * 128, :], writes=["Kf%d" % b])
                S.dma(Vf[b][:], vc_d[sq, j * 128:(j + 1) * 128, :], writes=["Vf%d" % b])
                S.dma(lgf[b][:], lfc_d[sq, j * 128:(j + 1) * 128, :], writes=["lgf%d" % b])

        def do_tile(i, tl):
            b = i % 2; P = tl["P"]
            if i + 1 < len(tiles):
                load(i + 1)
            if tl["reset"]:
                S.dve(lambda e: e.memset(Lacc[:], 0.0), ["Lacc"], ["Lacc"])
            if tl["kind"] == "cache":
                sq, j = tl["sq"], tl["j"]
                S.pool(lambda e, b=b: e.tensor_copy(out=Kb[b][:], in_=Kf[b][:]), ["Kf%d" % b], ["Kb%d" % b])
                S.pool(lambda e, b=b: e.tensor_copy(out=Vb[b][:], in_=Vf[b][:]), ["Vf%d" % b], ["Vb%d" % b])
                transpose8(Kb[b][:], 128, KTt[b][:], "Kb%d" % b, "KTt%d" % b)
                S.dma(kt_dst(kcT[sq], j * 128, 128), KTt[b][:], reads=["KTt%d" % b], writes=["kcT"])
                S.dma(vcb[sq, j * 128:(j + 1) * 128, :], Vb[b][:], reads=["Vb%d" % b], writes=["vcb"])
                cumsum_path(lgf[b], "lgf%d" % b, 128, b, ccc[sq, j * 128:(j + 1) * 128, :], None)
                return
            t0 = tl["t0"]
            rms_to_bf16(Xq[b][0:P], P, gain, hsq[0:P], "Xq%d" % b, "hsq", 3)
            transpose8(hsq[0:P], P, hsTq[b][:, :, 0:P], "hsq", "hsTq%d" % b)
            hT_ = hsTq[b]; hn = "hsTq%d" % b

            def mm(bank, cols, c0, n):
                for kc in range(8):
                    S.pe(lambda e, kc=kc: e.matmul(pbank[bank][0:P, cols], lhsT=hT_[:, kc, 0:P], rhs=Wq[:, kc, c0:c0 + n], start=(kc == 0), stop=(kc == 7)), [hn], ["pb%d" % bank])
            mm(2, slice(0, NH), 3 * D, NH)
            mm(3, slice(0, 512), 0, 512); mm(4, slice(0, 512), 512, 512)
            mm(5, slice(0, 512), D, 512); mm(6, slice(0, 512), D + 512, 512)
            mm(7, slice(0, 512), 2 * D, 512)
            S.dve(lambda e, b=b: e.tensor_tensor(out=zf[b][0:P], in0=pbank[2][0:P, 0:NH], in1=bft[0:P], op=ALU.add), ["pb2", "bft"], ["zf%d" % b])
            mm(2, slice(0, 512), 2 * D + 512, 512)
            S.act(lambda e, b=b: e.activation(out=zf[b][0:P], in_=zf[b][0:P], func=AF.Exp, scale=-1.0), ["zf%d" % b], ["zf%d" % b])
            S.act(lambda e, b=b: e.activation(out=zf[b][0:P], in_=zf[b][0:P], func=AF.Ln, bias=one1[0:P, 0:1]), ["zf%d" % b, "one1"], ["zf%d" % b])
            S.dve(lambda e, b=b: e.tensor_scalar(out=lgf[b][0:P], in0=zf[b][0:P], scalar1=-1.0, scalar2=None, op0=ALU.mult), ["zf%d" % b], ["lgf%d" % b])
            for hf in range(2):
                S.act(lambda e, hf=hf, b=b: e.activation(out=Qb[b][0:P, hf * 512:(hf + 1) * 512], in_=pbank[3 + hf][0:P, :], func=AF.Copy, scale=HD ** -0.5), ["pb%d" % (3 + hf)], ["Qb%d" % b])
                S.dve(lambda e, hf=hf, b=b: e.tensor_copy(out=Kf[b][0:P, hf * 512:(hf + 1) * 512], in_=pbank[5 + hf][0:P, :]), ["pb%d" % (5 + hf)], ["Kf%d" % b])
            S.act(lambda e, b=b: e.copy(out=Vf[b][0:P, 0:512], in_=pbank[7][0:P, :]), ["pb7"], ["Vf%d" % b])
            S.act(lambda e, b=b: e.copy(out=Vf[b][0:P, 512:1024], in_=pbank[2][0:P, :]), ["pb2"], ["Vf%d" % b])
            S.pool(lambda e, b=b: e.tensor_copy(out=Kb[b][0:P], in_=Kf[b][0:P]), ["Kf%d" % b], ["Kb%d" % b])
            S.pool(lambda e, b=b: e.tensor_copy(out=Vb[b][0:P], in_=Vf[b][0:P]), ["Vf%d" % b], ["Vb%d" % b])
            transpose8(Qb[b][0:P], P, QTt[b][:, :, 0:P], "Qb%d" % b, "QTt%d" % b)
            transpose8(Kb[b][0:P], P, KTt[b][:, :, 0:P], "Kb%d" % b, "KTt%d" % b)
            S.dma(kt_dst(qts, t0, P), QTt[b][:, :, 0:P], reads=["QTt%d" % b], writes=["qts"])
            S.dma(kt_dst(kts, t0, P), KTt[b][:, :, 0:P], reads=["KTt%d" % b], writes=["kts"])
            S.dma(vsb[t0:t0 + P, :], Vb[b][0:P], reads=["Vb%d" % b], writes=["vsb"])
            if t0 >= 2 * HALF:
                o0 = t0 - 2 * HALF
                S.dma(o_ks[o0:o0 + P, :], Kf[b][0:P], reads=["Kf%d" % b]); S.dma(o_vs[o0:o0 + P, :], Vf[b][0:P], reads=["Vf%d" % b])
                S.dma(o_lfs[o0:o0 + P, :], lgf[b][0:P], reads=["lgf%d" % b])
            elif t0 >= HALF:
                o0 = t0 - HALF
                S.dma(o_kp[o0:o0 + P, :], Kf[b][0:P], reads=["Kf%d" % b]); S.dma(o_vp[o0:o0 + P, :], Vf[b][0:P], reads=["Vf%d" % b])
                S.dma(o_lfp[o0:o0 + P, :], lgf[b][0:P], reads=["lgf%d" % b])
            cumsum_path(lgf[b], "lgf%d" % b, P, b, ccol[t0:t0 + P, :], crow[:, t0:t0 + P])

        load(0)
        for i, tl in enumerate(tiles):
            do_tile(i, tl)

    if KST >= 6:
        phase_d()


    ats = dscr("ats", [NH, HD, NT], BF16)
    QS0 = HALF - 512

    def phase_e():
        S.barrier(); aoff[0] = mP2
        NKB = 2 * HALF // 128
        KA = [sb("KA%d" % i, [128, 2 * HALF], BF16) for i in range(2)]
        VA = [sb("VA%d" % i, [128, NKB, 65], BF16) for i in range(2)]
        CK = [sb("CK%d" % i, [128, NKB]) for i in range(2)]
        QA = [sb("QA%d" % i, [128, 512], BF16) for i in range(2)]
        crq = [sb("crq%d" % i, [128, 512]) for i in range(2)]
        cref = [sb("cref%d" % i, [128, 1]) for i in range(2)]
        cbias = [sb("cbias%d" % i, [128, NKB]) for i in range(2)]
        dq = sb("dq", [128, 512]); hq = sb("hq", [128, 512], BF16)
        PT = [sb("PT%d" % i, [128, 512], BF16) for i in range(4)]
        Osb = [sb("Osb%d" % i, [128, 512]) for i in range(2)]
        rt = sb("rt", [128, 512]); rt2 = sb("rt2", [128, 512]); rl3 = sb("rl3", [128, 3, 512], BF16)
        ATb = [sb("ATb%d" % i, [128, 512], BF16) for i in range(2)]
        for i in range(2):
            S.pool(lambda e, i=i: e.memset(KA[i][32:33, :], 1.0), [], ["KA%d" % i])
            S.pool(lambda e, i=i: e.memset(KA[i][64:65, :], 1.0), [], ["KA%d" % i])
            S.pool(lambda e, i=i: e.memset(VA[i][:, :, 64:65], 1.0), [], ["VA%d" % i])
            S.pool(lambda e, i=i: e.memset(QA[i][:], 0.0), [], ["QA%d" % i])
        sel = sb("sel", [128, 64], BF16)
        S.pool(lambda e: e.memset(sel[:], 0.0), [], ["sel"])
        S.pool(lambda e: e.memset(sel[64:65, :], 1.0), ["sel"], ["sel"])
        S.pool(lambda e: e.memset(rl3[:], 0.0), [], ["rl3"])
        cnt = {"s": 0, "q": 0, "p": 0}
        pending = []

        def rows67(dst, src, c0, c1, name, wname):
            S.dma(dst[0:32, c0:c1], src[0:32, :], writes=[wname])
            S.dma(dst[33:64, c0:c1], src[32:63, :], writes=[wname])
            S.dma(dst[65:66, c0:c1], src[63:64, :], writes=[wname])

        def attend(h, hb, q_t0, nq, blocks, pm_nb, out_t0):
            qb = cnt["q"] % 2; cnt["q"] += 1
            qn = "QA%d" % qb
            rows67(QA[qb], qts[h, :, q_t0:q_t0 + nq], 0, nq, qn, qn)
            S.dma(crq[qb][32:33, 0:nq], crow[h:h + 1, q_t0:q_t0 + nq], writes=["crq%d" % qb])
            S.dma(crq[qb][64:65, 0:nq], crow[h:h + 1, q_t0:q_t0 + nq], writes=["crq%d" % qb])
            S.dma(cref[qb][:], crow[h:h + 1, q_t0:q_t0 + 1].partition_broadcast(128), writes=["cref%d" % qb])
            S.dve(lambda e: e.tensor_scalar(out=dq[64:65, 0:nq], in0=crq[qb][64:65, 0:nq], scalar1=crq[qb][64:65, 0:1], scalar2=None, op0=ALU.subtract), ["crq%d" % qb, "dq64"], ["dq64"])
            S.dve(lambda e: e.tensor_copy(out=QA[qb][64:65, 0:nq], in_=dq[64:65, 0:nq]), ["dq64", qn], [qn])
            S.dve(lambda e: e.tensor_scalar(out=dq[32:33, 0:nq], in0=crq[qb][32:33, 0:nq], scalar1=crq[qb][32:33, 0:1], scalar2=None, op0=ALU.subtract), ["crq%d" % qb, "dq32"], ["dq32"])
            S.dve(lambda e: e.tensor_copy(out=hq[32:33, 0:nq], in_=dq[32:33, 0:nq]), ["dq32", "hq"], ["hq"])
            S.dve(lambda e: e.tensor_tensor(out=dq[32:33, 0:nq], in0=dq[32:33, 0:nq], in1=hq[32:33, 0:nq], op=ALU.subtract), ["dq32", "hq"], ["dq32"])
            S.dve(lambda e: e.tensor_copy(out=QA[qb][32:33, 0:nq], in_=dq[32:33, 0:nq]), ["dq32", qn], [qn])
            nbk = max(bk for (bk, nk, c0) in blocks) + 1
            S.dve(lambda e: e.tensor_scalar(out=cbias[qb][:, 0:nbk], in0=CK[hb][:, 0:nbk], scalar1=-1.0, scalar2=cref[qb][:, 0:1], op0=ALU.mult, op1=ALU.add), ["CK%d" % hb, "cref%d" % qb, "cbias%d" % qb], ["cbias%d" % qb])
            if pm_nb > 0:
                S.dve(lambda e: e.tensor_scalar(out=cbias[qb][:, 0:pm_nb], in0=cbias[qb][:, 0:pm_nb], scalar1=pmk[:, 0:1], scalar2=None, op0=ALU.add), ["cbias%d" % qb, "pmk"], ["cbias%d" % qb])
            ob = pbank[5 + qb]; obn = "pb%d" % (5 + qb)
            KEL = int(os.environ.get("KE_LVL", "9"))
            if KEL < 2:
                return
            nb_ = len(blocks)
            meta = {}

            def qk(bi_):
                bk, nk, c0 = blocks[bi_]
                si = cnt["s"] % 3; cnt["s"] += 1
                pi = cnt["p"] % 4; cnt["p"] += 1
                sbk = pbank[2 + si]; sn = "pb%d" % (2 + si)
                cc = 0 if c0 is None else c0
                meta[bi_] = (pi, cc)
                S.pe(lambda e: e.matmul(sbk[0:nk, cc:nq], lhsT=KA[hb][0:66, bk * 128:bk * 128 + nk], rhs=QA[qb][0:66, cc:nq], start=True, stop=True), ["KA%d" % hb, qn], [sn])
                if c0 is not None:
                    w = min(128, nq - cc)
                    S.dve(lambda e: e.tensor_tensor(out=sbk[0:nk, cc:cc + w], in0=sbk[0:nk, cc:cc + w], in1=cmask[0:nk, 0:w], op=ALU.add), [sn, "cmask"], [sn])
                S.act(lambda e: e.activation(out=PT[pi][0:nk, cc:nq], in_=sbk[0:nk, cc:nq], func=AF.Exp, bias=cbias[qb][0:nk, bk:bk + 1]), [sn, "cbias%d" % qb], ["PT%d" % pi])

            def pv(bi_):
                bk, nk, c0 = blocks[bi_]
                pi, cc = meta[bi_]
                S.pe(lambda e: e.matmul(ob[0:65, cc:nq], lhsT=VA[hb][0:nk, bk, 0:65], rhs=PT[pi][0:nk, cc:nq], start=(bi_ == 0), stop=(bi_ == nb_ - 1)), ["VA%d" % hb, "PT%d" % pi], [obn])

            LOOK = 2
            for i_ in range(nb_ + LOOK):
                if i_ < nb_:
                    qk(i_)
                if i_ == LOOK and pending:
                    pending.pop()()
                if i_ - LOOK >= 0 and KEL >= 3:
                    pv(i_ - LOOK)
            if pending:
                pending.pop()()
            if KEL < 4:
                return
            pending.append(lambda: finish(qb, nq, ob, obn, h, out_t0))

        def finish(qb, nq, ob, obn, h, out_t0):
            S.act(lambda e: e.copy(out=Osb[qb][0:65, 0:nq], in_=ob[0:65, 0:nq]), [obn], ["Osb%d" % qb])
            S.dve(lambda e: e.reciprocal(out=rt[64:65, 0:nq], in_=Osb[qb][64:65, 0:nq]), ["Osb%d" % qb, "rt"], ["rt"])
            S.dve(lambda e: e.tensor_copy(out=rl3[64:65, 0, 0:nq], in_=rt[64:65, 0:nq]), ["rt", "rl3"], ["rl3"])
            S.dve(lambda e: e.tensor_tensor(out=rt2[64:65, 0:nq], in0=rt[64:65, 0:nq], in1=rl3[64:65, 0, 0:nq], op=ALU.subtract), ["rt", "rl3", "rt2"], ["rt2"])
            S.dve(lambda e: e.tensor_copy(out=rl3[64:65, 1, 0:nq], in_=rt2[64:65, 0:nq]), ["rt2", "rl3"], ["rl3"])
            S.dve(lambda e: e.tensor_tensor(out=rt2[64:65, 0:nq], in0=rt2[64:65, 0:nq], in1=rl3[64:65, 1, 0:nq], op=ALU.subtract), ["rt2", "rl3"], ["rt2"])
            S.dve(lambda e: e.tensor_copy(out=rl3[64:65, 2, 0:nq], in_=rt2[64:65, 0:nq]), ["rt2", "rl3"], ["rl3"])
            bc = pbank[7]
            for j in range(3):
                S.pe(lambda e, j=j: e.matmul(bc[0:64, 0:nq], lhsT=sel[0:65, 0:64], rhs=rl3[0:65, j, 0:nq], start=(j == 0), stop=(j == 2)), ["rl3", "sel"], ["pb7"])
            S.dve(lambda e: e.tensor_tensor(out=ATb[qb][0:64, 0:nq], in0=Osb[qb][0:64, 0:nq], in1=bc[0:64, 0:nq], op=ALU.mult), ["Osb%d" % qb, "pb7", "ATb%d" % qb], ["ATb%d" % qb])
            S.dma(ats[h, :, out_t0:out_t0 + nq], ATb[qb][0:64, 0:nq], reads=["ATb%d" % qb], writes=["ats"])

        hcount = [0]
        def load_head(h, hb):
            rows67(KA[hb], kts[h, :, 0:2 * HALF], 0, 2 * HALF, "k", "KA%d" % hb)
            S.dma(VA[hb][:, :, 0:64], vsb[0:2 * HALF, h * HD:(h + 1) * HD].rearrange("(kb p) d -> p kb d", p=128), writes=["VA%d" % hb])
            S.dma(CK[hb][:, :], ccol[0:2 * HALF, h:h + 1].rearrange("(kb p) o -> p (kb o)", p=128), writes=["CK%d" % hb], allow_slow_non_contiguous=True)

        NHE = int(os.environ.get('KE_H', NH)) if KST >= 7 else 0
        if NHE:
            load_head(0, 0)
        for h in range(NHE):
            hb = hcount[0] % 2; hcount[0] += 1
            if h + 1 < NHE:
                load_head(h + 1, (hb + 1) % 2)
            for j in range(9):
                qs = QS0 + 512 * j
                nfull = qs // 128
                blocks = [(kb, 128, None) for kb in range(nfull)] + [(nfull + d_, 128, 128 * d_) for d_ in range(4)]
                attend(h, hb, qs, 512, blocks, (QS0 // 128) if j == 0 else (HALF // 128), qs)
        for sq in range(int(os.environ.get('KE_S', 2)) if KST >= 7 else 0):
            tn = 2 * HALF + 64 * sq
            for h in range(NH):
                hb = hcount[0] % 2; hcount[0] += 1
                rows67(KA[hb], kcT[sq, h], 0, HALF, "k", "KA%d" % hb)
                rows67(KA[hb], kts[h, :, tn:tn + 64], HALF, HALF + 64, "k", "KA%d" % hb)
                S.dma(VA[hb][:, 0:32, 0:64], vcb[sq, :, h * HD:(h + 1) * HD].rearrange("(kb p) d -> p kb d", p=128), writes=["VA%d" % hb])
                S.dma(VA[hb][0:64, 32, 0:64], vsb[tn:tn + 64, h * HD:(h + 1) * HD], writes=["VA%d" % hb])
                S.dma(CK[hb][:, 0:32], ccc[sq, :, h:h + 1].rearrange("(kb p) o -> p (kb o)", p=128), writes=["CK%d" % hb], allow_slow_non_contiguous=True)
                S.dma(CK[hb][0:64, 32:33], ccol[tn:tn + 64, h:h + 1], writes=["CK%d" % hb], allow_slow_non_contiguous=True)
                blocks = [(kb, 128, None) for kb in range(32)] + [(32, 128, 0)]
                attend(h, hb, tn, 64, blocks, 0, tn)
        if pending:
            pending.pop()()

    if KST >= 7:
        phase_e()

    wo_d = din("w_o", [D, D])

    def phase_f():
        S.barrier(); aoff[0] = mP2
        Wo = sb("Wo", [128, 8, D], BF16)
        m1 = aoff[0]
        load_weight(Wo, wo_d, 8, 128, D, "wo")
        S.barrier(); aoff[0] = m1
        ATt = [sb("ATt%d" % i, [128, 8, 512], BF16) for i in range(2)]
        Xf = [sb("Xf%d" % i, [128, 4, D]) for i in range(2)]
        groups = [(t, 512) for t in range(QS0, 2 * HALF, 512)] + [(2 * HALF, 128)]

        def load(i):
            t0, n = groups[i]; b = i % 2
            for h2 in range(2):
                S.dma(ATt[b][64 * h2:64 * h2 + 64, :, 0:n], ats[:, :, t0:t0 + n].rearrange("(hp h2) d t -> h2 d hp t", h2=2)[h2], writes=["ATt%d" % b])
            S.dma(Xf[b][:, 0:n // 128, :], x2s[t0:t0 + n, :].rearrange("(i p) d -> p i d", p=128), writes=["Xf%d" % b])

        fcount = [0]

        def do(i):
            t0, n = groups[i]; b = i % 2
            if i + 1 < len(groups):
                load(i + 1)
            for ti in range(n // 128):
                for half in range(2):
                    k = fcount[0] % 4; fcount[0] += 1
                    ob = pbank[2 + k]; on_ = "pb%d" % (2 + k)
                    for h in range(8):
                        S.pe(lambda e, h=h, half=half, ob=ob, ti=ti: e.matmul(ob[:, :], lhsT=ATt[b][:, h, ti * 128:(ti + 1) * 128], rhs=Wo[:, h, half * 512:(half + 1) * 512], start=(h == 0), stop=(h == 7)), ["ATt%d" % b], [on_])
                    S.dve(lambda e, half=half, ob=ob, ti=ti: e.tensor_tensor(out=Xf[b][:, ti, half * 512:(half + 1) * 512], in0=Xf[b][:, ti, half * 512:(half + 1) * 512], in1=ob[:, :], op=ALU.add), [on_, "Xf%d" % b], ["Xf%d" % b])
            S.dma(x3s[t0:t0 + n, :].rearrange("(i p) d -> p i d", p=128), Xf[b][:, 0:n // 128, :], reads=["Xf%d" % b], writes=["x3s"])

        load(0)
        for i in range(len(groups)):
            do(i)

    if KST >= 8:
        phase_f()
        groups1 = [dict(t0=HALF - 256, n=256, hist="zero", store=False, conv_out=None)]
        for gi in range(HALF // 256):
            groups1.append(dict(t0=HALF + gi * 256, n=256, hist="carry", store=True, yout=o_yp[gi * 256:(gi + 1) * 256, :],
                                conv_out=o_cvp[1] if gi == HALF // 256 - 1 else None))
        for sq in range(2):
            groups1.append(dict(t0=2 * HALF + 64 * sq, n=64, hist=sq, store=True, yout=o_ys[64 * sq:64 * sq + 64, :], conv_out=o_cvs[1, sq]))
        if not os.environ.get("KSKIPG"):
            ffn_phase(1, x3s, None, groups1, gffn_d[1], True)

    if os.environ.get("KDBG"):
        dbgG = dout("dbgG", [128, D], BF16); dbgX1 = dout("dbgX1", [128, D]); dbgX2 = dout("dbgX2", [128, D])
        S.barrier()
        S.dma(dbgG, gact[2 * HALF:2 * HALF + 128, :])
        S.dma(dbgX1, x1s[2 * HALF:2 * HALF + 128, :])
        S.dma(dbgX2, x2s[2 * HALF:2 * HALF + 128, :])
    S.emit()
    st.close()
    return nc


def _pair_layout(a):
    a = np.asarray(a)
    rest = a.shape[2:]
    a = a.reshape((32, 2, 64) + rest)
    a = np.moveaxis(a, 0, 2)
    return np.ascontiguousarray(a.reshape((128, 32) + rest))


def _from_pair_layout(a):
    a = a.reshape(2, 64, 32)
    return np.ascontiguousarray(np.moveaxis(a, 2, 0).reshape(64, 64))


def _host_inputs(inp):
    f = np.float32
    xpr = np.asarray(inp["x_prompt"], f)
    xsm = np.asarray(inp["x_sample"], f)
    rep = lambda v: np.ascontiguousarray(np.broadcast_to(np.asarray(v, f)[None, :], (128, np.asarray(v).shape[-1])))
    common = dict(
        lam_r=_pair_layout(np.asarray(inp["ssm_a_re"], f)[0]),
        lam_i=_pair_layout(np.asarray(inp["ssm_a_im"], f)[0]),
        lstep=_pair_layout(np.broadcast_to(np.asarray(inp["ssm_log_step"], f)[0][:, None], (64, 64))),
        b_r=_pair_layout(np.asarray(inp["ssm_b_re"], f)[0]),
        b_i=_pair_layout(np.asarray(inp["ssm_b_im"], f)[0]),
        c_r=_pair_layout(np.transpose(np.asarray(inp["ssm_c_re"], f)[0], (0, 2, 1))),
        c_i=_pair_layout(np.transpose(np.asarray(inp["ssm_c_im"], f)[0], (0, 2, 1))),
        gmix=np.stack([rep(inp["norm_mix"][i]) for i in range(2)]),
        gffn=np.stack([rep(inp["norm_ffn"][i]) for i in range(2)]),
        gfin=rep(inp["norm_final"]),
        dskip=rep(inp["ssm_d"][0]),
        ident=np.eye(128, dtype=f),
        tmask=(np.arange(128)[:, None] // 16 <= np.arange(128)[None, :] // 16).astype(f),
        tri=(np.arange(128)[:, None] <= np.arange(128)[None, :]).astype(f),
        cmask=np.where(np.arange(128)[:, None] <= np.arange(128)[None, :], 0.0, -30000.0).astype(f),
        bft=rep(inp["fox_b_f"][0]),
        w_glu=np.ascontiguousarray(np.asarray(inp["ssm_w_glu"], f)[0]),
        w_up=np.asarray(inp["ffn_w_up"], f), w_gate=np.asarray(inp["ffn_w_gate"], f), w_down=np.asarray(inp["ffn_w_down"], f),
        conv_w=np.asarray(inp["ffn_conv_w"], f), conv_b=np.asarray(inp["ffn_conv_b"], f),
        w_qkvf=np.ascontiguousarray(np.asarray(inp["fox_w_qkvf"], f)[0]),
        w_o=np.ascontiguousarray(np.asarray(inp["fox_w_o"], f)[0]),
    )
    maps = []
    for k in range(8):
        b, half = k // 2, k % 2
        prev = xpr[b, 0:HALF] if half == 1 else np.zeros((HALF, D), f)
        own = xpr[b, half * HALF:(half + 1) * HALF]
        m = dict(common)
        m["xp"] = np.ascontiguousarray(np.concatenate([prev, own], 0))
        m["xs"] = np.ascontiguousarray(xsm[2 * k:2 * k + 2].reshape(128, D))
        m["kcache"] = np.ascontiguousarray(np.asarray(inp["cache_fox_k"], f)[0, 2 * k:2 * k + 2].reshape(2, HALF, D))
        m["vcache"] = np.ascontiguousarray(np.asarray(inp["cache_fox_v"], f)[0, 2 * k:2 * k + 2].reshape(2, HALF, D))
        m["lfcache"] = np.ascontiguousarray(np.asarray(inp["cache_fox_logf"], f)[0, 2 * k:2 * k + 2])
        m["pmk"] = np.full((128, 1), 0.0 if half == 1 else -30000.0, f)
        m["cvst"] = np.ascontiguousarray(np.asarray(inp["state_ffn_conv"], f)[:, 2 * k:2 * k + 2])
        m["h0r"] = np.ascontiguousarray(np.stack([_pair_layout(np.asarray(inp["state_ssm_re"], f)[0, 2 * k + s]) for s in range(2)], 1))
        m["h0i"] = np.ascontiguousarray(np.stack([_pair_layout(np.asarray(inp["state_ssm_im"], f)[0, 2 * k + s]) for s in range(2)], 1))
        maps.append(m)
    return maps


_NC_CACHE = {}


def kernel(**inputs):
    maps = _host_inputs(inputs)
    if "nc" not in _NC_CACHE:
        _NC_CACHE["nc"] = build()
    nc = _NC_CACHE["nc"]
    res = run_bass_kernel_spmd(nc, maps, core_ids=list(range(8)))
    R = res.results
    f = np.float32
    y_prompt = np.zeros((4, 8192, D), f); y_sample = np.zeros((16, 64, D), f)
    ssm_re_p = np.zeros((1, 4, 64, 64), f); ssm_im_p = np.zeros((1, 4, 64, 64), f)
    ssm_re_s = np.zeros((1, 16, 64, 64), f); ssm_im_s = np.zeros((1, 16, 64, 64), f)
    k_p = np.zeros((1, 4, 8192, NH, HD), f); v_p = np.zeros((1, 4, 8192, NH, HD), f); lf_p = np.zeros((1, 4, 8192, NH), f)
    k_s = np.zeros((1, 16, 64, NH, HD), f); v_s = np.zeros((1, 16, 64, NH, HD), f); lf_s = np.zeros((1, 16, 64, NH), f)
    cv_p = np.zeros((2, 4, 2, DFF), f); cv_s = np.zeros((2, 16, 2, DFF), f)
    for k in range(8):
        b, half = k // 2, k % 2
        r = R[k]
        if half == 1:
            ssm_re_p[0, b] = _from_pair_layout(r["o_ssm_p"][0]); ssm_im_p[0, b] = _from_pair_layout(r["o_ssm_p"][1])
        if "o_cvp" in r:
            if half == 1:
                cv_p[:, b] = r["o_cvp"]
            cv_s[:, 2 * k:2 * k + 2] = r["o_cvs"]
        if "o_kp" in r:
            sl = slice(half * HALF, (half + 1) * HALF)
            k_p[0, b, sl] = r["o_kp"].reshape(HALF, NH, HD); v_p[0, b, sl] = r["o_vp"].reshape(HALF, NH, HD); lf_p[0, b, sl] = r["o_lfp"]
            k_s[0, 2 * k:2 * k + 2] = r["o_ks"].reshape(2, 64, NH, HD); v_s[0, 2 * k:2 * k + 2] = r["o_vs"].reshape(2, 64, NH, HD)
            lf_s[0, 2 * k:2 * k + 2] = r["o_lfs"].reshape(2, 64, NH)
        if "o_yp" in r:
            y_prompt[b, half * HALF:(half + 1) * HALF] = r["o_yp"]
            y_sample[2 * k:2 * k + 2] = r["o_ys"].reshape(2, 64, D)
        for s in range(2):
            ssm_re_s[0, 2 * k + s] = _from_pair_layout(r["o_ssm_s"][s, 0]); ssm_im_s[0, 2 * k + s] = _from_pair_layout(r["o_ssm_s"][s, 1])
    kernel.last = R
    return (y_prompt, y_sample, ssm_re_p, ssm_im_p, k_p, v_p, lf_p, cv_p, ssm_re_s, ssm_im_s, k_s, v_s, lf_s, cv_s)
```

```python
import contextlib
import math
import numpy as np
import concourse.bass as bass
import concourse.mybir as mybir
from concourse.bass_utils import run_bass_kernel_spmd

F32 = mybir.dt.float32
BF16 = mybir.dt.bfloat16
AF = mybir.ActivationFunctionType
ALU = mybir.AluOpType
AX = mybir.AxisListType

D = 1024
DFF = 2816
NH = 16
HD = 64
HALF = 4096
NPAIR = 32
EPS = 1e-6
TWO_PI = 2.0 * math.pi

ENGS = ("pe", "act", "dve", "pool", "sp")
NO_SELF_WAIT = ("pe", "sp")
DMA_RING = 8


class Op:
    __slots__ = ("eng", "fn", "deps", "dma", "idx", "needed", "cnt", "slot", "tgt")

    def __init__(self, eng, fn, deps, dma, idx):
        self.eng, self.fn, self.deps, self.dma, self.idx = eng, fn, deps, dma, idx
        self.needed = False
        self.cnt = 0
        self.slot = -1
        self.tgt = 0


class Sched:
    def __init__(self, nc):
        self.nc = nc
        self.ops = {e: [] for e in ENGS}
        self.last_w = {}
        self.readers = {}
        self.ndma = {e: 0 for e in ENGS}
        self.last_slot = {}

    def add(self, eng, fn, reads=(), writes=(), dma=False):
        lst = self.ops[eng]
        idx = len(lst)
        deps = set()
        for r in reads:
            lw = self.last_w.get(r)
            if lw is not None:
                deps.add(lw)
        for w in writes:
            lw = self.last_w.get(w)
            if lw is not None:
                deps.add(lw)
            rd = self.readers.get(w)
            if rd:
                for e, i in rd.items():
                    deps.add((e, i))
        op = Op(eng, fn, deps, dma, idx)
        if dma:
            k = self.ndma[eng]
            self.ndma[eng] = k + 1
            op.slot = k % DMA_RING
            op.tgt = 16 * (k // DMA_RING + 1)
            op.needed = True
            prev = self.last_slot.get((eng, op.slot))
            if prev is not None:
                deps.add((eng, prev))
            self.last_slot[(eng, op.slot)] = idx
        lst.append(op)
        for r in reads:
            self.readers.setdefault(r, {})[eng] = idx
        for w in writes:
            self.last_w[w] = (eng, idx)
            self.readers[w] = {}
        return op

    def pe(self, fn, reads=(), writes=()):
        return self.add("pe", fn, reads, writes)

    def act(self, fn, reads=(), writes=()):
        return self.add("act", fn, reads, writes)

    def dve(self, fn, reads=(), writes=()):
        return self.add("dve", fn, reads, writes)

    def pool(self, fn, reads=(), writes=()):
        return self.add("pool", fn, reads, writes)

    def on(self, eng, fn, reads=(), writes=()):
        return self.add(eng, fn, reads, writes)

    def dma(self, out, in_, reads=(), writes=(), q="sp", **kw):
        return self.add(q, lambda e: e.dma_start(out=out, in_=in_, **kw), reads, writes, dma=True)

    def barrier(self):
        deps = set()
        for e in ENGS:
            lst = self.ops[e]
            seen = set()
            got_last = False
            for j in range(len(lst) - 1, -1, -1):
                o = lst[j]
                if o.dma:
                    if o.slot not in seen:
                        seen.add(o.slot)
                        deps.add((e, j))
                elif not got_last and o.fn is not None:
                    got_last = True
                    deps.add((e, j))
                if got_last and len(seen) >= DMA_RING:
                    break
        for e in ENGS:
            self.ops[e].append(Op(e, None, set(deps), False, len(self.ops[e])))
        self.last_w.clear()
        self.readers.clear()

    def emit(self, final_wait_eng="sp"):
        nc = self.nc
        for e in ENGS:
            for op in self.ops[e]:
                for (de, di) in op.deps:
                    dop = self.ops[de][di]
                    if dop.dma:
                        continue
                    if de == e and e in NO_SELF_WAIT and op.fn is not None:
                        continue
                    if de == e and op.fn is None:
                        continue
                    dop.needed = True
        finals = []
        for e in ENGS:
            if self.ops[e]:
                last = None
                for o in reversed(self.ops[e]):
                    if not o.dma and o.fn is not None:
                        last = o
                        break
                if last is not None:
                    last.needed = True
                    finals.append((e, last.idx))
                seen = {}
                for op in self.ops[e]:
                    if op.dma:
                        seen[op.slot] = op.idx
                for sl, ix in seen.items():
                    finals.append((e, ix))
        for e in ENGS:
            c = 0
            for op in self.ops[e]:
                if op.needed and not op.dma and op.fn is not None:
                    c += 1
                    op.cnt = c
        with contextlib.ExitStack() as st:
            esem = {e: st.enter_context(nc.semaphore("s_" + e)) for e in ENGS}
            dsem = {e: [st.enter_context(nc.semaphore("d_%s%d" % (e, i))) for i in range(DMA_RING)]
                    for e in ENGS if self.ndma[e] > 0}
            block = st.enter_context(nc.Block())
            ops = self.ops

            def run(ename, eng):
                waited = {}
                for op in ops[ename]:
                    for (de, di) in sorted(op.deps):
                        dop = ops[de][di]
                        if dop.dma:
                            key = ("d", de, dop.slot)
                            if waited.get(key, 0) >= dop.tgt:
                                continue
                            waited[key] = dop.tgt
                            eng.wait_ge(dsem[de][dop.slot], dop.tgt)
                        else:
                            if de == ename and (ename in NO_SELF_WAIT or op.fn is None):
                                continue
                            key = ("e", de)
                            if waited.get(key, 0) >= dop.cnt:
                                continue
                            waited[key] = dop.cnt
                            eng.wait_ge(esem[de], dop.cnt)
                    if op.fn is None:
                        continue
                    ins = op.fn(eng)
                    if op.dma:
                        ins.then_inc(dsem[ename][op.slot], 16)
                    elif op.needed:
                        ins.then_inc(esem[ename], 1)
                if ename == final_wait_eng:
                    for (de, di) in finals:
                        dop = ops[de][di]
                        if dop.dma:
                            eng.wait_ge(dsem[de][dop.slot], dop.tgt)
                        elif de != ename:
                            eng.wait_ge(esem[de], dop.cnt)

            @block.tensor
            def _(eng):
                run("pe", eng)

            @block.scalar
            def _(eng):
                run("act", eng)

            @block.vector
            def _(eng):
                run("dve", eng)

            @block.gpsimd
            def _(eng):
                run("pool", eng)

            @block.sync
            def _(eng):
                run("sp", eng)


STAGE = 9


def build(stage=STAGE):
    nc = bass.Bass("TRN2", target_bir_lowering=False)
    S = Sched(nc)
    st = contextlib.ExitStack()

    def din(name, shape, dt=F32):
        return nc.dram_tensor(name, list(shape), dt, kind="ExternalInput").ap()

    def dout(name, shape, dt=F32):
        return nc.dram_tensor(name, list(shape), dt, kind="ExternalOutput").ap()

    def dscr(name, shape, dt=F32):
        return nc.dram_tensor(name, list(shape), dt).ap()

    ARENA_F32 = 53200
    arena = st.enter_context(nc.sbuf_tensor("arena", [128, ARENA_F32], F32))
    aoff = [0]

    def sb(name, shape, dt=F32):
        n = 1
        for d_ in shape[1:]:
            n *= d_
        nf = n if dt == F32 else (n + 1) // 2
        nf = (nf + 7) // 8 * 8
        o = aoff[0]
        assert o + nf <= ARENA_F32, ("arena overflow", name, o, nf)
        aoff[0] = o + nf
        v = arena[:, o:o + nf]
        if dt != F32:
            v = v.bitcast(dt)
        v = v[:, 0:n]
        if len(shape) == 2:
            return v
        letters = "abcdefg"[:len(shape) - 1]
        pat = "p (" + " ".join(letters) + ") -> p " + " ".join(letters)
        kw = {letters[i]: shape[1 + i] for i in range(len(shape) - 2)}
        return v.rearrange(pat, **kw)

    def ps(name, shape, dt=F32):
        return st.enter_context(nc.psum_tensor("p_" + name, list(shape), dt))

    xp = din("xp", [2 * HALF, D])
    xs = din("xs", [128, D])
    h0r = din("h0r", [128, 2, NPAIR])
    h0i = din("h0i", [128, 2, NPAIR])
    lam_r_d = din("lam_r", [128, NPAIR])
    lam_i_d = din("lam_i", [128, NPAIR])
    lstep_d = din("lstep", [128, NPAIR])
    b_r_d = din("b_r", [128, NPAIR, 16])
    b_i_d = din("b_i", [128, NPAIR, 16])
    c_r_d = din("c_r", [128, NPAIR, 16])
    c_i_d = din("c_i", [128, NPAIR, 16])
    gmix_d = din("gmix", [2, 128, D])
    gffn_d = din("gffn", [2, 128, D])
    gfin_d = din("gfin", [128, D])
    dskip_d = din("dskip", [128, D])
    ident_d = din("ident", [128, 128])
    tmask_d = din("tmask", [128, 128])

    o_ssm_p = dout("o_ssm_p", [2, 128, NPAIR])
    o_ssm_s = dout("o_ssm_s", [2, 2, 128, NPAIR])
    NT = 2 * HALF + 128
    gact = dscr("gscr", [NT, D], BF16)

    pbank = [ps("pb%d" % i, [128, 512], F32) for i in range(8)]

    ident = sb("ident", [128, 128]); identb = sb("identb", [128, 128], BF16)
    tmask = sb("tmask", [128, 128])
    S.dma(ident[:], ident_d, writes=["ident"])
    S.dma(tmask[:], tmask_d, writes=["tmask"])
    S.dve(lambda e: e.tensor_copy(out=identb[:], in_=ident[:]), ["ident"], ["identb"])
    negpi = sb("negpi", [128, 1])
    S.pool(lambda e: e.memset(negpi[:], -math.pi), [], ["negpi"])
    tri = sb("tri", [128, 128], BF16); onesb = sb("onesb", [128, 128], BF16); cmask = sb("cmask", [128, 128])
    bft = sb("bft", [128, NH]); pmk = sb("pmk", [128, 1]); one1 = sb("one1", [128, 1])
    cstg = sb("cstg", [128, 128])
    S.dma(cstg[:], din("tri", [128, 128]), writes=["cstg"])
    S.dve(lambda e: e.tensor_copy(out=tri[:], in_=cstg[:]), ["cstg"], ["tri"])
    S.pool(lambda e: e.memset(onesb[:], 1.0), [], ["onesb"])
    S.pool(lambda e: e.memset(one1[:], 1.0), [], ["one1"])
    S.dma(cmask[:], din("cmask", [128, 128]), writes=["cmask"])
    S.dma(bft[:], din("bft", [128, NH]), writes=["bft"])
    S.dma(pmk[:], din("pmk", [128, 1]), writes=["pmk"])
    ssn = sb("ssn", [128, 4]); rvn = sb("rvn", [128, 4])
    mP = aoff[0]
    mP2 = mP

    lam_r = sb("lam_r", [128, NPAIR]); lam_i = sb("lam_i", [128, NPAIR]); dtt = sb("dtt", [128, NPAIR])
    lrdt = sb("lrdt", [128, NPAIR]); lidt = sb("lidt", [128, NPAIR])
    Er = sb("Er", [128, 9, NPAIR]); Ei = sb("Ei", [128, 9, NPAIR])
    Emr = sb("Emr", [128, 9, NPAIR]); Emi = sb("Emi", [128, 9, NPAIR])
    x_t = sb("x_t", [128, 8, D])
    _al = [x_t[:, i, 0:NPAIR * 16].rearrange("p (a h) -> p a h", h=16) for i in range(8)]
    Br, Bi, Cr, Ci, Bbr, Bbi, t16a, t16b = _al
    tA = sb("tA", [128, NPAIR]); tB = sb("tB", [128, NPAIR]); tC = sb("tC", [128, NPAIR]); tD = sb("tD", [128, NPAIR])
    zr = sb("zr", [128, NPAIR]); zi = sb("zi", [128, NPAIR])
    S.dma(lam_r[:], lam_r_d, writes=["lam_r"]); S.dma(lam_i[:], lam_i_d, writes=["lam_i"])
    S.dma(dtt[:], lstep_d, writes=["dtt"])
    S.dma(Br[:], b_r_d, writes=["Br"]); S.dma(Bi[:], b_i_d, writes=["Bi"])
    S.dma(Cr[:], c_r_d, writes=["Cr"]); S.dma(Ci[:], c_i_d, writes=["Ci"])
    S.act(lambda e: e.activation(out=dtt[:], in_=dtt[:], func=AF.Exp), ["dtt"], ["dtt"])
    S.dve(lambda e: e.tensor_tensor(out=lrdt[:], in0=lam_r[:], in1=dtt[:], op=ALU.mult), ["lam_r", "dtt"], ["lrdt"])
    S.dve(lambda e: e.tensor_tensor(out=lidt[:], in0=lam_i[:], in1=dtt[:], op=ALU.mult), ["lam_i", "dtt"], ["lidt"])
    S.dve(lambda e: e.memset(Er[:, 0, :], 1.0), [], ["E"])
    S.dve(lambda e: e.memset(Ei[:, 0, :], 0.0), [], ["E"])
    halfpi = sb("halfpi", [128, 1])
    S.pool(lambda e: e.memset(halfpi[:], 0.5 * math.pi), [], ["halfpi"])
    S.act(lambda e: e.activation(out=tA[:], in_=lidt[:], func=AF.Sin, scale=0.125), ["lidt"], ["tA"])
    S.act(lambda e: e.activation(out=tB[:], in_=lidt[:], func=AF.Sin, scale=-0.125, bias=halfpi[:, 0:1]), ["lidt", "halfpi"], ["tB"])
    for _ in range(3):
        S.dve(lambda e: e.tensor_tensor(out=tC[:], in0=tA[:], in1=tB[:], op=ALU.mult), ["tA", "tB", "tC"], ["tC"])
        S.dve(lambda e: e.tensor_tensor(out=tD[:], in0=tA[:], in1=tA[:], op=ALU.mult), ["tA", "tD"], ["tD"])
        S.dve(lambda e: e.tensor_scalar(out=tA[:], in0=tC[:], scalar1=2.0, scalar2=None, op0=ALU.mult), ["tC", "tA"], ["tA"])
        S.dve(lambda e: e.tensor_scalar(out=tB[:], in0=tD[:], scalar1=-2.0, scalar2=1.0, op0=ALU.mult, op1=ALU.add), ["tD", "tB"], ["tB"])
    S.act(lambda e: e.activation(out=tC[:], in_=lrdt[:], func=AF.Exp), ["lrdt", "tC"], ["tC"])
    S.dve(lambda e: e.tensor_tensor(out=Er[:, 1, :], in0=tB[:], in1=tC[:], op=ALU.mult), ["tB", "tC"], ["E"])
    S.dve(lambda e: e.tensor_tensor(out=Ei[:, 1, :], in0=tA[:], in1=tC[:], op=ALU.mult), ["tA", "tC"], ["E"])
    for k in range(2, 9):
        S.dve(lambda e, k=k: e.tensor_tensor(out=tA[:], in0=Er[:, k - 1, :], in1=Er[:, 1, :], op=ALU.mult), ["E", "tA"], ["tA"])
        S.dve(lambda e, k=k: e.tensor_tensor(out=tB[:], in0=Ei[:, k - 1, :], in1=Ei[:, 1, :], op=ALU.mult), ["E", "tB"], ["tB"])
        S.dve(lambda e, k=k: e.tensor_tensor(out=tC[:], in0=Er[:, k - 1, :], in1=Ei[:, 1, :], op=ALU.mult), ["E", "tC"], ["tC"])
        S.dve(lambda e, k=k: e.tensor_tensor(out=tD[:], in0=Ei[:, k - 1, :], in1=Er[:, 1, :], op=ALU.mult), ["E", "tD"], ["tD"])
        S.dve(lambda e, k=k: e.tensor_tensor(out=Er[:, k, :], in0=tA[:], in1=tB[:], op=ALU.subtract), ["tA", "tB", "E"], ["E"])
        S.dve(lambda e, k=k: e.tensor_tensor(out=Ei[:, k, :], in0=tC[:], in1=tD[:], op=ALU.add), ["tC", "tD", "E"], ["E"])
    for k in range(1, 9):
        S.act(lambda e, k=k: e.activation(out=tC[:], in_=lrdt[:], func=AF.Exp, scale=-2.0 * k), ["lrdt", "tC"], ["tC"])
        S.dve(lambda e, k=k: e.tensor_tensor(out=Emr[:, k, :], in0=Er[:, k, :], in1=tC[:], op=ALU.mult), ["E", "tC"], ["E"])
        S.dve(lambda e, k=k: e.scalar_tensor_tensor(out=Emi[:, k, :], in0=Ei[:, k, :], scalar=-1.0, in1=tC[:], op0=ALU.mult, op1=ALU.mult), ["E", "tC"], ["E"])
    S.dve(lambda e: e.tensor_tensor(out=tA[:], in0=lam_r[:], in1=lam_r[:], op=ALU.mult), ["lam_r", "tA"], ["tA"])
    S.dve(lambda e: e.tensor_tensor(out=tB[:], in0=lam_i[:], in1=lam_i[:], op=ALU.mult), ["lam_i", "tB"], ["tB"])
    S.dve(lambda e: e.tensor_tensor(out=tA[:], in0=tA[:], in1=tB[:], op=ALU.add), ["tA", "tB"], ["tA"])
    S.dve(lambda e: e.reciprocal(out=tA[:], in_=tA[:]), ["tA"], ["tA"])
    S.dve(lambda e: e.tensor_scalar(out=tB[:], in0=Er[:, 1, :], scalar1=-1.0, scalar2=None, op0=ALU.add), ["E", "tB"], ["tB"])
    S.dve(lambda e: e.tensor_tensor(out=tC[:], in0=tB[:], in1=lam_r[:], op=ALU.mult), ["tB", "lam_r", "tC"], ["tC"])
    S.dve(lambda e: e.tensor_tensor(out=tD[:], in0=Ei[:, 1, :], in1=lam_i[:], op=ALU.mult), ["E", "lam_i", "tD"], ["tD"])
    S.dve(lambda e: e.tensor_tensor(out=tC[:], in0=tC[:], in1=tD[:], op=ALU.add), ["tC", "tD"], ["tC"])
    S.dve(lambda e: e.tensor_tensor(out=zr[:], in0=tC[:], in1=tA[:], op=ALU.mult), ["tC", "tA"], ["zr"])
    S.dve(lambda e: e.tensor_tensor(out=tC[:], in0=Ei[:, 1, :], in1=lam_r[:], op=ALU.mult), ["E", "lam_r", "tC"], ["tC"])
    S.dve(lambda e: e.tensor_tensor(out=tD[:], in0=tB[:], in1=lam_i[:], op=ALU.mult), ["tB", "lam_i", "tD"], ["tD"])
    S.dve(lambda e: e.tensor_tensor(out=tC[:], in0=tC[:], in1=tD[:], op=ALU.subtract), ["tC", "tD"], ["tC"])
    S.dve(lambda e: e.tensor_tensor(out=zi[:], in0=tC[:], in1=tA[:], op=ALU.mult), ["tC", "tA"], ["zi"])

    import os
    KPREP = int(os.environ.get("KPREP", "9"))

    def bch(ap2, n=16):
        return ap2.unsqueeze(2).to_broadcast([128, ap2.shape[1], n])

    S.dve(lambda e: e.tensor_tensor(out=t16a[:], in0=Br[:], in1=bch(zr[:]), op=ALU.mult), ["Br", "zr", "t16a"], ["t16a"])
    S.dve(lambda e: e.tensor_tensor(out=t16b[:], in0=Bi[:], in1=bch(zi[:]), op=ALU.mult), ["Bi", "zi", "t16b"], ["t16b"])
    S.dve(lambda e: e.tensor_tensor(out=Bbr[:], in0=t16a[:], in1=t16b[:], op=ALU.subtract), ["t16a", "t16b"], ["Bbr"])
    S.dve(lambda e: e.tensor_tensor(out=t16a[:], in0=Bi[:], in1=bch(zr[:]), op=ALU.mult), ["Bi", "zr", "t16a"], ["t16a"])
    S.dve(lambda e: e.tensor_tensor(out=t16b[:], in0=Br[:], in1=bch(zi[:]), op=ALU.mult), ["Br", "zi", "t16b"], ["t16b"])
    S.dve(lambda e: e.tensor_tensor(out=Bbi[:], in0=t16a[:], in1=t16b[:], op=ALU.add), ["t16a", "t16b"], ["Bbi"])

    WinReT = sb("WinReT", [128, NPAIR, 128], BF16)
    WinImT = sb("WinImT", [128, NPAIR, 128], BF16)
    Toep = sb("Toep", [128, 2 * NPAIR, 128], BF16)
    WoRe = sb("WoRe", [128, 2, NPAIR, 128], BF16)
    WoIm = sb("WoIm", [128, 2, NPAIR, 128], BF16)
    S.pool(lambda e: e.memset(WoRe[:], 0.0), [], ["WoRe"])
    S.pool(lambda e: e.memset(WoIm[:], 0.0), [], ["WoIm"])
    NPC = 2
    XRe = sb("XRe", [128, NPC, 8, 16]); XIm = sb("XIm", [128, NPC, 8, 16])
    WiRe = sb("WiRe", [128, NPC, 8, 16]); WiIm = sb("WiIm", [128, NPC, 8, 16])
    WoReF = sb("WoReF", [128, NPC, 8, 16]); WoImF = sb("WoImF", [128, NPC, 8, 16])
    XReB = sb("XReB", [128, NPC, 128], BF16); XImB = sb("XImB", [128, NPC, 128], BF16)
    pa = sb("pa", [128, NPC, 16]); pb_ = sb("pb_", [128, NPC, 16])

    def cmul_into(outr, outi, ar, ai, br, bi, tagr, tagi, negate_im=False):
        S.dve(lambda e: e.tensor_tensor(out=pa[:], in0=br, in1=bch(ar), op=ALU.mult), ["E", "Bbr", "Bbi", "Cr", "Ci", "pa"], ["pa"])
        S.dve(lambda e: e.tensor_tensor(out=pb_[:], in0=bi, in1=bch(ai), op=ALU.mult), ["E", "Bbr", "Bbi", "Cr", "Ci", "pb_"], ["pb_"])
        S.dve(lambda e: e.tensor_tensor(out=outr, in0=pa[:], in1=pb_[:], op=ALU.subtract), ["pa", "pb_"], [tagr])
        S.dve(lambda e: e.tensor_tensor(out=pa[:], in0=bi, in1=bch(ar), op=ALU.mult), ["E", "Bbr", "Bbi", "Cr", "Ci", "pa"], ["pa"])
        S.dve(lambda e: e.tensor_tensor(out=pb_[:], in0=br, in1=bch(ai), op=ALU.mult), ["E", "Bbr", "Bbi", "Cr", "Ci", "pb_"], ["pb_"])
        if negate_im:
            S.dve(lambda e: e.scalar_tensor_tensor(out=outi, in0=pa[:], scalar=-1.0, in1=pb_[:], op0=ALU.mult, op1=ALU.subtract), ["pa", "pb_"], [tagi])
        else:
            S.dve(lambda e: e.tensor_tensor(out=outi, in0=pa[:], in1=pb_[:], op=ALU.add), ["pa", "pb_"], [tagi])

    for q in range(NPAIR // NPC):
        psl = slice(q * NPC, (q + 1) * NPC)
        for s in range(8):
            cmul_into(XRe[:, :, s, :], XIm[:, :, s, :], Emr[:, s + 1, psl], Emi[:, s + 1, psl], Bbr[:, psl, :], Bbi[:, psl, :], "XRe", "XIm")
            cmul_into(WiRe[:, :, s, :], WiIm[:, :, s, :], Er[:, 7 - s, psl], Ei[:, 7 - s, psl], Bbr[:, psl, :], Bbi[:, psl, :], "WiRe", "WiIm")
            cmul_into(WoReF[:, :, s, :], WoImF[:, :, s, :], Er[:, s + 1, psl], Ei[:, s + 1, psl], Cr[:, psl, :], Ci[:, psl, :], "WoReF", "WoImF", negate_im=True)
        for g2 in range(2):
            rs = slice(64 * g2, 64 * g2 + 64)
            S.act(lambda e, psl=psl, rs=rs, g2=g2: e.copy(out=WoRe[rs, g2, psl, :], in_=WoReF[rs].rearrange("p a s h -> p a (s h)")), ["WoReF", "WoRe"], ["WoRe"])
            S.act(lambda e, psl=psl, rs=rs, g2=g2: e.copy(out=WoIm[rs, g2, psl, :], in_=WoImF[rs].rearrange("p a s h -> p a (s h)")), ["WoImF", "WoIm"], ["WoIm"])
        S.act(lambda e: e.copy(out=XReB[:], in_=XRe[:].rearrange("p a s h -> p a (s h)")), ["XRe"], ["XReB"])
        S.act(lambda e: e.copy(out=XImB[:], in_=XIm[:].rearrange("p a s h -> p a (s h)")), ["XIm"], ["XImB"])
        if KPREP <= 3:
            continue
        for a in range(NPC):
            pair = q * NPC + a
            bk = pbank[a % 2]; bkn0 = 'pb%d' % (a % 2)
            S.pe(lambda e, a=a, bk=bk: e.transpose(out=bk[:, 0:128], in_=WiRe[:, a].rearrange("p s h -> p (s h)"), identity=ident[:]), ["WiRe", "ident"], ["pb%d" % (a % 2)])
            S.pe(lambda e, a=a, bk=bk: e.transpose(out=bk[:, 128:256], in_=WiIm[:, a].rearrange("p s h -> p (s h)"), identity=ident[:]), ["WiIm", "ident"], ["pb%d" % (a % 2)])
            for g2 in range(2 if KPREP >= 5 else 0):
                rs = slice(64 * g2, 64 * g2 + 64)
                cs = slice(128 * g2, 128 * g2 + 128)
                bk = pbank[2 + a % 2]
                S.pe(lambda e, a=a, bk=bk, rs=rs, cs=cs, q=q, g2=g2: e.matmul(bk[:, cs], lhsT=XReB[:, a, :], rhs=WoRe[:, g2, q * NPC + a, :], start=True, stop=False),
                     ["XReB", "WoRe"], ["pb%d" % (2 + a % 2)])
                S.pe(lambda e, a=a, bk=bk, rs=rs, cs=cs, q=q, g2=g2: e.matmul(bk[:, cs], lhsT=XImB[:, a, :], rhs=WoIm[:, g2, q * NPC + a, :], start=False, stop=True),
                     ["XImB", "WoIm"], ["pb%d" % (2 + a % 2)])
            S.act(lambda e, pair=pair, bk=pbank[a % 2]: e.copy(out=WinReT[:, pair, :], in_=bk[:, 0:128]), ["pb%d" % (a % 2)], ["WinReT"])
            S.act(lambda e, pair=pair, bk=pbank[a % 2]: e.copy(out=WinImT[:, pair, :], in_=bk[:, 128:256]), ["pb%d" % (a % 2)], ["WinImT"])
            for g2 in range(2 if KPREP >= 5 else 0):
                S.dve(lambda e, pair=pair, g2=g2, bk2=pbank[2 + a % 2]: e.tensor_tensor(out=Toep[:, 2 * pair + g2, :], in0=bk2[:, 128 * g2:128 + 128 * g2], in1=tmask[:], op=ALU.mult),
                      ["pb%d" % (2 + a % 2), "tmask"], ["Toep"])

    M1 = sb("M1", [128, 2, NPAIR]); M2n = sb("M2n", [128, NPAIR]); M2p = sb("M2p", [128, NPAIR])
    S.dve(lambda e: e.tensor_copy(out=M1[:, 0, :], in_=Er[:, 8, :]), ["E"], ["M1"])
    S.dve(lambda e: e.tensor_copy(out=M1[:, 1, :], in_=Er[:, 8, :]), ["E"], ["M1"])
    S.dve(lambda e: e.tensor_copy(out=M2p[:], in_=Ei[:, 8, :]), ["E"], ["M2"])
    S.dve(lambda e: e.tensor_scalar(out=M2n[:], in0=Ei[:, 8, :], scalar1=-1.0, scalar2=None, op0=ALU.mult), ["E"], ["M2"])

    gmix0 = sb("gmix0", [128, D]); dg = sb("dg", [128, D])
    S.dma(gmix0[:], gmix_d[0], writes=["gmix0"])
    S.dma(dg[:], dskip_d, writes=["dg"])
    S.dve(lambda e: e.tensor_tensor(out=dg[:], in0=dg[:], in1=gmix0[:], op=ALU.mult), ["dg", "gmix0"], ["dg"])

    S.dve(lambda e: e.memset(x_t[:, 0, 0:1], 0.0), ["Br", "Bi", "Cr", "Ci", "Bbr", "Bbi", "t16a", "t16b"], ["x_t"])
    ss = sb("ss", [128, 8]); rinv = sb("rinv", [128, 8])
    ub = sb("ub", [128, 64, 8, 16], BF16)
    UT = sb("UT", [128, 64, 128], BF16)
    SH = sb("SH", [128, 2, NPAIR, 128])
    Hpb = sb("Hpb", [128, 2, NPAIR, 128], BF16)
    carry = sb("carry", [128, 2, NPAIR])
    T1 = {"dve": sb("T1d", [128, 2, NPAIR // 2]), "pool": sb("T1p", [128, 2, NPAIR // 2])}
    T2 = {"dve": sb("T2d", [128, 2, NPAIR // 2]), "pool": sb("T2p", [128, 2, NPAIR // 2])}
    ytmps = [sb("ytmp0", [128, 8, 16]), sb("ytmp1", [128, 8, 16])]
    junk = UT[:].rearrange("p g c -> p (g c)")[:, 0:D]
    Gb = ub[:].rearrange("p g t h -> p (g t h)").rearrange("p (t d) -> p t d", t=8)

    def s5_macro(src_ap, nvalid, mt_tag, seqs, g_out_ap, final_out):
        P = nvalid
        S.dma(x_t[0:P], src_ap.rearrange("(c t) d -> c t d", t=8), writes=["x_t"])
        for t in range(8):
            S.act(lambda e, t=t: e.activation(out=junk[0:P], in_=x_t[0:P, t, :], func=AF.Square, accum_out=ss[0:P, t:t + 1]), ["x_t"], ["UT", "ss"])
        S.dve(lambda e: e.tensor_scalar(out=rinv[0:P], in0=ss[0:P], scalar1=1.0 / D, scalar2=EPS, op0=ALU.mult, op1=ALU.add), ["ss"], ["rinv"])
        S.act(lambda e: e.activation(out=rinv[0:P], in_=rinv[0:P], func=AF.Sqrt), ["rinv"], ["rinv"])
        S.dve(lambda e: e.reciprocal(out=rinv[0:P], in_=rinv[0:P]), ["rinv"], ["rinv"])
        for t in range(8):
            S.dve(lambda e, t=t: e.scalar_tensor_tensor(out=ub[0:P, :, t, :], in0=x_t[0:P, t, :].rearrange("p (g h) -> p g h", h=16), scalar=rinv[0:P, t:t + 1],
                                                        in1=gmix0[0:P].rearrange("p (g h) -> p g h", h=16), op0=ALU.mult, op1=ALU.mult),
                  ["x_t", "rinv", "gmix0", "ub"], ["ub"])
        for g8 in range(8):
            bk = pbank[g8 % 2]; bkn = "pb%d" % (g8 % 2)
            bkb = bk[:].bitcast(BF16)
            for j in range(8):
                g = g8 * 8 + j
                S.pe(lambda e, g=g, j=j, bkb=bkb: e.transpose(out=bkb[:, j * 128:j * 128 + P], in_=ub[0:P, g].rearrange("p t h -> p (t h)"), identity=identb[0:P, 0:P]),
                     ["ub", "identb"], [bkn])
            S.act(lambda e, g8=g8, bkb=bkb: e.copy(out=UT[:, g8 * 8:(g8 + 1) * 8, 0:P], in_=bkb[:, 0:1024].rearrange("p (j c) -> p j c", j=8)[:, :, 0:P]), [bkn], ["UT"])
        for q in range(8):
            bre = pbank[4 + 2 * (q % 2)]; bim = pbank[5 + 2 * (q % 2)]
            nre = "pb%d" % (4 + 2 * (q % 2)); nim = "pb%d" % (5 + 2 * (q % 2))
            for a in range(4):
                pair = q * 4 + a
                for g2 in range(2):
                    g = 2 * pair + g2
                    S.pe(lambda e, a=a, g2=g2, g=g, pair=pair, bre=bre: e.matmul(bre[64 * g2:64 * g2 + 64, a * 128:a * 128 + P], lhsT=WinReT[:, pair, 64 * g2:64 * g2 + 64], rhs=UT[:, g, 0:P], start=True, stop=True),
                         ["WinReT", "UT"], [nre])
                    S.pe(lambda e, a=a, g2=g2, g=g, pair=pair, bim=bim: e.matmul(bim[64 * g2:64 * g2 + 64, a * 128:a * 128 + P], lhsT=WinImT[:, pair, 64 * g2:64 * g2 + 64], rhs=UT[:, g, 0:P], start=True, stop=True),
                         ["WinImT", "UT"], [nim])
            S.act(lambda e, q=q, bre=bre: e.copy(out=SH[:, 0, q * 4:q * 4 + 4, 0:P], in_=bre[:].rearrange("p (a c) -> p a c", a=4)[:, :, 0:P]), [nre], ["SHr"])
            S.dve(lambda e, q=q, bim=bim: e.tensor_copy(out=SH[:, 1, q * 4:q * 4 + 4, 0:P], in_=bim[:].rearrange("p (a c) -> p a c", a=4)[:, :, 0:P]), [nim], ["SHi"])
        for (c0, c1, init) in seqs:
            for ei, en in enumerate(("dve", "pool")):
                hp = slice(ei * 16, ei * 16 + 16)
                t1 = T1[en]; t2 = T2[en]
                if init[0] == "h0":
                    S.dma(carry[:, 0, hp], h0r[:, init[1], hp], writes=["carry" + en], q="sp")
                    S.dma(carry[:, 1, hp], h0i[:, init[1], hp], writes=["carry" + en], q="sp")
                elif init[0] == "zero":
                    S.on(en, lambda e, hp=hp: e.memset(carry[:, :, hp], 0.0), [], ["carry" + en])
                for c in range(c0, c1):
                    if c == c0:
                        prev = carry[:, :, hp]; prev_r = carry[:, 0, hp]; prev_i = carry[:, 1, hp]
                    else:
                        prev = SH[:, :, hp, c - 1]; prev_r = SH[:, 0, hp, c - 1]; prev_i = SH[:, 1, hp, c - 1]
                    rd = ["SH" + en, "carry" + en, "M1", "M2", "SHr", "SHi"]
                    if c == c0:
                        S.on(en, lambda e, prev=prev, hp=hp, c=c: e.tensor_copy(out=Hpb[:, :, hp, c], in_=prev), rd, ["Hpb" + en])
                    S.on(en, lambda e, prev=prev, hp=hp, t1=t1: e.tensor_tensor(out=t1[:], in0=prev, in1=M1[:, :, hp], op=ALU.mult), rd + ["t1" + en], ["t1" + en])
                    S.on(en, lambda e, prev_i=prev_i, hp=hp, t2=t2: e.tensor_tensor(out=t2[:, 0, :], in0=prev_i, in1=M2n[:, hp], op=ALU.mult), rd + ["t2" + en], ["t2" + en])
                    S.on(en, lambda e, prev_r=prev_r, hp=hp, t2=t2: e.tensor_tensor(out=t2[:, 1, :], in0=prev_r, in1=M2p[:, hp], op=ALU.mult), rd + ["t2" + en], ["t2" + en])
                    S.on(en, lambda e, t1=t1, t2=t2: e.tensor_tensor(out=t1[:], in0=t1[:], in1=t2[:], op=ALU.add), ["t1" + en, "t2" + en], ["t1" + en])
                    S.on(en, lambda e, t1=t1, hp=hp, c=c: e.tensor_tensor(out=SH[:, :, hp, c], in0=SH[:, :, hp, c], in1=t1[:], op=ALU.add), ["t1" + en, "SHr", "SHi", "SH" + en], ["SH" + en])
                for ri in range(2):
                    S.on(en, lambda e, hp=hp, c0=c0, c1=c1, ri=ri: e.tensor_copy(out=Hpb[:, ri, hp, c0 + 1:c1], in_=SH[:, ri, hp, c0:c1 - 1]), ["SH" + en, "SHr", "SHi"], ["Hpb" + en])
                S.on(en, lambda e, hp=hp, c1=c1: e.tensor_copy(out=carry[:, :, hp], in_=SH[:, :, hp, c1 - 1]), ["SH" + en, "SHr", "SHi"], ["carry" + en])
        for (cl, ore, oim) in final_out:
            S.dma(ore, SH[:, 0, :, cl], reads=["SHdve", "SHpool", "SHr", "SHi"], allow_slow_non_contiguous=True)
            S.dma(oim, SH[:, 1, :, cl], reads=["SHdve", "SHpool", "SHr", "SHi"], allow_slow_non_contiguous=True)
        for g in range(64):
            pair, g2 = g // 2, g % 2
            bk = pbank[2 + g % 2]; bkn = "pb%d" % (2 + g % 2)
            rs = slice(64 * g2, 64 * g2 + 64)
            S.pe(lambda e, g=g, bk=bk: e.matmul(bk[0:P, 0:128], lhsT=UT[:, g, 0:P], rhs=Toep[:, g, :], start=True, stop=False), ["UT", "Toep"], [bkn])
            S.pe(lambda e, pair=pair, rs=rs, bk=bk, g2=g2: e.matmul(bk[0:P, 0:128], lhsT=Hpb[:, 0, pair, 0:P], rhs=WoRe[:, g2, pair, :], start=False, stop=False), ["Hpbdve", "Hpbpool", "WoRe"], [bkn])
            S.pe(lambda e, pair=pair, rs=rs, bk=bk, g2=g2: e.matmul(bk[0:P, 0:128], lhsT=Hpb[:, 1, pair, 0:P], rhs=WoIm[:, g2, pair, :], start=False, stop=True), ["Hpbdve", "Hpbpool", "WoIm"], [bkn])
            dsl = slice(16 * g, 16 * g + 16)
            ytmp = ytmps[g % 2]; yn = "ytmp%d" % (g % 2)
            S.dve(lambda e, dsl=dsl, ytmp=ytmp: e.tensor_tensor(out=ytmp[0:P], in0=x_t[0:P, :, dsl], in1=rinv[0:P].unsqueeze(2).to_broadcast([P, 8, 16]), op=ALU.mult), ["x_t", "rinv"], [yn])
            S.dve(lambda e, dsl=dsl, ytmp=ytmp: e.tensor_tensor(out=ytmp[0:P], in0=ytmp[0:P], in1=dg[0:P, dsl].unsqueeze(1).to_broadcast([P, 8, 16]), op=ALU.mult), [yn, "dg"], [yn])
            S.dve(lambda e, bk=bk, ytmp=ytmp: e.tensor_tensor(out=ytmp[0:P], in0=ytmp[0:P], in1=bk[0:P, 0:128].rearrange("p (t h) -> p t h", t=8), op=ALU.add), [yn, bkn], [yn])
            S.act(lambda e, dsl=dsl, ytmp=ytmp: e.activation(out=Gb[0:P, :, dsl], in_=ytmp[0:P], func=AF.Gelu_apprx_tanh), [yn], ["ub"])
        S.dma(g_out_ap.rearrange("(c t) d -> c t d", t=8), Gb[0:P], reads=["ub"])

    import os
    KST = int(os.environ.get("KSTAGE", "9"))
    for mt in range(8 if KST >= 3 else (1 if KST == 2 else 0)):
        seqs = [(0, 128, ("zero",) if mt == 0 else ("carry",))]
        fin = [(127, o_ssm_p[0], o_ssm_p[1])] if mt == 7 or KST == 2 else []
        s5_macro(xp[mt * 1024:(mt + 1) * 1024, :], 128, mt, seqs, gact[mt * 1024:(mt + 1) * 1024, :], fin)
    if KST >= 1:
      s5_macro(xs, 16, 8, [(0, 8, ("h0", 0)), (8, 16, ("h0", 1))], gact[2 * HALF:2 * HALF + 128, :],
             [(7, o_ssm_s[0, 0], o_ssm_s[0, 1]), (15, o_ssm_s[1, 0], o_ssm_s[1, 1])])


    x1s = dscr("x1s", [NT, D]); x2s = dscr("x2s", [NT, D]); x3s = dscr("x3s", [NT, D])

    def xsrc(t0, n):
        return xp[t0:t0 + n, :] if t0 < 2 * HALF else xs[t0 - 2 * HALF:t0 - 2 * HALF + n, :]

    def load_weight(dst3, src2, nchunks, P, ncol, tag):
        stg = [sb(tag + "_s0", [128, ncol]), sb(tag + "_s1", [128, ncol])]
        engs = ["act", "pool", "dve"]
        for c in range(nchunks):
            b = stg[c % 2]; bn = tag + "_s%d" % (c % 2)
            S.dma(b[0:P], src2[c * P:(c + 1) * P, :], writes=[bn])
            en = engs[c % 3]
            if en == "act":
                S.act(lambda e, b=b, c=c: e.copy(out=dst3[0:P, c, :], in_=b[0:P]), [bn], [tag + str(c)])
            else:
                S.on(en, lambda e, b=b, c=c: e.tensor_copy(out=dst3[0:P, c, :], in_=b[0:P]), [bn], [tag + str(c)])


    def rms_to_bf16(x_ap, P, gain, out_ap, xname, oname, col):
        S.act(lambda e: e.activation(out=out_ap, in_=x_ap, func=AF.Square, accum_out=ssn[0:P, col:col + 1]), [xname], [oname, "ssn%d" % col])
        S.dve(lambda e: e.tensor_scalar(out=rvn[0:P, col:col + 1], in0=ssn[0:P, col:col + 1], scalar1=1.0 / D, scalar2=EPS, op0=ALU.mult, op1=ALU.add), ["ssn%d" % col], ["rvn%d" % col])
        S.act(lambda e: e.activation(out=rvn[0:P, col:col + 1], in_=rvn[0:P, col:col + 1], func=AF.Sqrt), ["rvn%d" % col], ["rvn%d" % col])
        S.dve(lambda e: e.reciprocal(out=rvn[0:P, col:col + 1], in_=rvn[0:P, col:col + 1]), ["rvn%d" % col], ["rvn%d" % col])
        S.dve(lambda e: e.scalar_tensor_tensor(out=out_ap, in0=x_ap, scalar=rvn[0:P, col:col + 1], in1=gain[0:P], op0=ALU.mult, op1=ALU.mult), [xname, "rvn%d" % col, oname], [oname])

    tcount = [0]
    tforce = [None]

    def transpose8(src_ap, P, dstT_ap, sname, dname):
        bi = tcount[0] % 2 if tforce[0] is None else tforce[0]
        tcount[0] += 1
        bkb = pbank[bi][:].bitcast(BF16)
        bn = "pb%d" % bi
        for dc in range(8):
            S.pe(lambda e, dc=dc: e.transpose(out=bkb[:, dc * 128:dc * 128 + P], in_=src_ap[:, dc * 128:(dc + 1) * 128], identity=identb[0:P, 0:P]), [sname], [bn])
        S.act(lambda e: e.copy(out=dstT_ap, in_=bkb[:, 0:1024].rearrange("p (j c) -> p j c", j=8)[:, :, 0:P]), [bn], [dname])

    S.barrier(); aoff[0] = mP2
    wglu_d = din("w_glu", [D, 2 * D])
    Wglu = sb("Wglu", [128, 8, 2 * D], BF16)
    m1 = aoff[0]
    load_weight(Wglu, wglu_d, 8, 128, 2 * D, "wg")
    S.barrier(); aoff[0] = m1
    Gt = [sb("Gt%d" % i, [128, D], BF16) for i in range(2)]
    Xt = [sb("Xt%d" % i, [128, D]) for i in range(2)]
    GTt = [sb("GTt%d" % i, [128, 8, 128], BF16) for i in range(2)]
    sgb = [sb("sgb%d" % i, [128, 512]) for i in range(2)]
    mmb = [sb("mmb%d" % i, [128, 512]) for i in range(2)]
    tilesB = [(t, 128) for t in range(0, NT, 128)]
    if KST < 4:
        tilesB = []

    def loadB(i):
        t0, P = tilesB[i]; b = i % 2
        S.dma(Gt[b][0:P], gact[t0:t0 + P, :], reads=["gscr"], writes=["Gt%d" % b])
        S.dma(Xt[b][0:P], xsrc(t0, P), writes=["Xt%d" % b])

    if tilesB:
        loadB(0)
    for i, (t0, P) in enumerate(tilesB):
        b = i % 2
        if i + 1 < len(tilesB):
            loadB(i + 1)
        transpose8(Gt[b][0:P], P, GTt[b][:, :, 0:P], "Gt%d" % b, "GTt%d" % b)
        for nh in range(2):
            j = 2 * i + nh
            vb = pbank[2 + 2 * (j % 3)]; gb = pbank[3 + 2 * (j % 3)]
            vn = "pb%d" % (2 + 2 * (j % 3)); gn = "pb%d" % (3 + 2 * (j % 3))
            for dc in range(8):
                S.pe(lambda e, dc=dc, vb=vb, nh=nh, b=b: e.matmul(vb[0:P, :], lhsT=GTt[b][:, dc, 0:P], rhs=Wglu[:, dc, nh * 512:(nh + 1) * 512], start=(dc == 0), stop=(dc == 7)), ["GTt%d" % b], [vn])
            for dc in range(8):
                S.pe(lambda e, dc=dc, gb=gb, nh=nh, b=b: e.matmul(gb[0:P, :], lhsT=GTt[b][:, dc, 0:P], rhs=Wglu[:, dc, D + nh * 512:D + (nh + 1) * 512], start=(dc == 0), stop=(dc == 7)), ["GTt%d" % b], [gn])
            sg = sgb[j % 2]; mm = mmb[j % 2]
            S.act(lambda e, sg=sg, gb=gb: e.activation(out=sg[0:P], in_=gb[0:P, :], func=AF.Sigmoid), [gn], ["sg%d" % (j % 2)])
            S.dve(lambda e, sg=sg, mm=mm, vb=vb: e.tensor_tensor(out=mm[0:P], in0=vb[0:P, :], in1=sg[0:P], op=ALU.mult), [vn, "sg%d" % (j % 2)], ["mm%d" % (j % 2)])
            S.pool(lambda e, mm=mm, b=b, nh=nh: e.tensor_tensor(out=Xt[b][0:P, nh * 512:(nh + 1) * 512], in0=Xt[b][0:P, nh * 512:(nh + 1) * 512], in1=mm[0:P], op=ALU.add), ["mm%d" % (j % 2), "Xt%d" % b], ["Xt%d" % b])
        S.dma(x1s[t0:t0 + P, :], Xt[b][0:P], reads=["Xt%d" % b], writes=["x1s"])

    wup_d = din("w_up", [2, D, DFF]); wgt_d = din("w_gate", [2, D, DFF]); wdn_d = din("w_down", [2, DFF, D])
    cw_d = din("conv_w", [2, 3, DFF]); cb_d = din("conv_b", [2, DFF]); cvst_d = din("cvst", [2, 2, 2, DFF])
    o_cvp = dout("o_cvp", [2, 2, DFF]); o_cvs = dout("o_cvs", [2, 2, 2, DFF])
    o_yp = dout("o_yp", [HALF, D]); o_ys = dout("o_ys", [128, D])
    NFC = DFF // 128

    def ffn_phase(layer, xin, xout, groups, gain_d, final):
        S.barrier(); aoff[0] = mP2
        tforce[0] = 0
        Wup = sb("Wup", [128, 8, DFF], BF16); Wgt = sb("Wgt", [128, 8, DFF], BF16); Wdn = sb("Wdn", [128, NFC, D], BF16)
        cw = sb("cw", [128, NFC, 3]); cbb = sb("cbb", [128, NFC]); hist = sb("hist", [128, NFC, 2])
        gain = sb("gain", [128, D]); gfin = sb("gfinal", [128, D])
        m1 = aoff[0]
        load_weight(Wup, wup_d[layer], 8, 128, DFF, "wu")
        aoff[0] = m1
        S.barrier()
        load_weight(Wgt, wgt_d[layer], 8, 128, DFF, "wt")
        aoff[0] = m1
        S.barrier()
        load_weight(Wdn, wdn_d[layer], NFC, 128, D, "wd")
        for j in range(3):
            S.dma(cw[:, :, j], cw_d[layer, j].rearrange("(fc f) -> f fc", f=128), writes=["cw"], allow_slow_non_contiguous=True)
        S.dma(cbb[:], cb_d[layer].rearrange("(fc f) -> f fc", f=128), writes=["cbb"], allow_slow_non_contiguous=True)
        S.dma(gain[:], gain_d, writes=["gain"])
        S.dma(gfin[:], gfin_d, writes=["gfin"])
        S.barrier(); aoff[0] = m1
        Xg = [sb("Xg%d" % i, [128, 2, D]) for i in range(2)]
        hsb2 = sb("hsb2", [128, 2, D], BF16)
        hsT = [sb("hsT%d" % i, [128, 8, 256], BF16) for i in range(2)]
        hT = [sb("hT%d" % i, [128, NFC, 256], BF16) for i in range(2)]
        abuf = [sb("abuf%d" % i, [128, 258]) for i in range(4)]
        cbuf = [sb("cbuf%d" % i, [128, 256]) for i in range(4)]
        ybuf = sb("ybuf", [128, D])

        def tiles_of(g):
            n = g["n"]
            return [(0, 128), (1, 128)] if n == 256 else [(0, n)]

        def load(gi):
            g = groups[gi]; b = gi % 2
            if g["n"] == 256:
                S.dma(Xg[b][:], xin[g["t0"]:g["t0"] + 256, :].rearrange("(i p) d -> p i d", i=2), writes=["Xg%d" % b])
            else:
                S.dma(Xg[b][0:g["n"], 0, :], xin[g["t0"]:g["t0"] + g["n"], :], writes=["Xg%d" % b])

        def prologue_norm(gi):
            g = groups[gi]; b = gi % 2
            for (ti, P) in tiles_of(g):
                rms_to_bf16(Xg[b][0:P, ti, :], P, gain, hsb2[0:P, ti, :], "Xg%d" % b, "hsb%d" % ti, ti)

        def prologue_T(gi):
            g = groups[gi]; b = gi % 2
            for (ti, P) in tiles_of(g):
                transpose8(hsb2[0:P, ti, :], P, hsT[b][:, :, ti * 128:ti * 128 + P], "hsb%d" % ti, "hsT%d" % b)

        def upgate(gi):
            g = groups[gi]; b = gi % 2; N = g["n"]
            if g["hist"] == "zero":
                S.pool(lambda e: e.memset(hist[:], 0.0), [], ["hist"])
            elif g["hist"] != "carry":
                for j in range(2):
                    S.dma(hist[:, :, j], cvst_d[layer, g["hist"], j].rearrange("(fc f) -> f fc", f=128), writes=["hist"], allow_slow_non_contiguous=True)
            deferred = []
            for fc in range(NFC):
                if fc == 11 and gi + 1 < len(groups):
                    prologue_norm(gi + 1)
                if fc == 19 and gi + 1 < len(groups):
                    prologue_T(gi + 1)
                ug = pbank[1 + fc % 4]; un = "pbu%d" % (1 + fc % 4); gn_ = un
                for kc in range(8):
                    S.pe(lambda e, kc=kc, fc=fc, ug=ug: e.matmul(ug[:, 0:N], lhsT=Wup[:, kc, fc * 128:(fc + 1) * 128], rhs=hsT[b][:, kc, 0:N], start=(kc == 0), stop=(kc == 7)), ["hsT%d" % b], [un])
                for kc in range(8):
                    S.pe(lambda e, kc=kc, fc=fc, ug=ug: e.matmul(ug[:, 256:256 + N], lhsT=Wgt[:, kc, fc * 128:(fc + 1) * 128], rhs=hsT[b][:, kc, 0:N], start=(kc == 0), stop=(kc == 7)), ["hsT%d" % b], [gn_])
                ac = abuf[fc % 4]; an = "ab%d" % (fc % 4); ct = cbuf[fc % 4]; cn = "cb%d" % (fc % 4)
                S.pool(lambda e, ac=ac, fc=fc: e.tensor_copy(out=ac[:, 0:2], in_=hist[:, fc, :]), ["hist", an], [an])
                S.act(lambda e, ac=ac, ug=ug: e.copy(out=ac[:, 2:2 + N], in_=ug[:, 0:N]), [un, an], [an])
                S.pool(lambda e, ct=ct, ac=ac, fc=fc: e.tensor_scalar(out=ct[:, 0:N], in0=ac[:, 2:2 + N], scalar1=cw[:, fc, 2:3], scalar2=cbb[:, fc:fc + 1], op0=ALU.mult, op1=ALU.add), [an, "cw", "cbb", cn], [cn])
                S.dve(lambda e, ct=ct, ac=ac, fc=fc: e.scalar_tensor_tensor(out=ct[:, 0:N], in0=ac[:, 1:1 + N], scalar=cw[:, fc, 1:2], in1=ct[:, 0:N], op0=ALU.mult, op1=ALU.add), [an, cn, "cw"], [cn])
                S.dve(lambda e, ct=ct, ac=ac, fc=fc: e.scalar_tensor_tensor(out=ct[:, 0:N], in0=ac[:, 0:N], scalar=cw[:, fc, 0:1], in1=ct[:, 0:N], op0=ALU.mult, op1=ALU.add), [an, cn, "cw"], [cn])
                S.pool(lambda e, ac=ac, fc=fc: e.tensor_copy(out=hist[:, fc, :], in_=ac[:, N:N + 2]), [an, "hist"], ["hist"])
                if deferred:
                    deferred.pop()()

                def tail(ct=ct, ug=ug, fc=fc, cn=cn, gn_=gn_):
                    S.act(lambda e: e.activation(out=ct[:, 0:N], in_=ct[:, 0:N], func=AF.Gelu_apprx_tanh), [cn], [cn])
                    S.dve(lambda e: e.tensor_tensor(out=hT[b][:, fc, 0:N], in0=ct[:, 0:N], in1=ug[:, 256:256 + N], op=ALU.mult), [cn, gn_, "hT%d" % b], ["hT%d" % b])
                deferred.append(tail)
            if deferred:
                deferred.pop()()
            if g.get("conv_out") is not None:
                for j in range(2):
                    S.dma(g["conv_out"][j].rearrange("(fc f) -> f fc", f=128), hist[:, :, j], reads=["hist"], allow_slow_non_contiguous=True)

        dcount = [0]

        def down(gi):
            g = groups[gi]; b = gi % 2
            if not g["store"]:
                return
            for (ti, P) in tiles_of(g):
                for half in range(2):
                    k = dcount[0] % 3; dcount[0] += 1
                    ob = pbank[5 + k]; on_ = "pb%d" % (5 + k)
                    for fc in range(NFC):
                        S.pe(lambda e, fc=fc, ob=ob, ti=ti, half=half: e.matmul(ob[0:P, :], lhsT=hT[b][:, fc, ti * 128:ti * 128 + P], rhs=Wdn[:, fc, half * 512:(half + 1) * 512], start=(fc == 0), stop=(fc == NFC - 1)), ["hT%d" % b], [on_])
                    S.dve(lambda e, ob=ob, ti=ti, half=half: e.tensor_tensor(out=Xg[b][0:P, ti, half * 512:(half + 1) * 512], in0=Xg[b][0:P, ti, half * 512:(half + 1) * 512], in1=ob[0:P, :], op=ALU.add), [on_, "Xg%d" % b], ["Xg%d" % b])
                t0 = g["t0"] + ti * 128
                if final:
                    dst = g["yout"][ti * 128:ti * 128 + P, :]
                    S.act(lambda e, ti=ti: e.activation(out=ybuf[0:P], in_=Xg[b][0:P, ti, :], func=AF.Square, accum_out=ssn[0:P, 2:3]), ["Xg%d" % b, "ybuf"], ["ybuf", "ssn2"])
                    S.dve(lambda e: e.tensor_scalar(out=rvn[0:P, 2:3], in0=ssn[0:P, 2:3], scalar1=1.0 / D, scalar2=EPS, op0=ALU.mult, op1=ALU.add), ["ssn2"], ["rvn2"])
                    S.act(lambda e: e.activation(out=rvn[0:P, 2:3], in_=rvn[0:P, 2:3], func=AF.Sqrt), ["rvn2"], ["rvn2"])
                    S.dve(lambda e: e.reciprocal(out=rvn[0:P, 2:3], in_=rvn[0:P, 2:3]), ["rvn2"], ["rvn2"])
                    S.dve(lambda e, ti=ti: e.scalar_tensor_tensor(out=ybuf[0:P], in0=Xg[b][0:P, ti, :], scalar=rvn[0:P, 2:3], in1=gfin[0:P], op0=ALU.mult, op1=ALU.mult), ["Xg%d" % b, "rvn2", "ybuf"], ["ybuf"])
                    S.dma(dst, ybuf[0:P], reads=["ybuf"])
                else:
                    S.dma(xout[t0:t0 + P, :], Xg[b][0:P, ti, :], reads=["Xg%d" % b], writes=["xout"])

        if groups:
            load(0)
            prologue_norm(0)
            prologue_T(0)
        for gi in range(len(groups)):
            if gi > 0:
                down(gi - 1)
            if gi + 1 < len(groups):
                load(gi + 1)
            upgate(gi)
        if groups:
            down(len(groups) - 1)
        tforce[0] = None

    groups0 = []
    for gi in range(2 * HALF // 256):
        groups0.append(dict(t0=gi * 256, n=256, hist="zero" if gi == 0 else "carry", store=True,
                            conv_out=o_cvp[0] if gi == 2 * HALF // 256 - 1 else None))
    for sq in range(2):
        groups0.append(dict(t0=2 * HALF + 64 * sq, n=64, hist=sq, store=True, conv_out=o_cvs[0, sq]))
    if KST >= 5:
        ffn_phase(0, x1s, x2s, groups0, gffn_d[0], False)


    wq_d = din("w_qkvf", [D, 3 * D + NH])
    kc_d = din("kcache", [2, HALF, D]); vc_d = din("vcache", [2, HALF, D]); lfc_d = din("lfcache", [2, HALF, NH])
    o_kp = dout("o_kp", [HALF, D]); o_vp = dout("o_vp", [HALF, D]); o_lfp = dout("o_lfp", [HALF, NH])
    o_ks = dout("o_ks", [128, D]); o_vs = dout("o_vs", [128, D]); o_lfs = dout("o_lfs", [128, NH])
    kts = dscr("kts", [NH, HD, NT], BF16); qts = dscr("qts", [NH, HD, NT], BF16); vsb = dscr("vsb", [NT, D], BF16)
    crow = dscr("crow", [NH, NT]); ccol = dscr("ccol", [NT, NH])
    kcT = dscr("kcT", [2, NH, HD, HALF], BF16); vcb = dscr("vcb", [2, HALF, D], BF16); ccc = dscr("ccc", [2, HALF, NH])

    def phase_d():
        S.barrier(); aoff[0] = mP2
        NQ = 3 * D + NH
        Wq = sb("Wq", [128, 8, NQ], BF16)
        gain = sb("gainq", [128, D])
        Lacc = sb("Lacc", [128, NH])
        m1 = aoff[0]
        load_weight(Wq, wq_d, 8, 128, NQ, "wq")
        S.dma(gain[:], gmix_d[1], writes=["gainq"])
        S.barrier(); aoff[0] = m1
        Xq = [sb("Xq%d" % i, [128, D]) for i in range(2)]
        hsq = sb("hsq", [128, D], BF16)
        hsTq = [sb("hsTq%d" % i, [128, 8, 128], BF16) for i in range(2)]
        Qb = [sb("Qb%d" % i, [128, D], BF16) for i in range(2)]
        Kf = [sb("Kf%d" % i, [128, D]) for i in range(2)]
        Kb = [sb("Kb%d" % i, [128, D], BF16) for i in range(2)]
        Vf = [sb("Vf%d" % i, [128, D]) for i in range(2)]
        Vb = [sb("Vb%d" % i, [128, D], BF16) for i in range(2)]
        QTt = [sb("QTt%d" % i, [128, 8, 128], BF16) for i in range(2)]
        KTt = [sb("KTt%d" % i, [128, 8, 128], BF16) for i in range(2)]
        zf = [sb("zf%d" % i, [128, NH]) for i in range(2)]
        lgf = [sb("lgf%d" % i, [128, NH]) for i in range(2)]
        r1 = sb("r1", [128, NH]); L3 = sb("L3", [128, 3, NH], BF16); La3 = sb("La3", [128, 3, NH], BF16); ra = sb("ra", [128, NH])
        csb = [sb("csb%d" % i, [128, NH]) for i in range(2)]
        cTs = [sb("cTs%d" % i, [NH, 128]) for i in range(2)]

        def split3(src, dst3, tmp, P, sname, dname, tname):
            S.dve(lambda e: e.tensor_copy(out=dst3[0:P, 0, :], in_=src[0:P]), [sname, dname], [dname])
            S.dve(lambda e: e.tensor_tensor(out=tmp[0:P], in0=src[0:P], in1=dst3[0:P, 0, :], op=ALU.subtract), [sname, dname, tname], [tname])
            S.dve(lambda e: e.tensor_copy(out=dst3[0:P, 1, :], in_=tmp[0:P]), [tname, dname], [dname])
            S.dve(lambda e: e.tensor_tensor(out=tmp[0:P], in0=tmp[0:P], in1=dst3[0:P, 1, :], op=ALU.subtract), [tname, dname], [tname])
            S.dve(lambda e: e.tensor_copy(out=dst3[0:P, 2, :], in_=tmp[0:P]), [tname, dname], [dname])

        def cumsum_path(lg, lgn, P, b, ccol_dst, crow_dst):
            split3(lg, L3, r1, P, lgn, "L3", "r1")
            split3(Lacc, La3, ra, 128, "Lacc", "La3", "ra")
            cb = pbank[3]
            for j in range(3):
                S.pe(lambda e, j=j: e.matmul(cb[0:P, 0:NH], lhsT=tri[0:P, 0:P], rhs=L3[0:P, j, :], start=(j == 0), stop=False), ["L3", "tri"], ["pb3"])
            for j in range(3):
                S.pe(lambda e, j=j: e.matmul(cb[0:P, 0:NH], lhsT=onesb[:, 0:P], rhs=La3[:, j, :], start=False, stop=(j == 2)), ["La3", "onesb"], ["pb3"])
            S.dve(lambda e: e.tensor_tensor(out=Lacc[0:P], in0=Lacc[0:P], in1=lg[0:P], op=ALU.add), ["Lacc", lgn, "La3"], ["Lacc"])
            S.act(lambda e: e.copy(out=csb[b][0:P], in_=cb[0:P, 0:NH]), ["pb3"], ["csb%d" % b])
            S.dma(ccol_dst, csb[b][0:P], reads=["csb%d" % b], writes=["ccol"])
            if crow_dst is not None:
                bi = tcount[0] % 2; tcount[0] += 1
                tb = pbank[bi]
                S.pe(lambda e: e.transpose(out=tb[0:NH, 0:P], in_=csb[b][0:P, :], identity=ident[0:P, 0:P]), ["csb%d" % b, "ident"], ["pb%d" % bi])
                S.act(lambda e: e.copy(out=cTs[b][0:NH, 0:P], in_=tb[0:NH, 0:P]), ["pb%d" % bi], ["cTs%d" % b])
                S.dma(crow_dst, cTs[b][0:NH, 0:P], reads=["cTs%d" % b], writes=["crow"])

        def kt_dst(scr, t0, P):
            return scr.rearrange("(hp h2) d t -> (h2 d) hp t", h2=2)[:, :, t0:t0 + P]

        tiles = []
        for i in range(2 * HALF // 128):
            tiles.append(dict(kind="full", t0=i * 128, P=128, reset=(i == 0)))
        for sq in range(2):
            for j in range(HALF // 128):
                tiles.append(dict(kind="cache", sq=sq, j=j, P=128, reset=(j == 0)))
            tiles.append(dict(kind="full", t0=2 * HALF + 64 * sq, P=64, reset=False))

        def load(i):
            tl = tiles[i]; b = i % 2; P = tl["P"]
            if tl["kind"] == "full":
                S.dma(Xq[b][0:P], x2s[tl["t0"]:tl["t0"] + P, :], writes=["Xq%d" % b])
            else:
                sq, j = tl["sq"], tl["j"]
                S.dma(Kf[b][:], kc_d[sq, j * 128:(j + 1) * 128, :], writes=["Kf%d" % b])
                S.dma(Vf[b][:], vc_d[sq, j * 128:(j + 1) * 128, :], writes=["Vf%d" % b])
                S.dma(lgf[b][:], lfc_d[sq, j * 128:(j + 1) * 128, :], writes=["lgf%d" % b])

        def do_tile(i, tl):
            b = i % 2; P = tl["P"]
            if i + 1 < len(tiles):
                load(i + 1)
            if tl["reset"]:
                S.dve(lambda e: e.memset(Lacc[:], 0.0), ["Lacc"], ["Lacc"])
            if tl["kind"] == "cache":
                sq, j = tl["sq"], tl["j"]
                S.pool(lambda e, b=b: e.tensor_copy(out=Kb[b][:], in_=Kf[b][:]), ["Kf%d" % b], ["Kb%d" % b])
                S.pool(lambda e, b=b: e.tensor_copy(out=Vb[b][:], in_=Vf[b][:]), ["Vf%d" % b], ["Vb%d" % b])
                transpose8(Kb[b][:], 128, KTt[b][:], "Kb%d" % b, "KTt%d" % b)
                S.dma(kt_dst(kcT[sq], j * 128, 128), KTt[b][:], reads=["KTt%d" % b], writes=["kcT"])
                S.dma(vcb[sq, j * 128:(j + 1) * 128, :], Vb[b][:], reads=["Vb%d" % b], writes=["vcb"])
                cumsum_path(lgf[b], "lgf%d" % b, 128, b, ccc[sq, j * 128:(j + 1) * 128, :], None)
                return
            t0 = tl["t0"]
            rms_to_bf16(Xq[b][0:P], P, gain, hsq[0:P], "Xq%d" % b, "hsq", 3)
            transpose8(hsq[0:P], P, hsTq[b][:, :, 0:P], "hsq", "hsTq%d" % b)
            hT_ = hsTq[b]; hn = "hsTq%d" % b

            def mm(bank, cols, c0, n):
                for kc in range(8):
                    S.pe(lambda e, kc=kc: e.matmul(pbank[bank][0:P, cols], lhsT=hT_[:, kc, 0:P], rhs=Wq[:, kc, c0:c0 + n], start=(kc == 0), stop=(kc == 7)), [hn], ["pb%d" % bank])
            mm(2, slice(0, NH), 3 * D, NH)
            mm(3, slice(0, 512), 0, 512); mm(4, slice(0, 512), 512, 512)
            mm(5, slice(0, 512), D, 512); mm(6, slice(0, 512), D + 512, 512)
            mm(7, slice(0, 512), 2 * D, 512)
            S.dve(lambda e, b=b: e.tensor_tensor(out=zf[b][0:P], in0=pbank[2][0:P, 0:NH], in1=bft[0:P], op=ALU.add), ["pb2", "bft"], ["zf%d" % b])
            mm(2, slice(0, 512), 2 * D + 512, 512)
            S.act(lambda e, b=b: e.activation(out=zf[b][0:P], in_=zf[b][0:P], func=AF.Exp, scale=-1.0), ["zf%d" % b], ["zf%d" % b])
            S.act(lambda e, b=b: e.activation(out=zf[b][0:P], in_=zf[b][0:P], func=AF.Ln, bias=one1[0:P, 0:1]), ["zf%d" % b, "one1"], ["zf%d" % b])
            S.dve(lambda e, b=b: e.tensor_scalar(out=lgf[b][0:P], in0=zf[b][0:P], scalar1=-1.0, scalar2=None, op0=ALU.mult), ["zf%d" % b], ["lgf%d" % b])
            for hf in range(2):
                S.act(lambda e, hf=hf, b=b: e.activation(out=Qb[b][0:P, hf * 512:(hf + 1) * 512], in_=pbank[3 + hf][0:P, :], func=AF.Copy, scale=HD ** -0.5), ["pb%d" % (3 + hf)], ["Qb%d" % b])
                S.dve(lambda e, hf=hf, b=b: e.tensor_copy(out=Kf[b][0:P, hf * 512:(hf + 1) * 512], in_=pbank[5 + hf][0:P, :]), ["pb%d" % (5 + hf)], ["Kf%d" % b])
            S.act(lambda e, b=b: e.copy(out=Vf[b][0:P, 0:512], in_=pbank[7][0:P, :]), ["pb7"], ["Vf%d" % b])
            S.act(lambda e, b=b: e.copy(out=Vf[b][0:P, 512:1024], in_=pbank[2][0:P, :]), ["pb2"], ["Vf%d" % b])
            S.pool(lambda e, b=b: e.tensor_copy(out=Kb[b][0:P], in_=Kf[b][0:P]), ["Kf%d" % b], ["Kb%d" % b])
            S.pool(lambda e, b=b: e.tensor_copy(out=Vb[b][0:P], in_=Vf[b][0:P]), ["Vf%d" % b], ["Vb%d" % b])
            transpose8(Qb[b][0:P], P, QTt[b][:, :, 0:P], "Qb%d" % b, "QTt%d" % b)
            transpose8(Kb[b][0:P], P, KTt[b][:, :, 0:P], "Kb%d" % b, "KTt%d" % b)
            S.dma(kt_dst(qts, t0, P), QTt[b][:, :, 0:P], reads=["QTt%d" % b], writes=["qts"])
            S.dma(kt_dst(kts, t0, P), KTt[b][:, :, 0:P], reads=["KTt%d" % b], writes=["kts"])
            S.dma(vsb[t0:t0 + P, :], Vb[b][0:P], reads=["Vb%d" % b], writes=["vsb"])
            if t0 >= 2 * HALF:
                o0 = t0 - 2 * HALF
                S.dma(o_ks[o0:o0 + P, :], Kf[b][0:P], reads=["Kf%d" % b]); S.dma(o_vs[o0:o0 + P, :], Vf[b][0:P], reads=["Vf%d" % b])
                S.dma(o_lfs[o0:o0 + P, :], lgf[b][0:P], reads=["lgf%d" % b])
            elif t0 >= HALF:
                o0 = t0 - HALF
                S.dma(o_kp[o0:o0 + P, :], Kf[b][0:P], reads=["Kf%d" % b]); S.dma(o_vp[o0:o0 + P, :], Vf[b][0:P], reads=["Vf%d" % b])
                S.dma(o_lfp[o0:o0 + P, :], lgf[b][0:P], reads=["lgf%d" % b])
            cumsum_path(lgf[b], "lgf%d" % b, P, b, ccol[t0:t0 + P, :], crow[:, t0:t0 + P])

        load(0)
        for i, tl in enumerate(tiles):
            do_tile(i, tl)

    if KST >= 6:
        phase_d()


    ats = dscr("ats", [NH, HD, NT], BF16)
    QS0 = HALF - 512

    def phase_e():
        S.barrier(); aoff[0] = mP2
        NKB = 2 * HALF // 128
        KA = [sb("KA%d" % i, [128, 2 * HALF], BF16) for i in range(2)]
        VA = [sb("VA%d" % i, [128, NKB, 65], BF16) for i in range(2)]
        CK = [sb("CK%d" % i, [128, NKB]) for i in range(2)]
        QA = [sb("QA%d" % i, [128, 512], BF16) for i in range(2)]
        crq = [sb("crq%d" % i, [128, 512]) for i in range(2)]
        cref = [sb("cref%d" % i, [128, 1]) for i in range(2)]
        cbias = [sb("cbias%d" % i, [128, NKB]) for i in range(2)]
        dq = sb("dq", [128, 512]); hq = sb("hq", [128, 512], BF16)
        PT = [sb("PT%d" % i, [128, 512], BF16) for i in range(4)]
        Osb = [sb("Osb%d" % i, [128, 512]) for i in range(2)]
        rt = sb("rt", [128, 512]); rt2 = sb("rt2", [128, 512]); rl3 = sb("rl3", [128, 3, 512], BF16)
        ATb = [sb("ATb%d" % i, [128, 512], BF16) for i in range(2)]
        for i in range(2):
            S.pool(lambda e, i=i: e.memset(KA[i][32:33, :], 1.0), [], ["KA%d" % i])
            S.pool(lambda e, i=i: e.memset(KA[i][64:65, :], 1.0), [], ["KA%d" % i])
            S.pool(lambda e, i=i: e.memset(VA[i][:, :, 64:65], 1.0), [], ["VA%d" % i])
            S.pool(lambda e, i=i: e.memset(QA[i][:], 0.0), [], ["QA%d" % i])
        sel = sb("sel", [128, 64], BF16)
        S.pool(lambda e: e.memset(sel[:], 0.0), [], ["sel"])
        S.pool(lambda e: e.memset(sel[64:65, :], 1.0), ["sel"], ["sel"])
        S.pool(lambda e: e.memset(rl3[:], 0.0), [], ["rl3"])
        cnt = {"s": 0, "q": 0, "p": 0}
        pending = []

        def rows67(dst, src, c0, c1, name, wname):
            S.dma(dst[0:32, c0:c1], src[0:32, :], writes=[wname])
            S.dma(dst[33:64, c0:c1], src[32:63, :], writes=[wname])
            S.dma(dst[65:66, c0:c1], src[63:64, :], writes=[wname])

        def attend(h, hb, q_t0, nq, blocks, pm_nb, out_t0):
            qb = cnt["q"] % 2; cnt["q"] += 1
            qn = "QA%d" % qb
            rows67(QA[qb], qts[h, :, q_t0:q_t0 + nq], 0, nq, qn, qn)
            S.dma(crq[qb][32:33, 0:nq], crow[h:h + 1, q_t0:q_t0 + nq], writes=["crq%d" % qb])
            S.dma(crq[qb][64:65, 0:nq], crow[h:h + 1, q_t0:q_t0 + nq], writes=["crq%d" % qb])
            S.dma(cref[qb][:], crow[h:h + 1, q_t0:q_t0 + 1].partition_broadcast(128), writes=["cref%d" % qb])
            S.dve(lambda e: e.tensor_scalar(out=dq[64:65, 0:nq], in0=crq[qb][64:65, 0:nq], scalar1=crq[qb][64:65, 0:1], scalar2=None, op0=ALU.subtract), ["crq%d" % qb, "dq64"], ["dq64"])
            S.dve(lambda e: e.tensor_copy(out=QA[qb][64:65, 0:nq], in_=dq[64:65, 0:nq]), ["dq64", qn], [qn])
            S.dve(lambda e: e.tensor_scalar(out=dq[32:33, 0:nq], in0=crq[qb][32:33, 0:nq], scalar1=crq[qb][32:33, 0:1], scalar2=None, op0=ALU.subtract), ["crq%d" % qb, "dq32"], ["dq32"])
            S.dve(lambda e: e.tensor_copy(out=hq[32:33, 0:nq], in_=dq[32:33, 0:nq]), ["dq32", "hq"], ["hq"])
            S.dve(lambda e: e.tensor_tensor(out=dq[32:33, 0:nq], in0=dq[32:33, 0:nq], in1=hq[32:33, 0:nq], op=ALU.subtract), ["dq32", "hq"], ["dq32"])
            S.dve(lambda e: e.tensor_copy(out=QA[qb][32:33, 0:nq], in_=dq[32:33, 0:nq]), ["dq32", qn], [qn])
            nbk = max(bk for (bk, nk, c0) in blocks) + 1
            S.dve(lambda e: e.tensor_scalar(out=cbias[qb][:, 0:nbk], in0=CK[hb][:, 0:nbk], scalar1=-1.0, scalar2=cref[qb][:, 0:1], op0=ALU.mult, op1=ALU.add), ["CK%d" % hb, "cref%d" % qb, "cbias%d" % qb], ["cbias%d" % qb])
            if pm_nb > 0:
                S.dve(lambda e: e.tensor_scalar(out=cbias[qb][:, 0:pm_nb], in0=cbias[qb][:, 0:pm_nb], scalar1=pmk[:, 0:1], scalar2=None, op0=ALU.add), ["cbias%d" % qb, "pmk"], ["cbias%d" % qb])
            ob = pbank[5 + qb]; obn = "pb%d" % (5 + qb)
            KEL = int(os.environ.get("KE_LVL", "9"))
            if KEL < 2:
                return
            nb_ = len(blocks)
            meta = {}

            def qk(bi_):
                bk, nk, c0 = blocks[bi_]
                si = cnt["s"] % 3; cnt["s"] += 1
                pi = cnt["p"] % 4; cnt["p"] += 1
                sbk = pbank[2 + si]; sn = "pb%d" % (2 + si)
                cc = 0 if c0 is None else c0
                meta[bi_] = (pi, cc)
                S.pe(lambda e: e.matmul(sbk[0:nk, cc:nq], lhsT=KA[hb][0:66, bk * 128:bk * 128 + nk], rhs=QA[qb][0:66, cc:nq], start=True, stop=True), ["KA%d" % hb, qn], [sn])
                if c0 is not None:
                    w = min(128, nq - cc)
                    S.dve(lambda e: e.tensor_tensor(out=sbk[0:nk, cc:cc + w], in0=sbk[0:nk, cc:cc + w], in1=cmask[0:nk, 0:w], op=ALU.add), [sn, "cmask"], [sn])
                S.act(lambda e: e.activation(out=PT[pi][0:nk, cc:nq], in_=sbk[0:nk, cc:nq], func=AF.Exp, bias=cbias[qb][0:nk, bk:bk + 1]), [sn, "cbias%d" % qb], ["PT%d" % pi])

            def pv(bi_):
                bk, nk, c0 = blocks[bi_]
                pi, cc = meta[bi_]
                S.pe(lambda e: e.matmul(ob[0:65, cc:nq], lhsT=VA[hb][0:nk, bk, 0:65], rhs=PT[pi][0:nk, cc:nq], start=(bi_ == 0), stop=(bi_ == nb_ - 1)), ["VA%d" % hb, "PT%d" % pi], [obn])

            LOOK = 2
            for i_ in range(nb_ + LOOK):
                if i_ < nb_:
                    qk(i_)
                if i_ == LOOK and pending:
                    pending.pop()()
                if i_ - LOOK >= 0 and KEL >= 3:
                    pv(i_ - LOOK)
            if pending:
                pending.pop()()
            if KEL < 4:
                return
            pending.append(lambda: finish(qb, nq, ob, obn, h, out_t0))

        def finish(qb, nq, ob, obn, h, out_t0):
            S.act(lambda e: e.copy(out=Osb[qb][0:65, 0:nq], in_=ob[0:65, 0:nq]), [obn], ["Osb%d" % qb])
            S.dve(lambda e: e.reciprocal(out=rt[64:65, 0:nq], in_=Osb[qb][64:65, 0:nq]), ["Osb%d" % qb, "rt"], ["rt"])
            S.dve(lambda e: e.tensor_copy(out=rl3[64:65, 0, 0:nq], in_=rt[64:65, 0:nq]), ["rt", "rl3"], ["rl3"])
            S.dve(lambda e: e.tensor_tensor(out=rt2[64:65, 0:nq], in0=rt[64:65, 0:nq], in1=rl3[64:65, 0, 0:nq], op=ALU.subtract), ["rt", "rl3", "rt2"], ["rt2"])
            S.dve(lambda e: e.tensor_copy(out=rl3[64:65, 1, 0:nq], in_=rt2[64:65, 0:nq]), ["rt2", "rl3"], ["rl3"])
            S.dve(lambda e: e.tensor_tensor(out=rt2[64:65, 0:nq], in0=rt2[64:65, 0:nq], in1=rl3[64:65, 1, 0:nq], op=ALU.subtract), ["rt2", "rl3"], ["rt2"])
            S.dve(lambda e: e.tensor_copy(out=rl3[64:65, 2, 0:nq], in_=rt2[64:65, 0:nq]), ["rt2", "rl3"], ["rl3"])
            bc = pbank[7]
            for j in range(3):
                S.pe(lambda e, j=j: e.matmul(bc[0:64, 0:nq], lhsT=sel[0:65, 0:64], rhs=rl3[0:65, j, 0:nq], start=(j == 0), stop=(j == 2)), ["rl3", "sel"], ["pb7"])
            S.dve(lambda e: e.tensor_tensor(out=ATb[qb][0:64, 0:nq], in0=Osb[qb][0:64, 0:nq], in1=bc[0:64, 0:nq], op=ALU.mult), ["Osb%d" % qb, "pb7", "ATb%d" % qb], ["ATb%d" % qb])
            S.dma(ats[h, :, out_t0:out_t0 + nq], ATb[qb][0:64, 0:nq], reads=["ATb%d" % qb], writes=["ats"])

        hcount = [0]
        def load_head(h, hb):
            rows67(KA[hb], kts[h, :, 0:2 * HALF], 0, 2 * HALF, "k", "KA%d" % hb)
            S.dma(VA[hb][:, :, 0:64], vsb[0:2 * HALF, h * HD:(h + 1) * HD].rearrange("(kb p) d -> p kb d", p=128), writes=["VA%d" % hb])
            S.dma(CK[hb][:, :], ccol[0:2 * HALF, h:h + 1].rearrange("(kb p) o -> p (kb o)", p=128), writes=["CK%d" % hb], allow_slow_non_contiguous=True)

        NHE = int(os.environ.get('KE_H', NH)) if KST >= 7 else 0
        if NHE:
            load_head(0, 0)
        for h in range(NHE):
            hb = hcount[0] % 2; hcount[0] += 1
            if h + 1 < NHE:
                load_head(h + 1, (hb + 1) % 2)
            for j in range(9):
                qs = QS0 + 512 * j
                nfull = qs // 128
                blocks = [(kb, 128, None) for kb in range(nfull)] + [(nfull + d_, 128, 128 * d_) for d_ in range(4)]
                attend(h, hb, qs, 512, blocks, (QS0 // 128) if j == 0 else (HALF // 128), qs)
        for sq in range(int(os.environ.get('KE_S', 2)) if KST >= 7 else 0):
            tn = 2 * HALF + 64 * sq
            for h in range(NH):
                hb = hcount[0] % 2; hcount[0] += 1
                rows67(KA[hb], kcT[sq, h], 0, HALF, "k", "KA%d" % hb)
                rows67(KA[hb], kts[h, :, tn:tn + 64], HALF, HALF + 64, "k", "KA%d" % hb)
                S.dma(VA[hb][:, 0:32, 0:64], vcb[sq, :, h * HD:(h + 1) * HD].rearrange("(kb p) d -> p kb d", p=128), writes=["VA%d" % hb])
                S.dma(VA[hb][0:64, 32, 0:64], vsb[tn:tn + 64, h * HD:(h + 1) * HD], writes=["VA%d" % hb])
                S.dma(CK[hb][:, 0:32], ccc[sq, :, h:h + 1].rearrange("(kb p) o -> p (kb o)", p=128), writes=["CK%d" % hb], allow_slow_non_contiguous=True)
                S.dma(CK[hb][0:64, 32:33], ccol[tn:tn + 64, h:h + 1], writes=["CK%d" % hb], allow_slow_non_contiguous=True)
                blocks = [(kb, 128, None) for kb in range(32)] + [(32, 128, 0)]
                attend(h, hb, tn, 64, blocks, 0, tn)
        if pending:
            pending.pop()()

    if KST >= 7:
        phase_e()

    wo_d = din("w_o", [D, D])

    def phase_f():
        S.barrier(); aoff[0] = mP2
        Wo = sb("Wo", [128, 8, D], BF16)
        m1 = aoff[0]
        load_weight(Wo, wo_d, 8, 128, D, "wo")
        S.barrier(); aoff[0] = m1
        ATt = [sb("ATt%d" % i, [128, 8, 512], BF16) for i in range(2)]
        Xf = [sb("Xf%d" % i, [128, 4, D]) for i in range(2)]
        groups = [(t, 512) for t in range(QS0, 2 * HALF, 512)] + [(2 * HALF, 128)]

        def load(i):
            t0, n = groups[i]; b = i % 2
            for h2 in range(2):
                S.dma(ATt[b][64 * h2:64 * h2 + 64, :, 0:n], ats[:, :, t0:t0 + n].rearrange("(hp h2) d t -> h2 d hp t", h2=2)[h2], writes=["ATt%d" % b])
            S.dma(Xf[b][:, 0:n // 128, :], x2s[t0:t0 + n, :].rearrange("(i p) d -> p i d", p=128), writes=["Xf%d" % b])

        fcount = [0]

        def do(i):
            t0, n = groups[i]; b = i % 2
            if i + 1 < len(groups):
                load(i + 1)
            for ti in range(n // 128):
                for half in range(2):
                    k = fcount[0] % 4; fcount[0] += 1
                    ob = pbank[2 + k]; on_ = "pb%d" % (2 + k)
                    for h in range(8):
                        S.pe(lambda e, h=h, half=half, ob=ob, ti=ti: e.matmul(ob[:, :], lhsT=ATt[b][:, h, ti * 128:(ti + 1) * 128], rhs=Wo[:, h, half * 512:(half + 1) * 512], start=(h == 0), stop=(h == 7)), ["ATt%d" % b], [on_])
                    S.dve(lambda e, half=half, ob=ob, ti=ti: e.tensor_tensor(out=Xf[b][:, ti, half * 512:(half + 1) * 512], in0=Xf[b][:, ti, half * 512:(half + 1) * 512], in1=ob[:, :], op=ALU.add), [on_, "Xf%d" % b], ["Xf%d" % b])
            S.dma(x3s[t0:t0 + n, :].rearrange("(i p) d -> p i d", p=128), Xf[b][:, 0:n // 128, :], reads=["Xf%d" % b], writes=["x3s"])

        load(0)
        for i in range(len(groups)):
            do(i)

    if KST >= 8:
        phase_f()
        groups1 = [dict(t0=HALF - 256, n=256, hist="zero", store=False, conv_out=None)]
        for gi in range(HALF // 256):
            groups1.append(dict(t0=HALF + gi * 256, n=256, hist="carry", store=True, yout=o_yp[gi * 256:(gi + 1) * 256, :],
                                conv_out=o_cvp[1] if gi == HALF // 256 - 1 else None))
        for sq in range(2):
            groups1.append(dict(t0=2 * HALF + 64 * sq, n=64, hist=sq, store=True, yout=o_ys[64 * sq:64 * sq + 64, :], conv_out=o_cvs[1, sq]))
        if not os.environ.get("KSKIPG"):
            ffn_phase(1, x3s, None, groups1, gffn_d[1], True)

    if os.environ.get("KDBG"):
        dbgG = dout("dbgG", [128, D], BF16); dbgX1 = dout("dbgX1", [128, D]); dbgX2 = dout("dbgX2", [128, D])
        S.barrier()
        S.dma(dbgG, gact[2 * HALF:2 * HALF + 128, :])
        S.dma(dbgX1, x1s[2 * HALF:2 * HALF + 128, :])
        S.dma(dbgX2, x2s[2 * HALF:2 * HALF + 128, :])
    S.emit()
    st.close()
    return nc


def _pair_layout(a):
    a = np.asarray(a)
    rest = a.shape[2:]
    a = a.reshape((32, 2, 64) + rest)
    a = np.moveaxis(a, 0, 2)
    return np.ascontiguousarray(a.reshape((128, 32) + rest))


def _from_pair_layout(a):
    a = a.reshape(2, 64, 32)
    return np.ascontiguousarray(np.moveaxis(a, 2, 0).reshape(64, 64))


def _host_inputs(inp):
    f = np.float32
    xpr = np.asarray(inp["x_prompt"], f)
    xsm = np.asarray(inp["x_sample"], f)
    rep = lambda v: np.ascontiguousarray(np.broadcast_to(np.asarray(v, f)[None, :], (128, np.asarray(v).shape[-1])))
    common = dict(
        lam_r=_pair_layout(np.asarray(inp["ssm_a_re"], f)[0]),
        lam_i=_pair_layout(np.asarray(inp["ssm_a_im"], f)[0]),
        lstep=_pair_layout(np.broadcast_to(np.asarray(inp["ssm_log_step"], f)[0][:, None], (64, 64))),
        b_r=_pair_layout(np.asarray(inp["ssm_b_re"], f)[0]),
        b_i=_pair_layout(np.asarray(inp["ssm_b_im"], f)[0]),
        c_r=_pair_layout(np.transpose(np.asarray(inp["ssm_c_re"], f)[0], (0, 2, 1))),
        c_i=_pair_layout(np.transpose(np.asarray(inp["ssm_c_im"], f)[0], (0, 2, 1))),
        gmix=np.stack([rep(inp["norm_mix"][i]) for i in range(2)]),
        gffn=np.stack([rep(inp["norm_ffn"][i]) for i in range(2)]),
        gfin=rep(inp["norm_final"]),
        dskip=rep(inp["ssm_d"][0]),
        ident=np.eye(128, dtype=f),
        tmask=(np.arange(128)[:, None] // 16 <= np.arange(128)[None, :] // 16).astype(f),
        tri=(np.arange(128)[:, None] <= np.arange(128)[None, :]).astype(f),
        cmask=np.where(np.arange(128)[:, None] <= np.arange(128)[None, :], 0.0, -30000.0).astype(f),
        bft=rep(inp["fox_b_f"][0]),
        w_glu=np.ascontiguousarray(np.asarray(inp["ssm_w_glu"], f)[0]),
        w_up=np.asarray(inp["ffn_w_up"], f), w_gate=np.asarray(inp["ffn_w_gate"], f), w_down=np.asarray(inp["ffn_w_down"], f),
        conv_w=np.asarray(inp["ffn_conv_w"], f), conv_b=np.asarray(inp["ffn_conv_b"], f),
        w_qkvf=np.ascontiguousarray(np.asarray(inp["fox_w_qkvf"], f)[0]),
        w_o=np.ascontiguousarray(np.asarray(inp["fox_w_o"], f)[0]),
    )
    maps = []
    for k in range(8):
        b, half = k // 2, k % 2
        prev = xpr[b, 0:HALF] if half == 1 else np.zeros((HALF, D), f)
        own = xpr[b, half * HALF:(half + 1) * HALF]
        m = dict(common)
        m["xp"] = np.ascontiguousarray(np.concatenate([prev, own], 0))
        m["xs"] = np.ascontiguousarray(xsm[2 * k:2 * k + 2].reshape(128, D))
        m["kcache"] = np.ascontiguousarray(np.asarray(inp["cache_fox_k"], f)[0, 2 * k:2 * k + 2].reshape(2, HALF, D))
        m["vcache"] = np.ascontiguousarray(np.asarray(inp["cache_fox_v"], f)[0, 2 * k:2 * k + 2].reshape(2, HALF, D))
        m["lfcache"] = np.ascontiguousarray(np.asarray(inp["cache_fox_logf"], f)[0, 2 * k:2 * k + 2])
        m["pmk"] = np.full((128, 1), 0.0 if half == 1 else -30000.0, f)
        m["cvst"] = np.ascontiguousarray(np.asarray(inp["state_ffn_conv"], f)[:, 2 * k:2 * k + 2])
        m["h0r"] = np.ascontiguousarray(np.stack([_pair_layout(np.asarray(inp["state_ssm_re"], f)[0, 2 * k + s]) for s in range(2)], 1))
        m["h0i"] = np.ascontiguousarray(np.stack([_pair_layout(np.asarray(inp["state_ssm_im"], f)[0, 2 * k + s]) for s in range(2)], 1))
        maps.append(m)
    return maps


_NC_CACHE = {}


def kernel(**inputs):
    maps = _host_inputs(inputs)
    if "nc" not in _NC_CACHE:
        _NC_CACHE["nc"] = build()
    nc = _NC_CACHE["nc"]
    res = run_bass_kernel_spmd(nc, maps, core_ids=list(range(8)))
    R = res.results
    f = np.float32
    y_prompt = np.zeros((4, 8192, D), f); y_sample = np.zeros((16, 64, D), f)
    ssm_re_p = np.zeros((1, 4, 64, 64), f); ssm_im_p = np.zeros((1, 4, 64, 64), f)
    ssm_re_s = np.zeros((1, 16, 64, 64), f); ssm_im_s = np.zeros((1, 16, 64, 64), f)
    k_p = np.zeros((1, 4, 8192, NH, HD), f); v_p = np.zeros((1, 4, 8192, NH, HD), f); lf_p = np.zeros((1, 4, 8192, NH), f)
    k_s = np.zeros((1, 16, 64, NH, HD), f); v_s = np.zeros((1, 16, 64, NH, HD), f); lf_s = np.zeros((1, 16, 64, NH), f)
    cv_p = np.zeros((2, 4, 2, DFF), f); cv_s = np.zeros((2, 16, 2, DFF), f)
    for k in range(8):
        b, half = k // 2, k % 2
        r = R[k]
        if half == 1:
            ssm_re_p[0, b] = _from_pair_layout(r["o_ssm_p"][0]); ssm_im_p[0, b] = _from_pair_layout(r["o_ssm_p"][1])
        if "o_cvp" in r:
            if half == 1:
                cv_p[:, b] = r["o_cvp"]
            cv_s[:, 2 * k:2 * k + 2] = r["o_cvs"]
        if "o_kp" in r:
            sl = slice(half * HALF, (half + 1) * HALF)
            k_p[0, b, sl] = r["o_kp"].reshape(HALF, NH, HD); v_p[0, b, sl] = r["o_vp"].reshape(HALF, NH, HD); lf_p[0, b, sl] = r["o_lfp"]
            k_s[0, 2 * k:2 * k + 2] = r["o_ks"].reshape(2, 64, NH, HD); v_s[0, 2 * k:2 * k + 2] = r["o_vs"].reshape(2, 64, NH, HD)
            lf_s[0, 2 * k:2 * k + 2] = r["o_lfs"].reshape(2, 64, NH)
        if "o_yp" in r:
            y_prompt[b, half * HALF:(half + 1) * HALF] = r["o_yp"]
            y_sample[2 * k:2 * k + 2] = r["o_ys"].reshape(2, 64, D)
        for s in range(2):
            ssm_re_s[0, 2 * k + s] = _from_pair_layout(r["o_ssm_s"][s, 0]); ssm_im_s[0, 2 * k + s] = _from_pair_layout(r["o_ssm_s"][s, 1])
    kernel.last = R
    return (y_prompt, y_sample, ssm_re_p, ssm_im_p, k_p, v_p, lf_p, cv_p, ssm_re_s, ssm_im_s, k_s, v_s, lf_s, cv_s)
```
